# Optimizing a Trainium2 kernel written in Bass

```python
import functools
import jax, jax.numpy as jnp
from jax import lax
import numpy as np

D_MODEL = 2048
BATCH = 2
SEQ = 4096
DEPTH = 2
DEC_BATCH = 32
DEC_SEQ = 1
PAST_LEN = 16384
PAGE_SIZE = 128

ATT_WIDTH = D_MODEL // 2
RWKV_WIDTH = D_MODEL - ATT_WIDTH
HEAD_DIM = 64
N_Q_HEADS = ATT_WIDTH // HEAD_DIM
N_KV_HEADS = 4
GQA_GROUP = N_Q_HEADS // N_KV_HEADS
KV_WIDTH = N_KV_HEADS * HEAD_DIM
WINDOW = 128
BLOCK = WINDOW
ROT_DIM = HEAD_DIM // 4
ROPE_THETA = 500000.0
RW_HEAD = 64
RW_HEADS = RWKV_WIDTH // RW_HEAD
DECAY_LORA = 64
ICLR_LORA = 64
NORM_EPS = 1e-5
GN_EPS = 64e-5
NEG_BIG = -1e30

Q_OFF = 0
KA_OFF = Q_OFF + ATT_WIDTH
VA_OFF = KA_OFF + KV_WIDTH
R_OFF = VA_OFF + KV_WIDTH
KR_OFF = R_OFF + RWKV_WIDTH
VR_OFF = KR_OFF + RWKV_WIDTH
WD_OFF = VR_OFF + RWKV_WIDTH
AD_OFF = WD_OFF + DECAY_LORA
GA_OFF = AD_OFF + ICLR_LORA
GR_OFF = GA_OFF + ATT_WIDTH
IN_WIDTH = GR_OFF + RWKV_WIDTH
SHIFT_DIM = GA_OFF - R_OFF

kernel_name = "hymba_swa_sink_rwkv7_adaln_step"


def rms_norm(x, g):
    xf = x.astype(jnp.float32)
    y = xf * lax.rsqrt(jnp.mean(xf * xf, axis=-1, keepdims=True) + NORM_EPS)
    return (y * g.astype(jnp.float32)).astype(x.dtype)


def apply_partial_rope(x, pos):
    half = ROT_DIM // 2
    inv_freq = ROPE_THETA ** (-jnp.arange(half, dtype=jnp.float32) * (2.0 / ROT_DIM))
    ang = pos.astype(jnp.float32)[:, None] * inv_freq[None, :]
    cos = jnp.cos(ang)[:, None, :]
    sin = jnp.sin(ang)[:, None, :]
    xf = x.astype(jnp.float32)
    x1 = xf[..., :half]
    x2 = xf[..., half:ROT_DIM]
    out = jnp.concatenate([x1 * cos - x2 * sin, x2 * cos + x1 * sin, xf[..., ROT_DIM:]], axis=-1)
    return out.astype(x.dtype)


def sink_softmax(scores, mask, sink):
    s = jnp.where(mask, scores.astype(jnp.float32), NEG_BIG)
    m = jnp.maximum(jnp.max(s, axis=-1, keepdims=True), sink)
    p = jnp.exp(s - m)
    return p / (jnp.sum(p, axis=-1, keepdims=True) + jnp.exp(sink - m))


def attention_prompt(q, k, v, sinks):
    Bn, T = q.shape[0], q.shape[1]
    nb = T // BLOCK
    qb = q.reshape(Bn, nb, BLOCK, N_KV_HEADS, GQA_GROUP, HEAD_DIM)

    def band(t):
        tb = t.reshape(Bn, nb, BLOCK, N_KV_HEADS, HEAD_DIM)
        prev = jnp.concatenate([jnp.zeros_like(tb[:, :1]), tb[:, :-1]], axis=1)
        return jnp.concatenate([prev, tb], axis=2)

    kb, vb = band(k), band(v)
    scores = jnp.einsum('bnqkgd,bnskd->bnkgqs', qb, kb) * (HEAD_DIM ** -0.5)
    qi = jnp.arange(BLOCK)[:, None]
    kj = jnp.arange(2 * BLOCK)[None, :]
    rel = BLOCK + qi - kj
    blk = jnp.arange(nb)[:, None, None]
    mask = (rel >= 0) & (rel <= WINDOW) & ((blk > 0) | (kj >= BLOCK))
    sink = sinks.astype(jnp.float32).reshape(1, 1, N_KV_HEADS, GQA_GROUP, 1, 1)
    p = sink_softmax(scores, mask[None, :, None, None], sink).astype(v.dtype)
    out = jnp.einsum('bnkgqs,bnskd->bnqkgd', p, vb).reshape(Bn, T, ATT_WIDTH)
    return out, k[:, -WINDOW:], v[:, -WINDOW:]


def attention_sample(q, k, v, sinks, ck, cv, pos):
    Bd, S = q.shape[0], q.shape[1]
    kall = jnp.concatenate([ck.astype(k.dtype), k], axis=1)
    vall = jnp.concatenate([cv.astype(v.dtype), v], axis=1)
    kpos = jnp.concatenate([PAST_LEN - WINDOW + jnp.arange(WINDOW, dtype=jnp.int32), pos])
    rel = pos[:, None] - kpos[None, :]
    mask = (rel >= 0) & (rel <= WINDOW)
    qg = q.reshape(Bd, S, N_KV_HEADS, GQA_GROUP, HEAD_DIM)
    scores = jnp.einsum('bqkgd,bskd->bkgqs', qg, kall) * (HEAD_DIM ** -0.5)
    sink = sinks.astype(jnp.float32).reshape(1, N_KV_HEADS, GQA_GROUP, 1, 1)
    p = sink_softmax(scores, mask[None, None, None], sink).astype(v.dtype)
    out = jnp.einsum('bkgqs,bskd->bqkgd', p, vall).reshape(Bd, S, ATT_WIDTH)
    return out, kall[:, -WINDOW:], vall[:, -WINDOW:]


def wkv_recurrence(r, decay, k, v, kk, a, S0):
    def step(S, inp):
        r_t, w_t, k_t, v_t, kk_t, a_t = inp
        sa = jnp.einsum('bhij,bhj->bhi', S, -kk_t)
        S = (S * w_t[:, :, None, :] + sa[..., None] * (kk_t * a_t)[:, :, None, :]
             + v_t[..., None] * k_t[:, :, None, :])
        return S, jnp.einsum('bhij,bhj->bhi', S, r_t)

    xs = tuple(jnp.moveaxis(t, 1, 0) for t in (r, decay, k, v, kk, a))
    S_T, ys = lax.scan(step, S0, xs)
    return jnp.moveaxis(ys, 0, 1), S_T


def hybrid_layer(x, c, pos, attend, S0, shift_prev, norm_g, w_ada, b_ada, w_in, mu_shift,
                 w0, w_decay, a0, w_iclr, k_k, k_a, r_k, ln_w, ln_b, sinks, w_out):
    f32 = jnp.float32
    Bn, T = x.shape[0], x.shape[1]
    mod = jax.nn.silu(c) @ w_ada + b_ada
    shift, scale, gate = jnp.split(mod, 3, axis=-1)
    h = rms_norm(x, norm_g) * (1 + scale[:, None, :]) + shift[:, None, :]
    proj = h @ w_in

    q = apply_partial_rope(proj[..., Q_OFF:KA_OFF].reshape(Bn, T, N_Q_HEADS, HEAD_DIM), pos)
    k = apply_partial_rope(proj[..., KA_OFF:VA_OFF].reshape(Bn, T, N_KV_HEADS, HEAD_DIM), pos)
    v = proj[..., VA_OFF:R_OFF].reshape(Bn, T, N_KV_HEADS, HEAD_DIM)
    att, k_buf, v_buf = attend(q, k, v, sinks)

    cur = proj[..., R_OFF:GA_OFF]
    prev = jnp.concatenate([shift_prev[:, None, :].astype(cur.dtype), cur[:, :-1]], axis=1)
    mixed = (cur + (prev - cur) * mu_shift).astype(f32)
    r = mixed[..., 0:KR_OFF - R_OFF]
    kr = mixed[..., KR_OFF - R_OFF:VR_OFF - R_OFF]
    vr = mixed[..., VR_OFF - R_OFF:WD_OFF - R_OFF]
    wd = mixed[..., WD_OFF - R_OFF:AD_OFF - R_OFF]
    ad = mixed[..., AD_OFF - R_OFF:]
    w_log = -jax.nn.softplus(-(w0.astype(f32) + jnp.tanh(wd) @ w_decay.astype(f32))) - 0.5
    decay = jnp.exp(-jnp.exp(w_log))
    a = jax.nn.sigmoid(a0.astype(f32) + ad @ w_iclr.astype(f32))
    heads = lambda t: t.reshape(Bn, T, RW_HEADS, RW_HEAD)
    kk = heads(kr * k_k.astype(f32))
    kk = kk / jnp.maximum(jnp.sqrt(jnp.sum(kk * kk, axis=-1, keepdims=True)), 1e-12)
    k_eff = kr * (1 + (a - 1) * k_a.astype(f32))
    r_h, k_h, v_h = heads(r), heads(k_eff), heads(vr)
    ys, S_T = wkv_recurrence(r_h, heads(decay), k_h, v_h, kk, heads(a), S0.astype(f32))
    mu = jnp.mean(ys, axis=-1, keepdims=True)
    var = jnp.mean(jnp.square(ys - mu), axis=-1, keepdims=True)
    yn = ((ys - mu) * lax.rsqrt(var + GN_EPS)).reshape(Bn, T, RWKV_WIDTH)
    yn = yn * ln_w.astype(f32) + ln_b.astype(f32)
    bonus = jnp.sum(r_h * k_h * r_k.astype(f32), axis=-1, keepdims=True) * v_h
    y_rw = (yn + bonus.reshape(Bn, T, RWKV_WIDTH)).astype(x.dtype)

    g_att = jax.nn.silu(proj[..., GA_OFF:GR_OFF])
    g_rw = jax.nn.silu(proj[..., GR_OFF:IN_WIDTH])
    out = jnp.concatenate([att * g_att, y_rw * g_rw], axis=-1) @ w_out
    x = x + gate[:, None, :] * out
    return x, k_buf, v_buf, S_T, cur[:, -1]


def setup_inputs(seed: int = 0) -> dict:
    key = jax.random.key(seed)
    ks = jax.random.split(key, 25)
    f32 = jnp.float32

    def nrm(k, shape, s):
        return jax.random.normal(k, shape, f32) * s

    return {
        "x_prompt": nrm(ks[0], (BATCH, SEQ, D_MODEL), 1.0),
        "x_sample": nrm(ks[1], (DEC_BATCH, DEC_SEQ, D_MODEL), 1.0),
        "cache_k": nrm(ks[2], (DEPTH, DEC_BATCH, WINDOW, N_KV_HEADS, HEAD_DIM), 1.0),
        "cache_v": nrm(ks[3], (DEPTH, DEC_BATCH, WINDOW, N_KV_HEADS, HEAD_DIM), 1.0),
        "state_wkv": nrm(ks[4], (DEPTH, DEC_BATCH, RW_HEADS, RW_HEAD, RW_HEAD), 1.0),
        "state_shift": nrm(ks[5], (DEPTH, DEC_BATCH, SHIFT_DIM), 1.0),
        "c_prompt": nrm(ks[6], (BATCH, D_MODEL), 1.0),
        "c_sample": nrm(ks[7], (DEC_BATCH, D_MODEL), 1.0),
        "norm_g": 1.0 + nrm(ks[8], (DEPTH, D_MODEL), 0.01),
        "w_ada": nrm(ks[9], (DEPTH, D_MODEL, 3 * D_MODEL), 0.5 * D_MODEL ** -0.5),
        "b_ada": nrm(ks[10], (DEPTH, 3 * D_MODEL), 0.01),
        "w_in": nrm(ks[11], (DEPTH, D_MODEL, IN_WIDTH), D_MODEL ** -0.5),
        "mu_shift": jax.random.uniform(ks[12], (DEPTH, SHIFT_DIM), f32),
        "w0": nrm(ks[13], (DEPTH, RWKV_WIDTH), 0.5),
        "w_decay": nrm(ks[14], (DEPTH, DECAY_LORA, RWKV_WIDTH), 0.1),
        "a0": nrm(ks[15], (DEPTH, RWKV_WIDTH), 0.1),
        "w_iclr": nrm(ks[16], (DEPTH, ICLR_LORA, RWKV_WIDTH), 0.1),
        "k_k": 1.0 + nrm(ks[17], (DEPTH, RWKV_WIDTH), 0.1),
        "k_a": 1.0 + nrm(ks[18], (DEPTH, RWKV_WIDTH), 0.1),
        "r_k": nrm(ks[19], (DEPTH, RW_HEADS, RW_HEAD), 0.1),
        "ln_w": 1.0 + nrm(ks[20], (DEPTH, RWKV_WIDTH), 0.1),
        "ln_b": nrm(ks[21], (DEPTH, RWKV_WIDTH), 0.01),
        "sinks": nrm(ks[22], (DEPTH, N_Q_HEADS), 0.5),
        "w_out": nrm(ks[23], (DEPTH, D_MODEL, D_MODEL), D_MODEL ** -0.5),
        "final_g": 1.0 + nrm(ks[24], (D_MODEL,), 0.01),
    }


def reference(x_prompt, x_sample, cache_k, cache_v, state_wkv, state_shift, c_prompt, c_sample,
              norm_g, w_ada, b_ada, w_in, mu_shift, w0, w_decay, a0, w_iclr, k_k, k_a, r_k,
              ln_w, ln_b, sinks, w_out, final_g):
    Bp, Tp = x_prompt.shape[0], x_prompt.shape[1]
    Ts = x_sample.shape[1]
    pos_p = jnp.arange(Tp, dtype=jnp.int32)
    pos_s = PAST_LEN + jnp.arange(Ts, dtype=jnp.int32)
    S0_p = jnp.zeros((Bp, RW_HEADS, RW_HEAD, RW_HEAD), jnp.float32)
    shift0_p = jnp.zeros((Bp, SHIFT_DIM), x_prompt.dtype)

    hp, hs = x_prompt, x_sample
    kp_l, vp_l, sp_l, shp_l = [], [], [], []
    ks_l, vs_l, ss_l, shs_l = [], [], [], []
    for l in range(DEPTH):
        lw = (norm_g[l], w_ada[l], b_ada[l], w_in[l], mu_shift[l], w0[l], w_decay[l], a0[l],
              w_iclr[l], k_k[l], k_a[l], r_k[l], ln_w[l], ln_b[l], sinks[l], w_out[l])
        hp, kb, vb, S_T, sh = hybrid_layer(hp, c_prompt, pos_p, attention_prompt, S0_p, shift0_p, *lw)
        kp_l.append(kb); vp_l.append(vb); sp_l.append(S_T); shp_l.append(sh)
        attend_s = functools.partial(attention_sample, ck=cache_k[l], cv=cache_v[l], pos=pos_s)
        hs, kb, vb, S_T, sh = hybrid_layer(hs, c_sample, pos_s, attend_s, state_wkv[l], state_shift[l], *lw)
        ks_l.append(kb); vs_l.append(vb); ss_l.append(S_T); shs_l.append(sh)

    y_prompt = rms_norm(hp, final_g)
    y_sample = rms_norm(hs, final_g)
    new_cache_k_prompt = jnp.stack(kp_l)
    new_cache_v_prompt = jnp.stack(vp_l)
    new_state_wkv_prompt = jnp.stack(sp_l)
    new_state_shift_prompt = jnp.stack(shp_l)
    new_cache_k_sample = jnp.stack(ks_l)
    new_cache_v_sample = jnp.stack(vs_l)
    new_state_wkv_sample = jnp.stack(ss_l)
    new_state_shift_sample = jnp.stack(shs_l)
    return (y_prompt, y_sample, new_cache_k_prompt, new_cache_v_prompt, new_state_wkv_prompt,
            new_state_shift_prompt, new_cache_k_sample, new_cache_v_sample, new_state_wkv_sample,
            new_state_shift_sample)
```

```python
import contextlib
import numpy as np
import concourse.bass as bass
import concourse.mybir as mybir
from concourse.bass_utils import run_bass_kernel_spmd

F32 = mybir.dt.float32
BF16 = mybir.dt.bfloat16
AF = mybir.ActivationFunctionType
ALU = mybir.AluOpType
AX = mybir.AxisListType

D = 2048
SEQ = 4096
NL = 2
HD = 64
WIN = 128
Q_OFF = 0
KA_OFF = 1024
VA_OFF = 1280
R_OFF = 1536
KR_OFF = 2560
VR_OFF = 3584
WD_OFF = 4608
AD_OFF = 4672
GA_OFF = 4736
GR_OFF = 5760
NCT = 14
WCOLS = NCT * 128 + 64
G = 256
NG = SEQ // G
CH = 64
CDEC = 0.6065306597126334
NEG = -30000.0
HC = 1088
ZC = 4 * SEQ + 64

ENGS = ("pe", "act", "dve", "pool", "sp")


class Op:
    __slots__ = ("eng", "fn", "deps", "signal", "sigval", "dma", "semkey", "cc", "epoch", "nb")

    def __init__(self, eng, fn, dma=False, semkey=None, cc=False):
        self.eng = eng
        self.fn = fn
        self.deps = []
        self.signal = False
        self.sigval = 0
        self.dma = dma
        self.semkey = semkey
        self.cc = cc
        self.epoch = 0
        self.nb = False


class Prog:
    def __init__(self, nc):
        self.nc = nc
        self.ops = {e: [] for e in ENGS}
        self.last_w = {}
        self.readers = {}
        self.all_ops = []
        self.last_dma = {}
        self.epoch = 0

    def op(self, eng, fn, reads=(), writes=(), dma=False, cc=False, semkey=None, nb=False):
        o = Op(eng, fn, dma=dma or cc, semkey=semkey, cc=cc)
        o.nb = nb
        o.epoch = self.epoch if eng == "pe" else 0
        deps = []
        for k in reads:
            w = self.last_w.get(k)
            if w is not None:
                deps.append(w)
        for k in writes:
            w = self.last_w.get(k)
            if w is not None:
                deps.append(w)
            deps.extend(self.readers.get(k, ()))
        seen = set()
        for d in deps:
            if id(d) in seen or d is o:
                continue
            seen.add(id(d))
            if (not d.dma) and d.eng == "pe" and eng == "pe" and not o.dma:
                continue
            o.deps.append(d)
            d.signal = True
        implied = set()
        for d2 in o.deps:
            for x_ in d2.deps:
                implied.add(id(x_))
        if implied:
            o.deps = [d for d in o.deps if id(d) not in implied]
        for k in reads:
            self.readers.setdefault(k, []).append(o)
        for k in writes:
            self.last_w[k] = o
            self.readers[k] = []
        if o.dma:
            if o.semkey is None:
                o.semkey = ("w",) + tuple(writes)
            HOT = ("hT", "Win", "ob", "ot", "xo", "x1st", "hTt_st", "yst", "xt", "wst", "Wo")
            if (not o.cc) and isinstance(o.semkey, str) and o.semkey.rstrip("0123456789_") in HOT:
                pass
            elif not o.cc:
                import zlib
                if eng == "pool":
                    o.semkey = ("dmapool_sw", zlib.crc32(repr(o.semkey).encode()) % 12)
                else:
                    o.semkey = ("dmapool_hw", zlib.crc32(repr(o.semkey).encode()) % 48)
            prev = self.last_dma.get(o.semkey)
            if prev is not None and all(prev is not d for d in o.deps):
                o.deps.append(prev)
                prev.signal = True
            self.last_dma[o.semkey] = o
        self.ops[eng].append(o)
        self.all_ops.append(o)
        return o

    def pe(self, fn, reads=(), writes=()):
        return self.op("pe", fn, reads, writes)

    def act(self, fn, reads=(), writes=()):
        return self.op("act", fn, reads, writes)

    def dve(self, fn, reads=(), writes=()):
        return self.op("dve", fn, reads, writes)

    def pool(self, fn, reads=(), writes=()):
        return self.op("pool", fn, reads, writes)

    def dma(self, fn, reads=(), writes=(), eng="sp", semkey=None):
        return self.op(eng, fn, reads, writes, dma=True, semkey=semkey)

    def barrier(self):
        lasts = []
        for e in ENGS:
            for o_ in reversed(self.ops[e]):
                if o_.fn is not None and not o_.nb:
                    lasts.append(o_)
                    break
        pend = [o for o in self.all_ops if o.dma and not o.signal and not o.nb]
        keep = {k: w for k, w in self.last_w.items() if w.nb}
        self.last_w = keep
        self.readers = {}
        for e in ENGS:
            o = Op(e, None)
            for d in lasts + pend:
                o.deps.append(d)
                d.signal = True
            self.ops[e].append(o)
            self.all_ops.append(o)

    def emit(self):
        nc = self.nc
        fin = Op("sp", None)
        for o in self.all_ops:
            if o.dma and not o.signal:
                fin.deps.append(o)
                o.signal = True
        cnt = {}
        keys = []
        for o in self.all_ops:
            if not o.signal:
                continue
            key = o.semkey if o.dma else ("eng", o.eng, o.epoch)
            if key not in cnt:
                cnt[key] = 0
                keys.append(key)
            cnt[key] += (16 if (o.dma and not o.cc) else 1)
            o.sigval = cnt[key]
        print("[mk] ops: " + ", ".join(f"{e}={len(self.ops[e])}" for e in ENGS) +
              f"; sems={len(cnt)}; maxval={max(cnt.values()) if cnt else 0}", flush=True)
        with contextlib.ExitStack() as st:
            st.enter_context(nc.allow_non_contiguous_dma(reason="small strided layout transfers"))
            sems = {}
            for i, key in enumerate(keys):
                sems[key] = st.enter_context(nc.semaphore(f"s{i}"))
            block = st.enter_context(nc.Block())

            def run(engname, eng):
                waited = {}
                lst = list(self.ops[engname])
                if engname == "sp":
                    lst = lst + [fin]
                for o in lst:
                    need = {}
                    for d in o.deps:
                        key = d.semkey if d.dma else ("eng", d.eng, d.epoch)
                        if d.sigval > need.get(key, 0):
                            need[key] = d.sigval
                    for key, v in need.items():
                        if waited.get(key, 0) >= v:
                            continue
                        eng.wait_ge(sems[key], v)
                        waited[key] = v
                    if o.fn is None:
                        continue
                    inst = o.fn(eng)
                    if o.signal:
                        key = o.semkey if o.dma else ("eng", o.eng, o.epoch)
                        inst.then_inc(sems[key], 16 if (o.dma and not o.cc) else 1)

            @block.sync
            def _(e):
                run("sp", e)

            @block.scalar
            def _(e):
                run("act", e)

            @block.vector
            def _(e):
                run("dve", e)

            @block.gpsimd
            def _(e):
                run("pool", e)

            @block.tensor
            def _(e):
                run("pe", e)


def col_index(r):
    cols = []
    for i in range(2):
        for hh in range(2):
            cols += list(range(Q_OFF + (4 * r + 2 * i + hh) * 64, Q_OFF + (4 * r + 2 * i + hh + 1) * 64))
    kc = list(range(KA_OFF + r * 64, KA_OFF + (r + 1) * 64))
    cols += kc + kc
    for off in (R_OFF, KR_OFF, VR_OFF):
        for i in range(2):
            cols += list(range(off + (4 * r + 2 * i) * 64, off + (4 * r + 2 * i + 2) * 64))
    cols += list(range(WD_OFF, WD_OFF + 64)) + list(range(AD_OFF, AD_OFF + 64))
    for off in (GA_OFF, GR_OFF):
        for i in range(2):
            cols += list(range(off + (4 * r + 2 * i) * 64, off + (4 * r + 2 * i + 2) * 64))
    cols += list(range(VA_OFF + r * 64, VA_OFF + (r + 1) * 64))
    assert len(cols) == WCOLS
    return np.array(cols)


def wout_row_perm():
    rows = []
    for r in range(4):
        rows += list(range(256 * r, 256 * r + 256))
        rows += list(range(1024 + 256 * r, 1024 + 256 * r + 256))
    return np.array(rows)


def make_consts():
    import ml_dtypes
    bf = ml_dtypes.bfloat16
    c = {}
    p = np.arange(128)
    d = p % 64
    inv_freq = (np.float32(500000.0) ** (-np.arange(8, dtype=np.float32) * np.float32(0.125))).astype(np.float32)
    pos = np.arange(SEQ, dtype=np.float32)
    ang = (pos[None, :] * inv_freq[(d % 8)][:, None]).astype(np.float32)
    rot = (d < 16)[:, None]
    c["cosT"] = np.where(rot, np.cos(ang), 1.0).astype(np.float32)
    c["sinT"] = np.where(rot, np.sin(ang), 0.0).astype(np.float32)
    angs = (np.float32(16384.0) * inv_freq[(d % 8)]).astype(np.float32)
    cs = np.stack([np.where(d < 16, np.cos(angs), 1.0), np.where(d < 16, np.sin(angs), 0.0)], 1)
    c["cs_s"] = cs.astype(np.float32)
    prot = np.zeros((128, 128), np.float32)
    for m in range(128):
        dm = m % 64
        if dm < 8:
            prot[m + 8, m] = -1.0
        elif dm < 16:
            prot[m - 8, m] = 1.0
    c["prot"] = prot.astype(bf)
    c["ident"] = np.eye(128, dtype=np.float32).astype(bf)
    c["identf"] = np.eye(128, dtype=np.float32)
    qi = np.arange(128)[:, None]
    kj = np.arange(256)[None, :]
    valid = (kj >= qi) & (kj <= qi + 128)
    c["maskb"] = np.where(valid, 0.0, NEG).astype(np.float32)
    c["maskb0"] = np.where(valid & (kj >= 128), 0.0, NEG).astype(np.float32)
    row = (np.arange(128) % 64)[:, None]
    col = np.arange(64)[None, :]
    strict = (row < col).astype(np.float32)
    incl = (row <= col).astype(np.float32)
    low = (row > col).astype(np.float32)
    c["maskG"] = np.concatenate([strict, incl, strict, incl, low], 1).astype(np.float32)
    c["id64"] = (row == col).astype(np.float32)
    c["bones"] = ((p[:, None] // 64) == (p[None, :] // 64)).astype(np.float32).astype(bf)
    c["ones"] = np.ones((128, 64), np.float32)
    return c


CONST_DT = {"cosT": F32, "sinT": F32, "cs_s": F32, "prot": BF16, "ident": BF16, "identf": F32,
            "maskb": F32, "maskb0": F32, "maskG": F32, "id64": F32, "bones": BF16, "ones": F32}
NPAR = 7


def build(consts, stop_after=None):
    nc = bass.Bass("TRN2", target_bir_lowering=False)
    P = Prog(nc)

    def din(name, shape, dt=F32):
        return nc.dram_tensor(name, list(shape), dt, kind="ExternalInput").ap()

    def dout(name, shape, dt=F32):
        return nc.dram_tensor(name, list(shape), dt, kind="ExternalOutput").ap()

    def dscr(name, shape, dt=F32):
        return nc.dram_tensor(name, list(shape), dt)

    sb_bytes = [0]

    def sb(name, shape, dt=F32):
        n = 1
        for s in shape[1:]:
            n *= s
        sb_bytes[0] += n * (4 if dt == F32 else 2)
        return nc.alloc_sbuf_tensor(name, list(shape), dt)

    x_own = din("x_own", [1024, D])
    xs_own = din("xs_own", [4, D])
    cT_in = din("cT", [128, 16, 17])
    wada = din("wada", [NL, D, 1536])
    bada = din("bada", [NL, 1536])
    ng_col = din("ng_col", [128, NL, 16])
    ng_row = din("ng_row", [NL, D])
    fg_row = din("fg_row", [1, D])
    w_in = din("w_in", [NL, D, WCOLS])
    mu_col = din("mu_col", [NL, 128, 7])
    prev_s = din("prev_s", [NL, 128, 7, 16])
    rpar = din("rpar", [NL, 128, NPAR, 2])
    loraw = din("loraw", [NL, 128, 2, 128])
    sinks_b = din("sinks_b", [NL, 128, 4])
    w_out = din("w_out", [NL, 512, D])
    ck_in = din("ck_in", [NL, 16, 128, 64])
    cv_in = din("cv_in", [NL, 16, 128, 64])
    st_in = din("st_in", [NL, 16, 4, 64, 64])
    selT_in = din("selT", [16, 4])
    shpar_in = din("shpar", [NL, 64, 193])
    cd = {k: din("c_" + k, v.shape, CONST_DT[k]) for k, v in consts.items()}
    y_own = dout("y_own", [1024, D])
    ys_own = dout("ys_own", [4, D])
    ckp = dout("ckp", [NL, 128, 64])
    cvp = dout("cvp", [NL, 128, 64])
    swp = dout("swp", [NL, 128, 2, 64])
    shp = dout("shp", [NL, 128, 7])
    cks = dout("cks", [NL, 16, 128, 64])
    cvs = dout("cvs", [NL, 16, 128, 64])
    sws = dout("sws", [NL, 16, 4, 64, 64])
    shs = dout("shs", [NL, 128, 7, 16])
    agi_mod = dscr("agi_mod", [17, NL * 1536])
    ago_mod = dscr("ago_mod", [4 * 17, NL * 1536])
    agi_h = [[dscr(f"agi_h{l}_{j}", [128, 2048], BF16) for j in range(8)] for l in range(NL)]
    agi_hs = [dscr(f"agi_hs{l}", [128, 64], BF16) for l in range(NL)]
    ago_hs = [dscr(f"ago_hs{l}", [512, 64], BF16) for l in range(NL)]
    ago_h = [[dscr(f"ago_h{l}_{j}", [4 * 128, 2048], BF16) for j in range(8)] for l in range(NL)]
    agi_z = [dscr(f"agi_z{l}", [128, ZC], BF16) for l in range(NL)]
    ago_z = [dscr(f"ago_z{l}", [4 * 128, ZC], BF16) for l in range(NL)]
    x1_d = dscr("x1_d", [1028, D])
    smod_d = dscr("smod_d", [NL, 3, 4, D])
    smp_d = [dscr(f"smp_d{l}", [4, 16, 9, 64]) for l in range(NL)]
    kn_d = [dscr(f"kn_d{l}", [16, 64]) for l in range(NL)]
    vn_d = [dscr(f"vn_d{l}", [16, 64]) for l in range(NL)]
    zs_d = [dscr(f"zs_d{l}", [4, 16, 2, 64]) for l in range(NL)]
    rs_in = [[dscr(f"rs_in{l}_{q}", [4 * 1024, 1024]) for q in range(2)] for l in range(NL)]
    rss_in = [dscr(f"rss_in{l}", [16, D]) for l in range(NL)]
    rss_out = [dscr(f"rss_out{l}", [4, D]) for l in range(NL)]
    rs_out = [[dscr(f"rs_out{l}_{q}", [1024, 1024]) for q in range(2)] for l in range(NL)]
    RG = [[0, 1, 2, 3], [4, 5, 6, 7]]

    cs = {}
    for k, v in consts.items():
        if k in ("cosT", "sinT"):
            continue
        cs[k] = sb("k_" + k, v.shape, CONST_DT[k])
        P.dma(lambda e, k=k: e.dma_start(out=cs[k][:], in_=cd[k]), writes=["k_" + k])
    WBIG = sb("WBIG", [128, 16 * 2048], BF16)
    SCR_N = 30000
    SCR = sb("SCR", [128, SCR_N], F32)
    modT = sb("modT", [128, NL, 48])
    ngc = sb("ngc", [128, NL, 16])
    Acol = sb("Acol", [128, NL, 16])
    P.dma(lambda e: e.dma_start(out=ngc[:], in_=ng_col), writes=["ngc"])
    epsc = sb("epsc", [128, 2])
    P.pool(lambda e: e.memset(epsc[:, 0:1], 1e-5), writes=["epsc"])
    P.pool(lambda e: e.memset(epsc[:, 1:2], 64e-5), writes=["epsc"])
    Wo = sb("Wo", [128, 4, D], BF16)
    psum = [nc.alloc_psum_tensor(f"ps{i}", [128, 512], F32) for i in range(8)]

    class Carve:
        def __init__(self):
            self.off = 0

        def f32(self, shape):
            n = int(np.prod(shape[1:]))
            ap = SCR[0:shape[0], self.off:self.off + n]
            self.off += n
            assert self.off <= SCR_N, self.off
            return ap, shape

        def bf(self, shape):
            n = int(np.prod(shape[1:]))
            w = (n + 1) // 2
            ap = SCR[0:shape[0], self.off:self.off + w].bitcast(BF16)[:, 0:n]
            self.off += w
            assert self.off <= SCR_N, self.off
            return ap, shape

    def v(t, pat=None, **kw):
        ap, shape = t
        if len(shape) == 2:
            return ap
        names = " ".join(f"d{i}" for i in range(1, len(shape)))
        kws = {f"d{i}": shape[i] for i in range(1, len(shape))}
        return ap.rearrange(f"p ({names}) -> p {names}", **kws)

    cv_ = Carve()
    cT = cv_.f32([128, 16, 17])
    wst = [cv_.f32([128, 1536]) for _ in range(4)]
    badab = cv_.f32([17, NL, 1536])
    modsb = cv_.f32([17, NL, 1536])
    P.dma(lambda e: e.dma_start(out=v(cT), in_=cT_in), writes=["cT"])
    P.act(lambda e: e.activation(out=cT[0], in_=cT[0], func=AF.Silu), reads=["cT"], writes=["cT"])
    for l in range(NL):
        P.dma(lambda e, l=l: e.dma_start(out=v(badab)[:, l, :], in_=bada[l:l + 1, :].partition_broadcast(17)[:, 0, :]),
              writes=["badab"], eng="act", semkey="badab")
    it = 0
    for l in range(NL):
        for k in range(16):
            s = it % 4
            it += 1
            P.dma(lambda e, l=l, k=k, s=s: e.dma_start(out=wst[s][0], in_=wada[l, k * 128:(k + 1) * 128, :]),
                  writes=[f"wst{s}"], eng=("sp" if s % 2 == 0 else "act"), semkey=f"wst{s}")
            for cg in range(3):
                P.pe(lambda e, k=k, s=s, cg=cg: e.matmul(psum[cg][0:17, :], lhsT=v(cT)[:, k, :],
                                                          rhs=wst[s][0][:, cg * 512:(cg + 1) * 512],
                                                          start=(k == 0), stop=(k == 15)),
                     reads=["cT", f"wst{s}"], writes=[f"ps{cg}"])
        for cg in range(3):
            P.dve(lambda e, l=l, cg=cg: e.tensor_tensor(out=v(modsb)[:, l, cg * 512:(cg + 1) * 512], in0=psum[cg][0:17, :],
                                                        in1=v(badab)[:, l, cg * 512:(cg + 1) * 512], op=ALU.add),
                  reads=[f"ps{cg}", "badab"], writes=["modsb"])
    P.dma(lambda e: e.dma_start(out=agi_mod.ap(), in_=modsb[0]), reads=["modsb"], writes=["agi_mod"])
    P.op("pool", lambda e: e.collective_compute("AllGather", ALU.bypass, replica_groups=RG,
                                                 ins=[agi_mod.ap().opt()], outs=[ago_mod.ap().opt()]),
         reads=["agi_mod"], writes=["ago_mod"], cc=True, semkey="cc")
    for l in range(NL):
        for r2 in range(4):
            P.dma(lambda e, l=l, r2=r2: e.dma_start(
                out=modT[:, l, r2 * 12:(r2 + 1) * 12],
                in_=ago_mod.ap()[r2 * 17, l * 1536:(l + 1) * 1536].rearrange("(c p) -> p c", p=128),
                allow_slow_non_contiguous=True), reads=["ago_mod"], writes=["modT"], semkey="modT")
    selT = cv_.f32([16, 4])
    smk = cv_.f32([16, D])
    smo = cv_.f32([4, D])
    P.dma(lambda e: e.dma_start(out=selT[0], in_=selT_in), writes=["selT"])
    SEG = {0: [(0, 0, 1536, 0), (1, 0, 512, 1536)], 1: [(1, 512, 1536, 0), (2, 0, 1024, 1024)], 2: [(2, 1024, 1536, 0), (3, 0, 1536, 512)]}
    for l in range(NL):
        for kind in range(3):
            for (r2, j0, j1, d0) in SEG[kind]:
                P.dma(lambda e, l=l, r2=r2, j0=j0, j1=j1, d0=d0: e.dma_start(
                    out=smk[0][:, d0:d0 + (j1 - j0)], in_=ago_mod.ap()[r2 * 17 + 1:r2 * 17 + 17, l * 1536 + j0:l * 1536 + j1]),
                    reads=["ago_mod"], writes=["smk"], semkey="smk")
            for cg in range(4):
                P.pe(lambda e, cg=cg: e.matmul(psum[cg][0:4, :], lhsT=selT[0], rhs=smk[0][:, cg * 512:(cg + 1) * 512], start=True, stop=True),
                     reads=["selT", "smk"], writes=[f"ps{cg}"])
                P.dve(lambda e, cg=cg: e.tensor_copy(out=smo[0][:, cg * 512:(cg + 1) * 512], in_=psum[cg][0:4, :]),
                      reads=[f"ps{cg}"], writes=["smo", f"ps{cg}"])
            P.dma(lambda e, l=l, kind=kind: e.dma_start(out=smod_d.ap()[l, kind], in_=smo[0]), reads=["smo"], writes=["smod_d"], semkey="smo")
    P.dve(lambda e: e.scalar_tensor_tensor(out=Acol[:], in0=modT[:, :, 16:32], scalar=1.0, in1=ngc[:],
                                           op0=ALU.add, op1=ALU.mult), reads=["modT", "ngc"], writes=["Acol"])
    P.barrier()
    import os
    STOP = os.environ.get("MK_STOP", "")
    if STOP == "A":
        P.emit()
        return nc

    def phase_N_tile(l, t, xt_ap, xkey):
        c2 = Carve()
        c2.off = NOFF
        junk = c2.f32([128, D])
        pp = t % 2
        bufs = [(c2.bf([128, D]), c2.f32([128, 2]), c2.bf([128, 16, 128])) for _ in range(2)]
        xn, ssq, hTt = bufs[pp]
        KJ, KX, KS, KH = f"junk{pp}", f"xn{pp}", f"ssq{pp}", f"hTt{pp}"
        P.act(lambda e: e.activation(out=junk[0], in_=xt_ap, func=AF.Square, accum_out=ssq[0][:, 0:1]),
              reads=[xkey], writes=[KJ, KS])
        P.act(lambda e: e.activation(out=ssq[0][:, 1:2], in_=ssq[0][:, 0:1], func=AF.Sqrt, scale=1.0 / D, bias=epsc[:, 0:1]),
              reads=[KS, "epsc"], writes=[KS])
        P.dve(lambda e: e.reciprocal(out=ssq[0][:, 1:2], in_=ssq[0][:, 1:2]), reads=[KS], writes=[KS])
        P.act(lambda e: e.activation(out=xn[0], in_=xt_ap, func=AF.Copy, scale=ssq[0][:, 1:2]),
              reads=[xkey, KS], writes=[KX])
        for half in range(2):
            pst = psum[4 + half]
            for kk in range(8):
                k = half * 8 + kk
                P.pe(lambda e, k=k, kk=kk, pst=pst: e.transpose(
                    out=pst[:].bitcast(BF16)[:, kk * 128:(kk + 1) * 128], in_=xn[0][:, k * 128:(k + 1) * 128],
                    identity=cs["ident"][:]), reads=[KX, "k_ident"], writes=[f"ps{4 + half}"])
            for kk in range(8):
                k = half * 8 + kk
                P.act(lambda e, k=k, kk=kk, pst=pst, l=l: e.activation(
                    out=v(hTt)[:, k, :], in_=pst[:].bitcast(BF16)[:, kk * 128:(kk + 1) * 128], func=AF.Identity,
                    scale=Acol[:, l, k:k + 1], bias=modT[:, l, k:k + 1]),
                    reads=[f"ps{4 + half}", "Acol", "modT"], writes=[KH])
        P.dma(lambda e, l=l, t=t: e.dma_start(out=agi_h[l][t].ap().rearrange("p (k c) -> p k c", k=16), in_=v(hTt)),
              reads=[KH], writes=[f"agi_h{l}_{t}"], semkey=f"hTt_st{t % 2}")
        P.op("pool", lambda e, l=l, t=t: e.collective_compute("AllGather", ALU.bypass, replica_groups=RG,
                                                           ins=[agi_h[l][t].ap().opt()], outs=[ago_h[l][t].ap().opt()]),
             reads=[f"agi_h{l}_{t}"], writes=[f"ago_h{l}_{t}"], cc=True, semkey="cc", nb=True)

    def phase_N_samples(l, xs_ap, xkey, off):
        c6 = Carve()
        c6.off = off
        As = c6.f32([4, D])
        Bs = c6.f32([4, D])
        jk = c6.f32([4, D])
        hs_ = c6.bf([4, D])
        sq4 = c6.f32([4, 2])
        hTs = c6.bf([128, 16, 4])
        P.dma(lambda e: e.dma_start(out=As[0], in_=smod_d.ap()[l, 1]), reads=["smod_d"], writes=["As"], eng="act")
        P.dma(lambda e: e.dma_start(out=Bs[0], in_=smod_d.ap()[l, 0]), reads=["smod_d"], writes=["Bs"], eng="act")
        P.dma(lambda e: e.dma_start(out=jk[0], in_=ng_row[l:l + 1, :].partition_broadcast(4)[:, 0, :]), writes=["jk"], eng="act")
        P.dve(lambda e: e.scalar_tensor_tensor(out=As[0], in0=As[0], scalar=1.0, in1=jk[0], op0=ALU.add, op1=ALU.mult),
              reads=["As", "jk"], writes=["As"])
        P.act(lambda e: e.activation(out=jk[0], in_=xs_ap, func=AF.Square, accum_out=sq4[0][:, 0:1]), reads=[xkey, "As"], writes=["jk", "sq4"])
        P.act(lambda e: e.activation(out=sq4[0][:, 1:2], in_=sq4[0][:, 0:1], func=AF.Sqrt, scale=1.0 / D, bias=epsc[0:4, 0:1]),
              reads=["sq4", "epsc"], writes=["sq4"])
        P.dve(lambda e: e.reciprocal(out=sq4[0][:, 1:2], in_=sq4[0][:, 1:2]), reads=["sq4"], writes=["sq4"])
        P.act(lambda e: e.activation(out=jk[0], in_=xs_ap, func=AF.Copy, scale=sq4[0][:, 1:2]), reads=[xkey, "sq4"], writes=["jk"])
        P.dve(lambda e: e.tensor_tensor(out=jk[0], in0=jk[0], in1=As[0], op=ALU.mult), reads=["jk", "As"], writes=["jk"])
        P.dve(lambda e: e.tensor_tensor(out=hs_[0], in0=jk[0], in1=Bs[0], op=ALU.add), reads=["jk", "Bs"], writes=["hs_"])
        for k in range(16):
            P.pe(lambda e, k=k: e.transpose(out=psum[6][:].bitcast(BF16)[:, k * 4:(k + 1) * 4], in_=hs_[0][:, k * 128:(k + 1) * 128],
                                            identity=cs["ident"][0:4, 0:4]), reads=["hs_", "k_ident"], writes=["ps6"])
        P.act(lambda e: e.activation(out=hTs[0], in_=psum[6][:].bitcast(BF16)[:, 0:64], func=AF.Copy), reads=["ps6"], writes=["hTs", "ps6"])
        P.dma(lambda e: e.dma_start(out=agi_hs[l].ap().rearrange("p (k c) -> p k c", k=16), in_=v(hTs)),
              reads=["hTs"], writes=[f"agi_hs{l}"], semkey="hTs_st")
        P.op("pool", lambda e: e.collective_compute("AllGather", ALU.bypass, replica_groups=RG,
                                                     ins=[agi_hs[l].ap().opt()], outs=[ago_hs[l].ap().opt()]),
             reads=[f"agi_hs{l}"], writes=[f"ago_hs{l}"], cc=True, semkey="cc", nb=True)

    NOFF = 0
    c0 = Carve()
    xt = [c0.f32([128, D]) for _ in range(2)]
    NOFF = c0.off
    def n_load(t):
        s = t % 2
        P.dma(lambda e, t=t, s=s: e.dma_start(out=xt[s][0], in_=x_own[t * 128:(t + 1) * 128, :]),
              writes=[f"xt{s}"], semkey=f"xt{s}")

    n_load(0)
    for t in range(8):
        s = t % 2
        if t + 1 < 8:
            n_load(t + 1)
        phase_N_tile(0, t, xt[s][0], f"xt{s}")

    zpad = sb("zpad", [128, 2, 64], BF16)
    P.pool(lambda e: e.memset(zpad[:], 0.0), writes=["zpad"])

    def allgather_h(l):
        pass

    cx = Carve()
    cx.off = NOFF + 20000
    xs0 = cx.f32([4, D])
    P.dma(lambda e: e.dma_start(out=xs0[0], in_=xs_own), writes=["xs0"], eng="act")
    phase_N_samples(0, xs0[0], "xs0", NOFF + 8000)
    allgather_h(0)
    P.barrier()
    if STOP == "N":
        P.emit()
        return nc

    def phase_M(l):
        c3 = Carve()
        Win = WBIG[:, 0:16 * WCOLS].rearrange("p (k c) -> p k c", k=16)
        import os
        SKIP = os.environ.get("MK_SKIP", "")
        for k in range(16 if "win" not in SKIP else 0):
            for hf in range(4):
                P.dma(lambda e, k=k, hf=hf: e.dma_start(out=Win[:, k, hf * 464:(hf + 1) * 464],
                                                       in_=w_in[l, k * 128:(k + 1) * 128, hf * 464:(hf + 1) * 464]),
                      writes=[f"Win{k}_{hf}"], eng="pool", semkey=f"Win{(k * 4 + hf) % 4}")
        WK = [f"Win{k}_{hf}" for k in range(16) for hf in range(4)]
        hT = [c3.bf([128, 16, G]) for _ in range(2)]
        cst = [c3.f32([128, 2, G])] * 2
        cur = c3.f32([128, 7, G + 1])
        qb = c3.bf([128, 3, G])
        t1 = c3.f32([128, 3, G])
        t2 = c3.f32([128, 3, G])
        qrot = c3.bf([128, 2, G])
        krotf = c3.f32([128, G])
        kT = c3.bf([128, 128 + G])
        vb = c3.bf([128, 3, 64])
        vf = c3.f32([128, 64])
        kcf = c3.f32([128, 64])
        mu = c3.f32([128, 7])
        gsil2 = [c3.bf([128, 4, G]) for _ in range(2)]
        zT = c3.bf([128, 4, G])
        sc = c3.f32([128, 4, 256])
        yc = c3.f32([128, 8, 64])
        pb = (yc[0].bitcast(BF16)[:, 0:1024], [128, 4, 256])
        pTs = c3.bf([128, 1024])
        attb = c3.bf([128, 4, 64])
        sm = c3.f32([128, 8, 4])
        snk = c3.f32([128, 4])
        rp = c3.f32([128, NPAR, 2])
        loraW = c3.bf([128, 2, 128])
        P.dma(lambda e: e.dma_start(out=v(rp), in_=rpar[l]), writes=["rp"])
        P.dma(lambda e: e.dma_start(out=v(loraW), in_=loraw[l]), writes=["loraW"], eng="pool", semkey="loraW")
        mixed = c3.f32([128, 7, G])
        twad = c3.bf([128, G])
        sig = c3.f32([128, 2, G])
        aa = c3.f32([128, 2, G])
        Ls = c3.f32([128, 2, G])
        eL = c3.f32([128, 2, G])
        eLi = c3.f32([128, 2, G])
        eLp = c3.f32([128, 2, G])
        kkr = c3.f32([128, 2, G])
        sqb = c3.bf([128, 2, G])
        rn = c3.f32([128, 2, G])
        kk = c3.f32([128, 2, G])
        uu = c3.f32([128, 2, G])
        keff = c3.f32([128, 2, G])
        AR = c3.bf([128, 2, 4, 2, 64])
        Bt = c3.bf([128, 2, G])
        Kt = c3.bf([128, 2, G])
        vrb = c3.bf([128, 2, G])
        rkk = c3.bf([128, 2, G])
        bonus = c3.f32([128, 2, G])
        TK = c3.bf([128, 2, 4, 3, 64])
        MS = c3.bf([128, 8, 320])
        PQ = [c3.bf([128, 8, 128]) for _ in range(2)]
        TT = [c3.bf([128, 8, 64]) for _ in range(2)]
        Sst = c3.f32([128, 2, 64])
        Sb = c3.bf([128, 2, 64])
        Wb = c3.bf([128, 2, 64])
        Ub = c3.bf([128, 2, 64])
        Ybuf = c3.f32([128, 4, 2, 64])
        ysq = (sc[0][:, 0:512], [128, 8, 64])
        gs = c3.f32([128, 4, 8])
        yh = c3.bf([128, 8, 64])
        yz = rn
        ob = c3.f32([128, D])
        for i in range(4):
            P.dma(lambda e, i=i: e.dma_start(out=Wo[:, i, :], in_=w_out[l, i * 128:(i + 1) * 128, :]), writes=[f"Wo{i}"], eng="pool",
                  semkey=f"Wo{i % 2}")
        P.pool(lambda e: e.memset(Sst[0], 0.0), writes=["Sst"])
        P.pool(lambda e: e.memset(Sb[0], 0.0), writes=["Sb"])
        P.dma(lambda e: e.dma_start(out=snk[0], in_=sinks_b[l]), writes=["snk"])
        P.dma(lambda e: e.dma_start(out=mu[0], in_=mu_col[l]), writes=["mu"])
        if 'ms' not in SKIP:
            P.pool(lambda e: e.memset(cur[0], 0.0), writes=["cur"])
            P.pool(lambda e: e.memset(kT[0], 0.0), writes=["kT"])
            P.pool(lambda e: e.memset(vb[0], 0.0), writes=["vb"])
        import os
        NGR = int(os.environ.get('MK_NG', NG))
        sched = [("inproj", 0)]
        for g_ in range(NGR):
            sched.append(("att", g_))
            if g_ + 1 < NGR:
                sched.append(("inproj", g_ + 1))
            sched.append(("rw", g_))
        for (sec, g) in sched:
            s = g % 2
            r2, cg = g // 4, (g % 4) * G
            if sec == "inproj":
                for hf in range(2):
                    t_ = 2 * (g % 4) + hf
                    P.dma(lambda e, s=s, r2=r2, hf=hf, t_=t_: e.dma_start(
                        out=v(hT[s])[:, :, hf * 128:(hf + 1) * 128],
                        in_=ago_h[l][t_].ap()[r2 * 128:(r2 + 1) * 128, :].rearrange("p (k c) -> p k c", k=16)),
                        reads=[f"ago_h{l}_{t_}"], writes=[f"hT{s}"], eng=("sp" if hf == 0 else "act"), semkey=f"hT{s}_{hf}")
                P.dma(lambda e, s=s, g=g: e.dma_start(out=v(cst[s])[:, 0, :], in_=cd["cosT"][:, g * G:(g + 1) * G]),
                      writes=["cst0"], eng="act", semkey="cst0")
                P.dma(lambda e, s=s, g=g: e.dma_start(out=v(cst[s])[:, 1, :], in_=cd["sinT"][:, g * G:(g + 1) * G]),
                      writes=["cst0"], eng="act", semkey="cst0")
                for ct in range(NCT if 'mm' not in SKIP else 0):
                    pi = ct % 4
                    for k in range(16):
                        P.pe(lambda e, ct=ct, k=k, s=s, pi=pi: e.matmul(
                            psum[pi][:, 0:G], lhsT=Win[:, k, ct * 128:(ct + 1) * 128], rhs=v(hT[s])[:, k, :],
                            start=(k == 0), stop=(k == 15)), reads=WK + [f"hT{s}"], writes=[f"ps{pi}"])
                    if 'ev' in SKIP:
                        continue
                    if ct < 3:
                        P.act(lambda e, ct=ct, pi=pi: e.activation(out=v(qb)[:, ct, :], in_=psum[pi][:, 0:G], func=AF.Copy),
                              reads=[f"ps{pi}"], writes=["qb", f"ps{pi}"])
                        if 'evd' not in SKIP:
                            P.dve(lambda e, ct=ct, pi=pi, s=s: e.tensor_tensor(out=v(t2)[:, ct, :], in0=psum[pi][:, 0:G],
                                                                             in1=v(cst[s])[:, 0, :], op=ALU.mult),
                                  reads=[f"ps{pi}", "cst0"], writes=["t2", f"ps{pi}"])
                    elif ct < 10:
                        P.act(lambda e, ct=ct, pi=pi: e.activation(out=v(cur)[:, ct - 3, 1:G + 1], in_=psum[pi][:, 0:G],
                                                                    func=AF.Copy), reads=[f"ps{pi}"], writes=["cur"])
                    else:
                        P.act(lambda e, ct=ct, pi=pi, gq=gsil2[g % 2]: e.activation(out=v(gq)[:, ct - 10, :], in_=psum[pi][:, 0:G], func=AF.Silu),
                              reads=[f"ps{pi}"], writes=[f"gsil{g % 2}", f"ps{pi}"])
                for ct in range(3 if 'rope' not in SKIP else 0):
                    pi = 4 + ct % 2
                    P.pe(lambda e, ct=ct, pi=pi: e.matmul(psum[pi][:, 0:G], lhsT=cs["prot"][:], rhs=v(qb)[:, ct, :],
                                                         start=True, stop=True), reads=["qb", "k_prot"], writes=[f"ps{pi}"])
                    P.dve(lambda e, ct=ct, pi=pi, s=s: e.tensor_tensor(out=v(t1)[:, ct, :], in0=psum[pi][:, 0:G],
                                                                     in1=v(cst[s])[:, 1, :], op=ALU.mult),
                          reads=[f"ps{pi}", "cst0"], writes=["t1"])
                P.dve(lambda e: e.tensor_tensor(out=v(qrot), in0=v(t1)[:, 0:2, :], in1=v(t2)[:, 0:2, :], op=ALU.add),
                      reads=["t1", "t2"], writes=["qrot"])
                P.dve(lambda e: e.tensor_tensor(out=krotf[0], in0=v(t1)[:, 2, :], in1=v(t2)[:, 2, :], op=ALU.add),
                      reads=["t1", "t2"], writes=["krotf"])
                P.act(lambda e: e.activation(out=kT[0][:, 128:128 + G], in_=krotf[0], func=AF.Copy),
                      reads=["krotf"], writes=["kT"])
                for bl in range(G // 128 if 'vv' not in SKIP else 0):
                    for k in range(16):
                        P.pe(lambda e, k=k, s=s, bl=bl: e.matmul(
                            psum[6][:, 0:64], lhsT=v(hT[s])[:, k, bl * 128:(bl + 1) * 128], rhs=Win[:, k, NCT * 128:NCT * 128 + 64],
                            start=(k == 0), stop=(k == 15)), reads=WK + [f"hT{s}"], writes=["ps6"])
                    P.act(lambda e, bl=bl: e.activation(out=v(vb)[:, 1 + bl, :], in_=psum[6][:, 0:64], func=AF.Copy),
                          reads=["ps6"], writes=["vb", "ps6"])
                    if g == NGR - 1 and bl == G // 128 - 1:
                        P.dve(lambda e: e.tensor_copy(out=vf[0], in_=psum[6][:, 0:64]), reads=["ps6"], writes=["vf", "ps6"])
                        P.dma(lambda e: e.dma_start(out=cvp[l], in_=vf[0]), reads=["vf"], writes=["cvp"], semkey="cvp")

            if sec == "att":
                SLOT_H = [0, 2, 1, 3]
                for bl in range(G // 128 if 'att' not in SKIP else 0):
                    mk = "maskb0" if (g == 0 and bl == 0) else "maskb"
                    for slot in range(4):
                        h = SLOT_H[slot]
                        i, base = h // 2, (h % 2) * 64
                        bank = 4 + slot // 2
                        P.pe(lambda e, i=i, base=base, bank=bank, slot=slot, bl=bl: e.matmul(
                            psum[bank][:, (slot % 2) * 256:(slot % 2) * 256 + 256],
                            lhsT=v(qrot)[base:base + 64, i, bl * 128:(bl + 1) * 128],
                            rhs=kT[0][base:base + 64, bl * 128:bl * 128 + 256], start=True, stop=True),
                            reads=["qrot", "kT"], writes=[f"ps{bank}"])
                    for bk in range(2):
                        P.dve(lambda e, bk=bk, mk=mk: e.tensor_tensor(
                            out=v(sc)[:, 2 * bk:2 * bk + 2, :], in0=psum[4 + bk][:, :].rearrange("p (a b) -> p a b", a=2),
                            in1=cs[mk][:].unsqueeze(1).to_broadcast([128, 2, 256]), op=ALU.add),
                            reads=[f"ps{4 + bk}", "k_" + mk], writes=["sc", f"ps{4 + bk}"])
                    smv = v(sm)
                    P.dve(lambda e: e.tensor_reduce(out=smv[:, 0, :], in_=v(sc), axis=AX.X, op=ALU.max), reads=["sc"], writes=["sm"])
                    P.dve(lambda e: e.scalar_tensor_tensor(out=smv[:, 1, :], in0=smv[:, 0, :], scalar=0.125, in1=snk[0],
                                                           op0=ALU.mult, op1=ALU.max), reads=["sm", "snk"], writes=["sm"])
                    P.dve(lambda e: e.tensor_scalar_mul(out=smv[:, 2, :], in0=smv[:, 1, :], scalar1=-1.0), reads=["sm"], writes=["sm"])
                    P.dve(lambda e: e.tensor_tensor(out=smv[:, 4, :], in0=snk[0], in1=smv[:, 1, :], op=ALU.subtract),
                          reads=["sm", "snk"], writes=["sm"])
                    for slot in range(4):
                        P.act(lambda e, slot=slot: e.activation(out=v(pb)[:, slot, :], in_=v(sc)[:, slot, :], func=AF.Exp, scale=0.125,
                                                                bias=smv[:, 2, slot:slot + 1], accum_out=smv[:, 3, slot:slot + 1]),
                              reads=["sc", "sm"], writes=["pb", "sm"])
                    P.act(lambda e: e.activation(out=smv[:, 4, :], in_=smv[:, 4, :], func=AF.Exp), reads=["sm"], writes=["sm"])
                    P.dve(lambda e: e.tensor_tensor(out=smv[:, 5, :], in0=smv[:, 3, :], in1=smv[:, 4, :], op=ALU.add), reads=["sm"], writes=["sm"])
                    P.dve(lambda e: e.reciprocal(out=smv[:, 6, :], in_=smv[:, 5, :]), reads=["sm"], writes=["sm"])
                    for slot in range(4):
                        for hf in range(2):
                            P.pe(lambda e, slot=slot, hf=hf: e.transpose(
                                out=psum[6][:].bitcast(BF16)[:, (slot * 2 + hf) * 128:(slot * 2 + hf + 1) * 128],
                                in_=v(pb)[:, slot, hf * 128:(hf + 1) * 128], identity=cs["ident"][:]),
                                reads=["pb", "k_ident"], writes=["ps6"])
                    P.act(lambda e: e.activation(out=pTs[0], in_=psum[6][:].bitcast(BF16), func=AF.Copy),
                          reads=["ps6"], writes=["pTs", "ps6"])
                    for slot in range(4):
                        for hf in range(2):
                            P.pe(lambda e, slot=slot, hf=hf, bl=bl: e.matmul(
                                psum[7][:, slot * 64:(slot + 1) * 64], lhsT=pTs[0][:, (slot * 2 + hf) * 128:(slot * 2 + hf + 1) * 128],
                                rhs=v(vb)[:, bl + hf, :], start=(hf == 0), stop=(hf == 1)),
                                reads=["pTs", "vb"], writes=["ps7"])
                    P.dve(lambda e: e.tensor_tensor(out=v(attb), in0=psum[7][:, 0:256].rearrange("p (a b) -> p a b", a=4),
                                                    in1=smv[:, 6, :].unsqueeze(2).to_broadcast([128, 4, 64]), op=ALU.mult),
                          reads=["ps7", "sm"], writes=["attb", "ps7"])
                    for slot in range(4):
                        h = SLOT_H[slot]
                        i, base = h // 2, (h % 2) * 64
                        P.pe(lambda e, slot=slot, i=i, base=base: e.transpose(
                            out=psum[4][:].bitcast(BF16)[base:base + 64, i * 128:(i + 1) * 128],
                            in_=v(attb)[:, slot, :], identity=cs["ident"][:]),
                            reads=["attb", "k_ident"], writes=["ps4"])
                    P.dve(lambda e, bl=bl, gq=gsil2[g % 2]: e.tensor_tensor(
                        out=v(zT)[:, 0:2, bl * 128:(bl + 1) * 128],
                        in0=psum[4][:].bitcast(BF16)[:, 0:256].rearrange("p (a b) -> p a b", a=2),
                        in1=v(gq)[:, 0:2, bl * 128:(bl + 1) * 128], op=ALU.mult),
                        reads=["ps4", f"gsil{g % 2}"], writes=["zT", "ps4"])

                if 'rw' not in SKIP:
                    curn = v(cur)[:, :, 1:G + 1]
                    P.pool(lambda e: e.tensor_tensor(out=v(mixed), in0=v(cur)[:, :, 0:G], in1=curn, op=ALU.subtract),
                           reads=["cur"], writes=["mixed"])
                    P.pool(lambda e: e.tensor_tensor(out=v(mixed), in0=v(mixed), in1=mu[0].unsqueeze(2).to_broadcast([128, 7, G]), op=ALU.mult),
                           reads=["mixed", "mu"], writes=["mixed"])
                    P.pool(lambda e: e.tensor_tensor(out=v(mixed), in0=v(mixed), in1=curn, op=ALU.add), reads=["mixed", "cur"], writes=["mixed"])
                if g == NGR - 1 and 'tr' not in SKIP:
                    P.pe(lambda e: e.transpose(out=psum[7][:, 0:128], in_=krotf[0][:, G - 128:G], identity=cs["identf"][:]),
                         reads=["krotf", "k_identf"], writes=["ps7"])
                    P.dve(lambda e: e.tensor_copy(out=kcf[0], in_=psum[7][:, 0:64]), reads=["ps7"], writes=["kcf"])
                    P.dma(lambda e: e.dma_start(out=ckp[l], in_=kcf[0]), reads=["kcf"], writes=["ckp"], semkey="ckp")
                    P.dma(lambda e: e.dma_start(out=shp[l], in_=v(cur)[:, :, G]), reads=["cur"], writes=["shp"], semkey="shp")
                if 'carry' in SKIP:
                    continue
                P.pool(lambda e: e.tensor_copy(out=v(cur)[:, :, 0:1], in_=v(cur)[:, :, G:G + 1]), reads=["cur"], writes=["cur"])
                P.pool(lambda e: e.tensor_copy(out=kT[0][:, 0:128], in_=kT[0][:, G:G + 128]), reads=["kT"], writes=["kT"])
                P.pool(lambda e: e.tensor_copy(out=v(vb)[:, 0, :], in_=v(vb)[:, G // 128, :]), reads=["vb"], writes=["vb"])
            if sec == "rw":
                if 'rw' not in SKIP:
                    curn = v(cur)[:, :, 1:G + 1]
                    mx = v(mixed)
                    P.act(lambda e: e.activation(out=twad[0][0:64, :], in_=mx[0:64, 6, :], func=AF.Tanh), reads=["mixed"], writes=["twad"])
                    P.act(lambda e: e.activation(out=twad[0][64:128, :], in_=mx[64:128, 6, :], func=AF.Copy), reads=["mixed"], writes=["twad"])
                    P.act(lambda e: e.activation(out=v(vrb), in_=mx[:, 4:6, :], func=AF.Copy), reads=["mixed"], writes=["vrb"])
                    rpv = v(rp)
                    for p in range(2):
                        P.pe(lambda e, p=p: e.matmul(psum[p][:, 0:G], lhsT=v(loraW)[0:64, p, :], rhs=twad[0][0:64, :], start=True, stop=True),
                             reads=["loraW", "twad"], writes=[f"ps{p}"])
                        P.act(lambda e, p=p: e.activation(out=v(sig)[:, p, :], in_=psum[p][:, 0:G], func=AF.Sigmoid, bias=rpv[:, 0, p:p + 1]),
                              reads=[f"ps{p}", "rp"], writes=["sig", f"ps{p}"])
                        P.pe(lambda e, p=p: e.matmul(psum[2 + p][:, 0:G], lhsT=v(loraW)[64:128, p, :], rhs=twad[0][64:128, :], start=True, stop=True),
                             reads=["loraW", "twad"], writes=[f"ps{2 + p}"])
                        P.act(lambda e, p=p: e.activation(out=v(aa)[:, p, :], in_=psum[2 + p][:, 0:G], func=AF.Sigmoid, bias=rpv[:, 1, p:p + 1]),
                              reads=[f"ps{2 + p}", "rp"], writes=["aa", f"ps{2 + p}"])
                    for p in range(2):
                        for c in range(4):
                            P.dve(lambda e, p=p, c=c: e.tensor_tensor_scan(
                                out=v(Ls)[:, p, c * 64:(c + 1) * 64], data0=cs["ones"][:, 0:64], data1=v(sig)[:, p, c * 64:(c + 1) * 64],
                                initial=0.0, op0=ALU.mult, op1=ALU.add), reads=["sig", "k_ones"], writes=["Ls"])
                    P.act(lambda e: e.activation(out=v(eL), in_=v(Ls), func=AF.Exp, scale=-CDEC), reads=["Ls"], writes=["eL"])
                    P.act(lambda e: e.activation(out=v(eLi), in_=v(Ls), func=AF.Exp, scale=CDEC), reads=["Ls"], writes=["eLi"])
                    P.pool(lambda e: e.tensor_tensor(out=v(eLp), in0=v(Ls), in1=v(sig), op=ALU.subtract), reads=["Ls", "sig"], writes=["eLp"])
                    P.act(lambda e: e.activation(out=v(eLp), in_=v(eLp), func=AF.Exp, scale=-CDEC), reads=["eLp"], writes=["eLp"])
                    bc = lambda j: rpv[:, j, :].unsqueeze(2).to_broadcast([128, 2, G])
                    P.pool(lambda e: e.tensor_tensor(out=v(kkr), in0=mx[:, 2:4, :], in1=bc(2), op=ALU.mult), reads=["mixed", "rp"], writes=["kkr"])
                    P.pool(lambda e: e.tensor_tensor(out=v(sqb), in0=v(kkr), in1=v(kkr), op=ALU.mult), reads=["kkr"], writes=["sqb"])
                    for p in range(2):
                        P.pe(lambda e, p=p: e.matmul(psum[p][:, 0:G], lhsT=cs["bones"][:], rhs=v(sqb)[:, p, :], start=True, stop=True),
                             reads=["sqb", "k_bones"], writes=[f"ps{p}"])
                        P.act(lambda e, p=p: e.activation(out=v(rn)[:, p, :], in_=psum[p][:, 0:G], func=AF.Sqrt),
                              reads=[f"ps{p}"], writes=["rn", f"ps{p}"])
                    P.dve(lambda e: e.tensor_scalar_max(out=v(rn), in0=v(rn), scalar1=1e-12), reads=["rn"], writes=["rn"])
                    P.dve(lambda e: e.reciprocal(out=v(rn), in_=v(rn)), reads=["rn"], writes=["rn"])
                    P.pool(lambda e: e.tensor_tensor(out=v(kk), in0=v(kkr), in1=v(rn), op=ALU.mult), reads=["kkr", "rn"], writes=["kk"])
                    P.dve(lambda e: e.scalar_tensor_tensor(out=v(uu), in0=v(aa), scalar=-1.0, in1=bc(3), op0=ALU.add, op1=ALU.mult),
                          reads=["aa", "rp"], writes=["uu"])
                    P.dve(lambda e: e.scalar_tensor_tensor(out=v(keff), in0=v(uu), scalar=1.0, in1=mx[:, 2:4, :], op0=ALU.add, op1=ALU.mult),
                          reads=["uu", "mixed"], writes=["keff"])
                    v4 = lambda t_: v(t_).rearrange("p a (c t) -> p a c t", c=4)
                    ARv = v(AR)
                    P.dve(lambda e: e.scalar_tensor_tensor(out=ARv[:, :, :, 0, :], in0=v4(kk), scalar=-1.0, in1=v4(eLp), op0=ALU.mult, op1=ALU.mult),
                          reads=["kk", "eLp"], writes=["AR"])
                    P.pool(lambda e: e.tensor_tensor(out=ARv[:, :, :, 1, :], in0=mx[:, 0:2, :].rearrange("p a (c t) -> p a c t", c=4), in1=v4(eL), op=ALU.mult),
                           reads=["mixed", "eL"], writes=["AR"])
                    P.pool(lambda e: e.tensor_tensor(out=v(uu), in0=v(kk), in1=v(aa), op=ALU.mult), reads=["kk", "aa", "keff"], writes=["uu"])
                    P.pool(lambda e: e.tensor_tensor(out=v(Bt), in0=v(uu), in1=v(eLi), op=ALU.mult), reads=["uu", "eLi"], writes=["Bt"])
                    P.dve(lambda e: e.tensor_tensor(out=v(Kt), in0=v(keff), in1=v(eLi), op=ALU.mult), reads=["keff", "eLi"], writes=["Kt"])
                    P.pool(lambda e: e.tensor_tensor(out=v(kkr), in0=mx[:, 0:2, :], in1=v(keff), op=ALU.mult), reads=["mixed", "keff", "kk"], writes=["kkr"])
                    P.pool(lambda e: e.tensor_tensor(out=v(rkk), in0=v(kkr), in1=bc(4), op=ALU.mult), reads=["kkr", "rp"], writes=["rkk"])
                    for p in range(2):
                        P.pe(lambda e, p=p: e.matmul(psum[2 + p][:, 0:G], lhsT=cs["bones"][:], rhs=v(rkk)[:, p, :], start=True, stop=True),
                             reads=["rkk", "k_bones"], writes=[f"ps{2 + p}"])
                        P.dve(lambda e, p=p: e.tensor_tensor(out=v(bonus)[:, p, :], in0=psum[2 + p][:, 0:G], in1=mx[:, 4 + p, :], op=ALU.mult),
                              reads=[f"ps{2 + p}", "mixed"], writes=["bonus", f"ps{2 + p}"])
                    TKv = v(TK)
                    for p in range(2):
                        for c in range(4):
                            for wi, src in enumerate((Bt, Kt, vrb)):
                                for hh in range(2):
                                    hs = slice(hh * 64, hh * 64 + 64)
                                    P.pe(lambda e, p=p, c=c, wi=wi, src=src, hs=hs: e.transpose(
                                        out=psum[4 + p][:].bitcast(BF16)[hs, (c * 3 + wi) * 64:(c * 3 + wi + 1) * 64],
                                        in_=v(src)[hs, p, c * 64:(c + 1) * 64], identity=cs["ident"][hs, hs]),
                                        reads=["Bt", "Kt", "vrb", "k_ident"], writes=[f"ps{4 + p}"])
                        P.act(lambda e, p=p: e.activation(out=TKv[:, p, :, :, :].rearrange("p c w t -> p (c w t)"),
                                                          in_=psum[4 + p][:].bitcast(BF16)[:, 0:768], func=AF.Copy),
                              reads=[f"ps{4 + p}"], writes=["TK", f"ps{4 + p}"])
                    MSv = v(MS)
                    for p in range(2):
                        for c in range(4):
                            it = p * 4 + c
                            bank = 6 + it % 2
                            for hh in range(2):
                                hs = slice(hh * 64, hh * 64 + 64)
                                P.pe(lambda e, p=p, c=c, hs=hs, bank=bank: e.matmul(
                                    psum[bank][hs, 0:128], lhsT=v(Bt)[hs, p, c * 64:(c + 1) * 64],
                                    rhs=ARv[hs, p, c, :, :].rearrange("p a t -> p (a t)"), start=True, stop=True),
                                    reads=["Bt", "AR"], writes=[f"ps{bank}"])
                                P.pe(lambda e, p=p, c=c, hs=hs, bank=bank: e.matmul(
                                    psum[bank][hs, 128:256], lhsT=v(Kt)[hs, p, c * 64:(c + 1) * 64],
                                    rhs=ARv[hs, p, c, :, :].rearrange("p a t -> p (a t)"), start=True, stop=True),
                                    reads=["Kt", "AR"], writes=[f"ps{bank}"])
                                P.pe(lambda e, p=p, c=c, hs=hs, bank=bank: e.matmul(
                                    psum[bank][hs, 256:320], lhsT=ARv[hs, p, c, 0, :], rhs=v(Bt)[hs, p, c * 64:(c + 1) * 64],
                                    start=True, stop=True), reads=["Bt", "AR"], writes=[f"ps{bank}"])
                            P.dve(lambda e, it=it, bank=bank: e.tensor_tensor(out=MSv[:, it, :], in0=psum[bank][:, 0:320], in1=cs["maskG"][:], op=ALU.mult),
                                  reads=[f"ps{bank}", "k_maskG"], writes=["MS", f"ps{bank}"])
                    P.dve(lambda e: e.tensor_tensor(out=v(TT[0]), in0=MSv[:, :, 0:64], in1=cs["id64"][:].unsqueeze(1).to_broadcast([128, 8, 64]), op=ALU.add),
                          reads=["MS", "k_id64"], writes=["TT0"])
                    for k in range(1, 6):
                        src_i, dst_i = (k - 1) % 2, k % 2
                        PQs, PQd = v(PQ[src_i]), v(PQ[dst_i])
                        Pprev = (lambda it_: MSv[:, it_, 0:64]) if k == 1 else (lambda it_, PQs=PQs: PQs[:, it_, 0:64])
                        Qprev = (lambda it_: MSv[:, it_, 256:320]) if k == 1 else (lambda it_, PQs=PQs: PQs[:, it_, 64:128])
                        rk_ = ["MS"] if k == 1 else [f"PQ{src_i}"]
                        for it in range(8):
                            bank = it // 4
                            for hh in range(2):
                                hs = slice(hh * 64, hh * 64 + 64)
                                P.pe(lambda e, it=it, hs=hs, bank=bank, Pprev=Pprev, Qprev=Qprev: e.matmul(
                                    psum[bank][hs, (it % 4) * 128:(it % 4) * 128 + 64], lhsT=Qprev(it)[hs, :], rhs=Pprev(it)[hs, :], start=True, stop=True),
                                    reads=rk_, writes=[f"ps{bank}"])
                                P.pe(lambda e, it=it, hs=hs, bank=bank, Pprev=Pprev, Qprev=Qprev: e.matmul(
                                    psum[bank][hs, (it % 4) * 128 + 64:(it % 4) * 128 + 128], lhsT=Pprev(it)[hs, :], rhs=Qprev(it)[hs, :], start=True, stop=True),
                                    reads=rk_, writes=[f"ps{bank}"])
                        for bank in range(2):
                            P.act(lambda e, bank=bank, PQd=PQd: e.activation(out=PQd[:, bank * 4:(bank + 1) * 4, :].rearrange("p a b -> p (a b)"),
                                                                            in_=psum[bank][:, 0:512], func=AF.Copy),
                                  reads=[f"ps{bank}"], writes=[f"PQ{dst_i}", f"ps{bank}"])
                        Ts, Td = v(TT[src_i]), v(TT[dst_i])
                        for it in range(8):
                            for hh in range(2):
                                hs = slice(hh * 64, hh * 64 + 64)
                                P.pe(lambda e, it=it, hs=hs, PQd=PQd, Ts=Ts: e.matmul(
                                    psum[2][hs, it * 64:(it + 1) * 64], lhsT=PQd[hs, it, 64:128], rhs=Ts[hs, it, :], start=True, stop=True),
                                    reads=[f"PQ{dst_i}", f"TT{src_i}"], writes=["ps2"])
                        P.dve(lambda e, Ts=Ts, Td=Td: e.tensor_tensor(out=Td, in0=psum[2][:, 0:512].rearrange("p (a b) -> p a b", a=8), in1=Ts, op=ALU.add),
                              reads=["ps2", f"TT{src_i}"], writes=[f"TT{dst_i}", "ps2"])
                    Tfin = v(TT[1])
                    Sv, Sbv, Wbv, Ubv, Yv = v(Sst), v(Sb), v(Wb), v(Ub), v(Ybuf)
                    for c in range(4):
                        for p in range(2):
                            it = p * 4 + c
                            for hh in range(2):
                                hs = slice(hh * 64, hh * 64 + 64)
                                P.pe(lambda e, p=p, c=c, hs=hs: e.matmul(psum[3][hs, p * 64:(p + 1) * 64], lhsT=ARv[hs, p, c, 0, :], rhs=Sbv[hs, p, :],
                                                                         start=True, stop=False), reads=["AR", "Sb"], writes=["ps3"])
                                P.pe(lambda e, p=p, c=c, hs=hs, it=it: e.matmul(psum[3][hs, p * 64:(p + 1) * 64], lhsT=MSv[hs, it, 128:192], rhs=TKv[hs, p, c, 2, :],
                                                                                start=False, stop=True), reads=["MS", "TK"], writes=["ps3"])
                        P.act(lambda e: e.activation(out=Wbv.rearrange("p a b -> p (a b)"), in_=psum[3][:, 0:128], func=AF.Copy),
                              reads=["ps3"], writes=["Wb", "ps3"])
                        for p in range(2):
                            it = p * 4 + c
                            for hh in range(2):
                                hs = slice(hh * 64, hh * 64 + 64)
                                P.pe(lambda e, p=p, hs=hs, it=it: e.matmul(psum[4][hs, p * 64:(p + 1) * 64], lhsT=Tfin[hs, it, :], rhs=Wbv[hs, p, :],
                                                                           start=True, stop=True), reads=["TT1", "Wb"], writes=["ps4"])
                        P.act(lambda e: e.activation(out=Ubv.rearrange("p a b -> p (a b)"), in_=psum[4][:, 0:128], func=AF.Copy),
                              reads=["ps4"], writes=["Ub", "ps4"])
                        for p in range(2):
                            it = p * 4 + c
                            for hh in range(2):
                                hs = slice(hh * 64, hh * 64 + 64)
                                P.pe(lambda e, p=p, c=c, hs=hs: e.matmul(psum[5][hs, p * 64:(p + 1) * 64], lhsT=ARv[hs, p, c, 1, :], rhs=Sbv[hs, p, :],
                                                                         start=True, stop=False), reads=["AR", "Sb"], writes=["ps5"])
                                P.pe(lambda e, p=p, hs=hs, it=it: e.matmul(psum[5][hs, p * 64:(p + 1) * 64], lhsT=MSv[hs, it, 64:128], rhs=Ubv[hs, p, :],
                                                                           start=False, stop=False), reads=["MS", "Ub"], writes=["ps5"])
                                P.pe(lambda e, p=p, c=c, hs=hs, it=it: e.matmul(psum[5][hs, p * 64:(p + 1) * 64], lhsT=MSv[hs, it, 192:256], rhs=TKv[hs, p, c, 2, :],
                                                                                start=False, stop=True), reads=["MS", "TK"], writes=["ps5"])
                                P.pe(lambda e, p=p, c=c, hs=hs: e.matmul(psum[6][hs, p * 64:(p + 1) * 64], lhsT=TKv[hs, p, c, 0, :], rhs=Ubv[hs, p, :],
                                                                         start=True, stop=False), reads=["TK", "Ub"], writes=["ps6"])
                                P.pe(lambda e, p=p, c=c, hs=hs: e.matmul(psum[6][hs, p * 64:(p + 1) * 64], lhsT=TKv[hs, p, c, 1, :], rhs=TKv[hs, p, c, 2, :],
                                                                         start=False, stop=True), reads=["TK"], writes=["ps6"])
                        P.dve(lambda e, c=c: e.tensor_copy(out=Yv[:, c, :, :], in_=psum[5][:, 0:128].rearrange("p (a b) -> p a b", a=2)),
                              reads=["ps5"], writes=["Ybuf", "ps5"])
                        P.dve(lambda e: e.tensor_tensor(out=Sv, in0=psum[6][:, 0:128].rearrange("p (a b) -> p a b", a=2), in1=Sv, op=ALU.add),
                              reads=["ps6", "Sst"], writes=["Sst", "ps6"])
                        P.dve(lambda e, c=c: e.tensor_tensor(out=Sv, in0=Sv, in1=v(eL)[:, :, c * 64 + 63:c * 64 + 64].to_broadcast([128, 2, 64]), op=ALU.mult),
                              reads=["Sst", "eL"], writes=["Sst"])
                        P.act(lambda e: e.activation(out=Sbv, in_=Sv, func=AF.Copy), reads=["Sst"], writes=["Sb"])
                    if g == NGR - 1:
                        P.dma(lambda e: e.dma_start(out=swp[l], in_=Sv), reads=["Sst"], writes=["swp"], semkey="swp")
                    gsv = v(gs)
                    Y8 = Yv.rearrange("p c a b -> p (c a) b")
                    P.dve(lambda e: e.tensor_reduce(out=gsv[:, 0, :], in_=Y8, axis=AX.X, op=ALU.add), reads=["Ybuf"], writes=["gs"])
                    P.dve(lambda e: e.tensor_scalar_mul(out=gsv[:, 0, :], in0=gsv[:, 0, :], scalar1=-1.0 / 64), reads=["gs"], writes=["gs"])
                    P.pool(lambda e: e.tensor_tensor(out=v(yc), in0=Y8, in1=gsv[:, 0, :].unsqueeze(2).to_broadcast([128, 8, 64]), op=ALU.add),
                           reads=["Ybuf", "gs"], writes=["pb"])
                    P.pool(lambda e: e.tensor_tensor(out=v(ysq), in0=v(yc), in1=v(yc), op=ALU.mult), reads=["pb"], writes=["sc"])
                    P.dve(lambda e: e.tensor_reduce(out=gsv[:, 1, :], in_=v(ysq), axis=AX.X, op=ALU.add), reads=["sc"], writes=["gs"])
                    P.act(lambda e: e.activation(out=gsv[:, 2, :], in_=gsv[:, 1, :], func=AF.Sqrt, scale=1.0 / 64, bias=epsc[:, 1:2]),
                          reads=["gs", "epsc"], writes=["gs"])
                    P.dve(lambda e: e.reciprocal(out=gsv[:, 3, :], in_=gsv[:, 2, :]), reads=["gs"], writes=["gs"])
                    P.dve(lambda e: e.tensor_tensor(out=v(yh), in0=v(yc), in1=gsv[:, 3, :].unsqueeze(2).to_broadcast([128, 8, 64]), op=ALU.mult),
                          reads=["pb", "gs"], writes=["yh"])
                    for c in range(4):
                        for p in range(2):
                            for hh in range(2):
                                hs = slice(hh * 64, hh * 64 + 64)
                                P.pe(lambda e, c=c, p=p, hs=hs: e.transpose(
                                    out=psum[7][:].bitcast(BF16)[hs, p * G + c * 64:p * G + (c + 1) * 64],
                                    in_=v(yh)[hs, c * 2 + p, :], identity=cs["ident"][hs, hs]), reads=["yh", "k_ident"], writes=["ps7"])
                    for p in range(2):
                        P.act(lambda e, p=p: e.activation(out=v(yz)[:, p, :], in_=psum[7][:].bitcast(BF16)[:, p * G:(p + 1) * G], func=AF.Identity,
                                                          scale=rpv[:, 5, p:p + 1], bias=rpv[:, 6, p:p + 1]),
                              reads=["ps7", "rp"], writes=["rn", "ps7"])
                    P.pool(lambda e: e.tensor_tensor(out=v(yz), in0=v(yz), in1=v(bonus), op=ALU.add), reads=["rn", "bonus"], writes=["rn"])
                    P.dve(lambda e, gq=gsil2[g % 2]: e.tensor_tensor(out=v(zT)[:, 2:4, :], in0=v(yz), in1=v(gq)[:, 2:4, :], op=ALU.mult),
                          reads=["rn", f"gsil{g % 2}"], writes=["zT"])

                for bl in range(G // 128 if 'op' not in SKIP else 0):
                    for cgp in range(4):
                        for i in range(4):
                            P.pe(lambda e, bl=bl, cgp=cgp, i=i: e.matmul(psum[cgp][:, 0:512], lhsT=v(zT)[:, i, bl * 128:(bl + 1) * 128],
                                                                         rhs=Wo[:, i, cgp * 512:(cgp + 1) * 512], start=(i == 0), stop=(i == 3)),
                                 reads=["zT"] + [f"Wo{j}" for j in range(4)], writes=[f"ps{cgp}"])
                        if cgp % 2 == 0:
                            P.act(lambda e, cgp=cgp: e.activation(out=ob[0][:, cgp * 512:(cgp + 1) * 512], in_=psum[cgp][:, 0:512], func=AF.Copy),
                                  reads=[f"ps{cgp}"], writes=["ob", f"ps{cgp}"])
                        else:
                            P.dve(lambda e, cgp=cgp: e.tensor_copy(out=ob[0][:, cgp * 512:(cgp + 1) * 512], in_=psum[cgp][:, 0:512]),
                                  reads=[f"ps{cgp}"], writes=["ob", f"ps{cgp}"])
                    tok0 = g * G + bl * 128
                    row0 = tok0
                    for q in range(2):
                        P.dma(lambda e, q=q, row0=row0: e.dma_start(out=rs_in[l][q].ap()[row0:row0 + 128, :], in_=ob[0][:, q * 1024:(q + 1) * 1024]),
                              reads=["ob"], writes=[f"rs_in{l}"], semkey=f"ob{q}")

    def phase_M_samples(l, Win, WK):
        P.barrier()
        NS = 16
        c7 = Carve()
        hTs16 = c7.bf([128, 16, NS])
        rp = c7.f32([128, NPAR, 2])
        loraW = c7.bf([128, 2, 128])
        mu = c7.f32([128, 7])
        prv = c7.f32([128, 7, NS])
        shpt = c7.f32([64, 193])
        P.dma(lambda e: e.dma_start(out=v(rp), in_=rpar[l]), writes=["s_rp"])
        loraF = c7.f32([128, 2, 128])
        P.dma(lambda e: e.dma_start(out=v(loraF), in_=loraw[l]), writes=["s_loraF"])
        P.act(lambda e: e.activation(out=v(loraW), in_=v(loraF), func=AF.Copy), reads=["s_loraF"], writes=["s_loraW"])
        P.dma(lambda e: e.dma_start(out=mu[0], in_=mu_col[l]), writes=["s_mu"])
        P.dma(lambda e: e.dma_start(out=v(prv), in_=prev_s[l]), writes=["s_prv"])
        P.dma(lambda e: e.dma_start(out=shpt[0], in_=shpar_in[l]), writes=["s_shpt"])
        curS = c7.f32([128, 7, NS])
        mixS = c7.f32([128, 7, NS])
        qf = c7.f32([128, 3, NS])
        qbS = c7.bf([128, 3, NS])
        t1S = c7.f32([128, 3, NS])
        qrotS = c7.f32([128, 3, NS])
        gsS = c7.f32([128, 4, NS])
        vS = c7.f32([16, 64])
        for r2 in range(4):
            P.dma(lambda e, r2=r2: e.dma_start(out=v(hTs16)[:, :, 4 * r2:4 * r2 + 4],
                                               in_=ago_hs[l].ap()[r2 * 128:(r2 + 1) * 128, :].rearrange("p (k c) -> p k c", k=16)),
                  reads=[f"ago_hs{l}"], writes=["s_hT"], eng=("sp" if r2 % 2 == 0 else "act"), semkey=f"s_hT{r2 % 2}")
        for ct in range(NCT):
            pi = ct % 4
            for k in range(16):
                P.pe(lambda e, ct=ct, k=k, pi=pi: e.matmul(psum[pi][:, 0:NS], lhsT=Win[:, k, ct * 128:(ct + 1) * 128], rhs=v(hTs16)[:, k, :],
                                                           start=(k == 0), stop=(k == 15)), reads=["s_hT"], writes=[f"ps{pi}"])
            if ct < 3:
                P.act(lambda e, ct=ct, pi=pi: e.activation(out=v(qf)[:, ct, :], in_=psum[pi][:, 0:NS], func=AF.Copy),
                      reads=[f"ps{pi}"], writes=["s_qf", f"ps{pi}"])
            elif ct < 10:
                P.act(lambda e, ct=ct, pi=pi: e.activation(out=v(curS)[:, ct - 3, :], in_=psum[pi][:, 0:NS], func=AF.Copy),
                      reads=[f"ps{pi}"], writes=["s_cur", f"ps{pi}"])
            else:
                P.act(lambda e, ct=ct, pi=pi: e.activation(out=v(gsS)[:, ct - 10, :], in_=psum[pi][:, 0:NS], func=AF.Silu),
                      reads=[f"ps{pi}"], writes=["s_gs", f"ps{pi}"])
        for k in range(16):
            P.pe(lambda e, k=k: e.matmul(psum[4][0:NS, 0:64], lhsT=v(hTs16)[:, k, :], rhs=Win[:, k, NCT * 128:NCT * 128 + 64],
                                         start=(k == 0), stop=(k == 15)), reads=["s_hT"], writes=["ps4"])
        P.act(lambda e: e.activation(out=vS[0], in_=psum[4][0:NS, 0:64], func=AF.Copy), reads=["ps4"], writes=["s_vS", "ps4"])
        P.dma(lambda e: e.dma_start(out=shs[l], in_=v(curS)), reads=["s_cur"], writes=["shs"], semkey="shs")
        P.act(lambda e: e.activation(out=v(qbS), in_=v(qf), func=AF.Copy), reads=["s_qf"], writes=["s_qb"])
        for ct in range(3):
            P.pe(lambda e, ct=ct: e.matmul(psum[5][:, ct * NS:(ct + 1) * NS], lhsT=cs["prot"][:], rhs=v(qbS)[:, ct, :], start=True, stop=True),
                 reads=["s_qb", "k_prot"], writes=["ps5"])
        P.dve(lambda e: e.tensor_scalar_mul(out=t1S[0], in0=psum[5][:, 0:3 * NS], scalar1=cs["cs_s"][:, 1:2]), reads=["ps5", "k_cs_s"],
              writes=["s_t1", "ps5"])
        P.dve(lambda e: e.scalar_tensor_tensor(out=qrotS[0], in0=qf[0], scalar=cs["cs_s"][:, 0:1], in1=t1S[0], op0=ALU.mult, op1=ALU.add),
              reads=["s_qf", "s_t1", "k_cs_s"], writes=["s_qrot"])
        P.dve(lambda e: e.tensor_tensor(out=v(mixS), in0=v(prv), in1=v(curS), op=ALU.subtract), reads=["s_prv", "s_cur"], writes=["s_mix"])
        P.dve(lambda e: e.tensor_tensor(out=v(mixS), in0=v(mixS), in1=mu[0].unsqueeze(2).to_broadcast([128, 7, NS]), op=ALU.mult),
              reads=["s_mix", "s_mu"], writes=["s_mix"])
        P.dve(lambda e: e.tensor_tensor(out=v(mixS), in0=v(mixS), in1=v(curS), op=ALU.add), reads=["s_mix", "s_cur"], writes=["s_mix"])
        mx = v(mixS)
        rpv = v(rp)
        twadS = c7.bf([128, NS])
        sigS = c7.f32([128, 2, NS])
        aS = c7.f32([128, 2, NS])
        wS = c7.f32([128, 2, NS])
        kkrS = c7.f32([128, 2, NS])
        sqS = c7.bf([128, 2, NS])
        rnS = c7.f32([128, 2, NS])
        kkS = c7.f32([128, 2, NS])
        uuS = c7.f32([128, 2, NS])
        keffS = c7.f32([128, 2, NS])
        P.act(lambda e: e.activation(out=twadS[0][0:64, :], in_=mx[0:64, 6, :], func=AF.Tanh), reads=["s_mix"], writes=["s_twad"])
        P.act(lambda e: e.activation(out=twadS[0][64:128, :], in_=mx[64:128, 6, :], func=AF.Copy), reads=["s_mix"], writes=["s_twad"])
        for p in range(2):
            P.pe(lambda e, p=p: e.matmul(psum[p][:, 0:NS], lhsT=v(loraW)[0:64, p, :], rhs=twadS[0][0:64, :], start=True, stop=True),
                 reads=["s_loraW", "s_twad"], writes=[f"ps{p}"])
            P.act(lambda e, p=p: e.activation(out=v(sigS)[:, p, :], in_=psum[p][:, 0:NS], func=AF.Sigmoid, bias=rpv[:, 0, p:p + 1]),
                  reads=[f"ps{p}", "s_rp"], writes=["s_sig", f"ps{p}"])
            P.pe(lambda e, p=p: e.matmul(psum[2 + p][:, 0:NS], lhsT=v(loraW)[64:128, p, :], rhs=twadS[0][64:128, :], start=True, stop=True),
                 reads=["s_loraW", "s_twad"], writes=[f"ps{2 + p}"])
            P.act(lambda e, p=p: e.activation(out=v(aS)[:, p, :], in_=psum[2 + p][:, 0:NS], func=AF.Sigmoid, bias=rpv[:, 1, p:p + 1]),
                  reads=[f"ps{2 + p}", "s_rp"], writes=["s_a", f"ps{2 + p}"])
        P.act(lambda e: e.activation(out=v(wS), in_=v(sigS), func=AF.Exp, scale=-CDEC), reads=["s_sig"], writes=["s_w"])
        bc = lambda j: rpv[:, j, :].unsqueeze(2).to_broadcast([128, 2, NS])
        P.dve(lambda e: e.tensor_tensor(out=v(kkrS), in0=mx[:, 2:4, :], in1=bc(2), op=ALU.mult), reads=["s_mix", "s_rp"], writes=["s_kkr"])
        P.dve(lambda e: e.tensor_tensor(out=v(sqS), in0=v(kkrS), in1=v(kkrS), op=ALU.mult), reads=["s_kkr"], writes=["s_sq"])
        for p in range(2):
            P.pe(lambda e, p=p: e.matmul(psum[p][:, 0:NS], lhsT=cs["bones"][:], rhs=v(sqS)[:, p, :], start=True, stop=True),
                 reads=["s_sq", "k_bones"], writes=[f"ps{p}"])
            P.act(lambda e, p=p: e.activation(out=v(rnS)[:, p, :], in_=psum[p][:, 0:NS], func=AF.Sqrt), reads=[f"ps{p}"], writes=["s_rn", f"ps{p}"])
        P.dve(lambda e: e.tensor_scalar_max(out=v(rnS), in0=v(rnS), scalar1=1e-12), reads=["s_rn"], writes=["s_rn"])
        P.dve(lambda e: e.reciprocal(out=v(rnS), in_=v(rnS)), reads=["s_rn"], writes=["s_rn"])
        P.dve(lambda e: e.tensor_tensor(out=v(kkS), in0=v(kkrS), in1=v(rnS), op=ALU.mult), reads=["s_kkr", "s_rn"], writes=["s_kk"])
        P.dve(lambda e: e.scalar_tensor_tensor(out=v(uuS), in0=v(aS), scalar=-1.0, in1=bc(3), op0=ALU.add, op1=ALU.mult),
              reads=["s_a", "s_rp"], writes=["s_uu"])
        P.dve(lambda e: e.scalar_tensor_tensor(out=v(keffS), in0=v(uuS), scalar=1.0, in1=mx[:, 2:4, :], op0=ALU.add, op1=ALU.mult),
              reads=["s_uu", "s_mix"], writes=["s_keff"])
        tmS = c7.f32([16, 20, 128])
        srcs = []
        for (t_, key, lo) in ((mixS, "s_mix", 0), (wS, "s_w", 0), (keffS, "s_keff", 0), (mixS, "s_mix", 4), (kkS, "s_kk", 0), (aS, "s_a", 0),
                              (qrotS, "s_qrot", 0), (gsS, "s_gs", 0), (gsS, "s_gs", 2)):
            for p in range(2):
                srcs.append((v(t_)[:, lo + p, :], key))
        srcs.append((v(qrotS)[:, 2, :], "s_qrot"))
        for n0 in range(0, len(srcs), 4):
            bank = 4 + (n0 // 4) % 4
            grp = srcs[n0:n0 + 4]
            for i_, (ap_, key) in enumerate(grp):
                P.pe(lambda e, ap_=ap_, i_=i_, bank=bank: e.transpose(out=psum[bank][0:NS, i_ * 128:(i_ + 1) * 128], in_=ap_, identity=cs["identf"][:]),
                     reads=[key, "k_identf"], writes=[f"ps{bank}"])
            n_ = len(grp)
            evac = P.act if (n0 // 4) % 2 == 0 else P.dve
            if (n0 // 4) % 2 == 0:
                P.act(lambda e, n0=n0, n_=n_, bank=bank: e.activation(out=v(tmS)[:, n0:n0 + n_, :].rearrange("p a b -> p (a b)"),
                                                                   in_=psum[bank][0:NS, 0:n_ * 128], func=AF.Copy),
                      reads=[f"ps{bank}"], writes=["s_tm", f"ps{bank}"])
            else:
                P.dve(lambda e, n0=n0, n_=n_, bank=bank: e.tensor_copy(out=v(tmS)[:, n0:n0 + n_, :].rearrange("p a b -> p (a b)"),
                                                                    in_=psum[bank][0:NS, 0:n_ * 128]),
                      reads=[f"ps{bank}"], writes=["s_tm", f"ps{bank}"])
        for kind in range(9):
            P.dma(lambda e, kind=kind: e.dma_start(out=smp_d[l].ap()[:, :, kind, :].rearrange("h s d -> s h d"),
                                                   in_=v(tmS)[:, 2 * kind:2 * kind + 2, :].rearrange("s t (h d) -> s (t h) d", h=2)),
                  reads=["s_tm"], writes=["smp_d"], eng=("sp" if kind % 2 == 0 else "act"), semkey=f"smp{kind % 2}")
        P.dma(lambda e: e.dma_start(out=kn_d[l].ap(), in_=v(tmS)[:, 18, 0:64]), reads=["s_tm"], writes=["kn_d"], semkey="kn")
        P.dma(lambda e: e.dma_start(out=vn_d[l].ap(), in_=vS[0]), reads=["s_vS"], writes=["vn_d"], semkey="vn")
        P.dma(lambda e: e.dma_start(out=cks[l][:, 0:127, :], in_=ck_in[l][:, 1:128, :]), writes=["cks"], semkey="cks0")
        P.dma(lambda e: e.dma_start(out=cvs[l][:, 0:127, :], in_=cv_in[l][:, 1:128, :]), writes=["cvs"], eng="act", semkey="cvs0")
        P.dma(lambda e: e.dma_start(out=cks[l][:, 127, :], in_=v(tmS)[:, 18, 0:64]), reads=["s_tm"], writes=["cks"], semkey="cks1")
        P.dma(lambda e: e.dma_start(out=cvs[l][:, 127, :], in_=vS[0]), reads=["s_vS"], writes=["cvs"], eng="act", semkey="cvs1")
        SH = c7.f32([64, 9, 64])
        SHv = v(SH)
        P.dma(lambda e: e.dma_start(out=SHv, in_=smp_d[l].ap().rearrange("h s k d -> (h s) k d")), reads=["smp_d"], writes=["s_SH"])
        KV = c7.f32([64, 129, 64])
        KVv = v(KV)
        tmpA = c7.f32([64, 33 * 64])
        scs = c7.f32([64, 129])
        pS = c7.f32([64, 129])
        sm2 = c7.f32([64, 8])
        oS = c7.f32([64, 64])
        prt = c7.f32([64, 64])
        zs = c7.f32([64, 2, 64])
        for h_ in range(4):
            q_ = "sp" if h_ % 2 == 0 else "act"
            P.dma(lambda e, h_=h_: e.dma_start(out=KVv[16 * h_:16 * h_ + 16, 0:128, :], in_=ck_in[l]), writes=["s_KV"], eng=q_, semkey=f"s_KV{h_}")
            P.dma(lambda e, h_=h_: e.dma_start(out=KVv[16 * h_:16 * h_ + 16, 128, :], in_=kn_d[l].ap()), reads=["kn_d"], writes=["s_KV"], eng=q_,
                  semkey=f"s_KV{h_}")
        PCH = [(0, 32), (32, 64), (64, 96), (96, 129)]
        for ci, (a_, b_) in enumerate(PCH):
            n_ = b_ - a_
            tv = tmpA[0][:, 0:n_ * 64].rearrange("p (n d) -> p n d", d=64)
            f_ = P.dve
            f_(lambda e, a_=a_, b_=b_, n_=n_, tv=tv: e.tensor_tensor(out=tv, in0=KVv[:, a_:b_, :], in1=SHv[:, 6, :].unsqueeze(1).to_broadcast([64, n_, 64]),
                                                              op=ALU.mult), reads=["s_KV", "s_SH"], writes=["s_tmpA"])
            P.dve(lambda e, a_=a_, b_=b_, tv=tv: e.tensor_reduce(out=scs[0][:, a_:b_], in_=tv, axis=AX.X, op=ALU.add), reads=["s_tmpA"], writes=["s_scs"])
        s2 = sm2[0]
        snkc = shpt[0][:, 192:193]
        P.dve(lambda e: e.tensor_reduce(out=s2[:, 0:1], in_=scs[0], axis=AX.X, op=ALU.max), reads=["s_scs"], writes=["s_sm2"])
        P.dve(lambda e: e.scalar_tensor_tensor(out=s2[:, 1:2], in0=s2[:, 0:1], scalar=0.125, in1=snkc, op0=ALU.mult, op1=ALU.max),
              reads=["s_sm2", "s_shpt"], writes=["s_sm2"])
        P.dve(lambda e: e.tensor_scalar_mul(out=s2[:, 2:3], in0=s2[:, 1:2], scalar1=-1.0), reads=["s_sm2"], writes=["s_sm2"])
        P.dve(lambda e: e.tensor_tensor(out=s2[:, 4:5], in0=snkc, in1=s2[:, 1:2], op=ALU.subtract), reads=["s_sm2", "s_shpt"], writes=["s_sm2"])
        P.act(lambda e: e.activation(out=pS[0], in_=scs[0], func=AF.Exp, scale=0.125, bias=s2[:, 2:3], accum_out=s2[:, 3:4]),
              reads=["s_scs", "s_sm2"], writes=["s_pS", "s_sm2"])
        P.act(lambda e: e.activation(out=s2[:, 4:5], in_=s2[:, 4:5], func=AF.Exp), reads=["s_sm2"], writes=["s_sm2"])
        P.dve(lambda e: e.tensor_tensor(out=s2[:, 5:6], in0=s2[:, 3:4], in1=s2[:, 4:5], op=ALU.add), reads=["s_sm2"], writes=["s_sm2"])
        P.dve(lambda e: e.reciprocal(out=s2[:, 6:7], in_=s2[:, 5:6]), reads=["s_sm2"], writes=["s_sm2"])
        for h_ in range(4):
            q_ = "sp" if h_ % 2 == 0 else "act"
            P.dma(lambda e, h_=h_: e.dma_start(out=KVv[16 * h_:16 * h_ + 16, 0:128, :], in_=cv_in[l]), reads=["s_scs"], writes=["s_KV"], eng=q_,
                  semkey=f"s_KV{h_}")
            P.dma(lambda e, h_=h_: e.dma_start(out=KVv[16 * h_:16 * h_ + 16, 128, :], in_=vn_d[l].ap()), reads=["vn_d", "s_scs"], writes=["s_KV"],
                  eng=q_, semkey=f"s_KV{h_}")
        for ci, (a_, b_) in enumerate(PCH):
            n_ = b_ - a_
            tv = tmpA[0][:, 0:n_ * 64].rearrange("p (d n) -> p d n", d=64)
            f_ = P.dve
            f_(lambda e, a_=a_, b_=b_, n_=n_, tv=tv: e.tensor_tensor(out=tv, in0=KVv[:, a_:b_, :].rearrange("p n d -> p d n"),
                                                              in1=pS[0][:, a_:b_].unsqueeze(1).to_broadcast([64, 64, n_]), op=ALU.mult),
               reads=["s_KV", "s_pS"], writes=["s_tmpA"])
            if ci == 0:
                P.dve(lambda e, tv=tv: e.tensor_reduce(out=oS[0], in_=tv, axis=AX.X, op=ALU.add), reads=["s_tmpA"], writes=["s_oS"])
            else:
                P.dve(lambda e, tv=tv: e.tensor_reduce(out=prt[0], in_=tv, axis=AX.X, op=ALU.add), reads=["s_tmpA"], writes=["s_prt"])
                P.dve(lambda e: e.tensor_tensor(out=oS[0], in0=oS[0], in1=prt[0], op=ALU.add), reads=["s_oS", "s_prt"], writes=["s_oS"])
        zsv = v(zs)
        P.dve(lambda e: e.tensor_scalar_mul(out=oS[0], in0=oS[0], scalar1=s2[:, 6:7]), reads=["s_oS", "s_sm2"], writes=["s_oS"])
        P.dve(lambda e: e.tensor_tensor(out=zsv[:, 0, :], in0=oS[0], in1=SHv[:, 7, :], op=ALU.mult), reads=["s_oS", "s_SH"], writes=["s_zs"])
        Ssm = c7.f32([64, 64, 64])
        tmpS = c7.f32([64, 64, 64])
        sv = c7.f32([64, 6, 64])
        g2 = c7.f32([64, 8])
        Sv_, Tv_, svv = v(Ssm), v(tmpS), v(sv)
        for h_ in range(4):
            P.dma(lambda e, h_=h_: e.dma_start(out=Sv_[16 * h_:16 * h_ + 16], in_=st_in[l][:, h_]), writes=["s_Ssm"],
                  eng=("sp" if h_ % 2 == 0 else "act"), semkey=f"s_Sl{h_}")
        bi = lambda ap_: ap_.unsqueeze(1).to_broadcast([64, 64, 64])
        bj = lambda ap_: ap_.unsqueeze(2).to_broadcast([64, 64, 64])
        P.dve(lambda e: e.tensor_scalar_mul(out=svv[:, 0, :], in0=SHv[:, 4, :], scalar1=-1.0), reads=["s_SH"], writes=["s_sv0"])
        P.dve(lambda e: e.tensor_tensor(out=svv[:, 1, :], in0=SHv[:, 4, :], in1=SHv[:, 5, :], op=ALU.mult), reads=["s_SH"], writes=["s_sv1"])
        P.dve(lambda e: e.tensor_tensor(out=Tv_, in0=Sv_, in1=bi(svv[:, 0, :]), op=ALU.mult), reads=["s_Ssm", "s_sv0"], writes=["s_tmpS"])
        P.dve(lambda e: e.tensor_reduce(out=svv[:, 2, :], in_=Tv_, axis=AX.X, op=ALU.add), reads=["s_tmpS"], writes=["s_sv2"])
        P.dve(lambda e: e.tensor_tensor(out=Sv_, in0=Sv_, in1=bi(SHv[:, 1, :]), op=ALU.mult), reads=["s_Ssm", "s_SH", "s_tmpS"], writes=["s_Ssm"])
        P.dve(lambda e: e.tensor_tensor(out=Tv_, in0=bj(svv[:, 2, :]), in1=bi(svv[:, 1, :]), op=ALU.mult), reads=["s_sv2", "s_sv1"], writes=["s_tmpS"])
        P.dve(lambda e: e.tensor_tensor(out=Sv_, in0=Sv_, in1=Tv_, op=ALU.add), reads=["s_Ssm", "s_tmpS"], writes=["s_Ssm"])
        P.dve(lambda e: e.tensor_tensor(out=Tv_, in0=bj(SHv[:, 3, :]), in1=bi(SHv[:, 2, :]), op=ALU.mult), reads=["s_SH"], writes=["s_tmpS"])
        P.dve(lambda e: e.tensor_tensor(out=Sv_, in0=Sv_, in1=Tv_, op=ALU.add), reads=["s_Ssm", "s_tmpS"], writes=["s_Ssm"])
        for h_ in range(4):
            P.dma(lambda e, h_=h_: e.dma_start(out=sws[l][:, h_], in_=Sv_[16 * h_:16 * h_ + 16]), reads=["s_Ssm"], writes=["sws"],
                  eng=("sp" if h_ % 2 == 0 else "act"), semkey=f"s_Ss{h_}")
        P.dve(lambda e: e.tensor_tensor(out=Tv_, in0=Sv_, in1=bi(SHv[:, 0, :]), op=ALU.mult), reads=["s_Ssm", "s_SH"], writes=["s_tmpS"])
        P.dve(lambda e: e.tensor_reduce(out=svv[:, 3, :], in_=Tv_, axis=AX.X, op=ALU.add), reads=["s_tmpS"], writes=["s_sv3"])
        g2v = g2[0]
        P.dve(lambda e: e.tensor_reduce(out=g2v[:, 0:1], in_=svv[:, 3, :], axis=AX.X, op=ALU.add), reads=["s_sv3"], writes=["s_g2"])
        P.dve(lambda e: e.tensor_scalar_mul(out=g2v[:, 0:1], in0=g2v[:, 0:1], scalar1=-1.0 / 64), reads=["s_g2"], writes=["s_g2"])
        P.dve(lambda e: e.tensor_scalar_add(out=svv[:, 3, :], in0=svv[:, 3, :], scalar1=g2v[:, 0:1]), reads=["s_sv3", "s_g2"], writes=["s_sv3"])
        P.dve(lambda e: e.tensor_tensor(out=svv[:, 4, :], in0=svv[:, 3, :], in1=svv[:, 3, :], op=ALU.mult), reads=["s_sv3"], writes=["s_sv4"])
        P.dve(lambda e: e.tensor_reduce(out=g2v[:, 1:2], in_=svv[:, 4, :], axis=AX.X, op=ALU.add), reads=["s_sv4"], writes=["s_g2"])
        P.act(lambda e: e.activation(out=g2v[:, 2:3], in_=g2v[:, 1:2], func=AF.Sqrt, scale=1.0 / 64, bias=epsc[0:64, 1:2]),
              reads=["s_g2", "epsc"], writes=["s_g2"])
        P.dve(lambda e: e.reciprocal(out=g2v[:, 3:4], in_=g2v[:, 2:3]), reads=["s_g2"], writes=["s_g2"])
        P.dve(lambda e: e.tensor_scalar_mul(out=svv[:, 3, :], in0=svv[:, 3, :], scalar1=g2v[:, 3:4]), reads=["s_sv3", "s_g2"], writes=["s_sv3"])
        P.dve(lambda e: e.tensor_tensor(out=svv[:, 3, :], in0=svv[:, 3, :], in1=shpt[0][:, 0:64], op=ALU.mult), reads=["s_sv3", "s_shpt"], writes=["s_sv3"])
        P.dve(lambda e: e.tensor_tensor(out=svv[:, 3, :], in0=svv[:, 3, :], in1=shpt[0][:, 64:128], op=ALU.add), reads=["s_sv3", "s_shpt"], writes=["s_sv3"])
        P.dve(lambda e: e.tensor_tensor(out=svv[:, 4, :], in0=SHv[:, 0, :], in1=SHv[:, 2, :], op=ALU.mult), reads=["s_SH", "s_g2"], writes=["s_sv4"])
        P.dve(lambda e: e.tensor_tensor(out=svv[:, 4, :], in0=svv[:, 4, :], in1=shpt[0][:, 128:192], op=ALU.mult), reads=["s_sv4", "s_shpt"], writes=["s_sv4"])
        P.dve(lambda e: e.tensor_reduce(out=g2v[:, 4:5], in_=svv[:, 4, :], axis=AX.X, op=ALU.add), reads=["s_sv4"], writes=["s_g2"])
        P.dve(lambda e: e.tensor_scalar_mul(out=svv[:, 5, :], in0=SHv[:, 3, :], scalar1=g2v[:, 4:5]), reads=["s_SH", "s_g2"], writes=["s_sv5"])
        P.dve(lambda e: e.tensor_tensor(out=svv[:, 3, :], in0=svv[:, 3, :], in1=svv[:, 5, :], op=ALU.add), reads=["s_sv3", "s_sv5"], writes=["s_sv3"])
        P.dve(lambda e: e.tensor_tensor(out=zsv[:, 1, :], in0=svv[:, 3, :], in1=SHv[:, 8, :], op=ALU.mult), reads=["s_sv3", "s_SH"], writes=["s_zs"])
        zst = c7.f32([16, 2, 4, 64])
        zTs = c7.bf([128, 4, NS])
        obS = c7.f32([16, D])
        P.dma(lambda e: e.dma_start(out=zs_d[l].ap().rearrange("h s k d -> (h s) k d"), in_=zsv), reads=["s_zs"], writes=["zs_d"], semkey="zs_d")
        for k_ in range(2):
            P.dma(lambda e, k_=k_: e.dma_start(out=v(zst)[:, k_, :, :], in_=zs_d[l].ap()[:, :, k_, :].rearrange("h s d -> s h d")), reads=["zs_d"], writes=["s_zst"], semkey="zst")
        zstv = v(zst)
        for k_ in range(2):
            for pr in range(2):
                i_ = k_ * 2 + pr
                P.pe(lambda e, k_=k_, pr=pr, i_=i_: e.transpose(out=psum[6][:, i_ * NS:(i_ + 1) * NS],
                                                                in_=zstv[:, k_, 2 * pr:2 * pr + 2, :].rearrange("s h d -> s (h d)"),
                                                                identity=cs["identf"][0:NS, 0:NS]), reads=["s_zst", "k_identf"], writes=["ps6"])
        P.act(lambda e: e.activation(out=zTs[0], in_=psum[6][:, 0:4 * NS], func=AF.Copy), reads=["ps6"], writes=["s_zTs", "ps6"])
        for cgp in range(4):
            for i_ in range(4):
                P.pe(lambda e, cgp=cgp, i_=i_: e.matmul(psum[cgp][0:NS, 0:512], lhsT=v(zTs)[:, i_, :], rhs=Wo[:, i_, cgp * 512:(cgp + 1) * 512],
                                                        start=(i_ == 0), stop=(i_ == 3)), reads=["s_zTs"], writes=[f"ps{cgp}"])
            P.dve(lambda e, cgp=cgp: e.tensor_copy(out=obS[0][:, cgp * 512:(cgp + 1) * 512], in_=psum[cgp][0:NS, 0:512]),
                  reads=[f"ps{cgp}"], writes=["s_obS", f"ps{cgp}"])
        P.dma(lambda e: e.dma_start(out=rss_in[l].ap(), in_=obS[0]), reads=["s_obS"], writes=[f"rss_in{l}"], semkey="obS")
        P.op("pool", lambda e: e.collective_compute("ReduceScatter", ALU.add, replica_groups=RG,
                                                     ins=[rss_in[l].ap().opt()], outs=[rss_out[l].ap().opt()]),
             reads=[f"rss_in{l}"], writes=[f"rss_out{l}"], cc=True, semkey="cc")

    def reduce_scatter(l):
        for q in range(2):
            P.op("pool", lambda e, q=q: e.collective_compute("ReduceScatter", ALU.add, replica_groups=RG,
                                                          ins=[rs_in[l][q].ap().opt()], outs=[rs_out[l][q].ap().opt()], dma_qos="P1"),
                 reads=[f"rs_in{l}"], writes=[f"rs_out{l}"], cc=True, semkey="cc", nb=True)

    def phase_O(l):
        nonlocal NOFF
        c4 = Carve()
        gateB = c4.f32([128, D])
        fgB = c4.f32([128, D]) if l == NL - 1 else None
        ot = [c4.f32([128, D]) for _ in range(2)]
        xo = [c4.f32([128, D]) for _ in range(2)]
        fs = c4.f32([128, 2])
        NOFF = c4.off
        P.dma(lambda e: e.dma_start(out=gateB[0][:, 0:512],
                                    in_=ago_mod.ap()[2 * 17:2 * 17 + 1, l * 1536 + 1024:(l + 1) * 1536].partition_broadcast(128)[:, 0, :]),
              reads=["ago_mod"], writes=["gateB"], semkey="gateB")
        P.dma(lambda e: e.dma_start(out=gateB[0][:, 512:2048],
                                    in_=ago_mod.ap()[3 * 17:3 * 17 + 1, l * 1536:(l + 1) * 1536].partition_broadcast(128)[:, 0, :]),
              reads=["ago_mod"], writes=["gateB"], semkey="gateB")
        if l == NL - 1:
            P.dma(lambda e: e.dma_start(out=fgB[0], in_=fg_row[0:1, :].partition_broadcast(128)[:, 0, :]), writes=["fgB"])
        xsrc = x_own if l == 0 else x1_d.ap()

        def o_loads(t):
            s = t % 2
            for q in range(2):
                P.dma(lambda e, t=t, s=s, q=q: e.dma_start(out=ot[s][0][:, q * 1024:(q + 1) * 1024], in_=rs_out[l][q].ap()[t * 128:(t + 1) * 128, :]),
                      reads=[f"rs_out{l}"], writes=[f"ot{s}"], semkey=f"ot{s}")
            P.dma(lambda e, t=t, s=s: e.dma_start(out=xo[s][0], in_=xsrc[t * 128:(t + 1) * 128, :]),
                  reads=(["x1_d"] if l > 0 else []), writes=[f"xo{s}"], semkey=f"xo{s}")

        o_loads(0)
        for t in range(8):
            s = t % 2
            if t + 1 < 8:
                o_loads(t + 1)
            P.dve(lambda e, s=s: e.tensor_tensor(out=ot[s][0], in0=ot[s][0], in1=gateB[0], op=ALU.mult), reads=[f"ot{s}", "gateB"], writes=[f"ot{s}"])
            P.dve(lambda e, s=s: e.tensor_tensor(out=xo[s][0], in0=xo[s][0], in1=ot[s][0], op=ALU.add), reads=[f"ot{s}", f"xo{s}"], writes=[f"xo{s}"])
            if l < NL - 1:
                P.dma(lambda e, t=t, s=s: e.dma_start(out=x1_d.ap()[t * 128:(t + 1) * 128, :], in_=xo[s][0]), reads=[f"xo{s}"], writes=["x1_d"],
                      semkey=f"x1st{s}")
                phase_N_tile(l + 1, t, xo[s][0], f"xo{s}")
            elif 'fin' not in os.environ.get('MK_SKIP', ''):
                P.act(lambda e, s=s: e.activation(out=ot[s][0], in_=xo[s][0], func=AF.Square, accum_out=fs[0][:, 0:1]),
                      reads=[f"xo{s}"], writes=[f"ot{s}", "fs"])
                P.act(lambda e: e.activation(out=fs[0][:, 1:2], in_=fs[0][:, 0:1], func=AF.Sqrt, scale=1.0 / D, bias=epsc[:, 0:1]),
                      reads=["fs", "epsc"], writes=["fs"])
                P.dve(lambda e: e.reciprocal(out=fs[0][:, 1:2], in_=fs[0][:, 1:2]), reads=["fs"], writes=["fs"])
                P.act(lambda e, s=s: e.activation(out=ot[s][0], in_=xo[s][0], func=AF.Copy, scale=fs[0][:, 1:2]),
                      reads=[f"xo{s}", "fs"], writes=[f"ot{s}"])
                P.dve(lambda e, s=s: e.tensor_tensor(out=ot[s][0], in0=ot[s][0], in1=fgB[0], op=ALU.mult), reads=[f"ot{s}", "fgB"], writes=[f"ot{s}"])
                P.dma(lambda e, t=t, s=s: e.dma_start(out=y_own[t * 128:(t + 1) * 128, :], in_=ot[s][0]), reads=[f"ot{s}"], writes=["y_own"],
                      semkey=f"yst{s}")


        c8 = Carve()
        c8.off = NOFF + (6200 if l < NL - 1 else 100)
        osT = c8.f32([4, D])
        xsT = c8.f32([4, D])
        gS = c8.f32([4, D])
        fq = c8.f32([4, 2])
        P.dma(lambda e: e.dma_start(out=osT[0], in_=rss_out[l].ap()), reads=[f"rss_out{l}"], writes=["osT"], semkey="osT")
        xss = xs_own if l == 0 else x1_d.ap()[1024:1028, :]
        P.dma(lambda e: e.dma_start(out=xsT[0], in_=xss), reads=(["x1_d"] if l > 0 else []), writes=["xsT"], eng="act")
        P.dma(lambda e: e.dma_start(out=gS[0], in_=smod_d.ap()[l, 2]), reads=["smod_d"], writes=["gS"], eng="act")
        P.dve(lambda e: e.tensor_tensor(out=osT[0], in0=osT[0], in1=gS[0], op=ALU.mult), reads=["osT", "gS"], writes=["osT"])
        P.dve(lambda e: e.tensor_tensor(out=xsT[0], in0=xsT[0], in1=osT[0], op=ALU.add), reads=["osT", "xsT"], writes=["xsT"])
        if l < NL - 1:
            P.dma(lambda e: e.dma_start(out=x1_d.ap()[1024:1028, :], in_=xsT[0]), reads=["xsT"], writes=["x1_d"], semkey="x1s")
            phase_N_samples(l + 1, xsT[0], "xsT", c8.off)
        else:
            P.act(lambda e: e.activation(out=osT[0], in_=xsT[0], func=AF.Square, accum_out=fq[0][:, 0:1]), reads=["xsT"], writes=["osT", "fq"])
            P.act(lambda e: e.activation(out=fq[0][:, 1:2], in_=fq[0][:, 0:1], func=AF.Sqrt, scale=1.0 / D, bias=epsc[0:4, 0:1]),
                  reads=["fq", "epsc"], writes=["fq"])
            P.dve(lambda e: e.reciprocal(out=fq[0][:, 1:2], in_=fq[0][:, 1:2]), reads=["fq"], writes=["fq"])
            P.act(lambda e: e.activation(out=osT[0], in_=xsT[0], func=AF.Copy, scale=fq[0][:, 1:2]), reads=["xsT", "fq"], writes=["osT"])
            P.dve(lambda e: e.tensor_tensor(out=osT[0], in0=osT[0], in1=fgB[0][0:4, :], op=ALU.mult), reads=["osT", "fgB"], writes=["osT"])
            P.dma(lambda e: e.dma_start(out=ys_own, in_=osT[0]), reads=["osT"], writes=["ys_own"], semkey="ysst")

    NLR = int(os.environ.get("MK_NL", NL))
    for l in range(NLR):
        P.epoch = l
        phase_M(l)
        reduce_scatter(l)
        if 'smp' not in os.environ.get('MK_SKIP', ''):
            phase_M_samples(l, WBIG[:, 0:16 * WCOLS].rearrange("p (k c) -> p k c", k=16), None)
        P.barrier()
        if STOP == f"R{l}":
            break
        phase_O(l)
        if STOP == f"O{l}a":
            break
        if l < NL - 1:
            allgather_h(l + 1)
        P.barrier()
        if STOP == f"O{l}":
            break
    print(f"[mk] sbuf bytes/partition = {sb_bytes[0]}", flush=True)
    P.emit()
    return nc


def prep_inputs(inp):
    f = lambda k: np.asarray(inp[k], dtype=np.float32)
    x_prompt, x_sample = f("x_prompt"), f("x_sample")
    consts = make_consts()
    w_in_full = f("w_in")
    maps = []
    for c in range(8):
        g, r = c // 4, c % 4
        ci = col_index(r)
        m = {}
        m["x_own"] = np.ascontiguousarray(x_prompt[g, r * 1024:(r + 1) * 1024])
        m["xs_own"] = np.ascontiguousarray(x_sample[16 * g + 4 * r:16 * g + 4 * r + 4, 0])
        cmat = np.concatenate([f("c_prompt")[g:g + 1], f("c_sample")[16 * g:16 * g + 16]], 0)
        m["cT"] = np.ascontiguousarray(cmat.T.reshape(16, 128, 17).transpose(1, 0, 2))
        m["wada"] = np.ascontiguousarray(f("w_ada")[:, :, r * 1536:(r + 1) * 1536])
        m["bada"] = np.ascontiguousarray(f("b_ada")[:, r * 1536:(r + 1) * 1536])
        m["ng_col"] = np.ascontiguousarray(f("norm_g").reshape(NL, 16, 128).transpose(2, 0, 1))
        m["ng_row"] = f("norm_g")
        m["fg_row"] = f("final_g").reshape(1, D)
        m["w_in"] = np.ascontiguousarray(w_in_full[:, :, ci])
        sh_cols = ci[3 * 128:10 * 128] - R_OFF
        m["mu_col"] = np.ascontiguousarray(f("mu_shift")[:, sh_cols].reshape(NL, 7, 128).transpose(0, 2, 1))
        ss = f("state_shift")[:, 16 * g:16 * g + 16][:, :, sh_cols]
        m["prev_s"] = np.ascontiguousarray(ss.reshape(NL, 16, 7, 128).transpose(0, 3, 2, 1))
        own = (np.arange(256) + 256 * r)
        pars = [f("w0"), f("a0"), f("k_k"), f("k_a"), f("r_k").reshape(NL, 1024), f("ln_w"), f("ln_b")]
        rp = np.stack([p_[:, own].reshape(NL, 2, 128) for p_ in pars], 2)
        m["rpar"] = np.ascontiguousarray(rp.transpose(0, 3, 2, 1))
        lw = np.concatenate([f("w_decay")[:, :, own], f("w_iclr")[:, :, own]], 1)
        m["loraw"] = np.ascontiguousarray(lw.reshape(NL, 128, 2, 128))
        sk = f("sinks")[:, 4 * r:4 * r + 4][:, [0, 2, 1, 3]]
        m["sinks_b"] = np.ascontiguousarray(np.broadcast_to(sk[:, None, :], (NL, 128, 4)))
        rows = np.concatenate([np.arange(256 * r, 256 * r + 256), np.arange(1024 + 256 * r, 1024 + 256 * r + 256)])
        m["w_out"] = np.ascontiguousarray(f("w_out")[:, rows, :])
        m["ck_in"] = np.ascontiguousarray(f("cache_k")[:, 16 * g:16 * g + 16, :, r, :])
        m["cv_in"] = np.ascontiguousarray(f("cache_v")[:, 16 * g:16 * g + 16, :, r, :])
        m["st_in"] = np.ascontiguousarray(f("state_wkv")[:, 16 * g:16 * g + 16, 4 * r:4 * r + 4])
        sel = np.zeros((16, 4), np.float32)
        for si in range(4):
            sel[4 * r + si, si] = 1.0
        m["selT"] = sel
        hsel = np.arange(4) + 4 * r
        lw_ = f("ln_w").reshape(NL, 16, 64)[:, hsel]
        lb_ = f("ln_b").reshape(NL, 16, 64)[:, hsel]
        rk_ = f("r_k")[:, hsel]
        sk_ = f("sinks")[:, hsel][:, :, None]
        shp_ = np.concatenate([lw_, lb_, rk_, sk_], 2)
        m["shpar"] = np.ascontiguousarray(np.broadcast_to(shp_[:, :, None], (NL, 4, 16, 193)).reshape(NL, 64, 193))
        for k, v_ in consts.items():
            m["c_" + k] = v_
        maps.append(m)
    return consts, maps


def assemble(res):
    y_prompt = np.zeros((2, SEQ, D), np.float32)
    y_sample = np.zeros((32, 1, D), np.float32)
    ckp = np.zeros((NL, 2, 128, 4, 64), np.float32)
    cvp = np.zeros_like(ckp)
    swp = np.zeros((NL, 2, 16, 64, 64), np.float32)
    shp = np.zeros((NL, 2, 3200), np.float32)
    cks = np.zeros((NL, 32, 128, 4, 64), np.float32)
    cvs = np.zeros_like(cks)
    sws = np.zeros((NL, 32, 16, 64, 64), np.float32)
    shs = np.zeros((NL, 32, 3200), np.float32)
    for c in range(8):
        g, r = c // 4, c % 4
        o = res[c]
        ci = col_index(r)
        sh_cols = ci[3 * 128:10 * 128] - R_OFF
        y_prompt[g, r * 1024:(r + 1) * 1024] = o["y_own"]
        y_sample[16 * g + 4 * r:16 * g + 4 * r + 4, 0] = o["ys_own"]
        ckp[:, g, :, r, :] = o["ckp"]
        cvp[:, g, :, r, :] = o["cvp"]
        t = o["swp"].reshape(NL, 2, 64, 2, 64)
        swp[:, g, 4 * r:4 * r + 4] = t.transpose(0, 3, 1, 4, 2).reshape(NL, 4, 64, 64)
        shp[:, g, sh_cols] = o["shp"].transpose(0, 2, 1).reshape(NL, 896)
        cks[:, 16 * g:16 * g + 16, :, r, :] = o["cks"]
        cvs[:, 16 * g:16 * g + 16, :, r, :] = o["cvs"]
        sws[:, 16 * g:16 * g + 16, 4 * r:4 * r + 4] = o["sws"]
        shs[:, 16 * g:16 * g + 16][:, :, sh_cols] = o["shs"].transpose(0, 3, 2, 1).reshape(NL, 16, 896)
    return (y_prompt, y_sample, ckp, cvp, swp, shp, cks, cvs, sws, shs)


def kernel(**inputs):
    consts, maps = prep_inputs(inputs)
    nc = build(consts)
    res = run_bass_kernel_spmd(nc, maps, core_ids=list(range(8)))
    global LAST_RES
    LAST_RES = res.results
    return assemble(res.results)
```

```python
import contextlib
import numpy as np
import concourse.bass as bass
import concourse.mybir as mybir
from concourse.bass_utils import run_bass_kernel_spmd

F32 = mybir.dt.float32
BF16 = mybir.dt.bfloat16
AF = mybir.ActivationFunctionType
ALU = mybir.AluOpType
AX = mybir.AxisListType

D = 2048
SEQ = 4096
NL = 2
HD = 64
WIN = 128
Q_OFF = 0
KA_OFF = 1024
VA_OFF = 1280
R_OFF = 1536
KR_OFF = 2560
VR_OFF = 3584
WD_OFF = 4608
AD_OFF = 4672
GA_OFF = 4736
GR_OFF = 5760
NCT = 14
WCOLS = NCT * 128 + 64
G = 256
NG = SEQ // G
CH = 64
CDEC = 0.6065306597126334
NEG = -30000.0
HC = 1088
ZC = 4 * SEQ + 64

ENGS = ("pe", "act", "dve", "pool", "sp")


class Op:
    __slots__ = ("eng", "fn", "deps", "signal", "sigval", "dma", "semkey", "cc", "epoch", "nb")

    def __init__(self, eng, fn, dma=False, semkey=None, cc=False):
        self.eng = eng
        self.fn = fn
        self.deps = []
        self.signal = False
        self.sigval = 0
        self.dma = dma
        self.semkey = semkey
        self.cc = cc
        self.epoch = 0
        self.nb = False


class Prog:
    def __init__(self, nc):
        self.nc = nc
        self.ops = {e: [] for e in ENGS}
        self.last_w = {}
        self.readers = {}
        self.all_ops = []
        self.last_dma = {}
        self.epoch = 0

    def op(self, eng, fn, reads=(), writes=(), dma=False, cc=False, semkey=None, nb=False):
        o = Op(eng, fn, dma=dma or cc, semkey=semkey, cc=cc)
        o.nb = nb
        o.epoch = self.epoch if eng == "pe" else 0
        deps = []
        for k in reads:
            w = self.last_w.get(k)
            if w is not None:
                deps.append(w)
        for k in writes:
            w = self.last_w.get(k)
            if w is not None:
                deps.append(w)
            deps.extend(self.readers.get(k, ()))
        seen = set()
        for d in deps:
            if id(d) in seen or d is o:
                continue
            seen.add(id(d))
            if (not d.dma) and d.eng == "pe" and eng == "pe" and not o.dma:
                continue
            o.deps.append(d)
            d.signal = True
        implied = set()
        for d2 in o.deps:
            for x_ in d2.deps:
                implied.add(id(x_))
        if implied:
            o.deps = [d for d in o.deps if id(d) not in implied]
        for k in reads:
            self.readers.setdefault(k, []).append(o)
        for k in writes:
            self.last_w[k] = o
            self.readers[k] = []
        if o.dma:
            if o.semkey is None:
                o.semkey = ("w",) + tuple(writes)
            HOT = ("hT", "Win", "ob", "ot", "xo", "x1st", "hTt_st", "yst", "xt", "wst", "Wo")
            if (not o.cc) and isinstance(o.semkey, str) and o.semkey.rstrip("0123456789_") in HOT:
                pass
            elif not o.cc:
                import zlib
                if eng == "pool":
                    o.semkey = ("dmapool_sw", zlib.crc32(repr(o.semkey).encode()) % 12)
                else:
                    o.semkey = ("dmapool_hw", zlib.crc32(repr(o.semkey).encode()) % 48)
            prev = self.last_dma.get(o.semkey)
            if prev is not None and all(prev is not d for d in o.deps):
                o.deps.append(prev)
                prev.signal = True
            self.last_dma[o.semkey] = o
        self.ops[eng].append(o)
        self.all_ops.append(o)
        return o

    def pe(self, fn, reads=(), writes=()):
        return self.op("pe", fn, reads, writes)

    def act(self, fn, reads=(), writes=()):
        return self.op("act", fn, reads, writes)

    def dve(self, fn, reads=(), writes=()):
        return self.op("dve", fn, reads, writes)

    def pool(self, fn, reads=(), writes=()):
        return self.op("pool", fn, reads, writes)

    def dma(self, fn, reads=(), writes=(), eng="sp", semkey=None):
        return self.op(eng, fn, reads, writes, dma=True, semkey=semkey)

    def barrier(self):
        lasts = []
        for e in ENGS:
            for o_ in reversed(self.ops[e]):
                if o_.fn is not None and not o_.nb:
                    lasts.append(o_)
                    break
        pend = [o for o in self.all_ops if o.dma and not o.signal and not o.nb]
        keep = {k: w for k, w in self.last_w.items() if w.nb}
        self.last_w = keep
        self.readers = {}
        for e in ENGS:
            o = Op(e, None)
            for d in lasts + pend:
                o.deps.append(d)
                d.signal = True
            self.ops[e].append(o)
            self.all_ops.append(o)

    def emit(self):
        nc = self.nc
        fin = Op("sp", None)
        for o in self.all_ops:
            if o.dma and not o.signal:
                fin.deps.append(o)
                o.signal = True
        cnt = {}
        keys = []
        for o in self.all_ops:
            if not o.signal:
                continue
            key = o.semkey if o.dma else ("eng", o.eng, o.epoch)
            if key not in cnt:
                cnt[key] = 0
                keys.append(key)
            cnt[key] += (16 if (o.dma and not o.cc) else 1)
            o.sigval = cnt[key]
        print("[mk] ops: " + ", ".join(f"{e}={len(self.ops[e])}" for e in ENGS) +
              f"; sems={len(cnt)}; maxval={max(cnt.values()) if cnt else 0}", flush=True)
        with contextlib.ExitStack() as st:
            st.enter_context(nc.allow_non_contiguous_dma(reason="small strided layout transfers"))
            sems = {}
            for i, key in enumerate(keys):
                sems[key] = st.enter_context(nc.semaphore(f"s{i}"))
            block = st.enter_context(nc.Block())

            def run(engname, eng):
                waited = {}
                lst = list(self.ops[engname])
                if engname == "sp":
                    lst = lst + [fin]
                for o in lst:
                    need = {}
                    for d in o.deps:
                        key = d.semkey if d.dma else ("eng", d.eng, d.epoch)
                        if d.sigval > need.get(key, 0):
                            need[key] = d.sigval
                    for key, v in need.items():
                        if waited.get(key, 0) >= v:
                            continue
                        eng.wait_ge(sems[key], v)
                        waited[key] = v
                    if o.fn is None:
                        continue
                    inst = o.fn(eng)
                    if o.signal:
                        key = o.semkey if o.dma else ("eng", o.eng, o.epoch)
                        inst.then_inc(sems[key], 16 if (o.dma and not o.cc) else 1)

            @block.sync
            def _(e):
                run("sp", e)

            @block.scalar
            def _(e):
                run("act", e)

            @block.vector
            def _(e):
                run("dve", e)

            @block.gpsimd
            def _(e):
                run("pool", e)

            @block.tensor
            def _(e):
                run("pe", e)


def col_index(r):
    cols = []
    for i in range(2):
        for hh in range(2):
            cols += list(range(Q_OFF + (4 * r + 2 * i + hh) * 64, Q_OFF + (4 * r + 2 * i + hh + 1) * 64))
    kc = list(range(KA_OFF + r * 64, KA_OFF + (r + 1) * 64))
    cols += kc + kc
    for off in (R_OFF, KR_OFF, VR_OFF):
        for i in range(2):
            cols += list(range(off + (4 * r + 2 * i) * 64, off + (4 * r + 2 * i + 2) * 64))
    cols += list(range(WD_OFF, WD_OFF + 64)) + list(range(AD_OFF, AD_OFF + 64))
    for off in (GA_OFF, GR_OFF):
        for i in range(2):
            cols += list(range(off + (4 * r + 2 * i) * 64, off + (4 * r + 2 * i + 2) * 64))
    cols += list(range(VA_OFF + r * 64, VA_OFF + (r + 1) * 64))
    assert len(cols) == WCOLS
    return np.array(cols)


def wout_row_perm():
    rows = []
    for r in range(4):
        rows += list(range(256 * r, 256 * r + 256))
        rows += list(range(1024 + 256 * r, 1024 + 256 * r + 256))
    return np.array(rows)


def make_consts():
    import ml_dtypes
    bf = ml_dtypes.bfloat16
    c = {}
    p = np.arange(128)
    d = p % 64
    inv_freq = (np.float32(500000.0) ** (-np.arange(8, dtype=np.float32) * np.float32(0.125))).astype(np.float32)
    pos = np.arange(SEQ, dtype=np.float32)
    ang = (pos[None, :] * inv_freq[(d % 8)][:, None]).astype(np.float32)
    rot = (d < 16)[:, None]
    c["cosT"] = np.where(rot, np.cos(ang), 1.0).astype(np.float32)
    c["sinT"] = np.where(rot, np.sin(ang), 0.0).astype(np.float32)
    angs = (np.float32(16384.0) * inv_freq[(d % 8)]).astype(np.float32)
    cs = np.stack([np.where(d < 16, np.cos(angs), 1.0), np.where(d < 16, np.sin(angs), 0.0)], 1)
    c["cs_s"] = cs.astype(np.float32)
    prot = np.zeros((128, 128), np.float32)
    for m in range(128):
        dm = m % 64
        if dm < 8:
            prot[m + 8, m] = -1.0
        elif dm < 16:
            prot[m - 8, m] = 1.0
    c["prot"] = prot.astype(bf)
    c["ident"] = np.eye(128, dtype=np.float32).astype(bf)
    c["identf"] = np.eye(128, dtype=np.float32)
    qi = np.arange(128)[:, None]
    kj = np.arange(256)[None, :]
    valid = (kj >= qi) & (kj <= qi + 128)
    c["maskb"] = np.where(valid, 0.0, NEG).astype(np.float32)
    c["maskb0"] = np.where(valid & (kj >= 128), 0.0, NEG).astype(np.float32)
    row = (np.arange(128) % 64)[:, None]
    col = np.arange(64)[None, :]
    strict = (row < col).astype(np.float32)
    incl = (row <= col).astype(np.float32)
    low = (row > col).astype(np.float32)
    c["maskG"] = np.concatenate([strict, incl, strict, incl, low], 1).astype(np.float32)
    c["id64"] = (row == col).astype(np.float32)
    c["bones"] = ((p[:, None] // 64) == (p[None, :] // 64)).astype(np.float32).astype(bf)
    c["ones"] = np.ones((128, 64), np.float32)
    return c


CONST_DT = {"cosT": F32, "sinT": F32, "cs_s": F32, "prot": BF16, "ident": BF16, "identf": F32,
            "maskb": F32, "maskb0": F32, "maskG": F32, "id64": F32, "bones": BF16, "ones": F32}
NPAR = 7


def build(consts, stop_after=None):
    nc = bass.Bass("TRN2", target_bir_lowering=False)
    P = Prog(nc)

    def din(name, shape, dt=F32):
        return nc.dram_tensor(name, list(shape), dt, kind="ExternalInput").ap()

    def dout(name, shape, dt=F32):
        return nc.dram_tensor(name, list(shape), dt, kind="ExternalOutput").ap()

    def dscr(name, shape, dt=F32):
        return nc.dram_tensor(name, list(shape), dt)

    sb_bytes = [0]

    def sb(name, shape, dt=F32):
        n = 1
        for s in shape[1:]:
            n *= s
        sb_bytes[0] += n * (4 if dt == F32 else 2)
        return nc.alloc_sbuf_tensor(name, list(shape), dt)

    x_own = din("x_own", [1024, D])
    xs_own = din("xs_own", [4, D])
    cT_in = din("cT", [128, 16, 17])
    wada = din("wada", [NL, D, 1536])
    bada = din("bada", [NL, 1536])
    ng_col = din("ng_col", [128, NL, 16])
    ng_row = din("ng_row", [NL, D])
    fg_row = din("fg_row", [1, D])
    w_in = din("w_in", [NL, D, WCOLS])
    mu_col = din("mu_col", [NL, 128, 7])
    prev_s = din("prev_s", [NL, 128, 7, 16])
    rpar = din("rpar", [NL, 128, NPAR, 2])
    loraw = din("loraw", [NL, 128, 2, 128])
    sinks_b = din("sinks_b", [NL, 128, 4])
    w_out = din("w_out", [NL, 512, D])
    ck_in = din("ck_in", [NL, 16, 128, 64])
    cv_in = din("cv_in", [NL, 16, 128, 64])
    st_in = din("st_in", [NL, 16, 4, 64, 64])
    selT_in = din("selT", [16, 4])
    shpar_in = din("shpar", [NL, 64, 193])
    cd = {k: din("c_" + k, v.shape, CONST_DT[k]) for k, v in consts.items()}
    y_own = dout("y_own", [1024, D])
    ys_own = dout("ys_own", [4, D])
    ckp = dout("ckp", [NL, 128, 64])
    cvp = dout("cvp", [NL, 128, 64])
    swp = dout("swp", [NL, 128, 2, 64])
    shp = dout("shp", [NL, 128, 7])
    cks = dout("cks", [NL, 16, 128, 64])
    cvs = dout("cvs", [NL, 16, 128, 64])
    sws = dout("sws", [NL, 16, 4, 64, 64])
    shs = dout("shs", [NL, 128, 7, 16])
    agi_mod = dscr("agi_mod", [17, NL * 1536])
    ago_mod = dscr("ago_mod", [4 * 17, NL * 1536])
    agi_h = [[dscr(f"agi_h{l}_{j}", [128, 2048], BF16) for j in range(8)] for l in range(NL)]
    agi_hs = [dscr(f"agi_hs{l}", [128, 64], BF16) for l in range(NL)]
    ago_hs = [dscr(f"ago_hs{l}", [512, 64], BF16) for l in range(NL)]
    ago_h = [[dscr(f"ago_h{l}_{j}", [4 * 128, 2048], BF16) for j in range(8)] for l in range(NL)]
    agi_z = [dscr(f"agi_z{l}", [128, ZC], BF16) for l in range(NL)]
    ago_z = [dscr(f"ago_z{l}", [4 * 128, ZC], BF16) for l in range(NL)]
    x1_d = dscr("x1_d", [1028, D])
    smod_d = dscr("smod_d", [NL, 3, 4, D])
    smp_d = [dscr(f"smp_d{l}", [4, 16, 9, 64]) for l in range(NL)]
    kn_d = [dscr(f"kn_d{l}", [16, 64]) for l in range(NL)]
    vn_d = [dscr(f"vn_d{l}", [16, 64]) for l in range(NL)]
    zs_d = [dscr(f"zs_d{l}", [4, 16, 2, 64]) for l in range(NL)]
    rs_in = [[dscr(f"rs_in{l}_{q}", [4 * 1024, 1024]) for q in range(2)] for l in range(NL)]
    rss_in = [dscr(f"rss_in{l}", [16, D]) for l in range(NL)]
    rss_out = [dscr(f"rss_out{l}", [4, D]) for l in range(NL)]
    rs_out = [[dscr(f"rs_out{l}_{q}", [1024, 1024]) for q in range(2)] for l in range(NL)]
    RG = [[0, 1, 2, 3], [4, 5, 6, 7]]

    cs = {}
    for k, v in consts.items():
        if k in ("cosT", "sinT"):
            continue
        cs[k] = sb("k_" + k, v.shape, CONST_DT[k])
        P.dma(lambda e, k=k: e.dma_start(out=cs[k][:], in_=cd[k]), writes=["k_" + k])
    WBIG = sb("WBIG", [128, 16 * 2048], BF16)
    SCR_N = 30000
    SCR = sb("SCR", [128, SCR_N], F32)
    modT = sb("modT", [128, NL, 48])
    ngc = sb("ngc", [128, NL, 16])
    Acol = sb("Acol", [128, NL, 16])
    P.dma(lambda e: e.dma_start(out=ngc[:], in_=ng_col), writes=["ngc"])
    epsc = sb("epsc", [128, 2])
    P.pool(lambda e: e.memset(epsc[:, 0:1], 1e-5), writes=["epsc"])
    P.pool(lambda e: e.memset(epsc[:, 1:2], 64e-5), writes=["epsc"])
    Wo = sb("Wo", [128, 4, D], BF16)
    psum = [nc.alloc_psum_tensor(f"ps{i}", [128, 512], F32) for i in range(8)]

    class Carve:
        def __init__(self):
            self.off = 0

        def f32(self, shape):
            n = int(np.prod(shape[1:]))
            ap = SCR[0:shape[0], self.off:self.off + n]
            self.off += n
            assert self.off <= SCR_N, self.off
            return ap, shape

        def bf(self, shape):
            n = int(np.prod(shape[1:]))
            w = (n + 1) // 2
            ap = SCR[0:shape[0], self.off:self.off + w].bitcast(BF16)[:, 0:n]
            self.off += w
            assert self.off <= SCR_N, self.off
            return ap, shape

    def v(t, pat=None, **kw):
        ap, shape = t
        if len(shape) == 2:
            return ap
        names = " ".join(f"d{i}" for i in range(1, len(shape)))
        kws = {f"d{i}": shape[i] for i in range(1, len(shape))}
        return ap.rearrange(f"p ({names}) -> p {names}", **kws)

    cv_ = Carve()
    cT = cv_.f32([128, 16, 17])
    wst = [cv_.f32([128, 1536]) for _ in range(4)]
    badab = cv_.f32([17, NL, 1536])
    modsb = cv_.f32([17, NL, 1536])
    P.dma(lambda e: e.dma_start(out=v(cT), in_=cT_in), writes=["cT"])
    P.act(lambda e: e.activation(out=cT[0], in_=cT[0], func=AF.Silu), reads=["cT"], writes=["cT"])
    for l in range(NL):
        P.dma(lambda e, l=l: e.dma_start(out=v(badab)[:, l, :], in_=bada[l:l + 1, :].partition_broadcast(17)[:, 0, :]),
              writes=["badab"], eng="act", semkey="badab")
    it = 0
    for l in range(NL):
        for k in range(16):
            s = it % 4
            it += 1
            P.dma(lambda e, l=l, k=k, s=s: e.dma_start(out=wst[s][0], in_=wada[l, k * 128:(k + 1) * 128, :]),
                  writes=[f"wst{s}"], eng=("sp" if s % 2 == 0 else "act"), semkey=f"wst{s}")
            for cg in range(3):
                P.pe(lambda e, k=k, s=s, cg=cg: e.matmul(psum[cg][0:17, :], lhsT=v(cT)[:, k, :],
                                                          rhs=wst[s][0][:, cg * 512:(cg + 1) * 512],
                                                          start=(k == 0), stop=(k == 15)),
                     reads=["cT", f"wst{s}"], writes=[f"ps{cg}"])
        for cg in range(3):
            P.dve(lambda e, l=l, cg=cg: e.tensor_tensor(out=v(modsb)[:, l, cg * 512:(cg + 1) * 512], in0=psum[cg][0:17, :],
                                                        in1=v(badab)[:, l, cg * 512:(cg + 1) * 512], op=ALU.add),
                  reads=[f"ps{cg}", "badab"], writes=["modsb"])
    P.dma(lambda e: e.dma_start(out=agi_mod.ap(), in_=modsb[0]), reads=["modsb"], writes=["agi_mod"])
    P.op("pool", lambda e: e.collective_compute("AllGather", ALU.bypass, replica_groups=RG,
                                                 ins=[agi_mod.ap().opt()], outs=[ago_mod.ap().opt()]),
         reads=["agi_mod"], writes=["ago_mod"], cc=True, semkey="cc")
    for l in range(NL):
        for r2 in range(4):
            P.dma(lambda e, l=l, r2=r2: e.dma_start(
                out=modT[:, l, r2 * 12:(r2 + 1) * 12],
                in_=ago_mod.ap()[r2 * 17, l * 1536:(l + 1) * 1536].rearrange("(c p) -> p c", p=128),
                allow_slow_non_contiguous=True), reads=["ago_mod"], writes=["modT"], semkey="modT")
    selT = cv_.f32([16, 4])
    smk = cv_.f32([16, D])
    smo = cv_.f32([4, D])
    P.dma(lambda e: e.dma_start(out=selT[0], in_=selT_in), writes=["selT"])
    SEG = {0: [(0, 0, 1536, 0), (1, 0, 512, 1536)], 1: [(1, 512, 1536, 0), (2, 0, 1024, 1024)], 2: [(2, 1024, 1536, 0), (3, 0, 1536, 512)]}
    for l in range(NL):
        for kind in range(3):
            for (r2, j0, j1, d0) in SEG[kind]:
                P.dma(lambda e, l=l, r2=r2, j0=j0, j1=j1, d0=d0: e.dma_start(
                    out=smk[0][:, d0:d0 + (j1 - j0)], in_=ago_mod.ap()[r2 * 17 + 1:r2 * 17 + 17, l * 1536 + j0:l * 1536 + j1]),
                    reads=["ago_mod"], writes=["smk"], semkey="smk")
            for cg in range(4):
                P.pe(lambda e, cg=cg: e.matmul(psum[cg][0:4, :], lhsT=selT[0], rhs=smk[0][:, cg * 512:(cg + 1) * 512], start=True, stop=True),
                     reads=["selT", "smk"], writes=[f"ps{cg}"])
                P.dve(lambda e, cg=cg: e.tensor_copy(out=smo[0][:, cg * 512:(cg + 1) * 512], in_=psum[cg][0:4, :]),
                      reads=[f"ps{cg}"], writes=["smo", f"ps{cg}"])
            P.dma(lambda e, l=l, kind=kind: e.dma_start(out=smod_d.ap()[l, kind], in_=smo[0]), reads=["smo"], writes=["smod_d"], semkey="smo")
    P.dve(lambda e: e.scalar_tensor_tensor(out=Acol[:], in0=modT[:, :, 16:32], scalar=1.0, in1=ngc[:],
                                           op0=ALU.add, op1=ALU.mult), reads=["modT", "ngc"], writes=["Acol"])
    P.barrier()
    import os
    STOP = os.environ.get("MK_STOP", "")
    if STOP == "A":
        P.emit()
        return nc

    def phase_N_tile(l, t, xt_ap, xkey):
        c2 = Carve()
        c2.off = NOFF
        junk = c2.f32([128, D])
        pp = t % 2
        bufs = [(c2.bf([128, D]), c2.f32([128, 2]), c2.bf([128, 16, 128])) for _ in range(2)]
        xn, ssq, hTt = bufs[pp]
        KJ, KX, KS, KH = f"junk{pp}", f"xn{pp}", f"ssq{pp}", f"hTt{pp}"
        P.act(lambda e: e.activation(out=junk[0], in_=xt_ap, func=AF.Square, accum_out=ssq[0][:, 0:1]),
              reads=[xkey], writes=[KJ, KS])
        P.act(lambda e: e.activation(out=ssq[0][:, 1:2], in_=ssq[0][:, 0:1], func=AF.Sqrt, scale=1.0 / D, bias=epsc[:, 0:1]),
              reads=[KS, "epsc"], writes=[KS])
        P.dve(lambda e: e.reciprocal(out=ssq[0][:, 1:2], in_=ssq[0][:, 1:2]), reads=[KS], writes=[KS])
        P.act(lambda e: e.activation(out=xn[0], in_=xt_ap, func=AF.Copy, scale=ssq[0][:, 1:2]),
              reads=[xkey, KS], writes=[KX])
        for half in range(2):
            pst = psum[4 + half]
            for kk in range(8):
                k = half * 8 + kk
                P.pe(lambda e, k=k, kk=kk, pst=pst: e.transpose(
                    out=pst[:].bitcast(BF16)[:, kk * 128:(kk + 1) * 128], in_=xn[0][:, k * 128:(k + 1) * 128],
                    identity=cs["ident"][:]), reads=[KX, "k_ident"], writes=[f"ps{4 + half}"])
            for kk in range(8):
                k = half * 8 + kk
                P.act(lambda e, k=k, kk=kk, pst=pst, l=l: e.activation(
                    out=v(hTt)[:, k, :], in_=pst[:].bitcast(BF16)[:, kk * 128:(kk + 1) * 128], func=AF.Identity,
                    scale=Acol[:, l, k:k + 1], bias=modT[:, l, k:k + 1]),
                    reads=[f"ps{4 + half}", "Acol", "modT"], writes=[KH])
        P.dma(lambda e, l=l, t=t: e.dma_start(out=agi_h[l][t].ap().rearrange("p (k c) -> p k c", k=16), in_=v(hTt)),
              reads=[KH], writes=[f"agi_h{l}_{t}"], semkey=f"hTt_st{t % 2}")
        P.op("pool", lambda e, l=l, t=t: e.collective_compute("AllGather", ALU.bypass, replica_groups=RG,
                                                           ins=[agi_h[l][t].ap().opt()], outs=[ago_h[l][t].ap().opt()]),
             reads=[f"agi_h{l}_{t}"], writes=[f"ago_h{l}_{t}"], cc=True, semkey="cc", nb=True)

    def phase_N_samples(l, xs_ap, xkey, off):
        c6 = Carve()
        c6.off = off
        As = c6.f32([4, D])
        Bs = c6.f32([4, D])
        jk = c6.f32([4, D])
        hs_ = c6.bf([4, D])
        sq4 = c6.f32([4, 2])
        hTs = c6.bf([128, 16, 4])
        P.dma(lambda e: e.dma_start(out=As[0], in_=smod_d.ap()[l, 1]), reads=["smod_d"], writes=["As"], eng="act")
        P.dma(lambda e: e.dma_start(out=Bs[0], in_=smod_d.ap()[l, 0]), reads=["smod_d"], writes=["Bs"], eng="act")
        P.dma(lambda e: e.dma_start(out=jk[0], in_=ng_row[l:l + 1, :].partition_broadcast(4)[:, 0, :]), writes=["jk"], eng="act")
        P.dve(lambda e: e.scalar_tensor_tensor(out=As[0], in0=As[0], scalar=1.0, in1=jk[0], op0=ALU.add, op1=ALU.mult),
              reads=["As", "jk"], writes=["As"])
        P.act(lambda e: e.activation(out=jk[0], in_=xs_ap, func=AF.Square, accum_out=sq4[0][:, 0:1]), reads=[xkey, "As"], writes=["jk", "sq4"])
        P.act(lambda e: e.activation(out=sq4[0][:, 1:2], in_=sq4[0][:, 0:1], func=AF.Sqrt, scale=1.0 / D, bias=epsc[0:4, 0:1]),
              reads=["sq4", "epsc"], writes=["sq4"])
        P.dve(lambda e: e.reciprocal(out=sq4[0][:, 1:2], in_=sq4[0][:, 1:2]), reads=["sq4"], writes=["sq4"])
        P.act(lambda e: e.activation(out=jk[0], in_=xs_ap, func=AF.Copy, scale=sq4[0][:, 1:2]), reads=[xkey, "sq4"], writes=["jk"])
        P.dve(lambda e: e.tensor_tensor(out=jk[0], in0=jk[0], in1=As[0], op=ALU.mult), reads=["jk", "As"], writes=["jk"])
        P.dve(lambda e: e.tensor_tensor(out=hs_[0], in0=jk[0], in1=Bs[0], op=ALU.add), reads=["jk", "Bs"], writes=["hs_"])
        for k in range(16):
            P.pe(lambda e, k=k: e.transpose(out=psum[6][:].bitcast(BF16)[:, k * 4:(k + 1) * 4], in_=hs_[0][:, k * 128:(k + 1) * 128],
                                            identity=cs["ident"][0:4, 0:4]), reads=["hs_", "k_ident"], writes=["ps6"])
        P.act(lambda e: e.activation(out=hTs[0], in_=psum[6][:].bitcast(BF16)[:, 0:64], func=AF.Copy), reads=["ps6"], writes=["hTs", "ps6"])
        P.dma(lambda e: e.dma_start(out=agi_hs[l].ap().rearrange("p (k c) -> p k c", k=16), in_=v(hTs)),
              reads=["hTs"], writes=[f"agi_hs{l}"], semkey="hTs_st")
        P.op("pool", lambda e: e.collective_compute("AllGather", ALU.bypass, replica_groups=RG,
                                                     ins=[agi_hs[l].ap().opt()], outs=[ago_hs[l].ap().opt()]),
             reads=[f"agi_hs{l}"], writes=[f"ago_hs{l}"], cc=True, semkey="cc", nb=True)

    NOFF = 0
    c0 = Carve()
    xt = [c0.f32([128, D]) for _ in range(2)]
    NOFF = c0.off
    def n_load(t):
        s = t % 2
        P.dma(lambda e, t=t, s=s: e.dma_start(out=xt[s][0], in_=x_own[t * 128:(t + 1) * 128, :]),
              writes=[f"xt{s}"], semkey=f"xt{s}")

    n_load(0)
    for t in range(8):
        s = t % 2
        if t + 1 < 8:
            n_load(t + 1)
        phase_N_tile(0, t, xt[s][0], f"xt{s}")

    zpad = sb("zpad", [128, 2, 64], BF16)
    P.pool(lambda e: e.memset(zpad[:], 0.0), writes=["zpad"])

    def allgather_h(l):
        pass

    cx = Carve()
    cx.off = NOFF + 20000
    xs0 = cx.f32([4, D])
    P.dma(lambda e: e.dma_start(out=xs0[0], in_=xs_own), writes=["xs0"], eng="act")
    phase_N_samples(0, xs0[0], "xs0", NOFF + 8000)
    allgather_h(0)
    P.barrier()
    if STOP == "N":
        P.emit()
        return nc

    def phase_M(l):
        c3 = Carve()
        Win = WBIG[:, 0:16 * WCOLS].rearrange("p (k c) -> p k c", k=16)
        import os
        SKIP = os.environ.get("MK_SKIP", "")
        for k in range(16 if "win" not in SKIP else 0):
            for hf in range(4):
                P.dma(lambda e, k=k, hf=hf: e.dma_start(out=Win[:, k, hf * 464:(hf + 1) * 464],
                                                       in_=w_in[l, k * 128:(k + 1) * 128, hf * 464:(hf + 1) * 464]),
                      writes=[f"Win{k}_{hf}"], eng="pool", semkey=f"Win{(k * 4 + hf) % 4}")
        WK = [f"Win{k}_{hf}" for k in range(16) for hf in range(4)]
        hT = [c3.bf([128, 16, G]) for _ in range(2)]
        cst = [c3.f32([128, 2, G])] * 2
        cur = c3.f32([128, 7, G + 1])
        qb = c3.bf([128, 3, G])
        t1 = c3.f32([128, 3, G])
        t2 = c3.f32([128, 3, G])
        qrot = c3.bf([128, 2, G])
        krotf = c3.f32([128, G])
        kT = c3.bf([128, 128 + G])
        vb = c3.bf([128, 3, 64])
        vf = c3.f32([128, 64])
        kcf = c3.f32([128, 64])
        mu = c3.f32([128, 7])
        gsil2 = [c3.bf([128, 4, G]) for _ in range(2)]
        zT = c3.bf([128, 4, G])
        sc = c3.f32([128, 4, 256])
        yc = c3.f32([128, 8, 64])
        pb = (yc[0].bitcast(BF16)[:, 0:1024], [128, 4, 256])
        pTs = c3.bf([128, 1024])
        attb = c3.bf([128, 4, 64])
        sm = c3.f32([128, 8, 4])
        snk = c3.f32([128, 4])
        rp = c3.f32([128, NPAR, 2])
        loraW = c3.bf([128, 2, 128])
        P.dma(lambda e: e.dma_start(out=v(rp), in_=rpar[l]), writes=["rp"])
        P.dma(lambda e: e.dma_start(out=v(loraW), in_=loraw[l]), writes=["loraW"], eng="pool", semkey="loraW")
        mixed = c3.f32([128, 7, G])
        twad = c3.bf([128, G])
        sig = c3.f32([128, 2, G])
        aa = c3.f32([128, 2, G])
        Ls = c3.f32([128, 2, G])
        eL = c3.f32([128, 2, G])
        eLi = c3.f32([128, 2, G])
        eLp = c3.f32([128, 2, G])
        kkr = c3.f32([128, 2, G])
        sqb = c3.bf([128, 2, G])
        rn = c3.f32([128, 2, G])
        kk = c3.f32([128, 2, G])
        uu = c3.f32([128, 2, G])
        keff = c3.f32([128, 2, G])
        AR = c3.bf([128, 2, 4, 2, 64])
        Bt = c3.bf([128, 2, G])
        Kt = c3.bf([128, 2, G])
        vrb = c3.bf([128, 2, G])
        rkk = c3.bf([128, 2, G])
        bonus = c3.f32([128, 2, G])
        TK = c3.bf([128, 2, 4, 3, 64])
        MS = c3.bf([128, 8, 320])
        PQ = [c3.bf([128, 8, 128]) for _ in range(2)]
        TT = [c3.bf([128, 8, 64]) for _ in range(2)]
        Sst = c3.f32([128, 2, 64])
        Sb = c3.bf([128, 2, 64])
        Wb = c3.bf([128, 2, 64])
        Ub = c3.bf([128, 2, 64])
        Ybuf = c3.f32([128, 4, 2, 64])
        ysq = (sc[0][:, 0:512], [128, 8, 64])
        gs = c3.f32([128, 4, 8])
        yh = c3.bf([128, 8, 64])
        yz = rn
        ob = c3.f32([128, D])
        for i in range(4):
            P.dma(lambda e, i=i: e.dma_start(out=Wo[:, i, :], in_=w_out[l, i * 128:(i + 1) * 128, :]), writes=[f"Wo{i}"], eng="pool",
                  semkey=f"Wo{i % 2}")
        P.pool(lambda e: e.memset(Sst[0], 0.0), writes=["Sst"])
        P.pool(lambda e: e.memset(Sb[0], 0.0), writes=["Sb"])
        P.dma(lambda e: e.dma_start(out=snk[0], in_=sinks_b[l]), writes=["snk"])
        P.dma(lambda e: e.dma_start(out=mu[0], in_=mu_col[l]), writes=["mu"])
        if 'ms' not in SKIP:
            P.pool(lambda e: e.memset(cur[0], 0.0), writes=["cur"])
            P.pool(lambda e: e.memset(kT[0], 0.0), writes=["kT"])
            P.pool(lambda e: e.memset(vb[0], 0.0), writes=["vb"])
        import os
        NGR = int(os.environ.get('MK_NG', NG))
        sched = [("inproj", 0)]
        for g_ in range(NGR):
            sched.append(("att", g_))
            if g_ + 1 < NGR:
                sched.append(("inproj", g_ + 1))
            sched.append(("rw", g_))
        for (sec, g) in sched:
            s = g % 2
            r2, cg = g // 4, (g % 4) * G
            if sec == "inproj":
                for hf in range(2):
                    t_ = 2 * (g % 4) + hf
                    P.dma(lambda e, s=s, r2=r2, hf=hf, t_=t_: e.dma_start(
                        out=v(hT[s])[:, :, hf * 128:(hf + 1) * 128],
                        in_=ago_h[l][t_].ap()[r2 * 128:(r2 + 1) * 128, :].rearrange("p (k c) -> p k c", k=16)),
                        reads=[f"ago_h{l}_{t_}"], writes=[f"hT{s}"], eng=("sp" if hf == 0 else "act"), semkey=f"hT{s}_{hf}")
                P.dma(lambda e, s=s, g=g: e.dma_start(out=v(cst[s])[:, 0, :], in_=cd["cosT"][:, g * G:(g + 1) * G]),
                      writes=["cst0"], eng="act", semkey="cst0")
                P.dma(lambda e, s=s, g=g: e.dma_start(out=v(cst[s])[:, 1, :], in_=cd["sinT"][:, g * G:(g + 1) * G]),
                      writes=["cst0"], eng="act", semkey="cst0")
                for ct in range(NCT if 'mm' not in SKIP else 0):
                    pi = ct % 4
                    for k in range(16):
                        P.pe(lambda e, ct=ct, k=k, s=s, pi=pi: e.matmul(
                            psum[pi][:, 0:G], lhsT=Win[:, k, ct * 128:(ct + 1) * 128], rhs=v(hT[s])[:, k, :],
                            start=(k == 0), stop=(k == 15)), reads=WK + [f"hT{s}"], writes=[f"ps{pi}"])
                    if 'ev' in SKIP:
                        continue
                    if ct < 3:
                        P.act(lambda e, ct=ct, pi=pi: e.activation(out=v(qb)[:, ct, :], in_=psum[pi][:, 0:G], func=AF.Copy),
                              reads=[f"ps{pi}"], writes=["qb", f"ps{pi}"])
                        if 'evd' not in SKIP:
                            P.dve(lambda e, ct=ct, pi=pi, s=s: e.tensor_tensor(out=v(t2)[:, ct, :], in0=psum[pi][:, 0:G],
                                                                             in1=v(cst[s])[:, 0, :], op=ALU.mult),
                                  reads=[f"ps{pi}", "cst0"], writes=["t2", f"ps{pi}"])
                    elif ct < 10:
                        P.act(lambda e, ct=ct, pi=pi: e.activation(out=v(cur)[:, ct - 3, 1:G + 1], in_=psum[pi][:, 0:G],
                                                                    func=AF.Copy), reads=[f"ps{pi}"], writes=["cur"])
                    else:
                        P.act(lambda e, ct=ct, pi=pi, gq=gsil2[g % 2]: e.activation(out=v(gq)[:, ct - 10, :], in_=psum[pi][:, 0:G], func=AF.Silu),
                              reads=[f"ps{pi}"], writes=[f"gsil{g % 2}", f"ps{pi}"])
                for ct in range(3 if 'rope' not in SKIP else 0):
                    pi = 4 + ct % 2
                    P.pe(lambda e, ct=ct, pi=pi: e.matmul(psum[pi][:, 0:G], lhsT=cs["prot"][:], rhs=v(qb)[:, ct, :],
                                                         start=True, stop=True), reads=["qb", "k_prot"], writes=[f"ps{pi}"])
                    P.dve(lambda e, ct=ct, pi=pi, s=s: e.tensor_tensor(out=v(t1)[:, ct, :], in0=psum[pi][:, 0:G],
                                                                     in1=v(cst[s])[:, 1, :], op=ALU.mult),
                          reads=[f"ps{pi}", "cst0"], writes=["t1"])
                P.dve(lambda e: e.tensor_tensor(out=v(qrot), in0=v(t1)[:, 0:2, :], in1=v(t2)[:, 0:2, :], op=ALU.add),
                      reads=["t1", "t2"], writes=["qrot"])
                P.dve(lambda e: e.tensor_tensor(out=krotf[0], in0=v(t1)[:, 2, :], in1=v(t2)[:, 2, :], op=ALU.add),
                      reads=["t1", "t2"], writes=["krotf"])
                P.act(lambda e: e.activation(out=kT[0][:, 128:128 + G], in_=krotf[0], func=AF.Copy),
                      reads=["krotf"], writes=["kT"])
                for bl in range(G // 128 if 'vv' not in SKIP else 0):
                    for k in range(16):
                        P.pe(lambda e, k=k, s=s, bl=bl: e.matmul(
                            psum[6][:, 0:64], lhsT=v(hT[s])[:, k, bl * 128:(bl + 1) * 128], rhs=Win[:, k, NCT * 128:NCT * 128 + 64],
                            start=(k == 0), stop=(k == 15)), reads=WK + [f"hT{s}"], writes=["ps6"])
                    P.act(lambda e, bl=bl: e.activation(out=v(vb)[:, 1 + bl, :], in_=psum[6][:, 0:64], func=AF.Copy),
                          reads=["ps6"], writes=["vb", "ps6"])
                    if g == NGR - 1 and bl == G // 128 - 1:
                        P.dve(lambda e: e.tensor_copy(out=vf[0], in_=psum[6][:, 0:64]), reads=["ps6"], writes=["vf", "ps6"])
                        P.dma(lambda e: e.dma_start(out=cvp[l], in_=vf[0]), reads=["vf"], writes=["cvp"], semkey="cvp")

            if sec == "att":
                SLOT_H = [0, 2, 1, 3]
                for bl in range(G // 128 if 'att' not in SKIP else 0):
                    mk = "maskb0" if (g == 0 and bl == 0) else "maskb"
                    for slot in range(4):
                        h = SLOT_H[slot]
                        i, base = h // 2, (h % 2) * 64
                        bank = 4 + slot // 2
                        P.pe(lambda e, i=i, base=base, bank=bank, slot=slot, bl=bl: e.matmul(
                            psum[bank][:, (slot % 2) * 256:(slot % 2) * 256 + 256],
                            lhsT=v(qrot)[base:base + 64, i, bl * 128:(bl + 1) * 128],
                            rhs=kT[0][base:base + 64, bl * 128:bl * 128 + 256], start=True, stop=True),
                            reads=["qrot", "kT"], writes=[f"ps{bank}"])
                    for bk in range(2):
                        P.dve(lambda e, bk=bk, mk=mk: e.tensor_tensor(
                            out=v(sc)[:, 2 * bk:2 * bk + 2, :], in0=psum[4 + bk][:, :].rearrange("p (a b) -> p a b", a=2),
                            in1=cs[mk][:].unsqueeze(1).to_broadcast([128, 2, 256]), op=ALU.add),
                            reads=[f"ps{4 + bk}", "k_" + mk], writes=["sc", f"ps{4 + bk}"])
                    smv = v(sm)
                    P.dve(lambda e: e.tensor_reduce(out=smv[:, 0, :], in_=v(sc), axis=AX.X, op=ALU.max), reads=["sc"], writes=["sm"])
                    P.dve(lambda e: e.scalar_tensor_tensor(out=smv[:, 1, :], in0=smv[:, 0, :], scalar=0.125, in1=snk[0],
                                                           op0=ALU.mult, op1=ALU.max), reads=["sm", "snk"], writes=["sm"])
                    P.dve(lambda e: e.tensor_scalar_mul(out=smv[:, 2, :], in0=smv[:, 1, :], scalar1=-1.0), reads=["sm"], writes=["sm"])
                    P.dve(lambda e: e.tensor_tensor(out=smv[:, 4, :], in0=snk[0], in1=smv[:, 1, :], op=ALU.subtract),
                          reads=["sm", "snk"], writes=["sm"])
                    for slot in range(4):
                        P.act(lambda e, slot=slot: e.activation(out=v(pb)[:, slot, :], in_=v(sc)[:, slot, :], func=AF.Exp, scale=0.125,
                                                                bias=smv[:, 2, slot:slot + 1], accum_out=smv[:, 3, slot:slot + 1]),
                              reads=["sc", "sm"], writes=["pb", "sm"])
                    P.act(lambda e: e.activation(out=smv[:, 4, :], in_=smv[:, 4, :], func=AF.Exp), reads=["sm"], writes=["sm"])
                    P.dve(lambda e: e.tensor_tensor(out=smv[:, 5, :], in0=smv[:, 3, :], in1=smv[:, 4, :], op=ALU.add), reads=["sm"], writes=["sm"])
                    P.dve(lambda e: e.reciprocal(out=smv[:, 6, :], in_=smv[:, 5, :]), reads=["sm"], writes=["sm"])
                    for slot in range(4):
                        for hf in range(2):
                            P.pe(lambda e, slot=slot, hf=hf: e.transpose(
                                out=psum[6][:].bitcast(BF16)[:, (slot * 2 + hf) * 128:(slot * 2 + hf + 1) * 128],
                                in_=v(pb)[:, slot, hf * 128:(hf + 1) * 128], identity=cs["ident"][:]),
                                reads=["pb", "k_ident"], writes=["ps6"])
                    P.act(lambda e: e.activation(out=pTs[0], in_=psum[6][:].bitcast(BF16), func=AF.Copy),
                          reads=["ps6"], writes=["pTs", "ps6"])
                    for slot in range(4):
                        for hf in range(2):
                            P.pe(lambda e, slot=slot, hf=hf, bl=bl: e.matmul(
                                psum[7][:, slot * 64:(slot + 1) * 64], lhsT=pTs[0][:, (slot * 2 + hf) * 128:(slot * 2 + hf + 1) * 128],
                                rhs=v(vb)[:, bl + hf, :], start=(hf == 0), stop=(hf == 1)),
                                reads=["pTs", "vb"], writes=["ps7"])
                    P.dve(lambda e: e.tensor_tensor(out=v(attb), in0=psum[7][:, 0:256].rearrange("p (a b) -> p a b", a=4),
                                                    in1=smv[:, 6, :].unsqueeze(2).to_broadcast([128, 4, 64]), op=ALU.mult),
                          reads=["ps7", "sm"], writes=["attb", "ps7"])
                    for slot in range(4):
                        h = SLOT_H[slot]
                        i, base = h // 2, (h % 2) * 64
                        P.pe(lambda e, slot=slot, i=i, base=base: e.transpose(
                            out=psum[4][:].bitcast(BF16)[base:base + 64, i * 128:(i + 1) * 128],
                            in_=v(attb)[:, slot, :], identity=cs["ident"][:]),
                            reads=["attb", "k_ident"], writes=["ps4"])
                    P.dve(lambda e, bl=bl, gq=gsil2[g % 2]: e.tensor_tensor(
                        out=v(zT)[:, 0:2, bl * 128:(bl + 1) * 128],
                        in0=psum[4][:].bitcast(BF16)[:, 0:256].rearrange("p (a b) -> p a b", a=2),
                        in1=v(gq)[:, 0:2, bl * 128:(bl + 1) * 128], op=ALU.mult),
                        reads=["ps4", f"gsil{g % 2}"], writes=["zT", "ps4"])

                if 'rw' not in SKIP:
                    curn = v(cur)[:, :, 1:G + 1]
                    P.dve(lambda e: e.tensor_tensor(out=v(mixed), in0=v(cur)[:, :, 0:G], in1=curn, op=ALU.subtract),
                           reads=["cur"], writes=["mixed"])
                    P.dve(lambda e: e.tensor_tensor(out=v(mixed), in0=v(mixed), in1=mu[0].unsqueeze(2).to_broadcast([128, 7, G]), op=ALU.mult),
                          reads=["mixed", "mu"], writes=["mixed"])
                    P.dve(lambda e: e.tensor_tensor(out=v(mixed), in0=v(mixed), in1=curn, op=ALU.add), reads=["mixed", "cur"], writes=["mixed"])
                if g == NGR - 1 and 'tr' not in SKIP:
                    P.pe(lambda e: e.transpose(out=psum[7][:, 0:128], in_=krotf[0][:, G - 128:G], identity=cs["identf"][:]),
                         reads=["krotf", "k_identf"], writes=["ps7"])
                    P.dve(lambda e: e.tensor_copy(out=kcf[0], in_=psum[7][:, 0:64]), reads=["ps7"], writes=["kcf"])
                    P.dma(lambda e: e.dma_start(out=ckp[l], in_=kcf[0]), reads=["kcf"], writes=["ckp"], semkey="ckp")
                    P.dma(lambda e: e.dma_start(out=shp[l], in_=v(cur)[:, :, G]), reads=["cur"], writes=["shp"], semkey="shp")
                if 'carry' in SKIP:
                    continue
                P.dve(lambda e: e.tensor_copy(out=v(cur)[:, :, 0:1], in_=v(cur)[:, :, G:G + 1]), reads=["cur"], writes=["cur"])
                P.dve(lambda e: e.tensor_copy(out=kT[0][:, 0:128], in_=kT[0][:, G:G + 128]), reads=["kT"], writes=["kT"])
                P.dve(lambda e: e.tensor_copy(out=v(vb)[:, 0, :], in_=v(vb)[:, G // 128, :]), reads=["vb"], writes=["vb"])
            if sec == "rw":
                if 'rw' not in SKIP:
                    curn = v(cur)[:, :, 1:G + 1]
                    mx = v(mixed)
                    P.act(lambda e: e.activation(out=twad[0][0:64, :], in_=mx[0:64, 6, :], func=AF.Tanh), reads=["mixed"], writes=["twad"])
                    P.act(lambda e: e.activation(out=twad[0][64:128, :], in_=mx[64:128, 6, :], func=AF.Copy), reads=["mixed"], writes=["twad"])
                    P.act(lambda e: e.activation(out=v(vrb), in_=mx[:, 4:6, :], func=AF.Copy), reads=["mixed"], writes=["vrb"])
                    rpv = v(rp)
                    for p in range(2):
                        P.pe(lambda e, p=p: e.matmul(psum[p][:, 0:G], lhsT=v(loraW)[0:64, p, :], rhs=twad[0][0:64, :], start=True, stop=True),
                             reads=["loraW", "twad"], writes=[f"ps{p}"])
                        P.act(lambda e, p=p: e.activation(out=v(sig)[:, p, :], in_=psum[p][:, 0:G], func=AF.Sigmoid, bias=rpv[:, 0, p:p + 1]),
                              reads=[f"ps{p}", "rp"], writes=["sig", f"ps{p}"])
                        P.pe(lambda e, p=p: e.matmul(psum[2 + p][:, 0:G], lhsT=v(loraW)[64:128, p, :], rhs=twad[0][64:128, :], start=True, stop=True),
                             reads=["loraW", "twad"], writes=[f"ps{2 + p}"])
                        P.act(lambda e, p=p: e.activation(out=v(aa)[:, p, :], in_=psum[2 + p][:, 0:G], func=AF.Sigmoid, bias=rpv[:, 1, p:p + 1]),
                              reads=[f"ps{2 + p}", "rp"], writes=["aa", f"ps{2 + p}"])
                    for p in range(2):
                        for c in range(4):
                            P.dve(lambda e, p=p, c=c: e.tensor_tensor_scan(
                                out=v(Ls)[:, p, c * 64:(c + 1) * 64], data0=cs["ones"][:, 0:64], data1=v(sig)[:, p, c * 64:(c + 1) * 64],
                                initial=0.0, op0=ALU.mult, op1=ALU.add), reads=["sig", "k_ones"], writes=["Ls"])
                    P.act(lambda e: e.activation(out=v(eL), in_=v(Ls), func=AF.Exp, scale=-CDEC), reads=["Ls"], writes=["eL"])
                    P.act(lambda e: e.activation(out=v(eLi), in_=v(Ls), func=AF.Exp, scale=CDEC), reads=["Ls"], writes=["eLi"])
                    P.dve(lambda e: e.tensor_tensor(out=v(eLp), in0=v(Ls), in1=v(sig), op=ALU.subtract), reads=["Ls", "sig"], writes=["eLp"])
                    P.act(lambda e: e.activation(out=v(eLp), in_=v(eLp), func=AF.Exp, scale=-CDEC), reads=["eLp"], writes=["eLp"])
                    bc = lambda j: rpv[:, j, :].unsqueeze(2).to_broadcast([128, 2, G])
                    P.dve(lambda e: e.tensor_tensor(out=v(kkr), in0=mx[:, 2:4, :], in1=bc(2), op=ALU.mult), reads=["mixed", "rp"], writes=["kkr"])
                    P.dve(lambda e: e.tensor_tensor(out=v(sqb), in0=v(kkr), in1=v(kkr), op=ALU.mult), reads=["kkr"], writes=["sqb"])
                    for p in range(2):
                        P.pe(lambda e, p=p: e.matmul(psum[p][:, 0:G], lhsT=cs["bones"][:], rhs=v(sqb)[:, p, :], start=True, stop=True),
                             reads=["sqb", "k_bones"], writes=[f"ps{p}"])
                        P.act(lambda e, p=p: e.activation(out=v(rn)[:, p, :], in_=psum[p][:, 0:G], func=AF.Sqrt),
                              reads=[f"ps{p}"], writes=["rn", f"ps{p}"])
                    P.dve(lambda e: e.tensor_scalar_max(out=v(rn), in0=v(rn), scalar1=1e-12), reads=["rn"], writes=["rn"])
                    P.dve(lambda e: e.reciprocal(out=v(rn), in_=v(rn)), reads=["rn"], writes=["rn"])
                    P.dve(lambda e: e.tensor_tensor(out=v(kk), in0=v(kkr), in1=v(rn), op=ALU.mult), reads=["kkr", "rn"], writes=["kk"])
                    P.dve(lambda e: e.scalar_tensor_tensor(out=v(uu), in0=v(aa), scalar=-1.0, in1=bc(3), op0=ALU.add, op1=ALU.mult),
                          reads=["aa", "rp"], writes=["uu"])
                    P.dve(lambda e: e.scalar_tensor_tensor(out=v(keff), in0=v(uu), scalar=1.0, in1=mx[:, 2:4, :], op0=ALU.add, op1=ALU.mult),
                          reads=["uu", "mixed"], writes=["keff"])
                    v4 = lambda t_: v(t_).rearrange("p a (c t) -> p a c t", c=4)
                    ARv = v(AR)
                    P.dve(lambda e: e.scalar_tensor_tensor(out=ARv[:, :, :, 0, :], in0=v4(kk), scalar=-1.0, in1=v4(eLp), op0=ALU.mult, op1=ALU.mult),
                          reads=["kk", "eLp"], writes=["AR"])
                    P.dve(lambda e: e.tensor_tensor(out=ARv[:, :, :, 1, :], in0=mx[:, 0:2, :].rearrange("p a (c t) -> p a c t", c=4), in1=v4(eL), op=ALU.mult),
                           reads=["mixed", "eL"], writes=["AR"])
                    P.dve(lambda e: e.tensor_tensor(out=v(uu), in0=v(kk), in1=v(aa), op=ALU.mult), reads=["kk", "aa", "keff"], writes=["uu"])
                    P.dve(lambda e: e.tensor_tensor(out=v(Bt), in0=v(uu), in1=v(eLi), op=ALU.mult), reads=["uu", "eLi"], writes=["Bt"])
                    P.dve(lambda e: e.tensor_tensor(out=v(Kt), in0=v(keff), in1=v(eLi), op=ALU.mult), reads=["keff", "eLi"], writes=["Kt"])
                    P.dve(lambda e: e.tensor_tensor(out=v(kkr), in0=mx[:, 0:2, :], in1=v(keff), op=ALU.mult), reads=["mixed", "keff", "kk"], writes=["kkr"])
                    P.dve(lambda e: e.tensor_tensor(out=v(rkk), in0=v(kkr), in1=bc(4), op=ALU.mult), reads=["kkr", "rp"], writes=["rkk"])
                    for p in range(2):
                        P.pe(lambda e, p=p: e.matmul(psum[2 + p][:, 0:G], lhsT=cs["bones"][:], rhs=v(rkk)[:, p, :], start=True, stop=True),
                             reads=["rkk", "k_bones"], writes=[f"ps{2 + p}"])
                        P.dve(lambda e, p=p: e.tensor_tensor(out=v(bonus)[:, p, :], in0=psum[2 + p][:, 0:G], in1=mx[:, 4 + p, :], op=ALU.mult),
                              reads=[f"ps{2 + p}", "mixed"], writes=["bonus", f"ps{2 + p}"])
                    TKv = v(TK)
                    for p in range(2):
                        for c in range(4):
                            for wi, src in enumerate((Bt, Kt, vrb)):
                                for hh in range(2):
                                    hs = slice(hh * 64, hh * 64 + 64)
                                    P.pe(lambda e, p=p, c=c, wi=wi, src=src, hs=hs: e.transpose(
                                        out=psum[4 + p][:].bitcast(BF16)[hs, (c * 3 + wi) * 64:(c * 3 + wi + 1) * 64],
                                        in_=v(src)[hs, p, c * 64:(c + 1) * 64], identity=cs["ident"][hs, hs]),
                                        reads=["Bt", "Kt", "vrb", "k_ident"], writes=[f"ps{4 + p}"])
                        P.act(lambda e, p=p: e.activation(out=TKv[:, p, :, :, :].rearrange("p c w t -> p (c w t)"),
                                                          in_=psum[4 + p][:].bitcast(BF16)[:, 0:768], func=AF.Copy),
                              reads=[f"ps{4 + p}"], writes=["TK", f"ps{4 + p}"])
                    MSv = v(MS)
                    for p in range(2):
                        for c in range(4):
                            it = p * 4 + c
                            bank = 6 + it % 2
                            for hh in range(2):
                                hs = slice(hh * 64, hh * 64 + 64)
                                P.pe(lambda e, p=p, c=c, hs=hs, bank=bank: e.matmul(
                                    psum[bank][hs, 0:128], lhsT=v(Bt)[hs, p, c * 64:(c + 1) * 64],
                                    rhs=ARv[hs, p, c, :, :].rearrange("p a t -> p (a t)"), start=True, stop=True),
                                    reads=["Bt", "AR"], writes=[f"ps{bank}"])
                                P.pe(lambda e, p=p, c=c, hs=hs, bank=bank: e.matmul(
                                    psum[bank][hs, 128:256], lhsT=v(Kt)[hs, p, c * 64:(c + 1) * 64],
                                    rhs=ARv[hs, p, c, :, :].rearrange("p a t -> p (a t)"), start=True, stop=True),
                                    reads=["Kt", "AR"], writes=[f"ps{bank}"])
                                P.pe(lambda e, p=p, c=c, hs=hs, bank=bank: e.matmul(
                                    psum[bank][hs, 256:320], lhsT=ARv[hs, p, c, 0, :], rhs=v(Bt)[hs, p, c * 64:(c + 1) * 64],
                                    start=True, stop=True), reads=["Bt", "AR"], writes=[f"ps{bank}"])
                            P.dve(lambda e, it=it, bank=bank: e.tensor_tensor(out=MSv[:, it, :], in0=psum[bank][:, 0:320], in1=cs["maskG"][:], op=ALU.mult),
                                  reads=[f"ps{bank}", "k_maskG"], writes=["MS", f"ps{bank}"])
                    P.dve(lambda e: e.tensor_tensor(out=v(TT[0]), in0=MSv[:, :, 0:64], in1=cs["id64"][:].unsqueeze(1).to_broadcast([128, 8, 64]), op=ALU.add),
                          reads=["MS", "k_id64"], writes=["TT0"])
                    for k in range(1, 6):
                        src_i, dst_i = (k - 1) % 2, k % 2
                        PQs, PQd = v(PQ[src_i]), v(PQ[dst_i])
                        Pprev = (lambda it_: MSv[:, it_, 0:64]) if k == 1 else (lambda it_, PQs=PQs: PQs[:, it_, 0:64])
                        Qprev = (lambda it_: MSv[:, it_, 256:320]) if k == 1 else (lambda it_, PQs=PQs: PQs[:, it_, 64:128])
                        rk_ = ["MS"] if k == 1 else [f"PQ{src_i}"]
                        for it in range(8):
                            bank = it // 4
                            for hh in range(2):
                                hs = slice(hh * 64, hh * 64 + 64)
                                P.pe(lambda e, it=it, hs=hs, bank=bank, Pprev=Pprev, Qprev=Qprev: e.matmul(
                                    psum[bank][hs, (it % 4) * 128:(it % 4) * 128 + 64], lhsT=Qprev(it)[hs, :], rhs=Pprev(it)[hs, :], start=True, stop=True),
                                    reads=rk_, writes=[f"ps{bank}"])
                                P.pe(lambda e, it=it, hs=hs, bank=bank, Pprev=Pprev, Qprev=Qprev: e.matmul(
                                    psum[bank][hs, (it % 4) * 128 + 64:(it % 4) * 128 + 128], lhsT=Pprev(it)[hs, :], rhs=Qprev(it)[hs, :], start=True, stop=True),
                                    reads=rk_, writes=[f"ps{bank}"])
                        for bank in range(2):
                            P.act(lambda e, bank=bank, PQd=PQd: e.activation(out=PQd[:, bank * 4:(bank + 1) * 4, :].rearrange("p a b -> p (a b)"),
                                                                            in_=psum[bank][:, 0:512], func=AF.Copy),
                                  reads=[f"ps{bank}"], writes=[f"PQ{dst_i}", f"ps{bank}"])
                        Ts, Td = v(TT[src_i]), v(TT[dst_i])
                        for it in range(8):
                            for hh in range(2):
                                hs = slice(hh * 64, hh * 64 + 64)
                                P.pe(lambda e, it=it, hs=hs, PQd=PQd, Ts=Ts: e.matmul(
                                    psum[2][hs, it * 64:(it + 1) * 64], lhsT=PQd[hs, it, 64:128], rhs=Ts[hs, it, :], start=True, stop=True),
                                    reads=[f"PQ{dst_i}", f"TT{src_i}"], writes=["ps2"])
                        P.dve(lambda e, Ts=Ts, Td=Td: e.tensor_tensor(out=Td, in0=psum[2][:, 0:512].rearrange("p (a b) -> p a b", a=8), in1=Ts, op=ALU.add),
                              reads=["ps2", f"TT{src_i}"], writes=[f"TT{dst_i}", "ps2"])
                    Tfin = v(TT[1])
                    Sv, Sbv, Wbv, Ubv, Yv = v(Sst), v(Sb), v(Wb), v(Ub), v(Ybuf)
                    for c in range(4):
                        for p in range(2):
                            it = p * 4 + c
                            for hh in range(2):
                                hs = slice(hh * 64, hh * 64 + 64)
                                P.pe(lambda e, p=p, c=c, hs=hs: e.matmul(psum[3][hs, p * 64:(p + 1) * 64], lhsT=ARv[hs, p, c, 0, :], rhs=Sbv[hs, p, :],
                                                                         start=True, stop=False), reads=["AR", "Sb"], writes=["ps3"])
                                P.pe(lambda e, p=p, c=c, hs=hs, it=it: e.matmul(psum[3][hs, p * 64:(p + 1) * 64], lhsT=MSv[hs, it, 128:192], rhs=TKv[hs, p, c, 2, :],
                                                                                start=False, stop=True), reads=["MS", "TK"], writes=["ps3"])
                        P.act(lambda e: e.activation(out=Wbv.rearrange("p a b -> p (a b)"), in_=psum[3][:, 0:128], func=AF.Copy),
                              reads=["ps3"], writes=["Wb", "ps3"])
                        for p in range(2):
                            it = p * 4 + c
                            for hh in range(2):
                                hs = slice(hh * 64, hh * 64 + 64)
                                P.pe(lambda e, p=p, hs=hs, it=it: e.matmul(psum[4][hs, p * 64:(p + 1) * 64], lhsT=Tfin[hs, it, :], rhs=Wbv[hs, p, :],
                                                                           start=True, stop=True), reads=["TT1", "Wb"], writes=["ps4"])
                        P.act(lambda e: e.activation(out=Ubv.rearrange("p a b -> p (a b)"), in_=psum[4][:, 0:128], func=AF.Copy),
                              reads=["ps4"], writes=["Ub", "ps4"])
                        for p in range(2):
                            it = p * 4 + c
                            for hh in range(2):
                                hs = slice(hh * 64, hh * 64 + 64)
                                P.pe(lambda e, p=p, c=c, hs=hs: e.matmul(psum[5][hs, p * 64:(p + 1) * 64], lhsT=ARv[hs, p, c, 1, :], rhs=Sbv[hs, p, :],
                                                                         start=True, stop=False), reads=["AR", "Sb"], writes=["ps5"])
                                P.pe(lambda e, p=p, hs=hs, it=it: e.matmul(psum[5][hs, p * 64:(p + 1) * 64], lhsT=MSv[hs, it, 64:128], rhs=Ubv[hs, p, :],
                                                                           start=False, stop=False), reads=["MS", "Ub"], writes=["ps5"])
                                P.pe(lambda e, p=p, c=c, hs=hs, it=it: e.matmul(psum[5][hs, p * 64:(p + 1) * 64], lhsT=MSv[hs, it, 192:256], rhs=TKv[hs, p, c, 2, :],
                                                                                start=False, stop=True), reads=["MS", "TK"], writes=["ps5"])
                                P.pe(lambda e, p=p, c=c, hs=hs: e.matmul(psum[6][hs, p * 64:(p + 1) * 64], lhsT=TKv[hs, p, c, 0, :], rhs=Ubv[hs, p, :],
                                                                         start=True, stop=False), reads=["TK", "Ub"], writes=["ps6"])
                                P.pe(lambda e, p=p, c=c, hs=hs: e.matmul(psum[6][hs, p * 64:(p + 1) * 64], lhsT=TKv[hs, p, c, 1, :], rhs=TKv[hs, p, c, 2, :],
                                                                         start=False, stop=True), reads=["TK"], writes=["ps6"])
                        P.dve(lambda e, c=c: e.tensor_copy(out=Yv[:, c, :, :], in_=psum[5][:, 0:128].rearrange("p (a b) -> p a b", a=2)),
                              reads=["ps5"], writes=["Ybuf", "ps5"])
                        P.dve(lambda e: e.tensor_tensor(out=Sv, in0=psum[6][:, 0:128].rearrange("p (a b) -> p a b", a=2), in1=Sv, op=ALU.add),
                              reads=["ps6", "Sst"], writes=["Sst", "ps6"])
                        P.dve(lambda e, c=c: e.tensor_tensor(out=Sv, in0=Sv, in1=v(eL)[:, :, c * 64 + 63:c * 64 + 64].to_broadcast([128, 2, 64]), op=ALU.mult),
                              reads=["Sst", "eL"], writes=["Sst"])
                        P.act(lambda e: e.activation(out=Sbv, in_=Sv, func=AF.Copy), reads=["Sst"], writes=["Sb"])
                    if g == NGR - 1:
                        P.dma(lambda e: e.dma_start(out=swp[l], in_=Sv), reads=["Sst"], writes=["swp"], semkey="swp")
                    gsv = v(gs)
                    Y8 = Yv.rearrange("p c a b -> p (c a) b")
                    P.dve(lambda e: e.tensor_reduce(out=gsv[:, 0, :], in_=Y8, axis=AX.X, op=ALU.add), reads=["Ybuf"], writes=["gs"])
                    P.dve(lambda e: e.tensor_scalar_mul(out=gsv[:, 0, :], in0=gsv[:, 0, :], scalar1=-1.0 / 64), reads=["gs"], writes=["gs"])
                    P.dve(lambda e: e.tensor_tensor(out=v(yc), in0=Y8, in1=gsv[:, 0, :].unsqueeze(2).to_broadcast([128, 8, 64]), op=ALU.add),
                           reads=["Ybuf", "gs"], writes=["pb"])
                    P.dve(lambda e: e.tensor_tensor(out=v(ysq), in0=v(yc), in1=v(yc), op=ALU.mult), reads=["pb"], writes=["sc"])
                    P.dve(lambda e: e.tensor_reduce(out=gsv[:, 1, :], in_=v(ysq), axis=AX.X, op=ALU.add), reads=["sc"], writes=["gs"])
                    P.act(lambda e: e.activation(out=gsv[:, 2, :], in_=gsv[:, 1, :], func=AF.Sqrt, scale=1.0 / 64, bias=epsc[:, 1:2]),
                          reads=["gs", "epsc"], writes=["gs"])
                    P.dve(lambda e: e.reciprocal(out=gsv[:, 3, :], in_=gsv[:, 2, :]), reads=["gs"], writes=["gs"])
                    P.dve(lambda e: e.tensor_tensor(out=v(yh), in0=v(yc), in1=gsv[:, 3, :].unsqueeze(2).to_broadcast([128, 8, 64]), op=ALU.mult),
                          reads=["pb", "gs"], writes=["yh"])
                    for c in range(4):
                        for p in range(2):
                            for hh in range(2):
                                hs = slice(hh * 64, hh * 64 + 64)
                                P.pe(lambda e, c=c, p=p, hs=hs: e.transpose(
                                    out=psum[7][:].bitcast(BF16)[hs, p * G + c * 64:p * G + (c + 1) * 64],
                                    in_=v(yh)[hs, c * 2 + p, :], identity=cs["ident"][hs, hs]), reads=["yh", "k_ident"], writes=["ps7"])
                    for p in range(2):
                        P.act(lambda e, p=p: e.activation(out=v(yz)[:, p, :], in_=psum[7][:].bitcast(BF16)[:, p * G:(p + 1) * G], func=AF.Identity,
                                                          scale=rpv[:, 5, p:p + 1], bias=rpv[:, 6, p:p + 1]),
                              reads=["ps7", "rp"], writes=["rn", "ps7"])
                    P.dve(lambda e: e.tensor_tensor(out=v(yz), in0=v(yz), in1=v(bonus), op=ALU.add), reads=["rn", "bonus"], writes=["rn"])
                    P.dve(lambda e, gq=gsil2[g % 2]: e.tensor_tensor(out=v(zT)[:, 2:4, :], in0=v(yz), in1=v(gq)[:, 2:4, :], op=ALU.mult),
                          reads=["rn", f"gsil{g % 2}"], writes=["zT"])

                for bl in range(G // 128 if 'op' not in SKIP else 0):
                    for cgp in range(4):
                        for i in range(4):
                            P.pe(lambda e, bl=bl, cgp=cgp, i=i: e.matmul(psum[cgp][:, 0:512], lhsT=v(zT)[:, i, bl * 128:(bl + 1) * 128],
                                                                         rhs=Wo[:, i, cgp * 512:(cgp + 1) * 512], start=(i == 0), stop=(i == 3)),
                                 reads=["zT"] + [f"Wo{j}" for j in range(4)], writes=[f"ps{cgp}"])
                        if cgp % 2 == 0:
                            P.act(lambda e, cgp=cgp: e.activation(out=ob[0][:, cgp * 512:(cgp + 1) * 512], in_=psum[cgp][:, 0:512], func=AF.Copy),
                                  reads=[f"ps{cgp}"], writes=["ob", f"ps{cgp}"])
                        else:
                            P.dve(lambda e, cgp=cgp: e.tensor_copy(out=ob[0][:, cgp * 512:(cgp + 1) * 512], in_=psum[cgp][:, 0:512]),
                                  reads=[f"ps{cgp}"], writes=["ob", f"ps{cgp}"])
                    tok0 = g * G + bl * 128
                    row0 = tok0
                    for q in range(2):
                        P.dma(lambda e, q=q, row0=row0: e.dma_start(out=rs_in[l][q].ap()[row0:row0 + 128, :], in_=ob[0][:, q * 1024:(q + 1) * 1024]),
                              reads=["ob"], writes=[f"rs_in{l}"], semkey=f"ob{q}")

    def phase_M_samples(l, Win, WK):
        P.barrier()
        NS = 16
        c7 = Carve()
        hTs16 = c7.bf([128, 16, NS])
        rp = c7.f32([128, NPAR, 2])
        loraW = c7.bf([128, 2, 128])
        mu = c7.f32([128, 7])
        prv = c7.f32([128, 7, NS])
        shpt = c7.f32([64, 193])
        P.dma(lambda e: e.dma_start(out=v(rp), in_=rpar[l]), writes=["s_rp"])
        loraF = c7.f32([128, 2, 128])
        P.dma(lambda e: e.dma_start(out=v(loraF), in_=loraw[l]), writes=["s_loraF"])
        P.act(lambda e: e.activation(out=v(loraW), in_=v(loraF), func=AF.Copy), reads=["s_loraF"], writes=["s_loraW"])
        P.dma(lambda e: e.dma_start(out=mu[0], in_=mu_col[l]), writes=["s_mu"])
        P.dma(lambda e: e.dma_start(out=v(prv), in_=prev_s[l]), writes=["s_prv"])
        P.dma(lambda e: e.dma_start(out=shpt[0], in_=shpar_in[l]), writes=["s_shpt"])
        curS = c7.f32([128, 7, NS])
        mixS = c7.f32([128, 7, NS])
        qf = c7.f32([128, 3, NS])
        qbS = c7.bf([128, 3, NS])
        t1S = c7.f32([128, 3, NS])
        qrotS = c7.f32([128, 3, NS])
        gsS = c7.f32([128, 4, NS])
        vS = c7.f32([16, 64])
        for r2 in range(4):
            P.dma(lambda e, r2=r2: e.dma_start(out=v(hTs16)[:, :, 4 * r2:4 * r2 + 4],
                                               in_=ago_hs[l].ap()[r2 * 128:(r2 + 1) * 128, :].rearrange("p (k c) -> p k c", k=16)),
                  reads=[f"ago_hs{l}"], writes=["s_hT"], eng=("sp" if r2 % 2 == 0 else "act"), semkey=f"s_hT{r2 % 2}")
        for ct in range(NCT):
            pi = ct % 4
            for k in range(16):
                P.pe(lambda e, ct=ct, k=k, pi=pi: e.matmul(psum[pi][:, 0:NS], lhsT=Win[:, k, ct * 128:(ct + 1) * 128], rhs=v(hTs16)[:, k, :],
                                                           start=(k == 0), stop=(k == 15)), reads=["s_hT"], writes=[f"ps{pi}"])
            if ct < 3:
                P.act(lambda e, ct=ct, pi=pi: e.activation(out=v(qf)[:, ct, :], in_=psum[pi][:, 0:NS], func=AF.Copy),
                      reads=[f"ps{pi}"], writes=["s_qf", f"ps{pi}"])
            elif ct < 10:
                P.act(lambda e, ct=ct, pi=pi: e.activation(out=v(curS)[:, ct - 3, :], in_=psum[pi][:, 0:NS], func=AF.Copy),
                      reads=[f"ps{pi}"], writes=["s_cur", f"ps{pi}"])
            else:
                P.act(lambda e, ct=ct, pi=pi: e.activation(out=v(gsS)[:, ct - 10, :], in_=psum[pi][:, 0:NS], func=AF.Silu),
                      reads=[f"ps{pi}"], writes=["s_gs", f"ps{pi}"])
        for k in range(16):
            P.pe(lambda e, k=k: e.matmul(psum[4][0:NS, 0:64], lhsT=v(hTs16)[:, k, :], rhs=Win[:, k, NCT * 128:NCT * 128 + 64],
                                         start=(k == 0), stop=(k == 15)), reads=["s_hT"], writes=["ps4"])
        P.act(lambda e: e.activation(out=vS[0], in_=psum[4][0:NS, 0:64], func=AF.Copy), reads=["ps4"], writes=["s_vS", "ps4"])
        P.dma(lambda e: e.dma_start(out=shs[l], in_=v(curS)), reads=["s_cur"], writes=["shs"], semkey="shs")
        P.act(lambda e: e.activation(out=v(qbS), in_=v(qf), func=AF.Copy), reads=["s_qf"], writes=["s_qb"])
        for ct in range(3):
            P.pe(lambda e, ct=ct: e.matmul(psum[5][:, ct * NS:(ct + 1) * NS], lhsT=cs["prot"][:], rhs=v(qbS)[:, ct, :], start=True, stop=True),
                 reads=["s_qb", "k_prot"], writes=["ps5"])
        P.dve(lambda e: e.tensor_scalar_mul(out=t1S[0], in0=psum[5][:, 0:3 * NS], scalar1=cs["cs_s"][:, 1:2]), reads=["ps5", "k_cs_s"],
              writes=["s_t1", "ps5"])
        P.dve(lambda e: e.scalar_tensor_tensor(out=qrotS[0], in0=qf[0], scalar=cs["cs_s"][:, 0:1], in1=t1S[0], op0=ALU.mult, op1=ALU.add),
              reads=["s_qf", "s_t1", "k_cs_s"], writes=["s_qrot"])
        P.dve(lambda e: e.tensor_tensor(out=v(mixS), in0=v(prv), in1=v(curS), op=ALU.subtract), reads=["s_prv", "s_cur"], writes=["s_mix"])
        P.dve(lambda e: e.tensor_tensor(out=v(mixS), in0=v(mixS), in1=mu[0].unsqueeze(2).to_broadcast([128, 7, NS]), op=ALU.mult),
              reads=["s_mix", "s_mu"], writes=["s_mix"])
        P.dve(lambda e: e.tensor_tensor(out=v(mixS), in0=v(mixS), in1=v(curS), op=ALU.add), reads=["s_mix", "s_cur"], writes=["s_mix"])
        mx = v(mixS)
        rpv = v(rp)
        twadS = c7.bf([128, NS])
        sigS = c7.f32([128, 2, NS])
        aS = c7.f32([128, 2, NS])
        wS = c7.f32([128, 2, NS])
        kkrS = c7.f32([128, 2, NS])
        sqS = c7.bf([128, 2, NS])
        rnS = c7.f32([128, 2, NS])
        kkS = c7.f32([128, 2, NS])
        uuS = c7.f32([128, 2, NS])
        keffS = c7.f32([128, 2, NS])
        P.act(lambda e: e.activation(out=twadS[0][0:64, :], in_=mx[0:64, 6, :], func=AF.Tanh), reads=["s_mix"], writes=["s_twad"])
        P.act(lambda e: e.activation(out=twadS[0][64:128, :], in_=mx[64:128, 6, :], func=AF.Copy), reads=["s_mix"], writes=["s_twad"])
        for p in range(2):
            P.pe(lambda e, p=p: e.matmul(psum[p][:, 0:NS], lhsT=v(loraW)[0:64, p, :], rhs=twadS[0][0:64, :], start=True, stop=True),
                 reads=["s_loraW", "s_twad"], writes=[f"ps{p}"])
            P.act(lambda e, p=p: e.activation(out=v(sigS)[:, p, :], in_=psum[p][:, 0:NS], func=AF.Sigmoid, bias=rpv[:, 0, p:p + 1]),
                  reads=[f"ps{p}", "s_rp"], writes=["s_sig", f"ps{p}"])
            P.pe(lambda e, p=p: e.matmul(psum[2 + p][:, 0:NS], lhsT=v(loraW)[64:128, p, :], rhs=twadS[0][64:128, :], start=True, stop=True),
                 reads=["s_loraW", "s_twad"], writes=[f"ps{2 + p}"])
            P.act(lambda e, p=p: e.activation(out=v(aS)[:, p, :], in_=psum[2 + p][:, 0:NS], func=AF.Sigmoid, bias=rpv[:, 1, p:p + 1]),
                  reads=[f"ps{2 + p}", "s_rp"], writes=["s_a", f"ps{2 + p}"])
        P.act(lambda e: e.activation(out=v(wS), in_=v(sigS), func=AF.Exp, scale=-CDEC), reads=["s_sig"], writes=["s_w"])
        bc = lambda j: rpv[:, j, :].unsqueeze(2).to_broadcast([128, 2, NS])
        P.dve(lambda e: e.tensor_tensor(out=v(kkrS), in0=mx[:, 2:4, :], in1=bc(2), op=ALU.mult), reads=["s_mix", "s_rp"], writes=["s_kkr"])
        P.dve(lambda e: e.tensor_tensor(out=v(sqS), in0=v(kkrS), in1=v(kkrS), op=ALU.mult), reads=["s_kkr"], writes=["s_sq"])
        for p in range(2):
            P.pe(lambda e, p=p: e.matmul(psum[p][:, 0:NS], lhsT=cs["bones"][:], rhs=v(sqS)[:, p, :], start=True, stop=True),
                 reads=["s_sq", "k_bones"], writes=[f"ps{p}"])
            P.act(lambda e, p=p: e.activation(out=v(rnS)[:, p, :], in_=psum[p][:, 0:NS], func=AF.Sqrt), reads=[f"ps{p}"], writes=["s_rn", f"ps{p}"])
        P.dve(lambda e: e.tensor_scalar_max(out=v(rnS), in0=v(rnS), scalar1=1e-12), reads=["s_rn"], writes=["s_rn"])
        P.dve(lambda e: e.reciprocal(out=v(rnS), in_=v(rnS)), reads=["s_rn"], writes=["s_rn"])
        P.dve(lambda e: e.tensor_tensor(out=v(kkS), in0=v(kkrS), in1=v(rnS), op=ALU.mult), reads=["s_kkr", "s_rn"], writes=["s_kk"])
        P.dve(lambda e: e.scalar_tensor_tensor(out=v(uuS), in0=v(aS), scalar=-1.0, in1=bc(3), op0=ALU.add, op1=ALU.mult),
              reads=["s_a", "s_rp"], writes=["s_uu"])
        P.dve(lambda e: e.scalar_tensor_tensor(out=v(keffS), in0=v(uuS), scalar=1.0, in1=mx[:, 2:4, :], op0=ALU.add, op1=ALU.mult),
              reads=["s_uu", "s_mix"], writes=["s_keff"])
        tmS = c7.f32([16, 20, 128])
        srcs = []
        for (t_, key, lo) in ((mixS, "s_mix", 0), (wS, "s_w", 0), (keffS, "s_keff", 0), (mixS, "s_mix", 4), (kkS, "s_kk", 0), (aS, "s_a", 0),
                              (qrotS, "s_qrot", 0), (gsS, "s_gs", 0), (gsS, "s_gs", 2)):
            for p in range(2):
                srcs.append((v(t_)[:, lo + p, :], key))
        srcs.append((v(qrotS)[:, 2, :], "s_qrot"))
        for n0 in range(0, len(srcs), 4):
            bank = 4 + (n0 // 4) % 4
            grp = srcs[n0:n0 + 4]
            for i_, (ap_, key) in enumerate(grp):
                P.pe(lambda e, ap_=ap_, i_=i_, bank=bank: e.transpose(out=psum[bank][0:NS, i_ * 128:(i_ + 1) * 128], in_=ap_, identity=cs["identf"][:]),
                     reads=[key, "k_identf"], writes=[f"ps{bank}"])
            n_ = len(grp)
            evac = P.act if (n0 // 4) % 2 == 0 else P.dve
            if (n0 // 4) % 2 == 0:
                P.act(lambda e, n0=n0, n_=n_, bank=bank: e.activation(out=v(tmS)[:, n0:n0 + n_, :].rearrange("p a b -> p (a b)"),
                                                                   in_=psum[bank][0:NS, 0:n_ * 128], func=AF.Copy),
                      reads=[f"ps{bank}"], writes=["s_tm", f"ps{bank}"])
            else:
                P.dve(lambda e, n0=n0, n_=n_, bank=bank: e.tensor_copy(out=v(tmS)[:, n0:n0 + n_, :].rearrange("p a b -> p (a b)"),
                                                                    in_=psum[bank][0:NS, 0:n_ * 128]),
                      reads=[f"ps{bank}"], writes=["s_tm", f"ps{bank}"])
        for kind in range(9):
            P.dma(lambda e, kind=kind: e.dma_start(out=smp_d[l].ap()[:, :, kind, :].rearrange("h s d -> s h d"),
                                                   in_=v(tmS)[:, 2 * kind:2 * kind + 2, :].rearrange("s t (h d) -> s (t h) d", h=2)),
                  reads=["s_tm"], writes=["smp_d"], eng=("sp" if kind % 2 == 0 else "act"), semkey=f"smp{kind % 2}")
        P.dma(lambda e: e.dma_start(out=kn_d[l].ap(), in_=v(tmS)[:, 18, 0:64]), reads=["s_tm"], writes=["kn_d"], semkey="kn")
        P.dma(lambda e: e.dma_start(out=vn_d[l].ap(), in_=vS[0]), reads=["s_vS"], writes=["vn_d"], semkey="vn")
        P.dma(lambda e: e.dma_start(out=cks[l][:, 0:127, :], in_=ck_in[l][:, 1:128, :]), writes=["cks"], semkey="cks0")
        P.dma(lambda e: e.dma_start(out=cvs[l][:, 0:127, :], in_=cv_in[l][:, 1:128, :]), writes=["cvs"], eng="act", semkey="cvs0")
        P.dma(lambda e: e.dma_start(out=cks[l][:, 127, :], in_=v(tmS)[:, 18, 0:64]), reads=["s_tm"], writes=["cks"], semkey="cks1")
        P.dma(lambda e: e.dma_start(out=cvs[l][:, 127, :], in_=vS[0]), reads=["s_vS"], writes=["cvs"], eng="act", semkey="cvs1")
        SH = c7.f32([64, 9, 64])
        SHv = v(SH)
        P.dma(lambda e: e.dma_start(out=SHv, in_=smp_d[l].ap().rearrange("h s k d -> (h s) k d")), reads=["smp_d"], writes=["s_SH"])
        KV = c7.f32([64, 129, 64])
        KVv = v(KV)
        tmpA = c7.f32([64, 33 * 64])
        scs = c7.f32([64, 129])
        pS = c7.f32([64, 129])
        sm2 = c7.f32([64, 8])
        oS = c7.f32([64, 64])
        prt = c7.f32([64, 64])
        zs = c7.f32([64, 2, 64])
        for h_ in range(4):
            q_ = "sp" if h_ % 2 == 0 else "act"
            P.dma(lambda e, h_=h_: e.dma_start(out=KVv[16 * h_:16 * h_ + 16, 0:128, :], in_=ck_in[l]), writes=["s_KV"], eng=q_, semkey=f"s_KV{h_}")
            P.dma(lambda e, h_=h_: e.dma_start(out=KVv[16 * h_:16 * h_ + 16, 128, :], in_=kn_d[l].ap()), reads=["kn_d"], writes=["s_KV"], eng=q_,
                  semkey=f"s_KV{h_}")
        PCH = [(0, 32), (32, 64), (64, 96), (96, 129)]
        for ci, (a_, b_) in enumerate(PCH):
            n_ = b_ - a_
            tv = tmpA[0][:, 0:n_ * 64].rearrange("p (n d) -> p n d", d=64)
            f_ = P.dve
            f_(lambda e, a_=a_, b_=b_, n_=n_, tv=tv: e.tensor_tensor(out=tv, in0=KVv[:, a_:b_, :], in1=SHv[:, 6, :].unsqueeze(1).to_broadcast([64, n_, 64]),
                                                              op=ALU.mult), reads=["s_KV", "s_SH"], writes=["s_tmpA"])
            P.dve(lambda e, a_=a_, b_=b_, tv=tv: e.tensor_reduce(out=scs[0][:, a_:b_], in_=tv, axis=AX.X, op=ALU.add), reads=["s_tmpA"], writes=["s_scs"])
        s2 = sm2[0]
        snkc = shpt[0][:, 192:193]
        P.dve(lambda e: e.tensor_reduce(out=s2[:, 0:1], in_=scs[0], axis=AX.X, op=ALU.max), reads=["s_scs"], writes=["s_sm2"])
        P.dve(lambda e: e.scalar_tensor_tensor(out=s2[:, 1:2], in0=s2[:, 0:1], scalar=0.125, in1=snkc, op0=ALU.mult, op1=ALU.max),
              reads=["s_sm2", "s_shpt"], writes=["s_sm2"])
        P.dve(lambda e: e.tensor_scalar_mul(out=s2[:, 2:3], in0=s2[:, 1:2], scalar1=-1.0), reads=["s_sm2"], writes=["s_sm2"])
        P.dve(lambda e: e.tensor_tensor(out=s2[:, 4:5], in0=snkc, in1=s2[:, 1:2], op=ALU.subtract), reads=["s_sm2", "s_shpt"], writes=["s_sm2"])
        P.act(lambda e: e.activation(out=pS[0], in_=scs[0], func=AF.Exp, scale=0.125, bias=s2[:, 2:3], accum_out=s2[:, 3:4]),
              reads=["s_scs", "s_sm2"], writes=["s_pS", "s_sm2"])
        P.act(lambda e: e.activation(out=s2[:, 4:5], in_=s2[:, 4:5], func=AF.Exp), reads=["s_sm2"], writes=["s_sm2"])
        P.dve(lambda e: e.tensor_tensor(out=s2[:, 5:6], in0=s2[:, 3:4], in1=s2[:, 4:5], op=ALU.add), reads=["s_sm2"], writes=["s_sm2"])
        P.dve(lambda e: e.reciprocal(out=s2[:, 6:7], in_=s2[:, 5:6]), reads=["s_sm2"], writes=["s_sm2"])
        for h_ in range(4):
            q_ = "sp" if h_ % 2 == 0 else "act"
            P.dma(lambda e, h_=h_: e.dma_start(out=KVv[16 * h_:16 * h_ + 16, 0:128, :], in_=cv_in[l]), reads=["s_scs"], writes=["s_KV"], eng=q_,
                  semkey=f"s_KV{h_}")
            P.dma(lambda e, h_=h_: e.dma_start(out=KVv[16 * h_:16 * h_ + 16, 128, :], in_=vn_d[l].ap()), reads=["vn_d", "s_scs"], writes=["s_KV"],
                  eng=q_, semkey=f"s_KV{h_}")
        for ci, (a_, b_) in enumerate(PCH):
            n_ = b_ - a_
            tv = tmpA[0][:, 0:n_ * 64].rearrange("p (d n) -> p d n", d=64)
            f_ = P.dve
            f_(lambda e, a_=a_, b_=b_, n_=n_, tv=tv: e.tensor_tensor(out=tv, in0=KVv[:, a_:b_, :].rearrange("p n d -> p d n"),
                                                              in1=pS[0][:, a_:b_].unsqueeze(1).to_broadcast([64, 64, n_]), op=ALU.mult),
               reads=["s_KV", "s_pS"], writes=["s_tmpA"])
            if ci == 0:
                P.dve(lambda e, tv=tv: e.tensor_reduce(out=oS[0], in_=tv, axis=AX.X, op=ALU.add), reads=["s_tmpA"], writes=["s_oS"])
            else:
                P.dve(lambda e, tv=tv: e.tensor_reduce(out=prt[0], in_=tv, axis=AX.X, op=ALU.add), reads=["s_tmpA"], writes=["s_prt"])
                P.dve(lambda e: e.tensor_tensor(out=oS[0], in0=oS[0], in1=prt[0], op=ALU.add), reads=["s_oS", "s_prt"], writes=["s_oS"])
        zsv = v(zs)
        P.dve(lambda e: e.tensor_scalar_mul(out=oS[0], in0=oS[0], scalar1=s2[:, 6:7]), reads=["s_oS", "s_sm2"], writes=["s_oS"])
        P.dve(lambda e: e.tensor_tensor(out=zsv[:, 0, :], in0=oS[0], in1=SHv[:, 7, :], op=ALU.mult), reads=["s_oS", "s_SH"], writes=["s_zs"])
        Ssm = c7.f32([64, 64, 64])
        tmpS = c7.f32([64, 64, 64])
        sv = c7.f32([64, 6, 64])
        g2 = c7.f32([64, 8])
        Sv_, Tv_, svv = v(Ssm), v(tmpS), v(sv)
        for h_ in range(4):
            P.dma(lambda e, h_=h_: e.dma_start(out=Sv_[16 * h_:16 * h_ + 16], in_=st_in[l][:, h_]), writes=["s_Ssm"],
                  eng=("sp" if h_ % 2 == 0 else "act"), semkey=f"s_Sl{h_}")
        bi = lambda ap_: ap_.unsqueeze(1).to_broadcast([64, 64, 64])
        bj = lambda ap_: ap_.unsqueeze(2).to_broadcast([64, 64, 64])
        P.dve(lambda e: e.tensor_scalar_mul(out=svv[:, 0, :], in0=SHv[:, 4, :], scalar1=-1.0), reads=["s_SH"], writes=["s_sv0"])
        P.dve(lambda e: e.tensor_tensor(out=svv[:, 1, :], in0=SHv[:, 4, :], in1=SHv[:, 5, :], op=ALU.mult), reads=["s_SH"], writes=["s_sv1"])
        P.dve(lambda e: e.tensor_tensor(out=Tv_, in0=Sv_, in1=bi(svv[:, 0, :]), op=ALU.mult), reads=["s_Ssm", "s_sv0"], writes=["s_tmpS"])
        P.dve(lambda e: e.tensor_reduce(out=svv[:, 2, :], in_=Tv_, axis=AX.X, op=ALU.add), reads=["s_tmpS"], writes=["s_sv2"])
        P.dve(lambda e: e.tensor_tensor(out=Sv_, in0=Sv_, in1=bi(SHv[:, 1, :]), op=ALU.mult), reads=["s_Ssm", "s_SH", "s_tmpS"], writes=["s_Ssm"])
        P.dve(lambda e: e.tensor_tensor(out=Tv_, in0=bj(svv[:, 2, :]), in1=bi(svv[:, 1, :]), op=ALU.mult), reads=["s_sv2", "s_sv1"], writes=["s_tmpS"])
        P.dve(lambda e: e.tensor_tensor(out=Sv_, in0=Sv_, in1=Tv_, op=ALU.add), reads=["s_Ssm", "s_tmpS"], writes=["s_Ssm"])
        P.dve(lambda e: e.tensor_tensor(out=Tv_, in0=bj(SHv[:, 3, :]), in1=bi(SHv[:, 2, :]), op=ALU.mult), reads=["s_SH"], writes=["s_tmpS"])
        P.dve(lambda e: e.tensor_tensor(out=Sv_, in0=Sv_, in1=Tv_, op=ALU.add), reads=["s_Ssm", "s_tmpS"], writes=["s_Ssm"])
        for h_ in range(4):
            P.dma(lambda e, h_=h_: e.dma_start(out=sws[l][:, h_], in_=Sv_[16 * h_:16 * h_ + 16]), reads=["s_Ssm"], writes=["sws"],
                  eng=("sp" if h_ % 2 == 0 else "act"), semkey=f"s_Ss{h_}")
        P.dve(lambda e: e.tensor_tensor(out=Tv_, in0=Sv_, in1=bi(SHv[:, 0, :]), op=ALU.mult), reads=["s_Ssm", "s_SH"], writes=["s_tmpS"])
        P.dve(lambda e: e.tensor_reduce(out=svv[:, 3, :], in_=Tv_, axis=AX.X, op=ALU.add), reads=["s_tmpS"], writes=["s_sv3"])
        g2v = g2[0]
        P.dve(lambda e: e.tensor_reduce(out=g2v[:, 0:1], in_=svv[:, 3, :], axis=AX.X, op=ALU.add), reads=["s_sv3"], writes=["s_g2"])
        P.dve(lambda e: e.tensor_scalar_mul(out=g2v[:, 0:1], in0=g2v[:, 0:1], scalar1=-1.0 / 64), reads=["s_g2"], writes=["s_g2"])
        P.dve(lambda e: e.tensor_scalar_add(out=svv[:, 3, :], in0=svv[:, 3, :], scalar1=g2v[:, 0:1]), reads=["s_sv3", "s_g2"], writes=["s_sv3"])
        P.dve(lambda e: e.tensor_tensor(out=svv[:, 4, :], in0=svv[:, 3, :], in1=svv[:, 3, :], op=ALU.mult), reads=["s_sv3"], writes=["s_sv4"])
        P.dve(lambda e: e.tensor_reduce(out=g2v[:, 1:2], in_=svv[:, 4, :], axis=AX.X, op=ALU.add), reads=["s_sv4"], writes=["s_g2"])
        P.act(lambda e: e.activation(out=g2v[:, 2:3], in_=g2v[:, 1:2], func=AF.Sqrt, scale=1.0 / 64, bias=epsc[0:64, 1:2]),
              reads=["s_g2", "epsc"], writes=["s_g2"])
        P.dve(lambda e: e.reciprocal(out=g2v[:, 3:4], in_=g2v[:, 2:3]), reads=["s_g2"], writes=["s_g2"])
        P.dve(lambda e: e.tensor_scalar_mul(out=svv[:, 3, :], in0=svv[:, 3, :], scalar1=g2v[:, 3:4]), reads=["s_sv3", "s_g2"], writes=["s_sv3"])
        P.dve(lambda e: e.tensor_tensor(out=svv[:, 3, :], in0=svv[:, 3, :], in1=shpt[0][:, 0:64], op=ALU.mult), reads=["s_sv3", "s_shpt"], writes=["s_sv3"])
        P.dve(lambda e: e.tensor_tensor(out=svv[:, 3, :], in0=svv[:, 3, :], in1=shpt[0][:, 64:128], op=ALU.add), reads=["s_sv3", "s_shpt"], writes=["s_sv3"])
        P.dve(lambda e: e.tensor_tensor(out=svv[:, 4, :], in0=SHv[:, 0, :], in1=SHv[:, 2, :], op=ALU.mult), reads=["s_SH", "s_g2"], writes=["s_sv4"])
        P.dve(lambda e: e.tensor_tensor(out=svv[:, 4, :], in0=svv[:, 4, :], in1=shpt[0][:, 128:192], op=ALU.mult), reads=["s_sv4", "s_shpt"], writes=["s_sv4"])
        P.dve(lambda e: e.tensor_reduce(out=g2v[:, 4:5], in_=svv[:, 4, :], axis=AX.X, op=ALU.add), reads=["s_sv4"], writes=["s_g2"])
        P.dve(lambda e: e.tensor_scalar_mul(out=svv[:, 5, :], in0=SHv[:, 3, :], scalar1=g2v[:, 4:5]), reads=["s_SH", "s_g2"], writes=["s_sv5"])
        P.dve(lambda e: e.tensor_tensor(out=svv[:, 3, :], in0=svv[:, 3, :], in1=svv[:, 5, :], op=ALU.add), reads=["s_sv3", "s_sv5"], writes=["s_sv3"])
        P.dve(lambda e: e.tensor_tensor(out=zsv[:, 1, :], in0=svv[:, 3, :], in1=SHv[:, 8, :], op=ALU.mult), reads=["s_sv3", "s_SH"], writes=["s_zs"])
        zst = c7.f32([16, 2, 4, 64])
        zTs = c7.bf([128, 4, NS])
        obS = c7.f32([16, D])
        P.dma(lambda e: e.dma_start(out=zs_d[l].ap().rearrange("h s k d -> (h s) k d"), in_=zsv), reads=["s_zs"], writes=["zs_d"], semkey="zs_d")
        for k_ in range(2):
            P.dma(lambda e, k_=k_: e.dma_start(out=v(zst)[:, k_, :, :], in_=zs_d[l].ap()[:, :, k_, :].rearrange("h s d -> s h d")), reads=["zs_d"], writes=["s_zst"], semkey="zst")
        zstv = v(zst)
        for k_ in range(2):
            for pr in range(2):
                i_ = k_ * 2 + pr
                P.pe(lambda e, k_=k_, pr=pr, i_=i_: e.transpose(out=psum[6][:, i_ * NS:(i_ + 1) * NS],
                                                                in_=zstv[:, k_, 2 * pr:2 * pr + 2, :].rearrange("s h d -> s (h d)"),
                                                                identity=cs["identf"][0:NS, 0:NS]), reads=["s_zst", "k_identf"], writes=["ps6"])
        P.act(lambda e: e.activation(out=zTs[0], in_=psum[6][:, 0:4 * NS], func=AF.Copy), reads=["ps6"], writes=["s_zTs", "ps6"])
        for cgp in range(4):
            for i_ in range(4):
                P.pe(lambda e, cgp=cgp, i_=i_: e.matmul(psum[cgp][0:NS, 0:512], lhsT=v(zTs)[:, i_, :], rhs=Wo[:, i_, cgp * 512:(cgp + 1) * 512],
                                                        start=(i_ == 0), stop=(i_ == 3)), reads=["s_zTs"], writes=[f"ps{cgp}"])
            P.dve(lambda e, cgp=cgp: e.tensor_copy(out=obS[0][:, cgp * 512:(cgp + 1) * 512], in_=psum[cgp][0:NS, 0:512]),
                  reads=[f"ps{cgp}"], writes=["s_obS", f"ps{cgp}"])
        P.dma(lambda e: e.dma_start(out=rss_in[l].ap(), in_=obS[0]), reads=["s_obS"], writes=[f"rss_in{l}"], semkey="obS")
        P.op("pool", lambda e: e.collective_compute("ReduceScatter", ALU.add, replica_groups=RG,
                                                     ins=[rss_in[l].ap().opt()], outs=[rss_out[l].ap().opt()]),
             reads=[f"rss_in{l}"], writes=[f"rss_out{l}"], cc=True, semkey="cc")

    def reduce_scatter(l):
        for q in range(2):
            P.op("pool", lambda e, q=q: e.collective_compute("ReduceScatter", ALU.add, replica_groups=RG,
                                                          ins=[rs_in[l][q].ap().opt()], outs=[rs_out[l][q].ap().opt()], dma_qos="P3"),
                 reads=[f"rs_in{l}"], writes=[f"rs_out{l}"], cc=True, semkey="cc", nb=True)

    def phase_O(l):
        nonlocal NOFF
        c4 = Carve()
        gateB = c4.f32([128, D])
        fgB = c4.f32([128, D]) if l == NL - 1 else None
        ot = [c4.f32([128, D]) for _ in range(2)]
        xo = [c4.f32([128, D]) for _ in range(2)]
        fs = c4.f32([128, 2])
        NOFF = c4.off
        P.dma(lambda e: e.dma_start(out=gateB[0][:, 0:512],
                                    in_=ago_mod.ap()[2 * 17:2 * 17 + 1, l * 1536 + 1024:(l + 1) * 1536].partition_broadcast(128)[:, 0, :]),
              reads=["ago_mod"], writes=["gateB"], semkey="gateB")
        P.dma(lambda e: e.dma_start(out=gateB[0][:, 512:2048],
                                    in_=ago_mod.ap()[3 * 17:3 * 17 + 1, l * 1536:(l + 1) * 1536].partition_broadcast(128)[:, 0, :]),
              reads=["ago_mod"], writes=["gateB"], semkey="gateB")
        if l == NL - 1:
            P.dma(lambda e: e.dma_start(out=fgB[0], in_=fg_row[0:1, :].partition_broadcast(128)[:, 0, :]), writes=["fgB"])
        xsrc = x_own if l == 0 else x1_d.ap()

        def o_loads(t):
            s = t % 2
            for q in range(2):
                P.dma(lambda e, t=t, s=s, q=q: e.dma_start(out=ot[s][0][:, q * 1024:(q + 1) * 1024], in_=rs_out[l][q].ap()[t * 128:(t + 1) * 128, :]),
                      reads=[f"rs_out{l}"], writes=[f"ot{s}"], semkey=f"ot{s}")
            P.dma(lambda e, t=t, s=s: e.dma_start(out=xo[s][0], in_=xsrc[t * 128:(t + 1) * 128, :]),
                  reads=(["x1_d"] if l > 0 else []), writes=[f"xo{s}"], semkey=f"xo{s}")

        o_loads(0)
        for t in range(8):
            s = t % 2
            if t + 1 < 8:
                o_loads(t + 1)
            P.dve(lambda e, s=s: e.tensor_tensor(out=ot[s][0], in0=ot[s][0], in1=gateB[0], op=ALU.mult), reads=[f"ot{s}", "gateB"], writes=[f"ot{s}"])
            P.dve(lambda e, s=s: e.tensor_tensor(out=xo[s][0], in0=xo[s][0], in1=ot[s][0], op=ALU.add), reads=[f"ot{s}", f"xo{s}"], writes=[f"xo{s}"])
            if l < NL - 1:
                P.dma(lambda e, t=t, s=s: e.dma_start(out=x1_d.ap()[t * 128:(t + 1) * 128, :], in_=xo[s][0]), reads=[f"xo{s}"], writes=["x1_d"],
                      semkey=f"x1st{s}")
                phase_N_tile(l + 1, t, xo[s][0], f"xo{s}")
            elif 'fin' not in os.environ.get('MK_SKIP', ''):
                P.act(lambda e, s=s: e.activation(out=ot[s][0], in_=xo[s][0], func=AF.Square, accum_out=fs[0][:, 0:1]),
                      reads=[f"xo{s}"], writes=[f"ot{s}", "fs"])
                P.act(lambda e: e.activation(out=fs[0][:, 1:2], in_=fs[0][:, 0:1], func=AF.Sqrt, scale=1.0 / D, bias=epsc[:, 0:1]),
                      reads=["fs", "epsc"], writes=["fs"])
                P.dve(lambda e: e.reciprocal(out=fs[0][:, 1:2], in_=fs[0][:, 1:2]), reads=["fs"], writes=["fs"])
                P.act(lambda e, s=s: e.activation(out=ot[s][0], in_=xo[s][0], func=AF.Copy, scale=fs[0][:, 1:2]),
                      reads=[f"xo{s}", "fs"], writes=[f"ot{s}"])
                P.dve(lambda e, s=s: e.tensor_tensor(out=ot[s][0], in0=ot[s][0], in1=fgB[0], op=ALU.mult), reads=[f"ot{s}", "fgB"], writes=[f"ot{s}"])
                P.dma(lambda e, t=t, s=s: e.dma_start(out=y_own[t * 128:(t + 1) * 128, :], in_=ot[s][0]), reads=[f"ot{s}"], writes=["y_own"],
                      semkey=f"yst{s}")


        c8 = Carve()
        c8.off = NOFF + (6200 if l < NL - 1 else 100)
        osT = c8.f32([4, D])
        xsT = c8.f32([4, D])
        gS = c8.f32([4, D])
        fq = c8.f32([4, 2])
        P.dma(lambda e: e.dma_start(out=osT[0], in_=rss_out[l].ap()), reads=[f"rss_out{l}"], writes=["osT"], semkey="osT")
        xss = xs_own if l == 0 else x1_d.ap()[1024:1028, :]
        P.dma(lambda e: e.dma_start(out=xsT[0], in_=xss), reads=(["x1_d"] if l > 0 else []), writes=["xsT"], eng="act")
        P.dma(lambda e: e.dma_start(out=gS[0], in_=smod_d.ap()[l, 2]), reads=["smod_d"], writes=["gS"], eng="act")
        P.dve(lambda e: e.tensor_tensor(out=osT[0], in0=osT[0], in1=gS[0], op=ALU.mult), reads=["osT", "gS"], writes=["osT"])
        P.dve(lambda e: e.tensor_tensor(out=xsT[0], in0=xsT[0], in1=osT[0], op=ALU.add), reads=["osT", "xsT"], writes=["xsT"])
        if l < NL - 1:
            P.dma(lambda e: e.dma_start(out=x1_d.ap()[1024:1028, :], in_=xsT[0]), reads=["xsT"], writes=["x1_d"], semkey="x1s")
            phase_N_samples(l + 1, xsT[0], "xsT", c8.off)
        else:
            P.act(lambda e: e.activation(out=osT[0], in_=xsT[0], func=AF.Square, accum_out=fq[0][:, 0:1]), reads=["xsT"], writes=["osT", "fq"])
            P.act(lambda e: e.activation(out=fq[0][:, 1:2], in_=fq[0][:, 0:1], func=AF.Sqrt, scale=1.0 / D, bias=epsc[0:4, 0:1]),
                  reads=["fq", "epsc"], writes=["fq"])
            P.dve(lambda e: e.reciprocal(out=fq[0][:, 1:2], in_=fq[0][:, 1:2]), reads=["fq"], writes=["fq"])
            P.act(lambda e: e.activation(out=osT[0], in_=xsT[0], func=AF.Copy, scale=fq[0][:, 1:2]), reads=["xsT", "fq"], writes=["osT"])
            P.dve(lambda e: e.tensor_tensor(out=osT[0], in0=osT[0], in1=fgB[0][0:4, :], op=ALU.mult), reads=["osT", "fgB"], writes=["osT"])
            P.dma(lambda e: e.dma_start(out=ys_own, in_=osT[0]), reads=["osT"], writes=["ys_own"], semkey="ysst")

    NLR = int(os.environ.get("MK_NL", NL))
    for l in range(NLR):
        P.epoch = l
        phase_M(l)
        reduce_scatter(l)
        if 'smp' not in os.environ.get('MK_SKIP', ''):
            phase_M_samples(l, WBIG[:, 0:16 * WCOLS].rearrange("p (k c) -> p k c", k=16), None)
        P.barrier()
        if STOP == f"R{l}":
            break
        phase_O(l)
        if STOP == f"O{l}a":
            break
        if l < NL - 1:
            allgather_h(l + 1)
        P.barrier()
        if STOP == f"O{l}":
            break
    print(f"[mk] sbuf bytes/partition = {sb_bytes[0]}", flush=True)
    P.emit()
    return nc


def prep_inputs(inp):
    f = lambda k: np.asarray(inp[k], dtype=np.float32)
    x_prompt, x_sample = f("x_prompt"), f("x_sample")
    consts = make_consts()
    w_in_full = f("w_in")
    maps = []
    for c in range(8):
        g, r = c // 4, c % 4
        ci = col_index(r)
        m = {}
        m["x_own"] = np.ascontiguousarray(x_prompt[g, r * 1024:(r + 1) * 1024])
        m["xs_own"] = np.ascontiguousarray(x_sample[16 * g + 4 * r:16 * g + 4 * r + 4, 0])
        cmat = np.concatenate([f("c_prompt")[g:g + 1], f("c_sample")[16 * g:16 * g + 16]], 0)
        m["cT"] = np.ascontiguousarray(cmat.T.reshape(16, 128, 17).transpose(1, 0, 2))
        m["wada"] = np.ascontiguousarray(f("w_ada")[:, :, r * 1536:(r + 1) * 1536])
        m["bada"] = np.ascontiguousarray(f("b_ada")[:, r * 1536:(r + 1) * 1536])
        m["ng_col"] = np.ascontiguousarray(f("norm_g").reshape(NL, 16, 128).transpose(2, 0, 1))
        m["ng_row"] = f("norm_g")
        m["fg_row"] = f("final_g").reshape(1, D)
        m["w_in"] = np.ascontiguousarray(w_in_full[:, :, ci])
        sh_cols = ci[3 * 128:10 * 128] - R_OFF
        m["mu_col"] = np.ascontiguousarray(f("mu_shift")[:, sh_cols].reshape(NL, 7, 128).transpose(0, 2, 1))
        ss = f("state_shift")[:, 16 * g:16 * g + 16][:, :, sh_cols]
        m["prev_s"] = np.ascontiguousarray(ss.reshape(NL, 16, 7, 128).transpose(0, 3, 2, 1))
        own = (np.arange(256) + 256 * r)
        pars = [f("w0"), f("a0"), f("k_k"), f("k_a"), f("r_k").reshape(NL, 1024), f("ln_w"), f("ln_b")]
        rp = np.stack([p_[:, own].reshape(NL, 2, 128) for p_ in pars], 2)
        m["rpar"] = np.ascontiguousarray(rp.transpose(0, 3, 2, 1))
        lw = np.concatenate([f("w_decay")[:, :, own], f("w_iclr")[:, :, own]], 1)
        m["loraw"] = np.ascontiguousarray(lw.reshape(NL, 128, 2, 128))
        sk = f("sinks")[:, 4 * r:4 * r + 4][:, [0, 2, 1, 3]]
        m["sinks_b"] = np.ascontiguousarray(np.broadcast_to(sk[:, None, :], (NL, 128, 4)))
        rows = np.concatenate([np.arange(256 * r, 256 * r + 256), np.arange(1024 + 256 * r, 1024 + 256 * r + 256)])
        m["w_out"] = np.ascontiguousarray(f("w_out")[:, rows, :])
        m["ck_in"] = np.ascontiguousarray(f("cache_k")[:, 16 * g:16 * g + 16, :, r, :])
        m["cv_in"] = np.ascontiguousarray(f("cache_v")[:, 16 * g:16 * g + 16, :, r, :])
        m["st_in"] = np.ascontiguousarray(f("state_wkv")[:, 16 * g:16 * g + 16, 4 * r:4 * r + 4])
        sel = np.zeros((16, 4), np.float32)
        for si in range(4):
            sel[4 * r + si, si] = 1.0
        m["selT"] = sel
        hsel = np.arange(4) + 4 * r
        lw_ = f("ln_w").reshape(NL, 16, 64)[:, hsel]
        lb_ = f("ln_b").reshape(NL, 16, 64)[:, hsel]
        rk_ = f("r_k")[:, hsel]
        sk_ = f("sinks")[:, hsel][:, :, None]
        shp_ = np.concatenate([lw_, lb_, rk_, sk_], 2)
        m["shpar"] = np.ascontiguousarray(np.broadcast_to(shp_[:, :, None], (NL, 4, 16, 193)).reshape(NL, 64, 193))
        for k, v_ in consts.items():
            m["c_" + k] = v_
        maps.append(m)
    return consts, maps


def assemble(res):
    y_prompt = np.zeros((2, SEQ, D), np.float32)
    y_sample = np.zeros((32, 1, D), np.float32)
    ckp = np.zeros((NL, 2, 128, 4, 64), np.float32)
    cvp = np.zeros_like(ckp)
    swp = np.zeros((NL, 2, 16, 64, 64), np.float32)
    shp = np.zeros((NL, 2, 3200), np.float32)
    cks = np.zeros((NL, 32, 128, 4, 64), np.float32)
    cvs = np.zeros_like(cks)
    sws = np.zeros((NL, 32, 16, 64, 64), np.float32)
    shs = np.zeros((NL, 32, 3200), np.float32)
    for c in range(8):
        g, r = c // 4, c % 4
        o = res[c]
        ci = col_index(r)
        sh_cols = ci[3 * 128:10 * 128] - R_OFF
        y_prompt[g, r * 1024:(r + 1) * 1024] = o["y_own"]
        y_sample[16 * g + 4 * r:16 * g + 4 * r + 4, 0] = o["ys_own"]
        ckp[:, g, :, r, :] = o["ckp"]
        cvp[:, g, :, r, :] = o["cvp"]
        t = o["swp"].reshape(NL, 2, 64, 2, 64)
        swp[:, g, 4 * r:4 * r + 4] = t.transpose(0, 3, 1, 4, 2).reshape(NL, 4, 64, 64)
        shp[:, g, sh_cols] = o["shp"].transpose(0, 2, 1).reshape(NL, 896)
        cks[:, 16 * g:16 * g + 16, :, r, :] = o["cks"]
        cvs[:, 16 * g:16 * g + 16, :, r, :] = o["cvs"]
        sws[:, 16 * g:16 * g + 16, 4 * r:4 * r + 4] = o["sws"]
        shs[:, 16 * g:16 * g + 16][:, :, sh_cols] = o["shs"].transpose(0, 3, 2, 1).reshape(NL, 16, 896)
    return (y_prompt, y_sample, ckp, cvp, swp, shp, cks, cvs, sws, shs)


def kernel(**inputs):
    consts, maps = prep_inputs(inputs)
    nc = build(consts)
    res = run_bass_kernel_spmd(nc, maps, core_ids=list(range(8)))
    global LAST_RES
    LAST_RES = res.results
    return assemble(res.results)
```

```python
import contextlib
import numpy as np
import concourse.bass as bass
import concourse.mybir as mybir
from concourse.bass_utils import run_bass_kernel_spmd

F32 = mybir.dt.float32
BF16 = mybir.dt.bfloat16
AF = mybir.ActivationFunctionType
ALU = mybir.AluOpType
AX = mybir.AxisListType

D = 2048
SEQ = 4096
NL = 2
HD = 64
WIN = 128
Q_OFF = 0
KA_OFF = 1024
VA_OFF = 1280
R_OFF = 1536
KR_OFF = 2560
VR_OFF = 3584
WD_OFF = 4608
AD_OFF = 4672
GA_OFF = 4736
GR_OFF = 5760
NCT = 14
WCOLS = NCT * 128 + 64
G = 256
NG = SEQ // G
CH = 64
CDEC = 0.6065306597126334
NEG = -30000.0
HC = 1088
ZC = 4 * SEQ + 64

ENGS = ("pe", "act", "dve", "pool", "sp")


class Op:
    __slots__ = ("eng", "fn", "deps", "signal", "sigval", "dma", "semkey", "cc", "epoch", "nb")

    def __init__(self, eng, fn, dma=False, semkey=None, cc=False):
        self.eng = eng
        self.fn = fn
        self.deps = []
        self.signal = False
        self.sigval = 0
        self.dma = dma
        self.semkey = semkey
        self.cc = cc
        self.epoch = 0
        self.nb = False


class Prog:
    def __init__(self, nc):
        self.nc = nc
        self.ops = {e: [] for e in ENGS}
        self.last_w = {}
        self.readers = {}
        self.all_ops = []
        self.last_dma = {}
        self.epoch = 0

    def op(self, eng, fn, reads=(), writes=(), dma=False, cc=False, semkey=None, nb=False):
        o = Op(eng, fn, dma=dma or cc, semkey=semkey, cc=cc)
        o.nb = nb
        o.epoch = self.epoch if eng == "pe" else 0
        deps = []
        for k in reads:
            w = self.last_w.get(k)
            if w is not None:
                deps.append(w)
        for k in writes:
            w = self.last_w.get(k)
            if w is not None:
                deps.append(w)
            deps.extend(self.readers.get(k, ()))
        seen = set()
        for d in deps:
            if id(d) in seen or d is o:
                continue
            seen.add(id(d))
            if (not d.dma) and d.eng == "pe" and eng == "pe" and not o.dma:
                continue
            o.deps.append(d)
            d.signal = True
        implied = set()
        for d2 in o.deps:
            for x_ in d2.deps:
                implied.add(id(x_))
        if implied:
            o.deps = [d for d in o.deps if id(d) not in implied]
        for k in reads:
            self.readers.setdefault(k, []).append(o)
        for k in writes:
            self.last_w[k] = o
            self.readers[k] = []
        if o.dma:
            if o.semkey is None:
                o.semkey = ("w",) + tuple(writes)
            HOT = ("hT", "Win", "ob", "ot", "xo", "x1st", "hTt_st", "yst", "xt", "wst", "Wo")
            if (not o.cc) and isinstance(o.semkey, str) and o.semkey.rstrip("0123456789_") in HOT:
                pass
            elif not o.cc:
                import zlib
                if eng == "pool":
                    o.semkey = ("dmapool_sw", zlib.crc32(repr(o.semkey).encode()) % 12)
                else:
                    o.semkey = ("dmapool_hw", zlib.crc32(repr(o.semkey).encode()) % 48)
            prev = self.last_dma.get(o.semkey)
            if prev is not None and all(prev is not d for d in o.deps):
                o.deps.append(prev)
                prev.signal = True
            self.last_dma[o.semkey] = o
        self.ops[eng].append(o)
        self.all_ops.append(o)
        return o

    def pe(self, fn, reads=(), writes=()):
        return self.op("pe", fn, reads, writes)

    def act(self, fn, reads=(), writes=()):
        return self.op("act", fn, reads, writes)

    def dve(self, fn, reads=(), writes=()):
        return self.op("dve", fn, reads, writes)

    def pool(self, fn, reads=(), writes=()):
        return self.op("pool", fn, reads, writes)

    def dma(self, fn, reads=(), writes=(), eng="sp", semkey=None):
        return self.op(eng, fn, reads, writes, dma=True, semkey=semkey)

    def barrier(self):
        lasts = []
        for e in ENGS:
            for o_ in reversed(self.ops[e]):
                if o_.fn is not None and not o_.nb:
                    lasts.append(o_)
                    break
        pend = [o for o in self.all_ops if o.dma and not o.signal and not o.nb]
        keep = {k: w for k, w in self.last_w.items() if w.nb}
        self.last_w = keep
        self.readers = {}
        for e in ENGS:
            o = Op(e, None)
            for d in lasts + pend:
                o.deps.append(d)
                d.signal = True
            self.ops[e].append(o)
            self.all_ops.append(o)

    def emit(self):
        nc = self.nc
        fin = Op("sp", None)
        for o in self.all_ops:
            if o.dma and not o.signal:
                fin.deps.append(o)
                o.signal = True
        cnt = {}
        keys = []
        for o in self.all_ops:
            if not o.signal:
                continue
            key = o.semkey if o.dma else ("eng", o.eng, o.epoch)
            if key not in cnt:
                cnt[key] = 0
                keys.append(key)
            cnt[key] += (16 if (o.dma and not o.cc) else 1)
            o.sigval = cnt[key]
        print("[mk] ops: " + ", ".join(f"{e}={len(self.ops[e])}" for e in ENGS) +
              f"; sems={len(cnt)}; maxval={max(cnt.values()) if cnt else 0}", flush=True)
        with contextlib.ExitStack() as st:
            st.enter_context(nc.allow_non_contiguous_dma(reason="small strided layout transfers"))
            sems = {}
            for i, key in enumerate(keys):
                sems[key] = st.enter_context(nc.semaphore(f"s{i}"))
            block = st.enter_context(nc.Block())

            def run(engname, eng):
                waited = {}
                lst = list(self.ops[engname])
                if engname == "sp":
                    lst = lst + [fin]
                for o in lst:
                    need = {}
                    for d in o.deps:
                        key = d.semkey if d.dma else ("eng", d.eng, d.epoch)
                        if d.sigval > need.get(key, 0):
                            need[key] = d.sigval
                    for key, v in need.items():
                        if waited.get(key, 0) >= v:
                            continue
                        eng.wait_ge(sems[key], v)
                        waited[key] = v
                    if o.fn is None:
                        continue
                    inst = o.fn(eng)
                    if o.signal:
                        key = o.semkey if o.dma else ("eng", o.eng, o.epoch)
                        inst.then_inc(sems[key], 16 if (o.dma and not o.cc) else 1)

            @block.sync
            def _(e):
                run("sp", e)

            @block.scalar
            def _(e):
                run("act", e)

            @block.vector
            def _(e):
                run("dve", e)

            @block.gpsimd
            def _(e):
                run("pool", e)

            @block.tensor
            def _(e):
                run("pe", e)


def col_index(r):
    cols = []
    for i in range(2):
        for hh in range(2):
            cols += list(range(Q_OFF + (4 * r + 2 * i + hh) * 64, Q_OFF + (4 * r + 2 * i + hh + 1) * 64))
    kc = list(range(KA_OFF + r * 64, KA_OFF + (r + 1) * 64))
    cols += kc + kc
    for off in (R_OFF, KR_OFF, VR_OFF):
        for i in range(2):
            cols += list(range(off + (4 * r + 2 * i) * 64, off + (4 * r + 2 * i + 2) * 64))
    cols += list(range(WD_OFF, WD_OFF + 64)) + list(range(AD_OFF, AD_OFF + 64))
    for off in (GA_OFF, GR_OFF):
        for i in range(2):
            cols += list(range(off + (4 * r + 2 * i) * 64, off + (4 * r + 2 * i + 2) * 64))
    cols += list(range(VA_OFF + r * 64, VA_OFF + (r + 1) * 64))
    assert len(cols) == WCOLS
    return np.array(cols)


def wout_row_perm():
    rows = []
    for r in range(4):
        rows += list(range(256 * r, 256 * r + 256))
        rows += list(range(1024 + 256 * r, 1024 + 256 * r + 256))
    return np.array(rows)


def make_consts():
    import ml_dtypes
    bf = ml_dtypes.bfloat16
    c = {}
    p = np.arange(128)
    d = p % 64
    inv_freq = (np.float32(500000.0) ** (-np.arange(8, dtype=np.float32) * np.float32(0.125))).astype(np.float32)
    pos = np.arange(SEQ, dtype=np.float32)
    ang = (pos[None, :] * inv_freq[(d % 8)][:, None]).astype(np.float32)
    rot = (d < 16)[:, None]
    c["cosT"] = np.where(rot, np.cos(ang), 1.0).astype(np.float32)
    c["sinT"] = np.where(rot, np.sin(ang), 0.0).astype(np.float32)
    angs = (np.float32(16384.0) * inv_freq[(d % 8)]).astype(np.float32)
    cs = np.stack([np.where(d < 16, np.cos(angs), 1.0), np.where(d < 16, np.sin(angs), 0.0)], 1)
    c["cs_s"] = cs.astype(np.float32)
    prot = np.zeros((128, 128), np.float32)
    for m in range(128):
        dm = m % 64
        if dm < 8:
            prot[m + 8, m] = -1.0
        elif dm < 16:
            prot[m - 8, m] = 1.0
    c["prot"] = prot.astype(bf)
    c["ident"] = np.eye(128, dtype=np.float32).astype(bf)
    c["identf"] = np.eye(128, dtype=np.float32)
    qi = np.arange(128)[:, None]
    kj = np.arange(256)[None, :]
    valid = (kj >= qi) & (kj <= qi + 128)
    c["maskb"] = np.where(valid, 0.0, NEG).astype(np.float32)
    c["maskb0"] = np.where(valid & (kj >= 128), 0.0, NEG).astype(np.float32)
    row = (np.arange(128) % 64)[:, None]
    col = np.arange(64)[None, :]
    strict = (row < col).astype(np.float32)
    incl = (row <= col).astype(np.float32)
    low = (row > col).astype(np.float32)
    c["maskG"] = np.concatenate([strict, incl, strict, incl, low], 1).astype(np.float32)
    c["id64"] = (row == col).astype(np.float32)
    c["bones"] = ((p[:, None] // 64) == (p[None, :] // 64)).astype(np.float32).astype(bf)
    c["ones"] = np.ones((128, 64), np.float32)
    return c


CONST_DT = {"cosT": F32, "sinT": F32, "cs_s": F32, "prot": BF16, "ident": BF16, "identf": F32,
            "maskb": F32, "maskb0": F32, "maskG": F32, "id64": F32, "bones": BF16, "ones": F32}
NPAR = 7


def build(consts, stop_after=None):
    nc = bass.Bass("TRN2", target_bir_lowering=False)
    P = Prog(nc)

    def din(name, shape, dt=F32):
        return nc.dram_tensor(name, list(shape), dt, kind="ExternalInput").ap()

    def dout(name, shape, dt=F32):
        return nc.dram_tensor(name, list(shape), dt, kind="ExternalOutput").ap()

    def dscr(name, shape, dt=F32):
        return nc.dram_tensor(name, list(shape), dt)

    sb_bytes = [0]

    def sb(name, shape, dt=F32):
        n = 1
        for s in shape[1:]:
            n *= s
        sb_bytes[0] += n * (4 if dt == F32 else 2)
        return nc.alloc_sbuf_tensor(name, list(shape), dt)

    x_own = din("x_own", [1024, D])
    xs_own = din("xs_own", [4, D])
    cT_in = din("cT", [128, 16, 17])
    wada = din("wada", [NL, D, 1536])
    bada = din("bada", [NL, 1536])
    ng_col = din("ng_col", [128, NL, 16])
    ng_row = din("ng_row", [NL, D])
    fg_row = din("fg_row", [1, D])
    w_in = din("w_in", [NL, D, WCOLS])
    mu_col = din("mu_col", [NL, 128, 7])
    prev_s = din("prev_s", [NL, 128, 7, 16])
    rpar = din("rpar", [NL, 128, NPAR, 2])
    loraw = din("loraw", [NL, 128, 2, 128])
    sinks_b = din("sinks_b", [NL, 128, 4])
    w_out = din("w_out", [NL, 512, D])
    ck_in = din("ck_in", [NL, 16, 128, 64])
    cv_in = din("cv_in", [NL, 16, 128, 64])
    st_in = din("st_in", [NL, 16, 4, 64, 64])
    selT_in = din("selT", [16, 4])
    shpar_in = din("shpar", [NL, 64, 193])
    cd = {k: din("c_" + k, v.shape, CONST_DT[k]) for k, v in consts.items()}
    y_own = dout("y_own", [1024, D])
    ys_own = dout("ys_own", [4, D])
    ckp = dout("ckp", [NL, 128, 64])
    cvp = dout("cvp", [NL, 128, 64])
    swp = dout("swp", [NL, 128, 2, 64])
    shp = dout("shp", [NL, 128, 7])
    cks = dout("cks", [NL, 16, 128, 64])
    cvs = dout("cvs", [NL, 16, 128, 64])
    sws = dout("sws", [NL, 16, 4, 64, 64])
    shs = dout("shs", [NL, 128, 7, 16])
    agi_mod = dscr("agi_mod", [17, NL * 1536])
    ago_mod = dscr("ago_mod", [4 * 17, NL * 1536])
    agi_h = [[dscr(f"agi_h{l}_{j}", [128, 2048], BF16) for j in range(8)] for l in range(NL)]
    agi_hs = [dscr(f"agi_hs{l}", [128, 64], BF16) for l in range(NL)]
    ago_hs = [dscr(f"ago_hs{l}", [512, 64], BF16) for l in range(NL)]
    ago_h = [[dscr(f"ago_h{l}_{j}", [4 * 128, 2048], BF16) for j in range(8)] for l in range(NL)]
    agi_z = [dscr(f"agi_z{l}", [128, ZC], BF16) for l in range(NL)]
    ago_z = [dscr(f"ago_z{l}", [4 * 128, ZC], BF16) for l in range(NL)]
    x1_d = dscr("x1_d", [1028, D])
    smod_d = dscr("smod_d", [NL, 3, 4, D])
    smp_d = [dscr(f"smp_d{l}", [4, 16, 9, 64]) for l in range(NL)]
    kn_d = [dscr(f"kn_d{l}", [16, 64]) for l in range(NL)]
    vn_d = [dscr(f"vn_d{l}", [16, 64]) for l in range(NL)]
    zs_d = [dscr(f"zs_d{l}", [4, 16, 2, 64]) for l in range(NL)]
    rs_in = [[dscr(f"rs_in{l}_{q}", [4 * 1024, 1024]) for q in range(2)] for l in range(NL)]
    rss_in = [dscr(f"rss_in{l}", [16, D]) for l in range(NL)]
    rss_out = [dscr(f"rss_out{l}", [4, D]) for l in range(NL)]
    rs_out = [[dscr(f"rs_out{l}_{q}", [1024, 1024]) for q in range(2)] for l in range(NL)]
    RG = [[0, 1, 2, 3], [4, 5, 6, 7]]

    cs = {}
    for k, v in consts.items():
        if k in ("cosT", "sinT"):
            continue
        cs[k] = sb("k_" + k, v.shape, CONST_DT[k])
        P.dma(lambda e, k=k: e.dma_start(out=cs[k][:], in_=cd[k]), writes=["k_" + k])
    WBIG = sb("WBIG", [128, 16 * 2048], BF16)
    SCR_N = 30000
    SCR = sb("SCR", [128, SCR_N], F32)
    modT = sb("modT", [128, NL, 48])
    ngc = sb("ngc", [128, NL, 16])
    Acol = sb("Acol", [128, NL, 16])
    P.dma(lambda e: e.dma_start(out=ngc[:], in_=ng_col), writes=["ngc"])
    epsc = sb("epsc", [128, 2])
    P.pool(lambda e: e.memset(epsc[:, 0:1], 1e-5), writes=["epsc"])
    P.pool(lambda e: e.memset(epsc[:, 1:2], 64e-5), writes=["epsc"])
    Wo = sb("Wo", [128, 4, D], BF16)
    psum = [nc.alloc_psum_tensor(f"ps{i}", [128, 512], F32) for i in range(8)]

    class Carve:
        def __init__(self):
            self.off = 0

        def f32(self, shape):
            n = int(np.prod(shape[1:]))
            ap = SCR[0:shape[0], self.off:self.off + n]
            self.off += n
            assert self.off <= SCR_N, self.off
            return ap, shape

        def bf(self, shape):
            n = int(np.prod(shape[1:]))
            w = (n + 1) // 2
            ap = SCR[0:shape[0], self.off:self.off + w].bitcast(BF16)[:, 0:n]
            self.off += w
            assert self.off <= SCR_N, self.off
            return ap, shape

    def v(t, pat=None, **kw):
        ap, shape = t
        if len(shape) == 2:
            return ap
        names = " ".join(f"d{i}" for i in range(1, len(shape)))
        kws = {f"d{i}": shape[i] for i in range(1, len(shape))}
        return ap.rearrange(f"p ({names}) -> p {names}", **kws)

    cv_ = Carve()
    cT = cv_.f32([128, 16, 17])
    wst = [cv_.f32([128, 1536]) for _ in range(4)]
    badab = cv_.f32([17, NL, 1536])
    modsb = cv_.f32([17, NL, 1536])
    P.dma(lambda e: e.dma_start(out=v(cT), in_=cT_in), writes=["cT"])
    P.act(lambda e: e.activation(out=cT[0], in_=cT[0], func=AF.Silu), reads=["cT"], writes=["cT"])
    for l in range(NL):
        P.dma(lambda e, l=l: e.dma_start(out=v(badab)[:, l, :], in_=bada[l:l + 1, :].partition_broadcast(17)[:, 0, :]),
              writes=["badab"], eng="act", semkey="badab")
    it = 0
    for l in range(NL):
        for k in range(16):
            s = it % 4
            it += 1
            P.dma(lambda e, l=l, k=k, s=s: e.dma_start(out=wst[s][0], in_=wada[l, k * 128:(k + 1) * 128, :]),
                  writes=[f"wst{s}"], eng=("sp" if s % 2 == 0 else "act"), semkey=f"wst{s}")
            for cg in range(3):
                P.pe(lambda e, k=k, s=s, cg=cg: e.matmul(psum[cg][0:17, :], lhsT=v(cT)[:, k, :],
                                                          rhs=wst[s][0][:, cg * 512:(cg + 1) * 512],
                                                          start=(k == 0), stop=(k == 15)),
                     reads=["cT", f"wst{s}"], writes=[f"ps{cg}"])
        for cg in range(3):
            P.dve(lambda e, l=l, cg=cg: e.tensor_tensor(out=v(modsb)[:, l, cg * 512:(cg + 1) * 512], in0=psum[cg][0:17, :],
                                                        in1=v(badab)[:, l, cg * 512:(cg + 1) * 512], op=ALU.add),
                  reads=[f"ps{cg}", "badab"], writes=["modsb"])
    P.dma(lambda e: e.dma_start(out=agi_mod.ap(), in_=modsb[0]), reads=["modsb"], writes=["agi_mod"])
    P.op("pool", lambda e: e.collective_compute("AllGather", ALU.bypass, replica_groups=RG,
                                                 ins=[agi_mod.ap().opt()], outs=[ago_mod.ap().opt()]),
         reads=["agi_mod"], writes=["ago_mod"], cc=True, semkey="cc")
    for l in range(NL):
        for r2 in range(4):
            P.dma(lambda e, l=l, r2=r2: e.dma_start(
                out=modT[:, l, r2 * 12:(r2 + 1) * 12],
                in_=ago_mod.ap()[r2 * 17, l * 1536:(l + 1) * 1536].rearrange("(c p) -> p c", p=128),
                allow_slow_non_contiguous=True), reads=["ago_mod"], writes=["modT"], semkey="modT")
    selT = cv_.f32([16, 4])
    smk = cv_.f32([16, D])
    smo = cv_.f32([4, D])
    P.dma(lambda e: e.dma_start(out=selT[0], in_=selT_in), writes=["selT"])
    SEG = {0: [(0, 0, 1536, 0), (1, 0, 512, 1536)], 1: [(1, 512, 1536, 0), (2, 0, 1024, 1024)], 2: [(2, 1024, 1536, 0), (3, 0, 1536, 512)]}
    for l in range(NL):
        for kind in range(3):
            for (r2, j0, j1, d0) in SEG[kind]:
                P.dma(lambda e, l=l, r2=r2, j0=j0, j1=j1, d0=d0: e.dma_start(
                    out=smk[0][:, d0:d0 + (j1 - j0)], in_=ago_mod.ap()[r2 * 17 + 1:r2 * 17 + 17, l * 1536 + j0:l * 1536 + j1]),
                    reads=["ago_mod"], writes=["smk"], semkey="smk")
            for cg in range(4):
                P.pe(lambda e, cg=cg: e.matmul(psum[cg][0:4, :], lhsT=selT[0], rhs=smk[0][:, cg * 512:(cg + 1) * 512], start=True, stop=True),
                     reads=["selT", "smk"], writes=[f"ps{cg}"])
                P.dve(lambda e, cg=cg: e.tensor_copy(out=smo[0][:, cg * 512:(cg + 1) * 512], in_=psum[cg][0:4, :]),
                      reads=[f"ps{cg}"], writes=["smo", f"ps{cg}"])
            P.dma(lambda e, l=l, kind=kind: e.dma_start(out=smod_d.ap()[l, kind], in_=smo[0]), reads=["smo"], writes=["smod_d"], semkey="smo")
    P.dve(lambda e: e.scalar_tensor_tensor(out=Acol[:], in0=modT[:, :, 16:32], scalar=1.0, in1=ngc[:],
                                           op0=ALU.add, op1=ALU.mult), reads=["modT", "ngc"], writes=["Acol"])
    P.barrier()
    import os
    STOP = os.environ.get("MK_STOP", "")
    if STOP == "A":
        P.emit()
        return nc

    def phase_N_tile(l, t, xt_ap, xkey):
        c2 = Carve()
        c2.off = NOFF
        junk = c2.f32([128, D])
        pp = t % 2
        bufs = [(c2.bf([128, D]), c2.f32([128, 2]), c2.bf([128, 16, 128])) for _ in range(2)]
        xn, ssq, hTt = bufs[pp]
        KJ, KX, KS, KH = f"junk{pp}", f"xn{pp}", f"ssq{pp}", f"hTt{pp}"
        P.act(lambda e: e.activation(out=junk[0], in_=xt_ap, func=AF.Square, accum_out=ssq[0][:, 0:1]),
              reads=[xkey], writes=[KJ, KS])
        P.act(lambda e: e.activation(out=ssq[0][:, 1:2], in_=ssq[0][:, 0:1], func=AF.Sqrt, scale=1.0 / D, bias=epsc[:, 0:1]),
              reads=[KS, "epsc"], writes=[KS])
        P.dve(lambda e: e.reciprocal(out=ssq[0][:, 1:2], in_=ssq[0][:, 1:2]), reads=[KS], writes=[KS])
        P.act(lambda e: e.activation(out=xn[0], in_=xt_ap, func=AF.Copy, scale=ssq[0][:, 1:2]),
              reads=[xkey, KS], writes=[KX])
        for half in range(2):
            pst = psum[4 + half]
            for kk in range(8):
                k = half * 8 + kk
                P.pe(lambda e, k=k, kk=kk, pst=pst: e.transpose(
                    out=pst[:].bitcast(BF16)[:, kk * 128:(kk + 1) * 128], in_=xn[0][:, k * 128:(k + 1) * 128],
                    identity=cs["ident"][:]), reads=[KX, "k_ident"], writes=[f"ps{4 + half}"])
            for kk in range(8):
                k = half * 8 + kk
                if half == 0:
                    P.act(lambda e, k=k, kk=kk, pst=pst, l=l: e.activation(
                        out=v(hTt)[:, k, :], in_=pst[:].bitcast(BF16)[:, kk * 128:(kk + 1) * 128], func=AF.Identity,
                        scale=Acol[:, l, k:k + 1], bias=modT[:, l, k:k + 1]),
                        reads=[f"ps{4 + half}", "Acol", "modT"], writes=[KH + "a"])
                else:
                    P.dve(lambda e, k=k, kk=kk, pst=pst, l=l: e.tensor_scalar(
                        out=v(hTt)[:, k, :], in0=pst[:].bitcast(BF16)[:, kk * 128:(kk + 1) * 128],
                        scalar1=Acol[:, l, k:k + 1], scalar2=modT[:, l, k:k + 1], op0=ALU.mult, op1=ALU.add),
                        reads=[f"ps{4 + half}", "Acol", "modT"], writes=[KH + "b"])
        P.dma(lambda e, l=l, t=t: e.dma_start(out=agi_h[l][t].ap().rearrange("p (k c) -> p k c", k=16), in_=v(hTt)),
              reads=[KH + "a", KH + "b"], writes=[f"agi_h{l}_{t}"], semkey=f"hTt_st{t % 2}")
        P.op("pool", lambda e, l=l, t=t: e.collective_compute("AllGather", ALU.bypass, replica_groups=RG,
                                                           ins=[agi_h[l][t].ap().opt()], outs=[ago_h[l][t].ap().opt()]),
             reads=[f"agi_h{l}_{t}"], writes=[f"ago_h{l}_{t}"], cc=True, semkey="cc", nb=True)

    def phase_N_samples(l, xs_ap, xkey, off):
        c6 = Carve()
        c6.off = off
        As = c6.f32([4, D])
        Bs = c6.f32([4, D])
        jk = c6.f32([4, D])
        hs_ = c6.bf([4, D])
        sq4 = c6.f32([4, 2])
        hTs = c6.bf([128, 16, 4])
        P.dma(lambda e: e.dma_start(out=As[0], in_=smod_d.ap()[l, 1]), reads=["smod_d"], writes=["As"], eng="act")
        P.dma(lambda e: e.dma_start(out=Bs[0], in_=smod_d.ap()[l, 0]), reads=["smod_d"], writes=["Bs"], eng="act")
        P.dma(lambda e: e.dma_start(out=jk[0], in_=ng_row[l:l + 1, :].partition_broadcast(4)[:, 0, :]), writes=["jk"], eng="act")
        P.dve(lambda e: e.scalar_tensor_tensor(out=As[0], in0=As[0], scalar=1.0, in1=jk[0], op0=ALU.add, op1=ALU.mult),
              reads=["As", "jk"], writes=["As"])
        P.act(lambda e: e.activation(out=jk[0], in_=xs_ap, func=AF.Square, accum_out=sq4[0][:, 0:1]), reads=[xkey, "As"], writes=["jk", "sq4"])
        P.act(lambda e: e.activation(out=sq4[0][:, 1:2], in_=sq4[0][:, 0:1], func=AF.Sqrt, scale=1.0 / D, bias=epsc[0:4, 0:1]),
              reads=["sq4", "epsc"], writes=["sq4"])
        P.dve(lambda e: e.reciprocal(out=sq4[0][:, 1:2], in_=sq4[0][:, 1:2]), reads=["sq4"], writes=["sq4"])
        P.act(lambda e: e.activation(out=jk[0], in_=xs_ap, func=AF.Copy, scale=sq4[0][:, 1:2]), reads=[xkey, "sq4"], writes=["jk"])
        P.dve(lambda e: e.tensor_tensor(out=jk[0], in0=jk[0], in1=As[0], op=ALU.mult), reads=["jk", "As"], writes=["jk"])
        P.dve(lambda e: e.tensor_tensor(out=hs_[0], in0=jk[0], in1=Bs[0], op=ALU.add), reads=["jk", "Bs"], writes=["hs_"])
        for k in range(16):
            P.pe(lambda e, k=k: e.transpose(out=psum[6][:].bitcast(BF16)[:, k * 4:(k + 1) * 4], in_=hs_[0][:, k * 128:(k + 1) * 128],
                                            identity=cs["ident"][0:4, 0:4]), reads=["hs_", "k_ident"], writes=["ps6"])
        P.act(lambda e: e.activation(out=hTs[0], in_=psum[6][:].bitcast(BF16)[:, 0:64], func=AF.Copy), reads=["ps6"], writes=["hTs", "ps6"])
        P.dma(lambda e: e.dma_start(out=agi_hs[l].ap().rearrange("p (k c) -> p k c", k=16), in_=v(hTs)),
              reads=["hTs"], writes=[f"agi_hs{l}"], semkey="hTs_st")
        P.op("pool", lambda e: e.collective_compute("AllGather", ALU.bypass, replica_groups=RG,
                                                     ins=[agi_hs[l].ap().opt()], outs=[ago_hs[l].ap().opt()]),
             reads=[f"agi_hs{l}"], writes=[f"ago_hs{l}"], cc=True, semkey="cc", nb=True)

    NOFF = 0
    c0 = Carve()
    xt = [c0.f32([128, D]) for _ in range(2)]
    NOFF = c0.off
    def n_load(t):
        s = t % 2
        P.dma(lambda e, t=t, s=s: e.dma_start(out=xt[s][0], in_=x_own[t * 128:(t + 1) * 128, :]),
              writes=[f"xt{s}"], semkey=f"xt{s}")

    n_load(0)
    for t in range(8):
        s = t % 2
        if t + 1 < 8:
            n_load(t + 1)
        phase_N_tile(0, t, xt[s][0], f"xt{s}")

    zpad = sb("zpad", [128, 2, 64], BF16)
    P.pool(lambda e: e.memset(zpad[:], 0.0), writes=["zpad"])

    def allgather_h(l):
        pass

    cx = Carve()
    cx.off = NOFF + 20000
    xs0 = cx.f32([4, D])
    P.dma(lambda e: e.dma_start(out=xs0[0], in_=xs_own), writes=["xs0"], eng="act")
    phase_N_samples(0, xs0[0], "xs0", NOFF + 8000)
    allgather_h(0)
    P.barrier()
    if STOP == "N":
        P.emit()
        return nc

    def phase_M(l):
        c3 = Carve()
        Win = WBIG[:, 0:16 * WCOLS].rearrange("p (k c) -> p k c", k=16)
        import os
        SKIP = os.environ.get("MK_SKIP", "")
        for k in range(16 if "win" not in SKIP else 0):
            for hf in range(4):
                P.dma(lambda e, k=k, hf=hf: e.dma_start(out=Win[:, k, hf * 464:(hf + 1) * 464],
                                                       in_=w_in[l, k * 128:(k + 1) * 128, hf * 464:(hf + 1) * 464]),
                      writes=[f"Win{k}_{hf}"], eng="pool", semkey=f"Win{(k * 4 + hf) % 4}")
        WK = [f"Win{k}_{hf}" for k in range(16) for hf in range(4)]
        hT = [c3.bf([128, 16, G]) for _ in range(2)]
        cst = [c3.f32([128, 2, G])] * 2
        cur = c3.f32([128, 7, G + 1])
        qb = c3.bf([128, 3, G])
        t1 = c3.f32([128, 3, G])
        t2 = c3.f32([128, 3, G])
        qrot = c3.bf([128, 2, G])
        krotf = c3.f32([128, G])
        kT = c3.bf([128, 128 + G])
        vb = c3.bf([128, 3, 64])
        vf = c3.f32([128, 64])
        kcf = c3.f32([128, 64])
        mu = c3.f32([128, 7])
        gsil2 = [c3.bf([128, 4, G]) for _ in range(2)]
        zT = c3.bf([128, 4, G])
        sc = c3.f32([128, 4, 256])
        yc = c3.f32([128, 8, 64])
        pb = (yc[0].bitcast(BF16)[:, 0:1024], [128, 4, 256])
        pTs = c3.bf([128, 1024])
        attb = c3.bf([128, 4, 64])
        sm = c3.f32([128, 8, 4])
        snk = c3.f32([128, 4])
        rp = c3.f32([128, NPAR, 2])
        loraW = c3.bf([128, 2, 128])
        P.dma(lambda e: e.dma_start(out=v(rp), in_=rpar[l]), writes=["rp"])
        P.dma(lambda e: e.dma_start(out=v(loraW), in_=loraw[l]), writes=["loraW"], eng="pool", semkey="loraW")
        mixed = c3.f32([128, 7, G])
        twad = c3.bf([128, G])
        sig = c3.f32([128, 2, G])
        aa = c3.f32([128, 2, G])
        Ls = c3.f32([128, 2, G])
        eL = c3.f32([128, 2, G])
        eLi = c3.f32([128, 2, G])
        eLp = c3.f32([128, 2, G])
        kkr = c3.f32([128, 2, G])
        sqb = c3.bf([128, 2, G])
        rn = c3.f32([128, 2, G])
        kk = c3.f32([128, 2, G])
        uu = c3.f32([128, 2, G])
        keff = c3.f32([128, 2, G])
        AR = c3.bf([128, 2, 4, 2, 64])
        Bt = c3.bf([128, 2, G])
        Kt = c3.bf([128, 2, G])
        vrb = c3.bf([128, 2, G])
        rkk = c3.bf([128, 2, G])
        bonus = c3.f32([128, 2, G])
        TK = c3.bf([128, 2, 4, 3, 64])
        MS = c3.bf([128, 8, 320])
        PQ = [c3.bf([128, 8, 128]) for _ in range(2)]
        TT = [c3.bf([128, 8, 64]) for _ in range(2)]
        Sst = c3.f32([128, 2, 64])
        Sb = c3.bf([128, 2, 64])
        Wb = c3.bf([128, 2, 64])
        Ub = c3.bf([128, 2, 64])
        Ybuf = c3.f32([128, 4, 2, 64])
        ysq = (sc[0][:, 0:512], [128, 8, 64])
        gs = c3.f32([128, 4, 8])
        yh = c3.bf([128, 8, 64])
        yz = rn
        ob = c3.f32([128, D])
        for i in range(4):
            P.dma(lambda e, i=i: e.dma_start(out=Wo[:, i, :], in_=w_out[l, i * 128:(i + 1) * 128, :]), writes=[f"Wo{i}"], eng="pool",
                  semkey=f"Wo{i % 2}")
        P.pool(lambda e: e.memset(Sst[0], 0.0), writes=["Sst"])
        P.pool(lambda e: e.memset(Sb[0], 0.0), writes=["Sb"])
        P.dma(lambda e: e.dma_start(out=snk[0], in_=sinks_b[l]), writes=["snk"])
        P.dma(lambda e: e.dma_start(out=mu[0], in_=mu_col[l]), writes=["mu"])
        if 'ms' not in SKIP:
            P.pool(lambda e: e.memset(cur[0], 0.0), writes=["cur"])
            P.pool(lambda e: e.memset(kT[0], 0.0), writes=["kT"])
            P.pool(lambda e: e.memset(vb[0], 0.0), writes=["vb"])
        import os
        NGR = int(os.environ.get('MK_NG', NG))
        sched = [("inproj", 0)]
        for g_ in range(NGR):
            sched.append(("att", g_))
            if g_ + 1 < NGR:
                sched.append(("inproj", g_ + 1))
            sched.append(("rw", g_))
        for (sec, g) in sched:
            s = g % 2
            r2, cg = g // 4, (g % 4) * G
            if sec == "inproj":
                for hf in range(2):
                    t_ = 2 * (g % 4) + hf
                    P.dma(lambda e, s=s, r2=r2, hf=hf, t_=t_: e.dma_start(
                        out=v(hT[s])[:, :, hf * 128:(hf + 1) * 128],
                        in_=ago_h[l][t_].ap()[r2 * 128:(r2 + 1) * 128, :].rearrange("p (k c) -> p k c", k=16)),
                        reads=[f"ago_h{l}_{t_}"], writes=[f"hT{s}"], eng=("sp" if hf == 0 else "act"), semkey=f"hT{s}_{hf}")
                P.dma(lambda e, s=s, g=g: e.dma_start(out=v(cst[s])[:, 0, :], in_=cd["cosT"][:, g * G:(g + 1) * G]),
                      writes=["cst0"], eng="act", semkey="cst0")
                P.dma(lambda e, s=s, g=g: e.dma_start(out=v(cst[s])[:, 1, :], in_=cd["sinT"][:, g * G:(g + 1) * G]),
                      writes=["cst0"], eng="act", semkey="cst0")
                for ct in range(NCT if 'mm' not in SKIP else 0):
                    pi = ct % 4
                    for k in range(16):
                        P.pe(lambda e, ct=ct, k=k, s=s, pi=pi: e.matmul(
                            psum[pi][:, 0:G], lhsT=Win[:, k, ct * 128:(ct + 1) * 128], rhs=v(hT[s])[:, k, :],
                            start=(k == 0), stop=(k == 15)), reads=WK + [f"hT{s}"], writes=[f"ps{pi}"])
                    if 'ev' in SKIP:
                        continue
                    if ct < 3:
                        P.act(lambda e, ct=ct, pi=pi: e.activation(out=v(qb)[:, ct, :], in_=psum[pi][:, 0:G], func=AF.Copy),
                              reads=[f"ps{pi}"], writes=["qb", f"ps{pi}"])
                        if 'evd' not in SKIP:
                            P.dve(lambda e, ct=ct, pi=pi, s=s: e.tensor_tensor(out=v(t2)[:, ct, :], in0=psum[pi][:, 0:G],
                                                                             in1=v(cst[s])[:, 0, :], op=ALU.mult),
                                  reads=[f"ps{pi}", "cst0"], writes=["t2", f"ps{pi}"])
                    elif ct < 10:
                        P.act(lambda e, ct=ct, pi=pi: e.activation(out=v(cur)[:, ct - 3, 1:G + 1], in_=psum[pi][:, 0:G],
                                                                    func=AF.Copy), reads=[f"ps{pi}"], writes=["cur"])
                    else:
                        P.act(lambda e, ct=ct, pi=pi, gq=gsil2[g % 2]: e.activation(out=v(gq)[:, ct - 10, :], in_=psum[pi][:, 0:G], func=AF.Silu),
                              reads=[f"ps{pi}"], writes=[f"gsil{g % 2}", f"ps{pi}"])
                for ct in range(3 if 'rope' not in SKIP else 0):
                    pi = 4 + ct % 2
                    P.pe(lambda e, ct=ct, pi=pi: e.matmul(psum[pi][:, 0:G], lhsT=cs["prot"][:], rhs=v(qb)[:, ct, :],
                                                         start=True, stop=True), reads=["qb", "k_prot"], writes=[f"ps{pi}"])
                    P.dve(lambda e, ct=ct, pi=pi, s=s: e.tensor_tensor(out=v(t1)[:, ct, :], in0=psum[pi][:, 0:G],
                                                                     in1=v(cst[s])[:, 1, :], op=ALU.mult),
                          reads=[f"ps{pi}", "cst0"], writes=["t1"])
                P.dve(lambda e: e.tensor_tensor(out=v(qrot), in0=v(t1)[:, 0:2, :], in1=v(t2)[:, 0:2, :], op=ALU.add),
                      reads=["t1", "t2"], writes=["qrot"])
                P.dve(lambda e: e.tensor_tensor(out=krotf[0], in0=v(t1)[:, 2, :], in1=v(t2)[:, 2, :], op=ALU.add),
                      reads=["t1", "t2"], writes=["krotf"])
                P.act(lambda e: e.activation(out=kT[0][:, 128:128 + G], in_=krotf[0], func=AF.Copy),
                      reads=["krotf"], writes=["kT"])
                for bl in range(G // 128 if 'vv' not in SKIP else 0):
                    for k in range(16):
                        P.pe(lambda e, k=k, s=s, bl=bl: e.matmul(
                            psum[6][:, 0:64], lhsT=v(hT[s])[:, k, bl * 128:(bl + 1) * 128], rhs=Win[:, k, NCT * 128:NCT * 128 + 64],
                            start=(k == 0), stop=(k == 15)), reads=WK + [f"hT{s}"], writes=["ps6"])
                    P.act(lambda e, bl=bl: e.activation(out=v(vb)[:, 1 + bl, :], in_=psum[6][:, 0:64], func=AF.Copy),
                          reads=["ps6"], writes=["vb", "ps6"])
                    if g == NGR - 1 and bl == G // 128 - 1:
                        P.dve(lambda e: e.tensor_copy(out=vf[0], in_=psum[6][:, 0:64]), reads=["ps6"], writes=["vf", "ps6"])
                        P.dma(lambda e: e.dma_start(out=cvp[l], in_=vf[0]), reads=["vf"], writes=["cvp"], semkey="cvp")

            if sec == "att":
                SLOT_H = [0, 2, 1, 3]
                for bl in range(G // 128 if 'att' not in SKIP else 0):
                    mk = "maskb0" if (g == 0 and bl == 0) else "maskb"
                    for slot in range(4):
                        h = SLOT_H[slot]
                        i, base = h // 2, (h % 2) * 64
                        bank = 4 + slot // 2
                        P.pe(lambda e, i=i, base=base, bank=bank, slot=slot, bl=bl: e.matmul(
                            psum[bank][:, (slot % 2) * 256:(slot % 2) * 256 + 256],
                            lhsT=v(qrot)[base:base + 64, i, bl * 128:(bl + 1) * 128],
                            rhs=kT[0][base:base + 64, bl * 128:bl * 128 + 256], start=True, stop=True),
                            reads=["qrot", "kT"], writes=[f"ps{bank}"])
                    for bk in range(2):
                        P.dve(lambda e, bk=bk, mk=mk: e.tensor_tensor(
                            out=v(sc)[:, 2 * bk:2 * bk + 2, :], in0=psum[4 + bk][:, :].rearrange("p (a b) -> p a b", a=2),
                            in1=cs[mk][:].unsqueeze(1).to_broadcast([128, 2, 256]), op=ALU.add),
                            reads=[f"ps{4 + bk}", "k_" + mk], writes=["sc", f"ps{4 + bk}"])
                    smv = v(sm)
                    P.dve(lambda e: e.tensor_reduce(out=smv[:, 0, :], in_=v(sc), axis=AX.X, op=ALU.max), reads=["sc"], writes=["sm"])
                    P.dve(lambda e: e.scalar_tensor_tensor(out=smv[:, 1, :], in0=smv[:, 0, :], scalar=0.125, in1=snk[0],
                                                           op0=ALU.mult, op1=ALU.max), reads=["sm", "snk"], writes=["sm"])
                    P.dve(lambda e: e.tensor_scalar_mul(out=smv[:, 2, :], in0=smv[:, 1, :], scalar1=-1.0), reads=["sm"], writes=["sm"])
                    P.dve(lambda e: e.tensor_tensor(out=smv[:, 4, :], in0=snk[0], in1=smv[:, 1, :], op=ALU.subtract),
                          reads=["sm", "snk"], writes=["sm"])
                    for slot in range(4):
                        P.act(lambda e, slot=slot: e.activation(out=v(pb)[:, slot, :], in_=v(sc)[:, slot, :], func=AF.Exp, scale=0.125,
                                                                bias=smv[:, 2, slot:slot + 1], accum_out=smv[:, 3, slot:slot + 1]),
                              reads=["sc", "sm"], writes=["pb", "sm"])
                    P.act(lambda e: e.activation(out=smv[:, 4, :], in_=smv[:, 4, :], func=AF.Exp), reads=["sm"], writes=["sm"])
                    P.dve(lambda e: e.tensor_tensor(out=smv[:, 5, :], in0=smv[:, 3, :], in1=smv[:, 4, :], op=ALU.add), reads=["sm"], writes=["sm"])
                    P.dve(lambda e: e.reciprocal(out=smv[:, 6, :], in_=smv[:, 5, :]), reads=["sm"], writes=["sm"])
                    for slot in range(4):
                        for hf in range(2):
                            P.pe(lambda e, slot=slot, hf=hf: e.transpose(
                                out=psum[6][:].bitcast(BF16)[:, (slot * 2 + hf) * 128:(slot * 2 + hf + 1) * 128],
                                in_=v(pb)[:, slot, hf * 128:(hf + 1) * 128], identity=cs["ident"][:]),
                                reads=["pb", "k_ident"], writes=["ps6"])
                    P.act(lambda e: e.activation(out=pTs[0], in_=psum[6][:].bitcast(BF16), func=AF.Copy),
                          reads=["ps6"], writes=["pTs", "ps6"])
                    for slot in range(4):
                        for hf in range(2):
                            P.pe(lambda e, slot=slot, hf=hf, bl=bl: e.matmul(
                                psum[7][:, slot * 64:(slot + 1) * 64], lhsT=pTs[0][:, (slot * 2 + hf) * 128:(slot * 2 + hf + 1) * 128],
                                rhs=v(vb)[:, bl + hf, :], start=(hf == 0), stop=(hf == 1)),
                                reads=["pTs", "vb"], writes=["ps7"])
                    P.dve(lambda e: e.tensor_tensor(out=v(attb), in0=psum[7][:, 0:256].rearrange("p (a b) -> p a b", a=4),
                                                    in1=smv[:, 6, :].unsqueeze(2).to_broadcast([128, 4, 64]), op=ALU.mult),
                          reads=["ps7", "sm"], writes=["attb", "ps7"])
                    for slot in range(4):
                        h = SLOT_H[slot]
                        i, base = h // 2, (h % 2) * 64
                        P.pe(lambda e, slot=slot, i=i, base=base: e.transpose(
                            out=psum[4][:].bitcast(BF16)[base:base + 64, i * 128:(i + 1) * 128],
                            in_=v(attb)[:, slot, :], identity=cs["ident"][:]),
                            reads=["attb", "k_ident"], writes=["ps4"])
                    P.dve(lambda e, bl=bl, gq=gsil2[g % 2]: e.tensor_tensor(
                        out=v(zT)[:, 0:2, bl * 128:(bl + 1) * 128],
                        in0=psum[4][:].bitcast(BF16)[:, 0:256].rearrange("p (a b) -> p a b", a=2),
                        in1=v(gq)[:, 0:2, bl * 128:(bl + 1) * 128], op=ALU.mult),
                        reads=["ps4", f"gsil{g % 2}"], writes=["zT", "ps4"])

                if 'rw' not in SKIP:
                    curn = v(cur)[:, :, 1:G + 1]
                    P.dve(lambda e: e.tensor_tensor(out=v(mixed), in0=v(cur)[:, :, 0:G], in1=curn, op=ALU.subtract),
                           reads=["cur"], writes=["mixed"])
                    P.dve(lambda e: e.tensor_tensor(out=v(mixed), in0=v(mixed), in1=mu[0].unsqueeze(2).to_broadcast([128, 7, G]), op=ALU.mult),
                          reads=["mixed", "mu"], writes=["mixed"])
                    P.dve(lambda e: e.tensor_tensor(out=v(mixed), in0=v(mixed), in1=curn, op=ALU.add), reads=["mixed", "cur"], writes=["mixed"])
                if g == NGR - 1 and 'tr' not in SKIP:
                    P.pe(lambda e: e.transpose(out=psum[7][:, 0:128], in_=krotf[0][:, G - 128:G], identity=cs["identf"][:]),
                         reads=["krotf", "k_identf"], writes=["ps7"])
                    P.dve(lambda e: e.tensor_copy(out=kcf[0], in_=psum[7][:, 0:64]), reads=["ps7"], writes=["kcf"])
                    P.dma(lambda e: e.dma_start(out=ckp[l], in_=kcf[0]), reads=["kcf"], writes=["ckp"], semkey="ckp")
                    P.dma(lambda e: e.dma_start(out=shp[l], in_=v(cur)[:, :, G]), reads=["cur"], writes=["shp"], semkey="shp")
                if 'carry' in SKIP:
                    continue
                P.dve(lambda e: e.tensor_copy(out=v(cur)[:, :, 0:1], in_=v(cur)[:, :, G:G + 1]), reads=["cur"], writes=["cur"])
                P.dve(lambda e: e.tensor_copy(out=kT[0][:, 0:128], in_=kT[0][:, G:G + 128]), reads=["kT"], writes=["kT"])
                P.dve(lambda e: e.tensor_copy(out=v(vb)[:, 0, :], in_=v(vb)[:, G // 128, :]), reads=["vb"], writes=["vb"])
            if sec == "rw":
                if 'rw' not in SKIP:
                    curn = v(cur)[:, :, 1:G + 1]
                    mx = v(mixed)
                    P.act(lambda e: e.activation(out=twad[0][0:64, :], in_=mx[0:64, 6, :], func=AF.Tanh), reads=["mixed"], writes=["twad"])
                    P.act(lambda e: e.activation(out=twad[0][64:128, :], in_=mx[64:128, 6, :], func=AF.Copy), reads=["mixed"], writes=["twad"])
                    P.act(lambda e: e.activation(out=v(vrb), in_=mx[:, 4:6, :], func=AF.Copy), reads=["mixed"], writes=["vrb"])
                    rpv = v(rp)
                    for p in range(2):
                        P.pe(lambda e, p=p: e.matmul(psum[p][:, 0:G], lhsT=v(loraW)[0:64, p, :], rhs=twad[0][0:64, :], start=True, stop=True),
                             reads=["loraW", "twad"], writes=[f"ps{p}"])
                        P.act(lambda e, p=p: e.activation(out=v(sig)[:, p, :], in_=psum[p][:, 0:G], func=AF.Sigmoid, bias=rpv[:, 0, p:p + 1]),
                              reads=[f"ps{p}", "rp"], writes=["sig", f"ps{p}"])
                        P.pe(lambda e, p=p: e.matmul(psum[2 + p][:, 0:G], lhsT=v(loraW)[64:128, p, :], rhs=twad[0][64:128, :], start=True, stop=True),
                             reads=["loraW", "twad"], writes=[f"ps{2 + p}"])
                        P.act(lambda e, p=p: e.activation(out=v(aa)[:, p, :], in_=psum[2 + p][:, 0:G], func=AF.Sigmoid, bias=rpv[:, 1, p:p + 1]),
                              reads=[f"ps{2 + p}", "rp"], writes=["aa", f"ps{2 + p}"])
                    for p in range(2):
                        for c in range(4):
                            P.dve(lambda e, p=p, c=c: e.tensor_tensor_scan(
                                out=v(Ls)[:, p, c * 64:(c + 1) * 64], data0=cs["ones"][:, 0:64], data1=v(sig)[:, p, c * 64:(c + 1) * 64],
                                initial=0.0, op0=ALU.mult, op1=ALU.add), reads=["sig", "k_ones"], writes=["Ls"])
                    P.act(lambda e: e.activation(out=v(eL), in_=v(Ls), func=AF.Exp, scale=-CDEC), reads=["Ls"], writes=["eL"])
                    P.act(lambda e: e.activation(out=v(eLi), in_=v(Ls), func=AF.Exp, scale=CDEC), reads=["Ls"], writes=["eLi"])
                    P.dve(lambda e: e.tensor_tensor(out=v(eLp), in0=v(Ls), in1=v(sig), op=ALU.subtract), reads=["Ls", "sig"], writes=["eLp"])
                    P.act(lambda e: e.activation(out=v(eLp), in_=v(eLp), func=AF.Exp, scale=-CDEC), reads=["eLp"], writes=["eLp"])
                    bc = lambda j: rpv[:, j, :].unsqueeze(2).to_broadcast([128, 2, G])
                    P.dve(lambda e: e.tensor_tensor(out=v(kkr), in0=mx[:, 2:4, :], in1=bc(2), op=ALU.mult), reads=["mixed", "rp"], writes=["kkr"])
                    P.dve(lambda e: e.tensor_tensor(out=v(sqb), in0=v(kkr), in1=v(kkr), op=ALU.mult), reads=["kkr"], writes=["sqb"])
                    for p in range(2):
                        P.pe(lambda e, p=p: e.matmul(psum[p][:, 0:G], lhsT=cs["bones"][:], rhs=v(sqb)[:, p, :], start=True, stop=True),
                             reads=["sqb", "k_bones"], writes=[f"ps{p}"])
                        P.act(lambda e, p=p: e.activation(out=v(rn)[:, p, :], in_=psum[p][:, 0:G], func=AF.Sqrt),
                              reads=[f"ps{p}"], writes=["rn", f"ps{p}"])
                    P.dve(lambda e: e.tensor_scalar_max(out=v(rn), in0=v(rn), scalar1=1e-12), reads=["rn"], writes=["rn"])
                    P.dve(lambda e: e.reciprocal(out=v(rn), in_=v(rn)), reads=["rn"], writes=["rn"])
                    P.dve(lambda e: e.tensor_tensor(out=v(kk), in0=v(kkr), in1=v(rn), op=ALU.mult), reads=["kkr", "rn"], writes=["kk"])
                    P.dve(lambda e: e.scalar_tensor_tensor(out=v(uu), in0=v(aa), scalar=-1.0, in1=bc(3), op0=ALU.add, op1=ALU.mult),
                          reads=["aa", "rp"], writes=["uu"])
                    P.dve(lambda e: e.scalar_tensor_tensor(out=v(keff), in0=v(uu), scalar=1.0, in1=mx[:, 2:4, :], op0=ALU.add, op1=ALU.mult),
                          reads=["uu", "mixed"], writes=["keff"])
                    v4 = lambda t_: v(t_).rearrange("p a (c t) -> p a c t", c=4)
                    ARv = v(AR)
                    P.dve(lambda e: e.scalar_tensor_tensor(out=ARv[:, :, :, 0, :], in0=v4(kk), scalar=-1.0, in1=v4(eLp), op0=ALU.mult, op1=ALU.mult),
                          reads=["kk", "eLp"], writes=["AR"])
                    P.dve(lambda e: e.tensor_tensor(out=ARv[:, :, :, 1, :], in0=mx[:, 0:2, :].rearrange("p a (c t) -> p a c t", c=4), in1=v4(eL), op=ALU.mult),
                           reads=["mixed", "eL"], writes=["AR"])
                    P.dve(lambda e: e.tensor_tensor(out=v(uu), in0=v(kk), in1=v(aa), op=ALU.mult), reads=["kk", "aa", "keff"], writes=["uu"])
                    P.dve(lambda e: e.tensor_tensor(out=v(Bt), in0=v(uu), in1=v(eLi), op=ALU.mult), reads=["uu", "eLi"], writes=["Bt"])
                    P.dve(lambda e: e.tensor_tensor(out=v(Kt), in0=v(keff), in1=v(eLi), op=ALU.mult), reads=["keff", "eLi"], writes=["Kt"])
                    P.dve(lambda e: e.tensor_tensor(out=v(kkr), in0=mx[:, 0:2, :], in1=v(keff), op=ALU.mult), reads=["mixed", "keff", "kk"], writes=["kkr"])
                    P.dve(lambda e: e.tensor_tensor(out=v(rkk), in0=v(kkr), in1=bc(4), op=ALU.mult), reads=["kkr", "rp"], writes=["rkk"])
                    for p in range(2):
                        P.pe(lambda e, p=p: e.matmul(psum[2 + p][:, 0:G], lhsT=cs["bones"][:], rhs=v(rkk)[:, p, :], start=True, stop=True),
                             reads=["rkk", "k_bones"], writes=[f"ps{2 + p}"])
                        P.dve(lambda e, p=p: e.tensor_tensor(out=v(bonus)[:, p, :], in0=psum[2 + p][:, 0:G], in1=mx[:, 4 + p, :], op=ALU.mult),
                              reads=[f"ps{2 + p}", "mixed"], writes=["bonus", f"ps{2 + p}"])
                    TKv = v(TK)
                    for p in range(2):
                        for c in range(4):
                            for wi, src in enumerate((Bt, Kt, vrb)):
                                for hh in range(2):
                                    hs = slice(hh * 64, hh * 64 + 64)
                                    P.pe(lambda e, p=p, c=c, wi=wi, src=src, hs=hs: e.transpose(
                                        out=psum[4 + p][:].bitcast(BF16)[hs, (c * 3 + wi) * 64:(c * 3 + wi + 1) * 64],
                                        in_=v(src)[hs, p, c * 64:(c + 1) * 64], identity=cs["ident"][hs, hs]),
                                        reads=["Bt", "Kt", "vrb", "k_ident"], writes=[f"ps{4 + p}"])
                        P.act(lambda e, p=p: e.activation(out=TKv[:, p, :, :, :].rearrange("p c w t -> p (c w t)"),
                                                          in_=psum[4 + p][:].bitcast(BF16)[:, 0:768], func=AF.Copy),
                              reads=[f"ps{4 + p}"], writes=["TK", f"ps{4 + p}"])
                    MSv = v(MS)
                    for p in range(2):
                        for c in range(4):
                            it = p * 4 + c
                            bank = 6 + it % 2
                            for hh in range(2):
                                hs = slice(hh * 64, hh * 64 + 64)
                                P.pe(lambda e, p=p, c=c, hs=hs, bank=bank: e.matmul(
                                    psum[bank][hs, 0:128], lhsT=v(Bt)[hs, p, c * 64:(c + 1) * 64],
                                    rhs=ARv[hs, p, c, :, :].rearrange("p a t -> p (a t)"), start=True, stop=True),
                                    reads=["Bt", "AR"], writes=[f"ps{bank}"])
                                P.pe(lambda e, p=p, c=c, hs=hs, bank=bank: e.matmul(
                                    psum[bank][hs, 128:256], lhsT=v(Kt)[hs, p, c * 64:(c + 1) * 64],
                                    rhs=ARv[hs, p, c, :, :].rearrange("p a t -> p (a t)"), start=True, stop=True),
                                    reads=["Kt", "AR"], writes=[f"ps{bank}"])
                                P.pe(lambda e, p=p, c=c, hs=hs, bank=bank: e.matmul(
                                    psum[bank][hs, 256:320], lhsT=ARv[hs, p, c, 0, :], rhs=v(Bt)[hs, p, c * 64:(c + 1) * 64],
                                    start=True, stop=True), reads=["Bt", "AR"], writes=[f"ps{bank}"])
                            P.dve(lambda e, it=it, bank=bank: e.tensor_tensor(out=MSv[:, it, :], in0=psum[bank][:, 0:320], in1=cs["maskG"][:], op=ALU.mult),
                                  reads=[f"ps{bank}", "k_maskG"], writes=["MS", f"ps{bank}"])
                    P.dve(lambda e: e.tensor_tensor(out=v(TT[0]), in0=MSv[:, :, 0:64], in1=cs["id64"][:].unsqueeze(1).to_broadcast([128, 8, 64]), op=ALU.add),
                          reads=["MS", "k_id64"], writes=["TT0"])
                    for k in range(1, 6):
                        src_i, dst_i = (k - 1) % 2, k % 2
                        PQs, PQd = v(PQ[src_i]), v(PQ[dst_i])
                        Pprev = (lambda it_: MSv[:, it_, 0:64]) if k == 1 else (lambda it_, PQs=PQs: PQs[:, it_, 0:64])
                        Qprev = (lambda it_: MSv[:, it_, 256:320]) if k == 1 else (lambda it_, PQs=PQs: PQs[:, it_, 64:128])
                        rk_ = ["MS"] if k == 1 else [f"PQ{src_i}"]
                        for it in range(8):
                            bank = it // 4
                            for hh in range(2):
                                hs = slice(hh * 64, hh * 64 + 64)
                                P.pe(lambda e, it=it, hs=hs, bank=bank, Pprev=Pprev, Qprev=Qprev: e.matmul(
                                    psum[bank][hs, (it % 4) * 128:(it % 4) * 128 + 64], lhsT=Qprev(it)[hs, :], rhs=Pprev(it)[hs, :], start=True, stop=True),
                                    reads=rk_, writes=[f"ps{bank}"])
                                P.pe(lambda e, it=it, hs=hs, bank=bank, Pprev=Pprev, Qprev=Qprev: e.matmul(
                                    psum[bank][hs, (it % 4) * 128 + 64:(it % 4) * 128 + 128], lhsT=Pprev(it)[hs, :], rhs=Qprev(it)[hs, :], start=True, stop=True),
                                    reads=rk_, writes=[f"ps{bank}"])
                        for bank in range(2):
                            P.act(lambda e, bank=bank, PQd=PQd: e.activation(out=PQd[:, bank * 4:(bank + 1) * 4, :].rearrange("p a b -> p (a b)"),
                                                                            in_=psum[bank][:, 0:512], func=AF.Copy),
                                  reads=[f"ps{bank}"], writes=[f"PQ{dst_i}", f"ps{bank}"])
                        Ts, Td = v(TT[src_i]), v(TT[dst_i])
                        for it in range(8):
                            for hh in range(2):
                                hs = slice(hh * 64, hh * 64 + 64)
                                P.pe(lambda e, it=it, hs=hs, PQd=PQd, Ts=Ts: e.matmul(
                                    psum[2][hs, it * 64:(it + 1) * 64], lhsT=PQd[hs, it, 64:128], rhs=Ts[hs, it, :], start=True, stop=True),
                                    reads=[f"PQ{dst_i}", f"TT{src_i}"], writes=["ps2"])
                        P.dve(lambda e, Ts=Ts, Td=Td: e.tensor_tensor(out=Td, in0=psum[2][:, 0:512].rearrange("p (a b) -> p a b", a=8), in1=Ts, op=ALU.add),
                              reads=["ps2", f"TT{src_i}"], writes=[f"TT{dst_i}", "ps2"])
                    Tfin = v(TT[1])
                    Sv, Sbv, Wbv, Ubv, Yv = v(Sst), v(Sb), v(Wb), v(Ub), v(Ybuf)
                    for c in range(4):
                        for p in range(2):
                            it = p * 4 + c
                            for hh in range(2):
                                hs = slice(hh * 64, hh * 64 + 64)
                                P.pe(lambda e, p=p, c=c, hs=hs: e.matmul(psum[3][hs, p * 64:(p + 1) * 64], lhsT=ARv[hs, p, c, 0, :], rhs=Sbv[hs, p, :],
                                                                         start=True, stop=False), reads=["AR", "Sb"], writes=["ps3"])
                                P.pe(lambda e, p=p, c=c, hs=hs, it=it: e.matmul(psum[3][hs, p * 64:(p + 1) * 64], lhsT=MSv[hs, it, 128:192], rhs=TKv[hs, p, c, 2, :],
                                                                                start=False, stop=True), reads=["MS", "TK"], writes=["ps3"])
                        P.act(lambda e: e.activation(out=Wbv.rearrange("p a b -> p (a b)"), in_=psum[3][:, 0:128], func=AF.Copy),
                              reads=["ps3"], writes=["Wb", "ps3"])
                        for p in range(2):
                            it = p * 4 + c
                            for hh in range(2):
                                hs = slice(hh * 64, hh * 64 + 64)
                                P.pe(lambda e, p=p, hs=hs, it=it: e.matmul(psum[4][hs, p * 64:(p + 1) * 64], lhsT=Tfin[hs, it, :], rhs=Wbv[hs, p, :],
                                                                           start=True, stop=True), reads=["TT1", "Wb"], writes=["ps4"])
                        P.act(lambda e: e.activation(out=Ubv.rearrange("p a b -> p (a b)"), in_=psum[4][:, 0:128], func=AF.Copy),
                              reads=["ps4"], writes=["Ub", "ps4"])
                        for p in range(2):
                            it = p * 4 + c
                            for hh in range(2):
                                hs = slice(hh * 64, hh * 64 + 64)
                                P.pe(lambda e, p=p, c=c, hs=hs: e.matmul(psum[5][hs, p * 64:(p + 1) * 64], lhsT=ARv[hs, p, c, 1, :], rhs=Sbv[hs, p, :],
                                                                         start=True, stop=False), reads=["AR", "Sb"], writes=["ps5"])
                                P.pe(lambda e, p=p, hs=hs, it=it: e.matmul(psum[5][hs, p * 64:(p + 1) * 64], lhsT=MSv[hs, it, 64:128], rhs=Ubv[hs, p, :],
                                                                           start=False, stop=False), reads=["MS", "Ub"], writes=["ps5"])
                                P.pe(lambda e, p=p, c=c, hs=hs, it=it: e.matmul(psum[5][hs, p * 64:(p + 1) * 64], lhsT=MSv[hs, it, 192:256], rhs=TKv[hs, p, c, 2, :],
                                                                                start=False, stop=True), reads=["MS", "TK"], writes=["ps5"])
                                P.pe(lambda e, p=p, c=c, hs=hs: e.matmul(psum[6][hs, p * 64:(p + 1) * 64], lhsT=TKv[hs, p, c, 0, :], rhs=Ubv[hs, p, :],
                                                                         start=True, stop=False), reads=["TK", "Ub"], writes=["ps6"])
                                P.pe(lambda e, p=p, c=c, hs=hs: e.matmul(psum[6][hs, p * 64:(p + 1) * 64], lhsT=TKv[hs, p, c, 1, :], rhs=TKv[hs, p, c, 2, :],
                                                                         start=False, stop=True), reads=["TK"], writes=["ps6"])
                        P.dve(lambda e, c=c: e.tensor_copy(out=Yv[:, c, :, :], in_=psum[5][:, 0:128].rearrange("p (a b) -> p a b", a=2)),
                              reads=["ps5"], writes=["Ybuf", "ps5"])
                        P.dve(lambda e: e.tensor_tensor(out=Sv, in0=psum[6][:, 0:128].rearrange("p (a b) -> p a b", a=2), in1=Sv, op=ALU.add),
                              reads=["ps6", "Sst"], writes=["Sst", "ps6"])
                        P.dve(lambda e, c=c: e.tensor_tensor(out=Sv, in0=Sv, in1=v(eL)[:, :, c * 64 + 63:c * 64 + 64].to_broadcast([128, 2, 64]), op=ALU.mult),
                              reads=["Sst", "eL"], writes=["Sst"])
                        P.act(lambda e: e.activation(out=Sbv, in_=Sv, func=AF.Copy), reads=["Sst"], writes=["Sb"])
                    if g == NGR - 1:
                        P.dma(lambda e: e.dma_start(out=swp[l], in_=Sv), reads=["Sst"], writes=["swp"], semkey="swp")
                    gsv = v(gs)
                    Y8 = Yv.rearrange("p c a b -> p (c a) b")
                    P.dve(lambda e: e.tensor_reduce(out=gsv[:, 0, :], in_=Y8, axis=AX.X, op=ALU.add), reads=["Ybuf"], writes=["gs"])
                    P.dve(lambda e: e.tensor_scalar_mul(out=gsv[:, 0, :], in0=gsv[:, 0, :], scalar1=-1.0 / 64), reads=["gs"], writes=["gs"])
                    P.dve(lambda e: e.tensor_tensor(out=v(yc), in0=Y8, in1=gsv[:, 0, :].unsqueeze(2).to_broadcast([128, 8, 64]), op=ALU.add),
                           reads=["Ybuf", "gs"], writes=["pb"])
                    P.dve(lambda e: e.tensor_tensor(out=v(ysq), in0=v(yc), in1=v(yc), op=ALU.mult), reads=["pb"], writes=["sc"])
                    P.dve(lambda e: e.tensor_reduce(out=gsv[:, 1, :], in_=v(ysq), axis=AX.X, op=ALU.add), reads=["sc"], writes=["gs"])
                    P.act(lambda e: e.activation(out=gsv[:, 2, :], in_=gsv[:, 1, :], func=AF.Sqrt, scale=1.0 / 64, bias=epsc[:, 1:2]),
                          reads=["gs", "epsc"], writes=["gs"])
                    P.dve(lambda e: e.reciprocal(out=gsv[:, 3, :], in_=gsv[:, 2, :]), reads=["gs"], writes=["gs"])
                    P.dve(lambda e: e.tensor_tensor(out=v(yh), in0=v(yc), in1=gsv[:, 3, :].unsqueeze(2).to_broadcast([128, 8, 64]), op=ALU.mult),
                          reads=["pb", "gs"], writes=["yh"])
                    for c in range(4):
                        for p in range(2):
                            for hh in range(2):
                                hs = slice(hh * 64, hh * 64 + 64)
                                P.pe(lambda e, c=c, p=p, hs=hs: e.transpose(
                                    out=psum[7][:].bitcast(BF16)[hs, p * G + c * 64:p * G + (c + 1) * 64],
                                    in_=v(yh)[hs, c * 2 + p, :], identity=cs["ident"][hs, hs]), reads=["yh", "k_ident"], writes=["ps7"])
                    for p in range(2):
                        P.act(lambda e, p=p: e.activation(out=v(yz)[:, p, :], in_=psum[7][:].bitcast(BF16)[:, p * G:(p + 1) * G], func=AF.Identity,
                                                          scale=rpv[:, 5, p:p + 1], bias=rpv[:, 6, p:p + 1]),
                              reads=["ps7", "rp"], writes=["rn", "ps7"])
                    P.dve(lambda e: e.tensor_tensor(out=v(yz), in0=v(yz), in1=v(bonus), op=ALU.add), reads=["rn", "bonus"], writes=["rn"])
                    P.dve(lambda e, gq=gsil2[g % 2]: e.tensor_tensor(out=v(zT)[:, 2:4, :], in0=v(yz), in1=v(gq)[:, 2:4, :], op=ALU.mult),
                          reads=["rn", f"gsil{g % 2}"], writes=["zT"])

                for bl in range(G // 128 if 'op' not in SKIP else 0):
                    for cgp in range(4):
                        for i in range(4):
                            P.pe(lambda e, bl=bl, cgp=cgp, i=i: e.matmul(psum[cgp][:, 0:512], lhsT=v(zT)[:, i, bl * 128:(bl + 1) * 128],
                                                                         rhs=Wo[:, i, cgp * 512:(cgp + 1) * 512], start=(i == 0), stop=(i == 3)),
                                 reads=["zT"] + [f"Wo{j}" for j in range(4)], writes=[f"ps{cgp}"])
                        if cgp % 2 == 0:
                            P.act(lambda e, cgp=cgp: e.activation(out=ob[0][:, cgp * 512:(cgp + 1) * 512], in_=psum[cgp][:, 0:512], func=AF.Copy),
                                  reads=[f"ps{cgp}"], writes=["ob", f"ps{cgp}"])
                        else:
                            P.dve(lambda e, cgp=cgp: e.tensor_copy(out=ob[0][:, cgp * 512:(cgp + 1) * 512], in_=psum[cgp][:, 0:512]),
                                  reads=[f"ps{cgp}"], writes=["ob", f"ps{cgp}"])
                    tok0 = g * G + bl * 128
                    row0 = tok0
                    for q in range(2):
                        P.dma(lambda e, q=q, row0=row0: e.dma_start(out=rs_in[l][q].ap()[row0:row0 + 128, :], in_=ob[0][:, q * 1024:(q + 1) * 1024]),
                              reads=["ob"], writes=[f"rs_in{l}"], semkey=f"ob{q}")

    def phase_M_samples(l, Win, WK):
        P.barrier()
        NS = 16
        c7 = Carve()
        hTs16 = c7.bf([128, 16, NS])
        rp = c7.f32([128, NPAR, 2])
        loraW = c7.bf([128, 2, 128])
        mu = c7.f32([128, 7])
        prv = c7.f32([128, 7, NS])
        shpt = c7.f32([64, 193])
        P.dma(lambda e: e.dma_start(out=v(rp), in_=rpar[l]), writes=["s_rp"])
        loraF = c7.f32([128, 2, 128])
        P.dma(lambda e: e.dma_start(out=v(loraF), in_=loraw[l]), writes=["s_loraF"])
        P.act(lambda e: e.activation(out=v(loraW), in_=v(loraF), func=AF.Copy), reads=["s_loraF"], writes=["s_loraW"])
        P.dma(lambda e: e.dma_start(out=mu[0], in_=mu_col[l]), writes=["s_mu"])
        P.dma(lambda e: e.dma_start(out=v(prv), in_=prev_s[l]), writes=["s_prv"])
        P.dma(lambda e: e.dma_start(out=shpt[0], in_=shpar_in[l]), writes=["s_shpt"])
        curS = c7.f32([128, 7, NS])
        mixS = c7.f32([128, 7, NS])
        qf = c7.f32([128, 3, NS])
        qbS = c7.bf([128, 3, NS])
        t1S = c7.f32([128, 3, NS])
        qrotS = c7.f32([128, 3, NS])
        gsS = c7.f32([128, 4, NS])
        vS = c7.f32([16, 64])
        for r2 in range(4):
            P.dma(lambda e, r2=r2: e.dma_start(out=v(hTs16)[:, :, 4 * r2:4 * r2 + 4],
                                               in_=ago_hs[l].ap()[r2 * 128:(r2 + 1) * 128, :].rearrange("p (k c) -> p k c", k=16)),
                  reads=[f"ago_hs{l}"], writes=["s_hT"], eng=("sp" if r2 % 2 == 0 else "act"), semkey=f"s_hT{r2 % 2}")
        for ct in range(NCT):
            pi = ct % 4
            for k in range(16):
                P.pe(lambda e, ct=ct, k=k, pi=pi: e.matmul(psum[pi][:, 0:NS], lhsT=Win[:, k, ct * 128:(ct + 1) * 128], rhs=v(hTs16)[:, k, :],
                                                           start=(k == 0), stop=(k == 15)), reads=["s_hT"], writes=[f"ps{pi}"])
            if ct < 3:
                P.act(lambda e, ct=ct, pi=pi: e.activation(out=v(qf)[:, ct, :], in_=psum[pi][:, 0:NS], func=AF.Copy),
                      reads=[f"ps{pi}"], writes=["s_qf", f"ps{pi}"])
            elif ct < 10:
                P.act(lambda e, ct=ct, pi=pi: e.activation(out=v(curS)[:, ct - 3, :], in_=psum[pi][:, 0:NS], func=AF.Copy),
                      reads=[f"ps{pi}"], writes=["s_cur", f"ps{pi}"])
            else:
                P.act(lambda e, ct=ct, pi=pi: e.activation(out=v(gsS)[:, ct - 10, :], in_=psum[pi][:, 0:NS], func=AF.Silu),
                      reads=[f"ps{pi}"], writes=["s_gs", f"ps{pi}"])
        for k in range(16):
            P.pe(lambda e, k=k: e.matmul(psum[4][0:NS, 0:64], lhsT=v(hTs16)[:, k, :], rhs=Win[:, k, NCT * 128:NCT * 128 + 64],
                                         start=(k == 0), stop=(k == 15)), reads=["s_hT"], writes=["ps4"])
        P.act(lambda e: e.activation(out=vS[0], in_=psum[4][0:NS, 0:64], func=AF.Copy), reads=["ps4"], writes=["s_vS", "ps4"])
        P.dma(lambda e: e.dma_start(out=shs[l], in_=v(curS)), reads=["s_cur"], writes=["shs"], semkey="shs")
        P.act(lambda e: e.activation(out=v(qbS), in_=v(qf), func=AF.Copy), reads=["s_qf"], writes=["s_qb"])
        for ct in range(3):
            P.pe(lambda e, ct=ct: e.matmul(psum[5][:, ct * NS:(ct + 1) * NS], lhsT=cs["prot"][:], rhs=v(qbS)[:, ct, :], start=True, stop=True),
                 reads=["s_qb", "k_prot"], writes=["ps5"])
        P.dve(lambda e: e.tensor_scalar_mul(out=t1S[0], in0=psum[5][:, 0:3 * NS], scalar1=cs["cs_s"][:, 1:2]), reads=["ps5", "k_cs_s"],
              writes=["s_t1", "ps5"])
        P.dve(lambda e: e.scalar_tensor_tensor(out=qrotS[0], in0=qf[0], scalar=cs["cs_s"][:, 0:1], in1=t1S[0], op0=ALU.mult, op1=ALU.add),
              reads=["s_qf", "s_t1", "k_cs_s"], writes=["s_qrot"])
        P.dve(lambda e: e.tensor_tensor(out=v(mixS), in0=v(prv), in1=v(curS), op=ALU.subtract), reads=["s_prv", "s_cur"], writes=["s_mix"])
        P.dve(lambda e: e.tensor_tensor(out=v(mixS), in0=v(mixS), in1=mu[0].unsqueeze(2).to_broadcast([128, 7, NS]), op=ALU.mult),
              reads=["s_mix", "s_mu"], writes=["s_mix"])
        P.dve(lambda e: e.tensor_tensor(out=v(mixS), in0=v(mixS), in1=v(curS), op=ALU.add), reads=["s_mix", "s_cur"], writes=["s_mix"])
        mx = v(mixS)
        rpv = v(rp)
        twadS = c7.bf([128, NS])
        sigS = c7.f32([128, 2, NS])
        aS = c7.f32([128, 2, NS])
        wS = c7.f32([128, 2, NS])
        kkrS = c7.f32([128, 2, NS])
        sqS = c7.bf([128, 2, NS])
        rnS = c7.f32([128, 2, NS])
        kkS = c7.f32([128, 2, NS])
        uuS = c7.f32([128, 2, NS])
        keffS = c7.f32([128, 2, NS])
        P.act(lambda e: e.activation(out=twadS[0][0:64, :], in_=mx[0:64, 6, :], func=AF.Tanh), reads=["s_mix"], writes=["s_twad"])
        P.act(lambda e: e.activation(out=twadS[0][64:128, :], in_=mx[64:128, 6, :], func=AF.Copy), reads=["s_mix"], writes=["s_twad"])
        for p in range(2):
            P.pe(lambda e, p=p: e.matmul(psum[p][:, 0:NS], lhsT=v(loraW)[0:64, p, :], rhs=twadS[0][0:64, :], start=True, stop=True),
                 reads=["s_loraW", "s_twad"], writes=[f"ps{p}"])
            P.act(lambda e, p=p: e.activation(out=v(sigS)[:, p, :], in_=psum[p][:, 0:NS], func=AF.Sigmoid, bias=rpv[:, 0, p:p + 1]),
                  reads=[f"ps{p}", "s_rp"], writes=["s_sig", f"ps{p}"])
            P.pe(lambda e, p=p: e.matmul(psum[2 + p][:, 0:NS], lhsT=v(loraW)[64:128, p, :], rhs=twadS[0][64:128, :], start=True, stop=True),
                 reads=["s_loraW", "s_twad"], writes=[f"ps{2 + p}"])
            P.act(lambda e, p=p: e.activation(out=v(aS)[:, p, :], in_=psum[2 + p][:, 0:NS], func=AF.Sigmoid, bias=rpv[:, 1, p:p + 1]),
                  reads=[f"ps{2 + p}", "s_rp"], writes=["s_a", f"ps{2 + p}"])
        P.act(lambda e: e.activation(out=v(wS), in_=v(sigS), func=AF.Exp, scale=-CDEC), reads=["s_sig"], writes=["s_w"])
        bc = lambda j: rpv[:, j, :].unsqueeze(2).to_broadcast([128, 2, NS])
        P.dve(lambda e: e.tensor_tensor(out=v(kkrS), in0=mx[:, 2:4, :], in1=bc(2), op=ALU.mult), reads=["s_mix", "s_rp"], writes=["s_kkr"])
        P.dve(lambda e: e.tensor_tensor(out=v(sqS), in0=v(kkrS), in1=v(kkrS), op=ALU.mult), reads=["s_kkr"], writes=["s_sq"])
        for p in range(2):
            P.pe(lambda e, p=p: e.matmul(psum[p][:, 0:NS], lhsT=cs["bones"][:], rhs=v(sqS)[:, p, :], start=True, stop=True),
                 reads=["s_sq", "k_bones"], writes=[f"ps{p}"])
            P.act(lambda e, p=p: e.activation(out=v(rnS)[:, p, :], in_=psum[p][:, 0:NS], func=AF.Sqrt), reads=[f"ps{p}"], writes=["s_rn", f"ps{p}"])
        P.dve(lambda e: e.tensor_scalar_max(out=v(rnS), in0=v(rnS), scalar1=1e-12), reads=["s_rn"], writes=["s_rn"])
        P.dve(lambda e: e.reciprocal(out=v(rnS), in_=v(rnS)), reads=["s_rn"], writes=["s_rn"])
        P.dve(lambda e: e.tensor_tensor(out=v(kkS), in0=v(kkrS), in1=v(rnS), op=ALU.mult), reads=["s_kkr", "s_rn"], writes=["s_kk"])
        P.dve(lambda e: e.scalar_tensor_tensor(out=v(uuS), in0=v(aS), scalar=-1.0, in1=bc(3), op0=ALU.add, op1=ALU.mult),
              reads=["s_a", "s_rp"], writes=["s_uu"])
        P.dve(lambda e: e.scalar_tensor_tensor(out=v(keffS), in0=v(uuS), scalar=1.0, in1=mx[:, 2:4, :], op0=ALU.add, op1=ALU.mult),
              reads=["s_uu", "s_mix"], writes=["s_keff"])
        tmS = c7.f32([16, 20, 128])
        srcs = []
        for (t_, key, lo) in ((mixS, "s_mix", 0), (wS, "s_w", 0), (keffS, "s_keff", 0), (mixS, "s_mix", 4), (kkS, "s_kk", 0), (aS, "s_a", 0),
                              (qrotS, "s_qrot", 0), (gsS, "s_gs", 0), (gsS, "s_gs", 2)):
            for p in range(2):
                srcs.append((v(t_)[:, lo + p, :], key))
        srcs.append((v(qrotS)[:, 2, :], "s_qrot"))
        for n0 in range(0, len(srcs), 4):
            bank = 4 + (n0 // 4) % 4
            grp = srcs[n0:n0 + 4]
            for i_, (ap_, key) in enumerate(grp):
                P.pe(lambda e, ap_=ap_, i_=i_, bank=bank: e.transpose(out=psum[bank][0:NS, i_ * 128:(i_ + 1) * 128], in_=ap_, identity=cs["identf"][:]),
                     reads=[key, "k_identf"], writes=[f"ps{bank}"])
            n_ = len(grp)
            evac = P.act if (n0 // 4) % 2 == 0 else P.dve
            if (n0 // 4) % 2 == 0:
                P.act(lambda e, n0=n0, n_=n_, bank=bank: e.activation(out=v(tmS)[:, n0:n0 + n_, :].rearrange("p a b -> p (a b)"),
                                                                   in_=psum[bank][0:NS, 0:n_ * 128], func=AF.Copy),
                      reads=[f"ps{bank}"], writes=["s_tm", f"ps{bank}"])
            else:
                P.dve(lambda e, n0=n0, n_=n_, bank=bank: e.tensor_copy(out=v(tmS)[:, n0:n0 + n_, :].rearrange("p a b -> p (a b)"),
                                                                    in_=psum[bank][0:NS, 0:n_ * 128]),
                      reads=[f"ps{bank}"], writes=["s_tm", f"ps{bank}"])
        for kind in range(9):
            P.dma(lambda e, kind=kind: e.dma_start(out=smp_d[l].ap()[:, :, kind, :].rearrange("h s d -> s h d"),
                                                   in_=v(tmS)[:, 2 * kind:2 * kind + 2, :].rearrange("s t (h d) -> s (t h) d", h=2)),
                  reads=["s_tm"], writes=["smp_d"], eng=("sp" if kind % 2 == 0 else "act"), semkey=f"smp{kind % 2}")
        P.dma(lambda e: e.dma_start(out=kn_d[l].ap(), in_=v(tmS)[:, 18, 0:64]), reads=["s_tm"], writes=["kn_d"], semkey="kn")
        P.dma(lambda e: e.dma_start(out=vn_d[l].ap(), in_=vS[0]), reads=["s_vS"], writes=["vn_d"], semkey="vn")
        P.dma(lambda e: e.dma_start(out=cks[l][:, 0:127, :], in_=ck_in[l][:, 1:128, :]), writes=["cks"], semkey="cks0")
        P.dma(lambda e: e.dma_start(out=cvs[l][:, 0:127, :], in_=cv_in[l][:, 1:128, :]), writes=["cvs"], eng="act", semkey="cvs0")
        P.dma(lambda e: e.dma_start(out=cks[l][:, 127, :], in_=v(tmS)[:, 18, 0:64]), reads=["s_tm"], writes=["cks"], semkey="cks1")
        P.dma(lambda e: e.dma_start(out=cvs[l][:, 127, :], in_=vS[0]), reads=["s_vS"], writes=["cvs"], eng="act", semkey="cvs1")
        SH = c7.f32([64, 9, 64])
        SHv = v(SH)
        P.dma(lambda e: e.dma_start(out=SHv, in_=smp_d[l].ap().rearrange("h s k d -> (h s) k d")), reads=["smp_d"], writes=["s_SH"])
        KV = c7.f32([64, 129, 64])
        KVv = v(KV)
        tmpA = c7.f32([64, 33 * 64])
        scs = c7.f32([64, 129])
        pS = c7.f32([64, 129])
        sm2 = c7.f32([64, 8])
        oS = c7.f32([64, 64])
        prt = c7.f32([64, 64])
        zs = c7.f32([64, 2, 64])
        for h_ in range(4):
            q_ = "sp" if h_ % 2 == 0 else "act"
            P.dma(lambda e, h_=h_: e.dma_start(out=KVv[16 * h_:16 * h_ + 16, 0:128, :], in_=ck_in[l]), writes=["s_KV"], eng=q_, semkey=f"s_KV{h_}")
            P.dma(lambda e, h_=h_: e.dma_start(out=KVv[16 * h_:16 * h_ + 16, 128, :], in_=kn_d[l].ap()), reads=["kn_d"], writes=["s_KV"], eng=q_,
                  semkey=f"s_KV{h_}")
        PCH = [(0, 32), (32, 64), (64, 96), (96, 129)]
        for ci, (a_, b_) in enumerate(PCH):
            n_ = b_ - a_
            tv = tmpA[0][:, 0:n_ * 64].rearrange("p (n d) -> p n d", d=64)
            f_ = P.dve
            f_(lambda e, a_=a_, b_=b_, n_=n_, tv=tv: e.tensor_tensor(out=tv, in0=KVv[:, a_:b_, :], in1=SHv[:, 6, :].unsqueeze(1).to_broadcast([64, n_, 64]),
                                                              op=ALU.mult), reads=["s_KV", "s_SH"], writes=["s_tmpA"])
            P.dve(lambda e, a_=a_, b_=b_, tv=tv: e.tensor_reduce(out=scs[0][:, a_:b_], in_=tv, axis=AX.X, op=ALU.add), reads=["s_tmpA"], writes=["s_scs"])
        s2 = sm2[0]
        snkc = shpt[0][:, 192:193]
        P.dve(lambda e: e.tensor_reduce(out=s2[:, 0:1], in_=scs[0], axis=AX.X, op=ALU.max), reads=["s_scs"], writes=["s_sm2"])
        P.dve(lambda e: e.scalar_tensor_tensor(out=s2[:, 1:2], in0=s2[:, 0:1], scalar=0.125, in1=snkc, op0=ALU.mult, op1=ALU.max),
              reads=["s_sm2", "s_shpt"], writes=["s_sm2"])
        P.dve(lambda e: e.tensor_scalar_mul(out=s2[:, 2:3], in0=s2[:, 1:2], scalar1=-1.0), reads=["s_sm2"], writes=["s_sm2"])
        P.dve(lambda e: e.tensor_tensor(out=s2[:, 4:5], in0=snkc, in1=s2[:, 1:2], op=ALU.subtract), reads=["s_sm2", "s_shpt"], writes=["s_sm2"])
        P.act(lambda e: e.activation(out=pS[0], in_=scs[0], func=AF.Exp, scale=0.125, bias=s2[:, 2:3], accum_out=s2[:, 3:4]),
              reads=["s_scs", "s_sm2"], writes=["s_pS", "s_sm2"])
        P.act(lambda e: e.activation(out=s2[:, 4:5], in_=s2[:, 4:5], func=AF.Exp), reads=["s_sm2"], writes=["s_sm2"])
        P.dve(lambda e: e.tensor_tensor(out=s2[:, 5:6], in0=s2[:, 3:4], in1=s2[:, 4:5], op=ALU.add), reads=["s_sm2"], writes=["s_sm2"])
        P.dve(lambda e: e.reciprocal(out=s2[:, 6:7], in_=s2[:, 5:6]), reads=["s_sm2"], writes=["s_sm2"])
        for h_ in range(4):
            q_ = "sp" if h_ % 2 == 0 else "act"
            P.dma(lambda e, h_=h_: e.dma_start(out=KVv[16 * h_:16 * h_ + 16, 0:128, :], in_=cv_in[l]), reads=["s_scs"], writes=["s_KV"], eng=q_,
                  semkey=f"s_KV{h_}")
            P.dma(lambda e, h_=h_: e.dma_start(out=KVv[16 * h_:16 * h_ + 16, 128, :], in_=vn_d[l].ap()), reads=["vn_d", "s_scs"], writes=["s_KV"],
                  eng=q_, semkey=f"s_KV{h_}")
        for ci, (a_, b_) in enumerate(PCH):
            n_ = b_ - a_
            tv = tmpA[0][:, 0:n_ * 64].rearrange("p (d n) -> p d n", d=64)
            f_ = P.dve
            f_(lambda e, a_=a_, b_=b_, n_=n_, tv=tv: e.tensor_tensor(out=tv, in0=KVv[:, a_:b_, :].rearrange("p n d -> p d n"),
                                                              in1=pS[0][:, a_:b_].unsqueeze(1).to_broadcast([64, 64, n_]), op=ALU.mult),
               reads=["s_KV", "s_pS"], writes=["s_tmpA"])
            if ci == 0:
                P.dve(lambda e, tv=tv: e.tensor_reduce(out=oS[0], in_=tv, axis=AX.X, op=ALU.add), reads=["s_tmpA"], writes=["s_oS"])
            else:
                P.dve(lambda e, tv=tv: e.tensor_reduce(out=prt[0], in_=tv, axis=AX.X, op=ALU.add), reads=["s_tmpA"], writes=["s_prt"])
                P.dve(lambda e: e.tensor_tensor(out=oS[0], in0=oS[0], in1=prt[0], op=ALU.add), reads=["s_oS", "s_prt"], writes=["s_oS"])
        zsv = v(zs)
        P.dve(lambda e: e.tensor_scalar_mul(out=oS[0], in0=oS[0], scalar1=s2[:, 6:7]), reads=["s_oS", "s_sm2"], writes=["s_oS"])
        P.dve(lambda e: e.tensor_tensor(out=zsv[:, 0, :], in0=oS[0], in1=SHv[:, 7, :], op=ALU.mult), reads=["s_oS", "s_SH"], writes=["s_zs"])
        Ssm = c7.f32([64, 64, 64])
        tmpS = c7.f32([64, 64, 64])
        sv = c7.f32([64, 6, 64])
        g2 = c7.f32([64, 8])
        Sv_, Tv_, svv = v(Ssm), v(tmpS), v(sv)
        for h_ in range(4):
            P.dma(lambda e, h_=h_: e.dma_start(out=Sv_[16 * h_:16 * h_ + 16], in_=st_in[l][:, h_]), writes=["s_Ssm"],
                  eng=("sp" if h_ % 2 == 0 else "act"), semkey=f"s_Sl{h_}")
        bi = lambda ap_: ap_.unsqueeze(1).to_broadcast([64, 64, 64])
        bj = lambda ap_: ap_.unsqueeze(2).to_broadcast([64, 64, 64])
        P.dve(lambda e: e.tensor_scalar_mul(out=svv[:, 0, :], in0=SHv[:, 4, :], scalar1=-1.0), reads=["s_SH"], writes=["s_sv0"])
        P.dve(lambda e: e.tensor_tensor(out=svv[:, 1, :], in0=SHv[:, 4, :], in1=SHv[:, 5, :], op=ALU.mult), reads=["s_SH"], writes=["s_sv1"])
        P.dve(lambda e: e.tensor_tensor(out=Tv_, in0=Sv_, in1=bi(svv[:, 0, :]), op=ALU.mult), reads=["s_Ssm", "s_sv0"], writes=["s_tmpS"])
        P.dve(lambda e: e.tensor_reduce(out=svv[:, 2, :], in_=Tv_, axis=AX.X, op=ALU.add), reads=["s_tmpS"], writes=["s_sv2"])
        P.dve(lambda e: e.tensor_tensor(out=Sv_, in0=Sv_, in1=bi(SHv[:, 1, :]), op=ALU.mult), reads=["s_Ssm", "s_SH", "s_tmpS"], writes=["s_Ssm"])
        P.dve(lambda e: e.tensor_tensor(out=Tv_, in0=bj(svv[:, 2, :]), in1=bi(svv[:, 1, :]), op=ALU.mult), reads=["s_sv2", "s_sv1"], writes=["s_tmpS"])
        P.dve(lambda e: e.tensor_tensor(out=Sv_, in0=Sv_, in1=Tv_, op=ALU.add), reads=["s_Ssm", "s_tmpS"], writes=["s_Ssm"])
        P.dve(lambda e: e.tensor_tensor(out=Tv_, in0=bj(SHv[:, 3, :]), in1=bi(SHv[:, 2, :]), op=ALU.mult), reads=["s_SH"], writes=["s_tmpS"])
        P.dve(lambda e: e.tensor_tensor(out=Sv_, in0=Sv_, in1=Tv_, op=ALU.add), reads=["s_Ssm", "s_tmpS"], writes=["s_Ssm"])
        for h_ in range(4):
            P.dma(lambda e, h_=h_: e.dma_start(out=sws[l][:, h_], in_=Sv_[16 * h_:16 * h_ + 16]), reads=["s_Ssm"], writes=["sws"],
                  eng=("sp" if h_ % 2 == 0 else "act"), semkey=f"s_Ss{h_}")
        P.dve(lambda e: e.tensor_tensor(out=Tv_, in0=Sv_, in1=bi(SHv[:, 0, :]), op=ALU.mult), reads=["s_Ssm", "s_SH"], writes=["s_tmpS"])
        P.dve(lambda e: e.tensor_reduce(out=svv[:, 3, :], in_=Tv_, axis=AX.X, op=ALU.add), reads=["s_tmpS"], writes=["s_sv3"])
        g2v = g2[0]
        P.dve(lambda e: e.tensor_reduce(out=g2v[:, 0:1], in_=svv[:, 3, :], axis=AX.X, op=ALU.add), reads=["s_sv3"], writes=["s_g2"])
        P.dve(lambda e: e.tensor_scalar_mul(out=g2v[:, 0:1], in0=g2v[:, 0:1], scalar1=-1.0 / 64), reads=["s_g2"], writes=["s_g2"])
        P.dve(lambda e: e.tensor_scalar_add(out=svv[:, 3, :], in0=svv[:, 3, :], scalar1=g2v[:, 0:1]), reads=["s_sv3", "s_g2"], writes=["s_sv3"])
        P.dve(lambda e: e.tensor_tensor(out=svv[:, 4, :], in0=svv[:, 3, :], in1=svv[:, 3, :], op=ALU.mult), reads=["s_sv3"], writes=["s_sv4"])
        P.dve(lambda e: e.tensor_reduce(out=g2v[:, 1:2], in_=svv[:, 4, :], axis=AX.X, op=ALU.add), reads=["s_sv4"], writes=["s_g2"])
        P.act(lambda e: e.activation(out=g2v[:, 2:3], in_=g2v[:, 1:2], func=AF.Sqrt, scale=1.0 / 64, bias=epsc[0:64, 1:2]),
              reads=["s_g2", "epsc"], writes=["s_g2"])
        P.dve(lambda e: e.reciprocal(out=g2v[:, 3:4], in_=g2v[:, 2:3]), reads=["s_g2"], writes=["s_g2"])
        P.dve(lambda e: e.tensor_scalar_mul(out=svv[:, 3, :], in0=svv[:, 3, :], scalar1=g2v[:, 3:4]), reads=["s_sv3", "s_g2"], writes=["s_sv3"])
        P.dve(lambda e: e.tensor_tensor(out=svv[:, 3, :], in0=svv[:, 3, :], in1=shpt[0][:, 0:64], op=ALU.mult), reads=["s_sv3", "s_shpt"], writes=["s_sv3"])
        P.dve(lambda e: e.tensor_tensor(out=svv[:, 3, :], in0=svv[:, 3, :], in1=shpt[0][:, 64:128], op=ALU.add), reads=["s_sv3", "s_shpt"], writes=["s_sv3"])
        P.dve(lambda e: e.tensor_tensor(out=svv[:, 4, :], in0=SHv[:, 0, :], in1=SHv[:, 2, :], op=ALU.mult), reads=["s_SH", "s_g2"], writes=["s_sv4"])
        P.dve(lambda e: e.tensor_tensor(out=svv[:, 4, :], in0=svv[:, 4, :], in1=shpt[0][:, 128:192], op=ALU.mult), reads=["s_sv4", "s_shpt"], writes=["s_sv4"])
        P.dve(lambda e: e.tensor_reduce(out=g2v[:, 4:5], in_=svv[:, 4, :], axis=AX.X, op=ALU.add), reads=["s_sv4"], writes=["s_g2"])
        P.dve(lambda e: e.tensor_scalar_mul(out=svv[:, 5, :], in0=SHv[:, 3, :], scalar1=g2v[:, 4:5]), reads=["s_SH", "s_g2"], writes=["s_sv5"])
        P.dve(lambda e: e.tensor_tensor(out=svv[:, 3, :], in0=svv[:, 3, :], in1=svv[:, 5, :], op=ALU.add), reads=["s_sv3", "s_sv5"], writes=["s_sv3"])
        P.dve(lambda e: e.tensor_tensor(out=zsv[:, 1, :], in0=svv[:, 3, :], in1=SHv[:, 8, :], op=ALU.mult), reads=["s_sv3", "s_SH"], writes=["s_zs"])
        zst = c7.f32([16, 2, 4, 64])
        zTs = c7.bf([128, 4, NS])
        obS = c7.f32([16, D])
        P.dma(lambda e: e.dma_start(out=zs_d[l].ap().rearrange("h s k d -> (h s) k d"), in_=zsv), reads=["s_zs"], writes=["zs_d"], semkey="zs_d")
        for k_ in range(2):
            P.dma(lambda e, k_=k_: e.dma_start(out=v(zst)[:, k_, :, :], in_=zs_d[l].ap()[:, :, k_, :].rearrange("h s d -> s h d")), reads=["zs_d"], writes=["s_zst"], semkey="zst")
        zstv = v(zst)
        for k_ in range(2):
            for pr in range(2):
                i_ = k_ * 2 + pr
                P.pe(lambda e, k_=k_, pr=pr, i_=i_: e.transpose(out=psum[6][:, i_ * NS:(i_ + 1) * NS],
                                                                in_=zstv[:, k_, 2 * pr:2 * pr + 2, :].rearrange("s h d -> s (h d)"),
                                                                identity=cs["identf"][0:NS, 0:NS]), reads=["s_zst", "k_identf"], writes=["ps6"])
        P.act(lambda e: e.activation(out=zTs[0], in_=psum[6][:, 0:4 * NS], func=AF.Copy), reads=["ps6"], writes=["s_zTs", "ps6"])
        for cgp in range(4):
            for i_ in range(4):
                P.pe(lambda e, cgp=cgp, i_=i_: e.matmul(psum[cgp][0:NS, 0:512], lhsT=v(zTs)[:, i_, :], rhs=Wo[:, i_, cgp * 512:(cgp + 1) * 512],
                                                        start=(i_ == 0), stop=(i_ == 3)), reads=["s_zTs"], writes=[f"ps{cgp}"])
            P.dve(lambda e, cgp=cgp: e.tensor_copy(out=obS[0][:, cgp * 512:(cgp + 1) * 512], in_=psum[cgp][0:NS, 0:512]),
                  reads=[f"ps{cgp}"], writes=["s_obS", f"ps{cgp}"])
        P.dma(lambda e: e.dma_start(out=rss_in[l].ap(), in_=obS[0]), reads=["s_obS"], writes=[f"rss_in{l}"], semkey="obS")
        P.op("pool", lambda e: e.collective_compute("ReduceScatter", ALU.add, replica_groups=RG,
                                                     ins=[rss_in[l].ap().opt()], outs=[rss_out[l].ap().opt()]),
             reads=[f"rss_in{l}"], writes=[f"rss_out{l}"], cc=True, semkey="cc")

    def reduce_scatter(l):
        for q in range(2):
            P.op("pool", lambda e, q=q: e.collective_compute("ReduceScatter", ALU.add, replica_groups=RG,
                                                          ins=[rs_in[l][q].ap().opt()], outs=[rs_out[l][q].ap().opt()], dma_qos="P3"),
                 reads=[f"rs_in{l}"], writes=[f"rs_out{l}"], cc=True, semkey="cc", nb=True)

    def phase_O(l):
        nonlocal NOFF
        c4 = Carve()
        gateB = c4.f32([128, D])
        fgB = c4.f32([128, D]) if l == NL - 1 else None
        ot = [c4.f32([128, D]) for _ in range(2)]
        xo = [c4.f32([128, D]) for _ in range(2)]
        fs = c4.f32([128, 2])
        NOFF = c4.off
        P.dma(lambda e: e.dma_start(out=gateB[0][:, 0:512],
                                    in_=ago_mod.ap()[2 * 17:2 * 17 + 1, l * 1536 + 1024:(l + 1) * 1536].partition_broadcast(128)[:, 0, :]),
              reads=["ago_mod"], writes=["gateB"], semkey="gateB")
        P.dma(lambda e: e.dma_start(out=gateB[0][:, 512:2048],
                                    in_=ago_mod.ap()[3 * 17:3 * 17 + 1, l * 1536:(l + 1) * 1536].partition_broadcast(128)[:, 0, :]),
              reads=["ago_mod"], writes=["gateB"], semkey="gateB")
        if l == NL - 1:
            P.dma(lambda e: e.dma_start(out=fgB[0], in_=fg_row[0:1, :].partition_broadcast(128)[:, 0, :]), writes=["fgB"])
        xsrc = x_own if l == 0 else x1_d.ap()

        def o_loads(t):
            s = t % 2
            for q in range(2):
                P.dma(lambda e, t=t, s=s, q=q: e.dma_start(out=ot[s][0][:, q * 1024:(q + 1) * 1024], in_=rs_out[l][q].ap()[t * 128:(t + 1) * 128, :]),
                      reads=[f"rs_out{l}"], writes=[f"ot{s}"], semkey=f"ot{s}")
            P.dma(lambda e, t=t, s=s: e.dma_start(out=xo[s][0], in_=xsrc[t * 128:(t + 1) * 128, :]),
                  reads=(["x1_d"] if l > 0 else []), writes=[f"xo{s}"], semkey=f"xo{s}")

        o_loads(0)
        for t in range(8):
            s = t % 2
            if t + 1 < 8:
                o_loads(t + 1)
            P.dve(lambda e, s=s: e.tensor_tensor(out=ot[s][0], in0=ot[s][0], in1=gateB[0], op=ALU.mult), reads=[f"ot{s}", "gateB"], writes=[f"ot{s}"])
            P.dve(lambda e, s=s: e.tensor_tensor(out=xo[s][0], in0=xo[s][0], in1=ot[s][0], op=ALU.add), reads=[f"ot{s}", f"xo{s}"], writes=[f"xo{s}"])
            if l < NL - 1:
                P.dma(lambda e, t=t, s=s: e.dma_start(out=x1_d.ap()[t * 128:(t + 1) * 128, :], in_=xo[s][0]), reads=[f"xo{s}"], writes=["x1_d"],
                      semkey=f"x1st{s}")
                phase_N_tile(l + 1, t, xo[s][0], f"xo{s}")
            elif 'fin' not in os.environ.get('MK_SKIP', ''):
                P.act(lambda e, s=s: e.activation(out=ot[s][0], in_=xo[s][0], func=AF.Square, accum_out=fs[0][:, 0:1]),
                      reads=[f"xo{s}"], writes=[f"ot{s}", "fs"])
                P.act(lambda e: e.activation(out=fs[0][:, 1:2], in_=fs[0][:, 0:1], func=AF.Sqrt, scale=1.0 / D, bias=epsc[:, 0:1]),
                      reads=["fs", "epsc"], writes=["fs"])
                P.dve(lambda e: e.reciprocal(out=fs[0][:, 1:2], in_=fs[0][:, 1:2]), reads=["fs"], writes=["fs"])
                P.act(lambda e, s=s: e.activation(out=ot[s][0], in_=xo[s][0], func=AF.Copy, scale=fs[0][:, 1:2]),
                      reads=[f"xo{s}", "fs"], writes=[f"ot{s}"])
                P.dve(lambda e, s=s: e.tensor_tensor(out=ot[s][0], in0=ot[s][0], in1=fgB[0], op=ALU.mult), reads=[f"ot{s}", "fgB"], writes=[f"ot{s}"])
                P.dma(lambda e, t=t, s=s: e.dma_start(out=y_own[t * 128:(t + 1) * 128, :], in_=ot[s][0]), reads=[f"ot{s}"], writes=["y_own"],
                      semkey=f"yst{s}")


        c8 = Carve()
        c8.off = NOFF + (6200 if l < NL - 1 else 100)
        osT = c8.f32([4, D])
        xsT = c8.f32([4, D])
        gS = c8.f32([4, D])
        fq = c8.f32([4, 2])
        P.dma(lambda e: e.dma_start(out=osT[0], in_=rss_out[l].ap()), reads=[f"rss_out{l}"], writes=["osT"], semkey="osT")
        xss = xs_own if l == 0 else x1_d.ap()[1024:1028, :]
        P.dma(lambda e: e.dma_start(out=xsT[0], in_=xss), reads=(["x1_d"] if l > 0 else []), writes=["xsT"], eng="act")
        P.dma(lambda e: e.dma_start(out=gS[0], in_=smod_d.ap()[l, 2]), reads=["smod_d"], writes=["gS"], eng="act")
        P.dve(lambda e: e.tensor_tensor(out=osT[0], in0=osT[0], in1=gS[0], op=ALU.mult), reads=["osT", "gS"], writes=["osT"])
        P.dve(lambda e: e.tensor_tensor(out=xsT[0], in0=xsT[0], in1=osT[0], op=ALU.add), reads=["osT", "xsT"], writes=["xsT"])
        if l < NL - 1:
            P.dma(lambda e: e.dma_start(out=x1_d.ap()[1024:1028, :], in_=xsT[0]), reads=["xsT"], writes=["x1_d"], semkey="x1s")
            phase_N_samples(l + 1, xsT[0], "xsT", c8.off)
        else:
            P.act(lambda e: e.activation(out=osT[0], in_=xsT[0], func=AF.Square, accum_out=fq[0][:, 0:1]), reads=["xsT"], writes=["osT", "fq"])
            P.act(lambda e: e.activation(out=fq[0][:, 1:2], in_=fq[0][:, 0:1], func=AF.Sqrt, scale=1.0 / D, bias=epsc[0:4, 0:1]),
                  reads=["fq", "epsc"], writes=["fq"])
            P.dve(lambda e: e.reciprocal(out=fq[0][:, 1:2], in_=fq[0][:, 1:2]), reads=["fq"], writes=["fq"])
            P.act(lambda e: e.activation(out=osT[0], in_=xsT[0], func=AF.Copy, scale=fq[0][:, 1:2]), reads=["xsT", "fq"], writes=["osT"])
            P.dve(lambda e: e.tensor_tensor(out=osT[0], in0=osT[0], in1=fgB[0][0:4, :], op=ALU.mult), reads=["osT", "fgB"], writes=["osT"])
            P.dma(lambda e: e.dma_start(out=ys_own, in_=osT[0]), reads=["osT"], writes=["ys_own"], semkey="ysst")

    NLR = int(os.environ.get("MK_NL", NL))
    for l in range(NLR):
        P.epoch = l
        phase_M(l)
        reduce_scatter(l)
        if 'smp' not in os.environ.get('MK_SKIP', ''):
            phase_M_samples(l, WBIG[:, 0:16 * WCOLS].rearrange("p (k c) -> p k c", k=16), None)
        P.barrier()
        if STOP == f"R{l}":
            break
        phase_O(l)
        if STOP == f"O{l}a":
            break
        if l < NL - 1:
            allgather_h(l + 1)
        P.barrier()
        if STOP == f"O{l}":
            break
    print(f"[mk] sbuf bytes/partition = {sb_bytes[0]}", flush=True)
    P.emit()
    return nc


def prep_inputs(inp):
    f = lambda k: np.asarray(inp[k], dtype=np.float32)
    x_prompt, x_sample = f("x_prompt"), f("x_sample")
    consts = make_consts()
    w_in_full = f("w_in")
    maps = []
    for c in range(8):
        g, r = c // 4, c % 4
        ci = col_index(r)
        m = {}
        m["x_own"] = np.ascontiguousarray(x_prompt[g, r * 1024:(r + 1) * 1024])
        m["xs_own"] = np.ascontiguousarray(x_sample[16 * g + 4 * r:16 * g + 4 * r + 4, 0])
        cmat = np.concatenate([f("c_prompt")[g:g + 1], f("c_sample")[16 * g:16 * g + 16]], 0)
        m["cT"] = np.ascontiguousarray(cmat.T.reshape(16, 128, 17).transpose(1, 0, 2))
        m["wada"] = np.ascontiguousarray(f("w_ada")[:, :, r * 1536:(r + 1) * 1536])
        m["bada"] = np.ascontiguousarray(f("b_ada")[:, r * 1536:(r + 1) * 1536])
        m["ng_col"] = np.ascontiguousarray(f("norm_g").reshape(NL, 16, 128).transpose(2, 0, 1))
        m["ng_row"] = f("norm_g")
        m["fg_row"] = f("final_g").reshape(1, D)
        m["w_in"] = np.ascontiguousarray(w_in_full[:, :, ci])
        sh_cols = ci[3 * 128:10 * 128] - R_OFF
        m["mu_col"] = np.ascontiguousarray(f("mu_shift")[:, sh_cols].reshape(NL, 7, 128).transpose(0, 2, 1))
        ss = f("state_shift")[:, 16 * g:16 * g + 16][:, :, sh_cols]
        m["prev_s"] = np.ascontiguousarray(ss.reshape(NL, 16, 7, 128).transpose(0, 3, 2, 1))
        own = (np.arange(256) + 256 * r)
        pars = [f("w0"), f("a0"), f("k_k"), f("k_a"), f("r_k").reshape(NL, 1024), f("ln_w"), f("ln_b")]
        rp = np.stack([p_[:, own].reshape(NL, 2, 128) for p_ in pars], 2)
        m["rpar"] = np.ascontiguousarray(rp.transpose(0, 3, 2, 1))
        lw = np.concatenate([f("w_decay")[:, :, own], f("w_iclr")[:, :, own]], 1)
        m["loraw"] = np.ascontiguousarray(lw.reshape(NL, 128, 2, 128))
        sk = f("sinks")[:, 4 * r:4 * r + 4][:, [0, 2, 1, 3]]
        m["sinks_b"] = np.ascontiguousarray(np.broadcast_to(sk[:, None, :], (NL, 128, 4)))
        rows = np.concatenate([np.arange(256 * r, 256 * r + 256), np.arange(1024 + 256 * r, 1024 + 256 * r + 256)])
        m["w_out"] = np.ascontiguousarray(f("w_out")[:, rows, :])
        m["ck_in"] = np.ascontiguousarray(f("cache_k")[:, 16 * g:16 * g + 16, :, r, :])
        m["cv_in"] = np.ascontiguousarray(f("cache_v")[:, 16 * g:16 * g + 16, :, r, :])
        m["st_in"] = np.ascontiguousarray(f("state_wkv")[:, 16 * g:16 * g + 16, 4 * r:4 * r + 4])
        sel = np.zeros((16, 4), np.float32)
        for si in range(4):
            sel[4 * r + si, si] = 1.0
        m["selT"] = sel
        hsel = np.arange(4) + 4 * r
        lw_ = f("ln_w").reshape(NL, 16, 64)[:, hsel]
        lb_ = f("ln_b").reshape(NL, 16, 64)[:, hsel]
        rk_ = f("r_k")[:, hsel]
        sk_ = f("sinks")[:, hsel][:, :, None]
        shp_ = np.concatenate([lw_, lb_, rk_, sk_], 2)
        m["shpar"] = np.ascontiguousarray(np.broadcast_to(shp_[:, :, None], (NL, 4, 16, 193)).reshape(NL, 64, 193))
        for k, v_ in consts.items():
            m["c_" + k] = v_
        maps.append(m)
    return consts, maps


def assemble(res):
    y_prompt = np.zeros((2, SEQ, D), np.float32)
    y_sample = np.zeros((32, 1, D), np.float32)
    ckp = np.zeros((NL, 2, 128, 4, 64), np.float32)
    cvp = np.zeros_like(ckp)
    swp = np.zeros((NL, 2, 16, 64, 64), np.float32)
    shp = np.zeros((NL, 2, 3200), np.float32)
    cks = np.zeros((NL, 32, 128, 4, 64), np.float32)
    cvs = np.zeros_like(cks)
    sws = np.zeros((NL, 32, 16, 64, 64), np.float32)
    shs = np.zeros((NL, 32, 3200), np.float32)
    for c in range(8):
        g, r = c // 4, c % 4
        o = res[c]
        ci = col_index(r)
        sh_cols = ci[3 * 128:10 * 128] - R_OFF
        y_prompt[g, r * 1024:(r + 1) * 1024] = o["y_own"]
        y_sample[16 * g + 4 * r:16 * g + 4 * r + 4, 0] = o["ys_own"]
        ckp[:, g, :, r, :] = o["ckp"]
        cvp[:, g, :, r, :] = o["cvp"]
        t = o["swp"].reshape(NL, 2, 64, 2, 64)
        swp[:, g, 4 * r:4 * r + 4] = t.transpose(0, 3, 1, 4, 2).reshape(NL, 4, 64, 64)
        shp[:, g, sh_cols] = o["shp"].transpose(0, 2, 1).reshape(NL, 896)
        cks[:, 16 * g:16 * g + 16, :, r, :] = o["cks"]
        cvs[:, 16 * g:16 * g + 16, :, r, :] = o["cvs"]
        sws[:, 16 * g:16 * g + 16, 4 * r:4 * r + 4] = o["sws"]
        shs[:, 16 * g:16 * g + 16][:, :, sh_cols] = o["shs"].transpose(0, 3, 2, 1).reshape(NL, 16, 896)
    return (y_prompt, y_sample, ckp, cvp, swp, shp, cks, cvs, sws, shs)


def kernel(**inputs):
    consts, maps = prep_inputs(inputs)
    nc = build(consts)
    res = run_bass_kernel_spmd(nc, maps, core_ids=list(range(8)))
    global LAST_RES
    LAST_RES = res.results
    return assemble(res.results)
```

```python
import contextlib
import numpy as np
import concourse.bass as bass
import concourse.mybir as mybir
from concourse.bass_utils import run_bass_kernel_spmd

F32 = mybir.dt.float32
BF16 = mybir.dt.bfloat16
AF = mybir.ActivationFunctionType
ALU = mybir.AluOpType
AX = mybir.AxisListType

D = 2048
SEQ = 4096
NL = 2
HD = 64
WIN = 128
Q_OFF = 0
KA_OFF = 1024
VA_OFF = 1280
R_OFF = 1536
KR_OFF = 2560
VR_OFF = 3584
WD_OFF = 4608
AD_OFF = 4672
GA_OFF = 4736
GR_OFF = 5760
NCT = 14
WCOLS = NCT * 128 + 64
G = 256
NG = SEQ // G
CH = 64
CDEC = 0.6065306597126334
NEG = -30000.0
HC = 1088
ZC = 4 * SEQ + 64

ENGS = ("pe", "act", "dve", "pool", "sp")


class Op:
    __slots__ = ("eng", "fn", "deps", "signal", "sigval", "dma", "semkey", "cc", "epoch", "nb")

    def __init__(self, eng, fn, dma=False, semkey=None, cc=False):
        self.eng = eng
        self.fn = fn
        self.deps = []
        self.signal = False
        self.sigval = 0
        self.dma = dma
        self.semkey = semkey
        self.cc = cc
        self.epoch = 0
        self.nb = False


class Prog:
    def __init__(self, nc):
        self.nc = nc
        self.ops = {e: [] for e in ENGS}
        self.last_w = {}
        self.readers = {}
        self.all_ops = []
        self.last_dma = {}
        self.epoch = 0

    def op(self, eng, fn, reads=(), writes=(), dma=False, cc=False, semkey=None, nb=False):
        o = Op(eng, fn, dma=dma or cc, semkey=semkey, cc=cc)
        o.nb = nb
        o.epoch = self.epoch if eng == "pe" else 0
        deps = []
        for k in reads:
            w = self.last_w.get(k)
            if w is not None:
                deps.append(w)
        for k in writes:
            w = self.last_w.get(k)
            if w is not None:
                deps.append(w)
            deps.extend(self.readers.get(k, ()))
        seen = set()
        for d in deps:
            if id(d) in seen or d is o:
                continue
            seen.add(id(d))
            if (not d.dma) and d.eng == "pe" and eng == "pe" and not o.dma:
                continue
            o.deps.append(d)
            d.signal = True
        implied = set()
        for d2 in o.deps:
            for x_ in d2.deps:
                implied.add(id(x_))
        if implied:
            o.deps = [d for d in o.deps if id(d) not in implied]
        for k in reads:
            self.readers.setdefault(k, []).append(o)
        for k in writes:
            self.last_w[k] = o
            self.readers[k] = []
        if o.dma:
            if o.semkey is None:
                o.semkey = ("w",) + tuple(writes)
            HOT = ("hT", "Win", "ob", "ot", "xo", "x1st", "hTt_st", "yst", "xt", "wst", "Wo")
            if (not o.cc) and isinstance(o.semkey, str) and o.semkey.rstrip("0123456789_") in HOT:
                pass
            elif not o.cc:
                import zlib
                if eng == "pool":
                    o.semkey = ("dmapool_sw", zlib.crc32(repr(o.semkey).encode()) % 12)
                else:
                    o.semkey = ("dmapool_hw", zlib.crc32(repr(o.semkey).encode()) % 48)
            prev = self.last_dma.get(o.semkey)
            if prev is not None and all(prev is not d for d in o.deps):
                o.deps.append(prev)
                prev.signal = True
            self.last_dma[o.semkey] = o
        self.ops[eng].append(o)
        self.all_ops.append(o)
        return o

    def pe(self, fn, reads=(), writes=()):
        return self.op("pe", fn, reads, writes)

    def act(self, fn, reads=(), writes=()):
        return self.op("act", fn, reads, writes)

    def dve(self, fn, reads=(), writes=()):
        return self.op("dve", fn, reads, writes)

    def pool(self, fn, reads=(), writes=()):
        return self.op("pool", fn, reads, writes)

    def dma(self, fn, reads=(), writes=(), eng="sp", semkey=None):
        return self.op(eng, fn, reads, writes, dma=True, semkey=semkey)

    def barrier(self):
        lasts = []
        for e in ENGS:
            for o_ in reversed(self.ops[e]):
                if o_.fn is not None and not o_.nb:
                    lasts.append(o_)
                    break
        pend = [o for o in self.all_ops if o.dma and not o.signal and not o.nb]
        keep = {k: w for k, w in self.last_w.items() if w.nb}
        self.last_w = keep
        self.readers = {}
        for e in ENGS:
            o = Op(e, None)
            for d in lasts + pend:
                o.deps.append(d)
                d.signal = True
            self.ops[e].append(o)
            self.all_ops.append(o)

    def emit(self):
        nc = self.nc
        fin = Op("sp", None)
        for o in self.all_ops:
            if o.dma and not o.signal:
                fin.deps.append(o)
                o.signal = True
        cnt = {}
        keys = []
        for o in self.all_ops:
            if not o.signal:
                continue
            key = o.semkey if o.dma else ("eng", o.eng, o.epoch)
            if key not in cnt:
                cnt[key] = 0
                keys.append(key)
            cnt[key] += (16 if (o.dma and not o.cc) else 1)
            o.sigval = cnt[key]
        print("[mk] ops: " + ", ".join(f"{e}={len(self.ops[e])}" for e in ENGS) +
              f"; sems={len(cnt)}; maxval={max(cnt.values()) if cnt else 0}", flush=True)
        with contextlib.ExitStack() as st:
            st.enter_context(nc.allow_non_contiguous_dma(reason="small strided layout transfers"))
            sems = {}
            for i, key in enumerate(keys):
                sems[key] = st.enter_context(nc.semaphore(f"s{i}"))
            block = st.enter_context(nc.Block())

            def run(engname, eng):
                waited = {}
                lst = list(self.ops[engname])
                if engname == "sp":
                    lst = lst + [fin]
                for o in lst:
                    need = {}
                    for d in o.deps:
                        key = d.semkey if d.dma else ("eng", d.eng, d.epoch)
                        if d.sigval > need.get(key, 0):
                            need[key] = d.sigval
                    for key, v in need.items():
                        if waited.get(key, 0) >= v:
                            continue
                        eng.wait_ge(sems[key], v)
                        waited[key] = v
                    if o.fn is None:
                        continue
                    inst = o.fn(eng)
                    if o.signal:
                        key = o.semkey if o.dma else ("eng", o.eng, o.epoch)
                        inst.then_inc(sems[key], 16 if (o.dma and not o.cc) else 1)

            @block.sync
            def _(e):
                run("sp", e)

            @block.scalar
            def _(e):
                run("act", e)

            @block.vector
            def _(e):
                run("dve", e)

            @block.gpsimd
            def _(e):
                run("pool", e)

            @block.tensor
            def _(e):
                run("pe", e)


def col_index(r):
    cols = []
    for i in range(2):
        for hh in range(2):
            cols += list(range(Q_OFF + (4 * r + 2 * i + hh) * 64, Q_OFF + (4 * r + 2 * i + hh + 1) * 64))
    kc = list(range(KA_OFF + r * 64, KA_OFF + (r + 1) * 64))
    cols += kc + kc
    for off in (R_OFF, KR_OFF, VR_OFF):
        for i in range(2):
            cols += list(range(off + (4 * r + 2 * i) * 64, off + (4 * r + 2 * i + 2) * 64))
    cols += list(range(WD_OFF, WD_OFF + 64)) + list(range(AD_OFF, AD_OFF + 64))
    for off in (GA_OFF, GR_OFF):
        for i in range(2):
            cols += list(range(off + (4 * r + 2 * i) * 64, off + (4 * r + 2 * i + 2) * 64))
    cols += list(range(VA_OFF + r * 64, VA_OFF + (r + 1) * 64))
    assert len(cols) == WCOLS
    return np.array(cols)


def wout_row_perm():
    rows = []
    for r in range(4):
        rows += list(range(256 * r, 256 * r + 256))
        rows += list(range(1024 + 256 * r, 1024 + 256 * r + 256))
    return np.array(rows)


def make_consts():
    import ml_dtypes
    bf = ml_dtypes.bfloat16
    c = {}
    p = np.arange(128)
    d = p % 64
    inv_freq = (np.float32(500000.0) ** (-np.arange(8, dtype=np.float32) * np.float32(0.125))).astype(np.float32)
    pos = np.arange(SEQ, dtype=np.float32)
    ang = (pos[None, :] * inv_freq[(d % 8)][:, None]).astype(np.float32)
    rot = (d < 16)[:, None]
    c["cosT"] = np.where(rot, np.cos(ang), 1.0).astype(np.float32)
    c["sinT"] = np.where(rot, np.sin(ang), 0.0).astype(np.float32)
    angs = (np.float32(16384.0) * inv_freq[(d % 8)]).astype(np.float32)
    cs = np.stack([np.where(d < 16, np.cos(angs), 1.0), np.where(d < 16, np.sin(angs), 0.0)], 1)
    c["cs_s"] = cs.astype(np.float32)
    prot = np.zeros((128, 128), np.float32)
    for m in range(128):
        dm = m % 64
        if dm < 8:
            prot[m + 8, m] = -1.0
        elif dm < 16:
            prot[m - 8, m] = 1.0
    c["prot"] = prot.astype(bf)
    c["ident"] = np.eye(128, dtype=np.float32).astype(bf)
    c["identf"] = np.eye(128, dtype=np.float32)
    qi = np.arange(128)[:, None]
    kj = np.arange(256)[None, :]
    valid = (kj >= qi) & (kj <= qi + 128)
    c["maskb"] = np.where(valid, 0.0, NEG).astype(np.float32)
    c["maskb0"] = np.where(valid & (kj >= 128), 0.0, NEG).astype(np.float32)
    row = (np.arange(128) % 64)[:, None]
    col = np.arange(64)[None, :]
    strict = (row < col).astype(np.float32)
    incl = (row <= col).astype(np.float32)
    low = (row > col).astype(np.float32)
    c["maskG"] = np.concatenate([strict, incl, strict, incl, low], 1).astype(np.float32)
    c["id64"] = (row == col).astype(np.float32)
    c["bones"] = ((p[:, None] // 64) == (p[None, :] // 64)).astype(np.float32).astype(bf)
    c["ones"] = np.ones((128, 64), np.float32)
    return c


CONST_DT = {"cosT": F32, "sinT": F32, "cs_s": F32, "prot": BF16, "ident": BF16, "identf": F32,
            "maskb": F32, "maskb0": F32, "maskG": F32, "id64": F32, "bones": BF16, "ones": F32}
NPAR = 7


def build(consts, stop_after=None):
    nc = bass.Bass("TRN2", target_bir_lowering=False)
    P = Prog(nc)

    def din(name, shape, dt=F32):
        return nc.dram_tensor(name, list(shape), dt, kind="ExternalInput").ap()

    def dout(name, shape, dt=F32):
        return nc.dram_tensor(name, list(shape), dt, kind="ExternalOutput").ap()

    def dscr(name, shape, dt=F32):
        return nc.dram_tensor(name, list(shape), dt)

    sb_bytes = [0]

    def sb(name, shape, dt=F32):
        n = 1
        for s in shape[1:]:
            n *= s
        sb_bytes[0] += n * (4 if dt == F32 else 2)
        return nc.alloc_sbuf_tensor(name, list(shape), dt)

    x_own = din("x_own", [1024, D])
    xs_own = din("xs_own", [4, D])
    cT_in = din("cT", [128, 16, 17])
    wada = din("wada", [NL, D, 1536])
    bada = din("bada", [NL, 1536])
    ng_col = din("ng_col", [128, NL, 16])
    ng_row = din("ng_row", [NL, D])
    fg_row = din("fg_row", [1, D])
    w_in = din("w_in", [NL, D, WCOLS])
    mu_col = din("mu_col", [NL, 128, 7])
    prev_s = din("prev_s", [NL, 128, 7, 16])
    rpar = din("rpar", [NL, 128, NPAR, 2])
    loraw = din("loraw", [NL, 128, 2, 128])
    sinks_b = din("sinks_b", [NL, 128, 4])
    w_out = din("w_out", [NL, 512, D])
    ck_in = din("ck_in", [NL, 16, 128, 64])
    cv_in = din("cv_in", [NL, 16, 128, 64])
    st_in = din("st_in", [NL, 16, 4, 64, 64])
    selT_in = din("selT", [16, 4])
    shpar_in = din("shpar", [NL, 64, 193])
    cd = {k: din("c_" + k, v.shape, CONST_DT[k]) for k, v in consts.items()}
    y_own = dout("y_own", [1024, D])
    ys_own = dout("ys_own", [4, D])
    ckp = dout("ckp", [NL, 128, 64])
    cvp = dout("cvp", [NL, 128, 64])
    swp = dout("swp", [NL, 128, 2, 64])
    shp = dout("shp", [NL, 128, 7])
    cks = dout("cks", [NL, 16, 128, 64])
    cvs = dout("cvs", [NL, 16, 128, 64])
    sws = dout("sws", [NL, 16, 4, 64, 64])
    shs = dout("shs", [NL, 128, 7, 16])
    agi_mod = dscr("agi_mod", [17, NL * 1536])
    ago_mod = dscr("ago_mod", [4 * 17, NL * 1536])
    agi_h = [[dscr(f"agi_h{l}_{j}", [128, 2048], BF16) for j in range(8)] for l in range(NL)]
    agi_hs = [dscr(f"agi_hs{l}", [128, 64], BF16) for l in range(NL)]
    ago_hs = [dscr(f"ago_hs{l}", [512, 64], BF16) for l in range(NL)]
    ago_h = [[dscr(f"ago_h{l}_{j}", [4 * 128, 2048], BF16) for j in range(8)] for l in range(NL)]
    agi_z = [dscr(f"agi_z{l}", [128, ZC], BF16) for l in range(NL)]
    ago_z = [dscr(f"ago_z{l}", [4 * 128, ZC], BF16) for l in range(NL)]
    x1_d = dscr("x1_d", [1028, D])
    smod_d = dscr("smod_d", [NL, 3, 4, D])
    smp_d = [dscr(f"smp_d{l}", [4, 16, 9, 64]) for l in range(NL)]
    kn_d = [dscr(f"kn_d{l}", [16, 64]) for l in range(NL)]
    vn_d = [dscr(f"vn_d{l}", [16, 64]) for l in range(NL)]
    zs_d = [dscr(f"zs_d{l}", [4, 16, 2, 64]) for l in range(NL)]
    rs_in = [[dscr(f"rs_in{l}_{q}", [4 * 1024, 1024]) for q in range(2)] for l in range(NL)]
    rss_in = [dscr(f"rss_in{l}", [16, D]) for l in range(NL)]
    rss_out = [dscr(f"rss_out{l}", [4, D]) for l in range(NL)]
    rs_out = [[dscr(f"rs_out{l}_{q}", [1024, 1024]) for q in range(2)] for l in range(NL)]
    RG = [[0, 1, 2, 3], [4, 5, 6, 7]]

    cs = {}
    for k, v in consts.items():
        if k in ("cosT", "sinT"):
            continue
        cs[k] = sb("k_" + k, v.shape, CONST_DT[k])
        P.dma(lambda e, k=k: e.dma_start(out=cs[k][:], in_=cd[k]), writes=["k_" + k])
    WBIG = sb("WBIG", [128, 16 * 2048], BF16)
    SCR_N = 30000
    SCR = sb("SCR", [128, SCR_N], F32)
    modT = sb("modT", [128, NL, 48])
    ngc = sb("ngc", [128, NL, 16])
    Acol = sb("Acol", [128, NL, 16])
    P.dma(lambda e: e.dma_start(out=ngc[:], in_=ng_col), writes=["ngc"])
    epsc = sb("epsc", [128, 2])
    P.pool(lambda e: e.memset(epsc[:, 0:1], 1e-5), writes=["epsc"])
    P.pool(lambda e: e.memset(epsc[:, 1:2], 64e-5), writes=["epsc"])
    Wo = sb("Wo", [128, 4, D], BF16)
    psum = [nc.alloc_psum_tensor(f"ps{i}", [128, 512], F32) for i in range(8)]

    class Carve:
        def __init__(self):
            self.off = 0

        def f32(self, shape):
            n = int(np.prod(shape[1:]))
            ap = SCR[0:shape[0], self.off:self.off + n]
            self.off += n
            assert self.off <= SCR_N, self.off
            return ap, shape

        def bf(self, shape):
            n = int(np.prod(shape[1:]))
            w = (n + 1) // 2
            ap = SCR[0:shape[0], self.off:self.off + w].bitcast(BF16)[:, 0:n]
            self.off += w
            assert self.off <= SCR_N, self.off
            return ap, shape

    def v(t, pat=None, **kw):
        ap, shape = t
        if len(shape) == 2:
            return ap
        names = " ".join(f"d{i}" for i in range(1, len(shape)))
        kws = {f"d{i}": shape[i] for i in range(1, len(shape))}
        return ap.rearrange(f"p ({names}) -> p {names}", **kws)

    cv_ = Carve()
    cT = cv_.f32([128, 16, 17])
    wst = [cv_.f32([128, 1536]) for _ in range(4)]
    badab = cv_.f32([17, NL, 1536])
    modsb = cv_.f32([17, NL, 1536])
    P.dma(lambda e: e.dma_start(out=v(cT), in_=cT_in), writes=["cT"])
    P.act(lambda e: e.activation(out=cT[0], in_=cT[0], func=AF.Silu), reads=["cT"], writes=["cT"])
    for l in range(NL):
        P.dma(lambda e, l=l: e.dma_start(out=v(badab)[:, l, :], in_=bada[l:l + 1, :].partition_broadcast(17)[:, 0, :]),
              writes=["badab"], eng="act", semkey="badab")
    it = 0
    for l in range(NL):
        for k in range(16):
            s = it % 4
            it += 1
            P.dma(lambda e, l=l, k=k, s=s: e.dma_start(out=wst[s][0], in_=wada[l, k * 128:(k + 1) * 128, :]),
                  writes=[f"wst{s}"], eng=("sp" if s % 2 == 0 else "act"), semkey=f"wst{s}")
            for cg in range(3):
                P.pe(lambda e, k=k, s=s, cg=cg: e.matmul(psum[cg][0:17, :], lhsT=v(cT)[:, k, :],
                                                          rhs=wst[s][0][:, cg * 512:(cg + 1) * 512],
                                                          start=(k == 0), stop=(k == 15)),
                     reads=["cT", f"wst{s}"], writes=[f"ps{cg}"])
        for cg in range(3):
            P.dve(lambda e, l=l, cg=cg: e.tensor_tensor(out=v(modsb)[:, l, cg * 512:(cg + 1) * 512], in0=psum[cg][0:17, :],
                                                        in1=v(badab)[:, l, cg * 512:(cg + 1) * 512], op=ALU.add),
                  reads=[f"ps{cg}", "badab"], writes=["modsb"])
    P.dma(lambda e: e.dma_start(out=agi_mod.ap(), in_=modsb[0]), reads=["modsb"], writes=["agi_mod"])
    P.op("pool", lambda e: e.collective_compute("AllGather", ALU.bypass, replica_groups=RG,
                                                 ins=[agi_mod.ap().opt()], outs=[ago_mod.ap().opt()]),
         reads=["agi_mod"], writes=["ago_mod"], cc=True, semkey="cc")
    for l in range(NL):
        for r2 in range(4):
            P.dma(lambda e, l=l, r2=r2: e.dma_start(
                out=modT[:, l, r2 * 12:(r2 + 1) * 12],
                in_=ago_mod.ap()[r2 * 17, l * 1536:(l + 1) * 1536].rearrange("(c p) -> p c", p=128),
                allow_slow_non_contiguous=True), reads=["ago_mod"], writes=["modT"], semkey="modT")
    selT = cv_.f32([16, 4])
    smk = cv_.f32([16, D])
    smo = cv_.f32([4, D])
    P.dma(lambda e: e.dma_start(out=selT[0], in_=selT_in), writes=["selT"])
    SEG = {0: [(0, 0, 1536, 0), (1, 0, 512, 1536)], 1: [(1, 512, 1536, 0), (2, 0, 1024, 1024)], 2: [(2, 1024, 1536, 0), (3, 0, 1536, 512)]}
    for l in range(NL):
        for kind in range(3):
            for (r2, j0, j1, d0) in SEG[kind]:
                P.dma(lambda e, l=l, r2=r2, j0=j0, j1=j1, d0=d0: e.dma_start(
                    out=smk[0][:, d0:d0 + (j1 - j0)], in_=ago_mod.ap()[r2 * 17 + 1:r2 * 17 + 17, l * 1536 + j0:l * 1536 + j1]),
                    reads=["ago_mod"], writes=["smk"], semkey="smk")
            for cg in range(4):
                P.pe(lambda e, cg=cg: e.matmul(psum[cg][0:4, :], lhsT=selT[0], rhs=smk[0][:, cg * 512:(cg + 1) * 512], start=True, stop=True),
                     reads=["selT", "smk"], writes=[f"ps{cg}"])
                P.dve(lambda e, cg=cg: e.tensor_copy(out=smo[0][:, cg * 512:(cg + 1) * 512], in_=psum[cg][0:4, :]),
                      reads=[f"ps{cg}"], writes=["smo", f"ps{cg}"])
            P.dma(lambda e, l=l, kind=kind: e.dma_start(out=smod_d.ap()[l, kind], in_=smo[0]), reads=["smo"], writes=["smod_d"], semkey="smo")
    P.dve(lambda e: e.scalar_tensor_tensor(out=Acol[:], in0=modT[:, :, 16:32], scalar=1.0, in1=ngc[:],
                                           op0=ALU.add, op1=ALU.mult), reads=["modT", "ngc"], writes=["Acol"])
    P.barrier()
    import os
    STOP = os.environ.get("MK_STOP", "")
    if STOP == "A":
        P.emit()
        return nc

    def phase_N_tile(l, t, xt_ap, xkey):
        c2 = Carve()
        c2.off = NOFF
        junk = c2.f32([128, D])
        pp = t % 2
        bufs = [(c2.bf([128, D]), c2.f32([128, 2]), c2.bf([128, 16, 128])) for _ in range(2)]
        xn, ssq, hTt = bufs[pp]
        KJ, KX, KS, KH = f"junk{pp}", f"xn{pp}", f"ssq{pp}", f"hTt{pp}"
        P.act(lambda e: e.activation(out=junk[0], in_=xt_ap, func=AF.Square, accum_out=ssq[0][:, 0:1]),
              reads=[xkey], writes=[KJ, KS])
        P.act(lambda e: e.activation(out=ssq[0][:, 1:2], in_=ssq[0][:, 0:1], func=AF.Sqrt, scale=1.0 / D, bias=epsc[:, 0:1]),
              reads=[KS, "epsc"], writes=[KS])
        P.dve(lambda e: e.reciprocal(out=ssq[0][:, 1:2], in_=ssq[0][:, 1:2]), reads=[KS], writes=[KS])
        P.act(lambda e: e.activation(out=xn[0], in_=xt_ap, func=AF.Copy, scale=ssq[0][:, 1:2]),
              reads=[xkey, KS], writes=[KX])
        for half in range(2):
            pst = psum[4 + half]
            for kk in range(8):
                k = half * 8 + kk
                P.pe(lambda e, k=k, kk=kk, pst=pst: e.transpose(
                    out=pst[:].bitcast(BF16)[:, kk * 128:(kk + 1) * 128], in_=xn[0][:, k * 128:(k + 1) * 128],
                    identity=cs["ident"][:]), reads=[KX, "k_ident"], writes=[f"ps{4 + half}"])
            for kk in range(8):
                k = half * 8 + kk
                if half == 0:
                    P.act(lambda e, k=k, kk=kk, pst=pst, l=l: e.activation(
                        out=v(hTt)[:, k, :], in_=pst[:].bitcast(BF16)[:, kk * 128:(kk + 1) * 128], func=AF.Identity,
                        scale=Acol[:, l, k:k + 1], bias=modT[:, l, k:k + 1]),
                        reads=[f"ps{4 + half}", "Acol", "modT"], writes=[KH + "a"])
                else:
                    P.dve(lambda e, k=k, kk=kk, pst=pst, l=l: e.tensor_scalar(
                        out=v(hTt)[:, k, :], in0=pst[:].bitcast(BF16)[:, kk * 128:(kk + 1) * 128],
                        scalar1=Acol[:, l, k:k + 1], scalar2=modT[:, l, k:k + 1], op0=ALU.mult, op1=ALU.add),
                        reads=[f"ps{4 + half}", "Acol", "modT"], writes=[KH + "b"])
        P.dma(lambda e, l=l, t=t: e.dma_start(out=agi_h[l][t].ap().rearrange("p (k c) -> p k c", k=16), in_=v(hTt)),
              reads=[KH + "a", KH + "b"], writes=[f"agi_h{l}_{t}"], semkey=f"hTt_st{t % 2}")
        P.op("pool", lambda e, l=l, t=t: e.collective_compute("AllGather", ALU.bypass, replica_groups=RG,
                                                           ins=[agi_h[l][t].ap().opt()], outs=[ago_h[l][t].ap().opt()]),
             reads=[f"agi_h{l}_{t}"], writes=[f"ago_h{l}_{t}"], cc=True, semkey="cc", nb=True)

    def phase_N_samples(l, xs_ap, xkey, off):
        c6 = Carve()
        c6.off = off
        As = c6.f32([4, D])
        Bs = c6.f32([4, D])
        jk = c6.f32([4, D])
        hs_ = c6.bf([4, D])
        sq4 = c6.f32([4, 2])
        hTs = c6.bf([128, 16, 4])
        P.dma(lambda e: e.dma_start(out=As[0], in_=smod_d.ap()[l, 1]), reads=["smod_d"], writes=["As"], eng="act")
        P.dma(lambda e: e.dma_start(out=Bs[0], in_=smod_d.ap()[l, 0]), reads=["smod_d"], writes=["Bs"], eng="act")
        P.dma(lambda e: e.dma_start(out=jk[0], in_=ng_row[l:l + 1, :].partition_broadcast(4)[:, 0, :]), writes=["jk"], eng="act")
        P.dve(lambda e: e.scalar_tensor_tensor(out=As[0], in0=As[0], scalar=1.0, in1=jk[0], op0=ALU.add, op1=ALU.mult),
              reads=["As", "jk"], writes=["As"])
        P.act(lambda e: e.activation(out=jk[0], in_=xs_ap, func=AF.Square, accum_out=sq4[0][:, 0:1]), reads=[xkey, "As"], writes=["jk", "sq4"])
        P.act(lambda e: e.activation(out=sq4[0][:, 1:2], in_=sq4[0][:, 0:1], func=AF.Sqrt, scale=1.0 / D, bias=epsc[0:4, 0:1]),
              reads=["sq4", "epsc"], writes=["sq4"])
        P.dve(lambda e: e.reciprocal(out=sq4[0][:, 1:2], in_=sq4[0][:, 1:2]), reads=["sq4"], writes=["sq4"])
        P.act(lambda e: e.activation(out=jk[0], in_=xs_ap, func=AF.Copy, scale=sq4[0][:, 1:2]), reads=[xkey, "sq4"], writes=["jk"])
        P.dve(lambda e: e.tensor_tensor(out=jk[0], in0=jk[0], in1=As[0], op=ALU.mult), reads=["jk", "As"], writes=["jk"])
        P.dve(lambda e: e.tensor_tensor(out=hs_[0], in0=jk[0], in1=Bs[0], op=ALU.add), reads=["jk", "Bs"], writes=["hs_"])
        for k in range(16):
            P.pe(lambda e, k=k: e.transpose(out=psum[6][:].bitcast(BF16)[:, k * 4:(k + 1) * 4], in_=hs_[0][:, k * 128:(k + 1) * 128],
                                            identity=cs["ident"][0:4, 0:4]), reads=["hs_", "k_ident"], writes=["ps6"])
        P.act(lambda e: e.activation(out=hTs[0], in_=psum[6][:].bitcast(BF16)[:, 0:64], func=AF.Copy), reads=["ps6"], writes=["hTs", "ps6"])
        P.dma(lambda e: e.dma_start(out=agi_hs[l].ap().rearrange("p (k c) -> p k c", k=16), in_=v(hTs)),
              reads=["hTs"], writes=[f"agi_hs{l}"], semkey="hTs_st")
        P.op("pool", lambda e: e.collective_compute("AllGather", ALU.bypass, replica_groups=RG,
                                                     ins=[agi_hs[l].ap().opt()], outs=[ago_hs[l].ap().opt()]),
             reads=[f"agi_hs{l}"], writes=[f"ago_hs{l}"], cc=True, semkey="cc", nb=True)

    NOFF = 0
    c0 = Carve()
    xt = [c0.f32([128, D]) for _ in range(2)]
    NOFF = c0.off
    def n_load(t):
        s = t % 2
        P.dma(lambda e, t=t, s=s: e.dma_start(out=xt[s][0], in_=x_own[t * 128:(t + 1) * 128, :]),
              writes=[f"xt{s}"], semkey=f"xt{s}")

    n_load(0)
    for t in range(8):
        s = t % 2
        if t + 1 < 8:
            n_load(t + 1)
        phase_N_tile(0, t, xt[s][0], f"xt{s}")

    zpad = sb("zpad", [128, 2, 64], BF16)
    P.pool(lambda e: e.memset(zpad[:], 0.0), writes=["zpad"])

    def allgather_h(l):
        pass

    cx = Carve()
    cx.off = NOFF + 20000
    xs0 = cx.f32([4, D])
    P.dma(lambda e: e.dma_start(out=xs0[0], in_=xs_own), writes=["xs0"], eng="act")
    phase_N_samples(0, xs0[0], "xs0", NOFF + 8000)
    allgather_h(0)
    P.barrier()
    if STOP == "N":
        P.emit()
        return nc

    def phase_M(l):
        c3 = Carve()
        Win = WBIG[:, 0:16 * WCOLS].rearrange("p (k c) -> p k c", k=16)
        import os
        SKIP = os.environ.get("MK_SKIP", "")
        for k in range(16 if "win" not in SKIP else 0):
            for hf in range(4):
                P.dma(lambda e, k=k, hf=hf: e.dma_start(out=Win[:, k, hf * 464:(hf + 1) * 464],
                                                       in_=w_in[l, k * 128:(k + 1) * 128, hf * 464:(hf + 1) * 464]),
                      writes=[f"Win{k}_{hf}"], eng="pool", semkey=f"Win{(k * 4 + hf) % 4}")
        WK = [f"Win{k}_{hf}" for k in range(16) for hf in range(4)]
        hT = [c3.bf([128, 16, G]) for _ in range(2)]
        cst = [c3.f32([128, 2, G])] * 2
        cur = c3.f32([128, 7, G + 1])
        qb = c3.bf([128, 3, G])
        t1 = c3.f32([128, 3, G])
        t2 = c3.f32([128, 3, G])
        qrot = c3.bf([128, 2, G])
        krotf = c3.f32([128, G])
        kT = c3.bf([128, 128 + G])
        vb = c3.bf([128, 3, 64])
        vf = c3.f32([128, 64])
        kcf = c3.f32([128, 64])
        mu = c3.f32([128, 7])
        gsil2 = [c3.bf([128, 4, G]) for _ in range(2)]
        zT = c3.bf([128, 4, G])
        sc = c3.f32([128, 4, 256])
        yc = c3.f32([128, 8, 64])
        pb = (yc[0].bitcast(BF16)[:, 0:1024], [128, 4, 256])
        pTs = c3.bf([128, 1024])
        attb = c3.bf([128, 4, 64])
        sm = c3.f32([128, 8, 4])
        snk = c3.f32([128, 4])
        rp = c3.f32([128, NPAR, 2])
        loraW = c3.bf([128, 2, 128])
        P.dma(lambda e: e.dma_start(out=v(rp), in_=rpar[l]), writes=["rp"])
        P.dma(lambda e: e.dma_start(out=v(loraW), in_=loraw[l]), writes=["loraW"], eng="pool", semkey="loraW")
        mixed = c3.f32([128, 7, G])
        twad = c3.bf([128, G])
        sig = c3.f32([128, 2, G])
        aa = c3.f32([128, 2, G])
        Ls = c3.f32([128, 2, G])
        eL = c3.f32([128, 2, G])
        eLi = c3.f32([128, 2, G])
        eLp = c3.f32([128, 2, G])
        kkr = c3.f32([128, 2, G])
        sqb = c3.bf([128, 2, G])
        rn = c3.f32([128, 2, G])
        kk = c3.f32([128, 2, G])
        uu = c3.f32([128, 2, G])
        keff = c3.f32([128, 2, G])
        AR = c3.bf([128, 2, 4, 2, 64])
        Bt = c3.bf([128, 2, G])
        Kt = c3.bf([128, 2, G])
        vrb = c3.bf([128, 2, G])
        rkk = c3.bf([128, 2, G])
        bonus = c3.f32([128, 2, G])
        TK = c3.bf([128, 2, 4, 3, 64])
        MS = c3.bf([128, 8, 320])
        PQ = [c3.bf([128, 8, 128]) for _ in range(2)]
        TT = [c3.bf([128, 8, 64]) for _ in range(2)]
        Sst = c3.f32([128, 2, 64])
        Sb = c3.bf([128, 2, 64])
        Wb = c3.bf([128, 2, 64])
        Ub = c3.bf([128, 2, 64])
        Ybuf = c3.f32([128, 4, 2, 64])
        ysq = (sc[0][:, 0:512], [128, 8, 64])
        gs = c3.f32([128, 4, 8])
        yh = c3.bf([128, 8, 64])
        yz = rn
        ob = c3.f32([128, D])
        for i in range(4):
            P.dma(lambda e, i=i: e.dma_start(out=Wo[:, i, :], in_=w_out[l, i * 128:(i + 1) * 128, :]), writes=[f"Wo{i}"], eng="pool",
                  semkey=f"Wo{i % 2}")
        P.pool(lambda e: e.memset(Sst[0], 0.0), writes=["Sst"])
        P.pool(lambda e: e.memset(Sb[0], 0.0), writes=["Sb"])
        P.dma(lambda e: e.dma_start(out=snk[0], in_=sinks_b[l]), writes=["snk"])
        P.dma(lambda e: e.dma_start(out=mu[0], in_=mu_col[l]), writes=["mu"])
        if 'ms' not in SKIP:
            P.pool(lambda e: e.memset(cur[0], 0.0), writes=["cur"])
            P.pool(lambda e: e.memset(kT[0], 0.0), writes=["kT"])
            P.pool(lambda e: e.memset(vb[0], 0.0), writes=["vb"])
        import os
        NGR = int(os.environ.get('MK_NG', NG))
        sched = [("inproj", 0)]
        for g_ in range(NGR):
            sched.append(("att", g_))
            if g_ + 1 < NGR:
                sched.append(("inproj", g_ + 1))
            sched.append(("rw", g_))
        for (sec, g) in sched:
            s = g % 2
            r2, cg = g // 4, (g % 4) * G
            if sec == "inproj":
                for hf in range(2):
                    t_ = 2 * (g % 4) + hf
                    P.dma(lambda e, s=s, r2=r2, hf=hf, t_=t_: e.dma_start(
                        out=v(hT[s])[:, :, hf * 128:(hf + 1) * 128],
                        in_=ago_h[l][t_].ap()[r2 * 128:(r2 + 1) * 128, :].rearrange("p (k c) -> p k c", k=16)),
                        reads=[f"ago_h{l}_{t_}"], writes=[f"hT{s}"], eng=("sp" if hf == 0 else "act"), semkey=f"hT{s}_{hf}")
                P.dma(lambda e, s=s, g=g: e.dma_start(out=v(cst[s])[:, 0, :], in_=cd["cosT"][:, g * G:(g + 1) * G]),
                      writes=["cst0"], eng="act", semkey="cst0")
                P.dma(lambda e, s=s, g=g: e.dma_start(out=v(cst[s])[:, 1, :], in_=cd["sinT"][:, g * G:(g + 1) * G]),
                      writes=["cst0"], eng="act", semkey="cst0")
                for ct in range(NCT if 'mm' not in SKIP else 0):
                    pi = ct % 4
                    for k in range(16):
                        P.pe(lambda e, ct=ct, k=k, s=s, pi=pi: e.matmul(
                            psum[pi][:, 0:G], lhsT=Win[:, k, ct * 128:(ct + 1) * 128], rhs=v(hT[s])[:, k, :],
                            start=(k == 0), stop=(k == 15)), reads=WK + [f"hT{s}"], writes=[f"ps{pi}"])
                    if 'ev' in SKIP:
                        continue
                    if ct < 3:
                        P.act(lambda e, ct=ct, pi=pi: e.activation(out=v(qb)[:, ct, :], in_=psum[pi][:, 0:G], func=AF.Copy),
                              reads=[f"ps{pi}"], writes=["qb", f"ps{pi}"])
                        if 'evd' not in SKIP:
                            P.dve(lambda e, ct=ct, pi=pi, s=s: e.tensor_tensor(out=v(t2)[:, ct, :], in0=psum[pi][:, 0:G],
                                                                             in1=v(cst[s])[:, 0, :], op=ALU.mult),
                                  reads=[f"ps{pi}", "cst0"], writes=["t2", f"ps{pi}"])
                    elif ct < 10:
                        P.act(lambda e, ct=ct, pi=pi: e.activation(out=v(cur)[:, ct - 3, 1:G + 1], in_=psum[pi][:, 0:G],
                                                                    func=AF.Copy), reads=[f"ps{pi}"], writes=["cur"])
                    else:
                        P.act(lambda e, ct=ct, pi=pi, gq=gsil2[g % 2]: e.activation(out=v(gq)[:, ct - 10, :], in_=psum[pi][:, 0:G], func=AF.Silu),
                              reads=[f"ps{pi}"], writes=[f"gsil{g % 2}", f"ps{pi}"])
                for ct in range(3 if 'rope' not in SKIP else 0):
                    pi = 4 + ct % 2
                    P.pe(lambda e, ct=ct, pi=pi: e.matmul(psum[pi][:, 0:G], lhsT=cs["prot"][:], rhs=v(qb)[:, ct, :],
                                                         start=True, stop=True), reads=["qb", "k_prot"], writes=[f"ps{pi}"])
                    P.dve(lambda e, ct=ct, pi=pi, s=s: e.tensor_tensor(out=v(t1)[:, ct, :], in0=psum[pi][:, 0:G],
                                                                     in1=v(cst[s])[:, 1, :], op=ALU.mult),
                          reads=[f"ps{pi}", "cst0"], writes=["t1"])
                P.dve(lambda e: e.tensor_tensor(out=v(qrot), in0=v(t1)[:, 0:2, :], in1=v(t2)[:, 0:2, :], op=ALU.add),
                      reads=["t1", "t2"], writes=["qrot"])
                P.dve(lambda e: e.tensor_tensor(out=krotf[0], in0=v(t1)[:, 2, :], in1=v(t2)[:, 2, :], op=ALU.add),
                      reads=["t1", "t2"], writes=["krotf"])
                P.act(lambda e: e.activation(out=kT[0][:, 128:128 + G], in_=krotf[0], func=AF.Copy),
                      reads=["krotf"], writes=["kT"])
                for bl in range(G // 128 if 'vv' not in SKIP else 0):
                    for k in range(16):
                        P.pe(lambda e, k=k, s=s, bl=bl: e.matmul(
                            psum[6][:, 0:64], lhsT=v(hT[s])[:, k, bl * 128:(bl + 1) * 128], rhs=Win[:, k, NCT * 128:NCT * 128 + 64],
                            start=(k == 0), stop=(k == 15)), reads=WK + [f"hT{s}"], writes=["ps6"])
                    P.act(lambda e, bl=bl: e.activation(out=v(vb)[:, 1 + bl, :], in_=psum[6][:, 0:64], func=AF.Copy),
                          reads=["ps6"], writes=["vb", "ps6"])
                    if g == NGR - 1 and bl == G // 128 - 1:
                        P.dve(lambda e: e.tensor_copy(out=vf[0], in_=psum[6][:, 0:64]), reads=["ps6"], writes=["vf", "ps6"])
                        P.dma(lambda e: e.dma_start(out=cvp[l], in_=vf[0]), reads=["vf"], writes=["cvp"], semkey="cvp")

            if sec == "att":
                SLOT_H = [0, 2, 1, 3]
                for bl in range(G // 128 if 'att' not in SKIP else 0):
                    mk = "maskb0" if (g == 0 and bl == 0) else "maskb"
                    for slot in range(4):
                        h = SLOT_H[slot]
                        i, base = h // 2, (h % 2) * 64
                        bank = 4 + slot // 2
                        P.pe(lambda e, i=i, base=base, bank=bank, slot=slot, bl=bl: e.matmul(
                            psum[bank][:, (slot % 2) * 256:(slot % 2) * 256 + 256],
                            lhsT=v(qrot)[base:base + 64, i, bl * 128:(bl + 1) * 128],
                            rhs=kT[0][base:base + 64, bl * 128:bl * 128 + 256], start=True, stop=True),
                            reads=["qrot", "kT"], writes=[f"ps{bank}"])
                    for bk in range(2):
                        P.dve(lambda e, bk=bk, mk=mk: e.tensor_tensor(
                            out=v(sc)[:, 2 * bk:2 * bk + 2, :], in0=psum[4 + bk][:, :].rearrange("p (a b) -> p a b", a=2),
                            in1=cs[mk][:].unsqueeze(1).to_broadcast([128, 2, 256]), op=ALU.add),
                            reads=[f"ps{4 + bk}", "k_" + mk], writes=["sc", f"ps{4 + bk}"])
                    smv = v(sm)
                    P.dve(lambda e: e.tensor_reduce(out=smv[:, 0, :], in_=v(sc), axis=AX.X, op=ALU.max), reads=["sc"], writes=["sm"])
                    P.dve(lambda e: e.scalar_tensor_tensor(out=smv[:, 1, :], in0=smv[:, 0, :], scalar=0.125, in1=snk[0],
                                                           op0=ALU.mult, op1=ALU.max), reads=["sm", "snk"], writes=["sm"])
                    P.dve(lambda e: e.tensor_scalar_mul(out=smv[:, 2, :], in0=smv[:, 1, :], scalar1=-1.0), reads=["sm"], writes=["sm"])
                    P.dve(lambda e: e.tensor_tensor(out=smv[:, 4, :], in0=snk[0], in1=smv[:, 1, :], op=ALU.subtract),
                          reads=["sm", "snk"], writes=["sm"])
                    for slot in range(4):
                        P.act(lambda e, slot=slot: e.activation(out=v(pb)[:, slot, :], in_=v(sc)[:, slot, :], func=AF.Exp, scale=0.125,
                                                                bias=smv[:, 2, slot:slot + 1], accum_out=smv[:, 3, slot:slot + 1]),
                              reads=["sc", "sm"], writes=["pb", "sm"])
                    P.act(lambda e: e.activation(out=smv[:, 4, :], in_=smv[:, 4, :], func=AF.Exp), reads=["sm"], writes=["sm"])
                    P.dve(lambda e: e.tensor_tensor(out=smv[:, 5, :], in0=smv[:, 3, :], in1=smv[:, 4, :], op=ALU.add), reads=["sm"], writes=["sm"])
                    P.dve(lambda e: e.reciprocal(out=smv[:, 6, :], in_=smv[:, 5, :]), reads=["sm"], writes=["sm"])
                    for slot in range(4):
                        for hf in range(2):
                            P.pe(lambda e, slot=slot, hf=hf: e.transpose(
                                out=psum[6][:].bitcast(BF16)[:, (slot * 2 + hf) * 128:(slot * 2 + hf + 1) * 128],
                                in_=v(pb)[:, slot, hf * 128:(hf + 1) * 128], identity=cs["ident"][:]),
                                reads=["pb", "k_ident"], writes=["ps6"])
                    P.act(lambda e: e.activation(out=pTs[0], in_=psum[6][:].bitcast(BF16), func=AF.Copy),
                          reads=["ps6"], writes=["pTs", "ps6"])
                    for slot in range(4):
                        for hf in range(2):
                            P.pe(lambda e, slot=slot, hf=hf, bl=bl: e.matmul(
                                psum[7][:, slot * 64:(slot + 1) * 64], lhsT=pTs[0][:, (slot * 2 + hf) * 128:(slot * 2 + hf + 1) * 128],
                                rhs=v(vb)[:, bl + hf, :], start=(hf == 0), stop=(hf == 1)),
                                reads=["pTs", "vb"], writes=["ps7"])
                    P.dve(lambda e: e.tensor_tensor(out=v(attb), in0=psum[7][:, 0:256].rearrange("p (a b) -> p a b", a=4),
                                                    in1=smv[:, 6, :].unsqueeze(2).to_broadcast([128, 4, 64]), op=ALU.mult),
                          reads=["ps7", "sm"], writes=["attb", "ps7"])
                    for slot in range(4):
                        h = SLOT_H[slot]
                        i, base = h // 2, (h % 2) * 64
                        P.pe(lambda e, slot=slot, i=i, base=base: e.transpose(
                            out=psum[4][:].bitcast(BF16)[base:base + 64, i * 128:(i + 1) * 128],
                            in_=v(attb)[:, slot, :], identity=cs["ident"][:]),
                            reads=["attb", "k_ident"], writes=["ps4"])
                    P.dve(lambda e, bl=bl, gq=gsil2[g % 2]: e.tensor_tensor(
                        out=v(zT)[:, 0:2, bl * 128:(bl + 1) * 128],
                        in0=psum[4][:].bitcast(BF16)[:, 0:256].rearrange("p (a b) -> p a b", a=2),
                        in1=v(gq)[:, 0:2, bl * 128:(bl + 1) * 128], op=ALU.mult),
                        reads=["ps4", f"gsil{g % 2}"], writes=["zT", "ps4"])

                if 'rw' not in SKIP:
                    curn = v(cur)[:, :, 1:G + 1]
                    P.dve(lambda e: e.tensor_tensor(out=v(mixed), in0=v(cur)[:, :, 0:G], in1=curn, op=ALU.subtract),
                           reads=["cur"], writes=["mixed"])
                    P.dve(lambda e: e.tensor_tensor(out=v(mixed), in0=v(mixed), in1=mu[0].unsqueeze(2).to_broadcast([128, 7, G]), op=ALU.mult),
                          reads=["mixed", "mu"], writes=["mixed"])
                    P.dve(lambda e: e.tensor_tensor(out=v(mixed), in0=v(mixed), in1=curn, op=ALU.add), reads=["mixed", "cur"], writes=["mixed"])
                if g == NGR - 1 and 'tr' not in SKIP:
                    P.pe(lambda e: e.transpose(out=psum[7][:, 0:128], in_=krotf[0][:, G - 128:G], identity=cs["identf"][:]),
                         reads=["krotf", "k_identf"], writes=["ps7"])
                    P.dve(lambda e: e.tensor_copy(out=kcf[0], in_=psum[7][:, 0:64]), reads=["ps7"], writes=["kcf"])
                    P.dma(lambda e: e.dma_start(out=ckp[l], in_=kcf[0]), reads=["kcf"], writes=["ckp"], semkey="ckp")
                    P.dma(lambda e: e.dma_start(out=shp[l], in_=v(cur)[:, :, G]), reads=["cur"], writes=["shp"], semkey="shp")
                if 'carry' in SKIP:
                    continue
                P.dve(lambda e: e.tensor_copy(out=v(cur)[:, :, 0:1], in_=v(cur)[:, :, G:G + 1]), reads=["cur"], writes=["cur"])
                P.dve(lambda e: e.tensor_copy(out=kT[0][:, 0:128], in_=kT[0][:, G:G + 128]), reads=["kT"], writes=["kT"])
                P.dve(lambda e: e.tensor_copy(out=v(vb)[:, 0, :], in_=v(vb)[:, G // 128, :]), reads=["vb"], writes=["vb"])
            if sec == "rw":
                if 'rw' not in SKIP:
                    curn = v(cur)[:, :, 1:G + 1]
                    mx = v(mixed)
                    P.act(lambda e: e.activation(out=twad[0][0:64, :], in_=mx[0:64, 6, :], func=AF.Tanh), reads=["mixed"], writes=["twad"])
                    P.act(lambda e: e.activation(out=twad[0][64:128, :], in_=mx[64:128, 6, :], func=AF.Copy), reads=["mixed"], writes=["twad"])
                    P.act(lambda e: e.activation(out=v(vrb), in_=mx[:, 4:6, :], func=AF.Copy), reads=["mixed"], writes=["vrb"])
                    rpv = v(rp)
                    for p in range(2):
                        P.pe(lambda e, p=p: e.matmul(psum[p][:, 0:G], lhsT=v(loraW)[0:64, p, :], rhs=twad[0][0:64, :], start=True, stop=True),
                             reads=["loraW", "twad"], writes=[f"ps{p}"])
                        P.act(lambda e, p=p: e.activation(out=v(sig)[:, p, :], in_=psum[p][:, 0:G], func=AF.Sigmoid, bias=rpv[:, 0, p:p + 1]),
                              reads=[f"ps{p}", "rp"], writes=["sig", f"ps{p}"])
                        P.pe(lambda e, p=p: e.matmul(psum[2 + p][:, 0:G], lhsT=v(loraW)[64:128, p, :], rhs=twad[0][64:128, :], start=True, stop=True),
                             reads=["loraW", "twad"], writes=[f"ps{2 + p}"])
                        P.act(lambda e, p=p: e.activation(out=v(aa)[:, p, :], in_=psum[2 + p][:, 0:G], func=AF.Sigmoid, bias=rpv[:, 1, p:p + 1]),
                              reads=[f"ps{2 + p}", "rp"], writes=["aa", f"ps{2 + p}"])
                    for p in range(2):
                        for c in range(4):
                            P.dve(lambda e, p=p, c=c: e.tensor_tensor_scan(
                                out=v(Ls)[:, p, c * 64:(c + 1) * 64], data0=cs["ones"][:, 0:64], data1=v(sig)[:, p, c * 64:(c + 1) * 64],
                                initial=0.0, op0=ALU.mult, op1=ALU.add), reads=["sig", "k_ones"], writes=["Ls"])
                    P.act(lambda e: e.activation(out=v(eL), in_=v(Ls), func=AF.Exp, scale=-CDEC), reads=["Ls"], writes=["eL"])
                    P.act(lambda e: e.activation(out=v(eLi), in_=v(Ls), func=AF.Exp, scale=CDEC), reads=["Ls"], writes=["eLi"])
                    P.dve(lambda e: e.tensor_tensor(out=v(eLp), in0=v(Ls), in1=v(sig), op=ALU.subtract), reads=["Ls", "sig"], writes=["eLp"])
                    P.act(lambda e: e.activation(out=v(eLp), in_=v(eLp), func=AF.Exp, scale=-CDEC), reads=["eLp"], writes=["eLp"])
                    bc = lambda j: rpv[:, j, :].unsqueeze(2).to_broadcast([128, 2, G])
                    P.dve(lambda e: e.tensor_tensor(out=v(kkr), in0=mx[:, 2:4, :], in1=bc(2), op=ALU.mult), reads=["mixed", "rp"], writes=["kkr"])
                    P.dve(lambda e: e.tensor_tensor(out=v(sqb), in0=v(kkr), in1=v(kkr), op=ALU.mult), reads=["kkr"], writes=["sqb"])
                    for p in range(2):
                        P.pe(lambda e, p=p: e.matmul(psum[p][:, 0:G], lhsT=cs["bones"][:], rhs=v(sqb)[:, p, :], start=True, stop=True),
                             reads=["sqb", "k_bones"], writes=[f"ps{p}"])
                        P.act(lambda e, p=p: e.activation(out=v(rn)[:, p, :], in_=psum[p][:, 0:G], func=AF.Sqrt),
                              reads=[f"ps{p}"], writes=["rn", f"ps{p}"])
                    P.dve(lambda e: e.tensor_scalar_max(out=v(rn), in0=v(rn), scalar1=1e-12), reads=["rn"], writes=["rn"])
                    P.dve(lambda e: e.reciprocal(out=v(rn), in_=v(rn)), reads=["rn"], writes=["rn"])
                    P.dve(lambda e: e.tensor_tensor(out=v(kk), in0=v(kkr), in1=v(rn), op=ALU.mult), reads=["kkr", "rn"], writes=["kk"])
                    P.dve(lambda e: e.scalar_tensor_tensor(out=v(uu), in0=v(aa), scalar=-1.0, in1=bc(3), op0=ALU.add, op1=ALU.mult),
                          reads=["aa", "rp"], writes=["uu"])
                    P.dve(lambda e: e.scalar_tensor_tensor(out=v(keff), in0=v(uu), scalar=1.0, in1=mx[:, 2:4, :], op0=ALU.add, op1=ALU.mult),
                          reads=["uu", "mixed"], writes=["keff"])
                    v4 = lambda t_: v(t_).rearrange("p a (c t) -> p a c t", c=4)
                    ARv = v(AR)
                    P.dve(lambda e: e.scalar_tensor_tensor(out=ARv[:, :, :, 0, :], in0=v4(kk), scalar=-1.0, in1=v4(eLp), op0=ALU.mult, op1=ALU.mult),
                          reads=["kk", "eLp"], writes=["AR"])
                    P.dve(lambda e: e.tensor_tensor(out=ARv[:, :, :, 1, :], in0=mx[:, 0:2, :].rearrange("p a (c t) -> p a c t", c=4), in1=v4(eL), op=ALU.mult),
                           reads=["mixed", "eL"], writes=["AR"])
                    P.dve(lambda e: e.tensor_tensor(out=v(uu), in0=v(kk), in1=v(aa), op=ALU.mult), reads=["kk", "aa", "keff"], writes=["uu"])
                    P.dve(lambda e: e.tensor_tensor(out=v(Bt), in0=v(uu), in1=v(eLi), op=ALU.mult), reads=["uu", "eLi"], writes=["Bt"])
                    P.dve(lambda e: e.tensor_tensor(out=v(Kt), in0=v(keff), in1=v(eLi), op=ALU.mult), reads=["keff", "eLi"], writes=["Kt"])
                    P.dve(lambda e: e.tensor_tensor(out=v(kkr), in0=mx[:, 0:2, :], in1=v(keff), op=ALU.mult), reads=["mixed", "keff", "kk"], writes=["kkr"])
                    P.dve(lambda e: e.tensor_tensor(out=v(rkk), in0=v(kkr), in1=bc(4), op=ALU.mult), reads=["kkr", "rp"], writes=["rkk"])
                    for p in range(2):
                        P.pe(lambda e, p=p: e.matmul(psum[2 + p][:, 0:G], lhsT=cs["bones"][:], rhs=v(rkk)[:, p, :], start=True, stop=True),
                             reads=["rkk", "k_bones"], writes=[f"ps{2 + p}"])
                        P.dve(lambda e, p=p: e.tensor_tensor(out=v(bonus)[:, p, :], in0=psum[2 + p][:, 0:G], in1=mx[:, 4 + p, :], op=ALU.mult),
                              reads=[f"ps{2 + p}", "mixed"], writes=["bonus", f"ps{2 + p}"])
                    TKv = v(TK)
                    for p in range(2):
                        for c in range(4):
                            for wi, src in enumerate((Bt, Kt, vrb)):
                                for hh in range(2):
                                    hs = slice(hh * 64, hh * 64 + 64)
                                    P.pe(lambda e, p=p, c=c, wi=wi, src=src, hs=hs: e.transpose(
                                        out=psum[4 + p][:].bitcast(BF16)[hs, (c * 3 + wi) * 64:(c * 3 + wi + 1) * 64],
                                        in_=v(src)[hs, p, c * 64:(c + 1) * 64], identity=cs["ident"][hs, hs]),
                                        reads=["Bt", "Kt", "vrb", "k_ident"], writes=[f"ps{4 + p}"])
                        P.act(lambda e, p=p: e.activation(out=TKv[:, p, :, :, :].rearrange("p c w t -> p (c w t)"),
                                                          in_=psum[4 + p][:].bitcast(BF16)[:, 0:768], func=AF.Copy),
                              reads=[f"ps{4 + p}"], writes=["TK", f"ps{4 + p}"])
                    MSv = v(MS)
                    for p in range(2):
                        for c in range(4):
                            it = p * 4 + c
                            bank = 6 + it % 2
                            for hh in range(2):
                                hs = slice(hh * 64, hh * 64 + 64)
                                P.pe(lambda e, p=p, c=c, hs=hs, bank=bank: e.matmul(
                                    psum[bank][hs, 0:128], lhsT=v(Bt)[hs, p, c * 64:(c + 1) * 64],
                                    rhs=ARv[hs, p, c, :, :].rearrange("p a t -> p (a t)"), start=True, stop=True),
                                    reads=["Bt", "AR"], writes=[f"ps{bank}"])
                                P.pe(lambda e, p=p, c=c, hs=hs, bank=bank: e.matmul(
                                    psum[bank][hs, 128:256], lhsT=v(Kt)[hs, p, c * 64:(c + 1) * 64],
                                    rhs=ARv[hs, p, c, :, :].rearrange("p a t -> p (a t)"), start=True, stop=True),
                                    reads=["Kt", "AR"], writes=[f"ps{bank}"])
                                P.pe(lambda e, p=p, c=c, hs=hs, bank=bank: e.matmul(
                                    psum[bank][hs, 256:320], lhsT=ARv[hs, p, c, 0, :], rhs=v(Bt)[hs, p, c * 64:(c + 1) * 64],
                                    start=True, stop=True), reads=["Bt", "AR"], writes=[f"ps{bank}"])
                            P.dve(lambda e, it=it, bank=bank: e.tensor_tensor(out=MSv[:, it, :], in0=psum[bank][:, 0:320], in1=cs["maskG"][:], op=ALU.mult),
                                  reads=[f"ps{bank}", "k_maskG"], writes=["MS", f"ps{bank}"])
                    P.dve(lambda e: e.tensor_tensor(out=v(TT[0]), in0=MSv[:, :, 0:64], in1=cs["id64"][:].unsqueeze(1).to_broadcast([128, 8, 64]), op=ALU.add),
                          reads=["MS", "k_id64"], writes=["TT0"])
                    for k in range(1, 6):
                        src_i, dst_i = (k - 1) % 2, k % 2
                        PQs, PQd = v(PQ[src_i]), v(PQ[dst_i])
                        Pprev = (lambda it_: MSv[:, it_, 0:64]) if k == 1 else (lambda it_, PQs=PQs: PQs[:, it_, 0:64])
                        Qprev = (lambda it_: MSv[:, it_, 256:320]) if k == 1 else (lambda it_, PQs=PQs: PQs[:, it_, 64:128])
                        rk_ = ["MS"] if k == 1 else [f"PQ{src_i}"]
                        for it in range(8):
                            bank = it // 4
                            for hh in range(2):
                                hs = slice(hh * 64, hh * 64 + 64)
                                P.pe(lambda e, it=it, hs=hs, bank=bank, Pprev=Pprev, Qprev=Qprev: e.matmul(
                                    psum[bank][hs, (it % 4) * 128:(it % 4) * 128 + 64], lhsT=Qprev(it)[hs, :], rhs=Pprev(it)[hs, :], start=True, stop=True),
                                    reads=rk_, writes=[f"ps{bank}"])
                                P.pe(lambda e, it=it, hs=hs, bank=bank, Pprev=Pprev, Qprev=Qprev: e.matmul(
                                    psum[bank][hs, (it % 4) * 128 + 64:(it % 4) * 128 + 128], lhsT=Pprev(it)[hs, :], rhs=Qprev(it)[hs, :], start=True, stop=True),
                                    reads=rk_, writes=[f"ps{bank}"])
                        for bank in range(2):
                            P.act(lambda e, bank=bank, PQd=PQd: e.activation(out=PQd[:, bank * 4:(bank + 1) * 4, :].rearrange("p a b -> p (a b)"),
                                                                            in_=psum[bank][:, 0:512], func=AF.Copy),
                                  reads=[f"ps{bank}"], writes=[f"PQ{dst_i}", f"ps{bank}"])
                        Ts, Td = v(TT[src_i]), v(TT[dst_i])
                        for it in range(8):
                            for hh in range(2):
                                hs = slice(hh * 64, hh * 64 + 64)
                                P.pe(lambda e, it=it, hs=hs, PQd=PQd, Ts=Ts: e.matmul(
                                    psum[2][hs, it * 64:(it + 1) * 64], lhsT=PQd[hs, it, 64:128], rhs=Ts[hs, it, :], start=True, stop=True),
                                    reads=[f"PQ{dst_i}", f"TT{src_i}"], writes=["ps2"])
                        P.dve(lambda e, Ts=Ts, Td=Td: e.tensor_tensor(out=Td, in0=psum[2][:, 0:512].rearrange("p (a b) -> p a b", a=8), in1=Ts, op=ALU.add),
                              reads=["ps2", f"TT{src_i}"], writes=[f"TT{dst_i}", "ps2"])
                    Tfin = v(TT[1])
                    Sv, Sbv, Wbv, Ubv, Yv = v(Sst), v(Sb), v(Wb), v(Ub), v(Ybuf)
                    for c in range(4):
                        for p in range(2):
                            it = p * 4 + c
                            for hh in range(2):
                                hs = slice(hh * 64, hh * 64 + 64)
                                P.pe(lambda e, p=p, c=c, hs=hs: e.matmul(psum[3][hs, p * 64:(p + 1) * 64], lhsT=ARv[hs, p, c, 0, :], rhs=Sbv[hs, p, :],
                                                                         start=True, stop=False), reads=["AR", "Sb"], writes=["ps3"])
                                P.pe(lambda e, p=p, c=c, hs=hs, it=it: e.matmul(psum[3][hs, p * 64:(p + 1) * 64], lhsT=MSv[hs, it, 128:192], rhs=TKv[hs, p, c, 2, :],
                                                                                start=False, stop=True), reads=["MS", "TK"], writes=["ps3"])
                        P.act(lambda e: e.activation(out=Wbv.rearrange("p a b -> p (a b)"), in_=psum[3][:, 0:128], func=AF.Copy),
                              reads=["ps3"], writes=["Wb", "ps3"])
                        for p in range(2):
                            it = p * 4 + c
                            for hh in range(2):
                                hs = slice(hh * 64, hh * 64 + 64)
                                P.pe(lambda e, p=p, hs=hs, it=it: e.matmul(psum[4][hs, p * 64:(p + 1) * 64], lhsT=Tfin[hs, it, :], rhs=Wbv[hs, p, :],
                                                                           start=True, stop=True), reads=["TT1", "Wb"], writes=["ps4"])
                        P.act(lambda e: e.activation(out=Ubv.rearrange("p a b -> p (a b)"), in_=psum[4][:, 0:128], func=AF.Copy),
                              reads=["ps4"], writes=["Ub", "ps4"])
                        for p in range(2):
                            for hh in range(2):
                                hs = slice(hh * 64, hh * 64 + 64)
                                P.pe(lambda e, p=p, c=c, hs=hs: e.matmul(psum[6][hs, p * 64:(p + 1) * 64], lhsT=TKv[hs, p, c, 0, :], rhs=Ubv[hs, p, :],
                                                                         start=True, stop=False), reads=["TK", "Ub"], writes=["ps6"])
                                P.pe(lambda e, p=p, c=c, hs=hs: e.matmul(psum[6][hs, p * 64:(p + 1) * 64], lhsT=TKv[hs, p, c, 1, :], rhs=TKv[hs, p, c, 2, :],
                                                                         start=False, stop=True), reads=["TK"], writes=["ps6"])
                        for p in range(2):
                            it = p * 4 + c
                            for hh in range(2):
                                hs = slice(hh * 64, hh * 64 + 64)
                                P.pe(lambda e, p=p, c=c, hs=hs: e.matmul(psum[5][hs, p * 64:(p + 1) * 64], lhsT=ARv[hs, p, c, 1, :], rhs=Sbv[hs, p, :],
                                                                         start=True, stop=False), reads=["AR", "Sb"], writes=["ps5"])
                                P.pe(lambda e, p=p, hs=hs, it=it: e.matmul(psum[5][hs, p * 64:(p + 1) * 64], lhsT=MSv[hs, it, 64:128], rhs=Ubv[hs, p, :],
                                                                           start=False, stop=False), reads=["MS", "Ub"], writes=["ps5"])
                                P.pe(lambda e, p=p, c=c, hs=hs, it=it: e.matmul(psum[5][hs, p * 64:(p + 1) * 64], lhsT=MSv[hs, it, 192:256], rhs=TKv[hs, p, c, 2, :],
                                                                                start=False, stop=True), reads=["MS", "TK"], writes=["ps5"])
                        P.dve(lambda e: e.tensor_tensor(out=Sv, in0=psum[6][:, 0:128].rearrange("p (a b) -> p a b", a=2), in1=Sv, op=ALU.add),
                              reads=["ps6", "Sst"], writes=["Sst", "ps6"])
                        P.dve(lambda e, c=c: e.tensor_tensor(out=Sv, in0=Sv, in1=v(eL)[:, :, c * 64 + 63:c * 64 + 64].to_broadcast([128, 2, 64]), op=ALU.mult),
                              reads=["Sst", "eL"], writes=["Sst"])
                        P.act(lambda e: e.activation(out=Sbv, in_=Sv, func=AF.Copy), reads=["Sst"], writes=["Sb"])
                        P.dve(lambda e, c=c: e.tensor_copy(out=Yv[:, c, :, :], in_=psum[5][:, 0:128].rearrange("p (a b) -> p a b", a=2)),
                              reads=["ps5"], writes=["Ybuf", "ps5"])
                    if g == NGR - 1:
                        P.dma(lambda e: e.dma_start(out=swp[l], in_=Sv), reads=["Sst"], writes=["swp"], semkey="swp")
                    gsv = v(gs)
                    Y8 = Yv.rearrange("p c a b -> p (c a) b")
                    P.dve(lambda e: e.tensor_reduce(out=gsv[:, 0, :], in_=Y8, axis=AX.X, op=ALU.add), reads=["Ybuf"], writes=["gs"])
                    P.dve(lambda e: e.tensor_scalar_mul(out=gsv[:, 0, :], in0=gsv[:, 0, :], scalar1=-1.0 / 64), reads=["gs"], writes=["gs"])
                    P.dve(lambda e: e.tensor_tensor(out=v(yc), in0=Y8, in1=gsv[:, 0, :].unsqueeze(2).to_broadcast([128, 8, 64]), op=ALU.add),
                           reads=["Ybuf", "gs"], writes=["pb"])
                    P.dve(lambda e: e.tensor_tensor(out=v(ysq), in0=v(yc), in1=v(yc), op=ALU.mult), reads=["pb"], writes=["sc"])
                    P.dve(lambda e: e.tensor_reduce(out=gsv[:, 1, :], in_=v(ysq), axis=AX.X, op=ALU.add), reads=["sc"], writes=["gs"])
                    P.act(lambda e: e.activation(out=gsv[:, 2, :], in_=gsv[:, 1, :], func=AF.Sqrt, scale=1.0 / 64, bias=epsc[:, 1:2]),
                          reads=["gs", "epsc"], writes=["gs"])
                    P.dve(lambda e: e.reciprocal(out=gsv[:, 3, :], in_=gsv[:, 2, :]), reads=["gs"], writes=["gs"])
                    P.dve(lambda e: e.tensor_tensor(out=v(yh), in0=v(yc), in1=gsv[:, 3, :].unsqueeze(2).to_broadcast([128, 8, 64]), op=ALU.mult),
                          reads=["pb", "gs"], writes=["yh"])
                    for c in range(4):
                        for p in range(2):
                            for hh in range(2):
                                hs = slice(hh * 64, hh * 64 + 64)
                                P.pe(lambda e, c=c, p=p, hs=hs: e.transpose(
                                    out=psum[7][:].bitcast(BF16)[hs, p * G + c * 64:p * G + (c + 1) * 64],
                                    in_=v(yh)[hs, c * 2 + p, :], identity=cs["ident"][hs, hs]), reads=["yh", "k_ident"], writes=["ps7"])
                    for p in range(2):
                        P.act(lambda e, p=p: e.activation(out=v(yz)[:, p, :], in_=psum[7][:].bitcast(BF16)[:, p * G:(p + 1) * G], func=AF.Identity,
                                                          scale=rpv[:, 5, p:p + 1], bias=rpv[:, 6, p:p + 1]),
                              reads=["ps7", "rp"], writes=["rn", "ps7"])
                    P.dve(lambda e: e.tensor_tensor(out=v(yz), in0=v(yz), in1=v(bonus), op=ALU.add), reads=["rn", "bonus"], writes=["rn"])
                    P.dve(lambda e, gq=gsil2[g % 2]: e.tensor_tensor(out=v(zT)[:, 2:4, :], in0=v(yz), in1=v(gq)[:, 2:4, :], op=ALU.mult),
                          reads=["rn", f"gsil{g % 2}"], writes=["zT"])

                for bl in range(G // 128 if 'op' not in SKIP else 0):
                    for cgp in range(4):
                        for i in range(4):
                            P.pe(lambda e, bl=bl, cgp=cgp, i=i: e.matmul(psum[cgp][:, 0:512], lhsT=v(zT)[:, i, bl * 128:(bl + 1) * 128],
                                                                         rhs=Wo[:, i, cgp * 512:(cgp + 1) * 512], start=(i == 0), stop=(i == 3)),
                                 reads=["zT"] + [f"Wo{j}" for j in range(4)], writes=[f"ps{cgp}"])
                        if cgp % 2 == 0:
                            P.act(lambda e, cgp=cgp: e.activation(out=ob[0][:, cgp * 512:(cgp + 1) * 512], in_=psum[cgp][:, 0:512], func=AF.Copy),
                                  reads=[f"ps{cgp}"], writes=["ob", f"ps{cgp}"])
                        else:
                            P.dve(lambda e, cgp=cgp: e.tensor_copy(out=ob[0][:, cgp * 512:(cgp + 1) * 512], in_=psum[cgp][:, 0:512]),
                                  reads=[f"ps{cgp}"], writes=["ob", f"ps{cgp}"])
                    tok0 = g * G + bl * 128
                    row0 = tok0
                    for q in range(2):
                        P.dma(lambda e, q=q, row0=row0: e.dma_start(out=rs_in[l][q].ap()[row0:row0 + 128, :], in_=ob[0][:, q * 1024:(q + 1) * 1024]),
                              reads=["ob"], writes=[f"rs_in{l}"], semkey=f"ob{q}")

    def phase_M_samples(l, Win, WK):
        P.barrier()
        NS = 16
        c7 = Carve()
        hTs16 = c7.bf([128, 16, NS])
        rp = c7.f32([128, NPAR, 2])
        loraW = c7.bf([128, 2, 128])
        mu = c7.f32([128, 7])
        prv = c7.f32([128, 7, NS])
        shpt = c7.f32([64, 193])
        P.dma(lambda e: e.dma_start(out=v(rp), in_=rpar[l]), writes=["s_rp"])
        loraF = c7.f32([128, 2, 128])
        P.dma(lambda e: e.dma_start(out=v(loraF), in_=loraw[l]), writes=["s_loraF"])
        P.act(lambda e: e.activation(out=v(loraW), in_=v(loraF), func=AF.Copy), reads=["s_loraF"], writes=["s_loraW"])
        P.dma(lambda e: e.dma_start(out=mu[0], in_=mu_col[l]), writes=["s_mu"])
        P.dma(lambda e: e.dma_start(out=v(prv), in_=prev_s[l]), writes=["s_prv"])
        P.dma(lambda e: e.dma_start(out=shpt[0], in_=shpar_in[l]), writes=["s_shpt"])
        curS = c7.f32([128, 7, NS])
        mixS = c7.f32([128, 7, NS])
        qf = c7.f32([128, 3, NS])
        qbS = c7.bf([128, 3, NS])
        t1S = c7.f32([128, 3, NS])
        qrotS = c7.f32([128, 3, NS])
        gsS = c7.f32([128, 4, NS])
        vS = c7.f32([16, 64])
        for r2 in range(4):
            P.dma(lambda e, r2=r2: e.dma_start(out=v(hTs16)[:, :, 4 * r2:4 * r2 + 4],
                                               in_=ago_hs[l].ap()[r2 * 128:(r2 + 1) * 128, :].rearrange("p (k c) -> p k c", k=16)),
                  reads=[f"ago_hs{l}"], writes=["s_hT"], eng=("sp" if r2 % 2 == 0 else "act"), semkey=f"s_hT{r2 % 2}")
        for ct in range(NCT):
            pi = ct % 4
            for k in range(16):
                P.pe(lambda e, ct=ct, k=k, pi=pi: e.matmul(psum[pi][:, 0:NS], lhsT=Win[:, k, ct * 128:(ct + 1) * 128], rhs=v(hTs16)[:, k, :],
                                                           start=(k == 0), stop=(k == 15)), reads=["s_hT"], writes=[f"ps{pi}"])
            if ct < 3:
                P.act(lambda e, ct=ct, pi=pi: e.activation(out=v(qf)[:, ct, :], in_=psum[pi][:, 0:NS], func=AF.Copy),
                      reads=[f"ps{pi}"], writes=["s_qf", f"ps{pi}"])
            elif ct < 10:
                P.act(lambda e, ct=ct, pi=pi: e.activation(out=v(curS)[:, ct - 3, :], in_=psum[pi][:, 0:NS], func=AF.Copy),
                      reads=[f"ps{pi}"], writes=["s_cur", f"ps{pi}"])
            else:
                P.act(lambda e, ct=ct, pi=pi: e.activation(out=v(gsS)[:, ct - 10, :], in_=psum[pi][:, 0:NS], func=AF.Silu),
                      reads=[f"ps{pi}"], writes=["s_gs", f"ps{pi}"])
        for k in range(16):
            P.pe(lambda e, k=k: e.matmul(psum[4][0:NS, 0:64], lhsT=v(hTs16)[:, k, :], rhs=Win[:, k, NCT * 128:NCT * 128 + 64],
                                         start=(k == 0), stop=(k == 15)), reads=["s_hT"], writes=["ps4"])
        P.act(lambda e: e.activation(out=vS[0], in_=psum[4][0:NS, 0:64], func=AF.Copy), reads=["ps4"], writes=["s_vS", "ps4"])
        P.dma(lambda e: e.dma_start(out=shs[l], in_=v(curS)), reads=["s_cur"], writes=["shs"], semkey="shs")
        P.act(lambda e: e.activation(out=v(qbS), in_=v(qf), func=AF.Copy), reads=["s_qf"], writes=["s_qb"])
        for ct in range(3):
            P.pe(lambda e, ct=ct: e.matmul(psum[5][:, ct * NS:(ct + 1) * NS], lhsT=cs["prot"][:], rhs=v(qbS)[:, ct, :], start=True, stop=True),
                 reads=["s_qb", "k_prot"], writes=["ps5"])
        P.dve(lambda e: e.tensor_scalar_mul(out=t1S[0], in0=psum[5][:, 0:3 * NS], scalar1=cs["cs_s"][:, 1:2]), reads=["ps5", "k_cs_s"],
              writes=["s_t1", "ps5"])
        P.dve(lambda e: e.scalar_tensor_tensor(out=qrotS[0], in0=qf[0], scalar=cs["cs_s"][:, 0:1], in1=t1S[0], op0=ALU.mult, op1=ALU.add),
              reads=["s_qf", "s_t1", "k_cs_s"], writes=["s_qrot"])
        P.dve(lambda e: e.tensor_tensor(out=v(mixS), in0=v(prv), in1=v(curS), op=ALU.subtract), reads=["s_prv", "s_cur"], writes=["s_mix"])
        P.dve(lambda e: e.tensor_tensor(out=v(mixS), in0=v(mixS), in1=mu[0].unsqueeze(2).to_broadcast([128, 7, NS]), op=ALU.mult),
              reads=["s_mix", "s_mu"], writes=["s_mix"])
        P.dve(lambda e: e.tensor_tensor(out=v(mixS), in0=v(mixS), in1=v(curS), op=ALU.add), reads=["s_mix", "s_cur"], writes=["s_mix"])
        mx = v(mixS)
        rpv = v(rp)
        twadS = c7.bf([128, NS])
        sigS = c7.f32([128, 2, NS])
        aS = c7.f32([128, 2, NS])
        wS = c7.f32([128, 2, NS])
        kkrS = c7.f32([128, 2, NS])
        sqS = c7.bf([128, 2, NS])
        rnS = c7.f32([128, 2, NS])
        kkS = c7.f32([128, 2, NS])
        uuS = c7.f32([128, 2, NS])
        keffS = c7.f32([128, 2, NS])
        P.act(lambda e: e.activation(out=twadS[0][0:64, :], in_=mx[0:64, 6, :], func=AF.Tanh), reads=["s_mix"], writes=["s_twad"])
        P.act(lambda e: e.activation(out=twadS[0][64:128, :], in_=mx[64:128, 6, :], func=AF.Copy), reads=["s_mix"], writes=["s_twad"])
        for p in range(2):
            P.pe(lambda e, p=p: e.matmul(psum[p][:, 0:NS], lhsT=v(loraW)[0:64, p, :], rhs=twadS[0][0:64, :], start=True, stop=True),
                 reads=["s_loraW", "s_twad"], writes=[f"ps{p}"])
            P.act(lambda e, p=p: e.activation(out=v(sigS)[:, p, :], in_=psum[p][:, 0:NS], func=AF.Sigmoid, bias=rpv[:, 0, p:p + 1]),
                  reads=[f"ps{p}", "s_rp"], writes=["s_sig", f"ps{p}"])
            P.pe(lambda e, p=p: e.matmul(psum[2 + p][:, 0:NS], lhsT=v(loraW)[64:128, p, :], rhs=twadS[0][64:128, :], start=True, stop=True),
                 reads=["s_loraW", "s_twad"], writes=[f"ps{2 + p}"])
            P.act(lambda e, p=p: e.activation(out=v(aS)[:, p, :], in_=psum[2 + p][:, 0:NS], func=AF.Sigmoid, bias=rpv[:, 1, p:p + 1]),
                  reads=[f"ps{2 + p}", "s_rp"], writes=["s_a", f"ps{2 + p}"])
        P.act(lambda e: e.activation(out=v(wS), in_=v(sigS), func=AF.Exp, scale=-CDEC), reads=["s_sig"], writes=["s_w"])
        bc = lambda j: rpv[:, j, :].unsqueeze(2).to_broadcast([128, 2, NS])
        P.dve(lambda e: e.tensor_tensor(out=v(kkrS), in0=mx[:, 2:4, :], in1=bc(2), op=ALU.mult), reads=["s_mix", "s_rp"], writes=["s_kkr"])
        P.dve(lambda e: e.tensor_tensor(out=v(sqS), in0=v(kkrS), in1=v(kkrS), op=ALU.mult), reads=["s_kkr"], writes=["s_sq"])
        for p in range(2):
            P.pe(lambda e, p=p: e.matmul(psum[p][:, 0:NS], lhsT=cs["bones"][:], rhs=v(sqS)[:, p, :], start=True, stop=True),
                 reads=["s_sq", "k_bones"], writes=[f"ps{p}"])
            P.act(lambda e, p=p: e.activation(out=v(rnS)[:, p, :], in_=psum[p][:, 0:NS], func=AF.Sqrt), reads=[f"ps{p}"], writes=["s_rn", f"ps{p}"])
        P.dve(lambda e: e.tensor_scalar_max(out=v(rnS), in0=v(rnS), scalar1=1e-12), reads=["s_rn"], writes=["s_rn"])
        P.dve(lambda e: e.reciprocal(out=v(rnS), in_=v(rnS)), reads=["s_rn"], writes=["s_rn"])
        P.dve(lambda e: e.tensor_tensor(out=v(kkS), in0=v(kkrS), in1=v(rnS), op=ALU.mult), reads=["s_kkr", "s_rn"], writes=["s_kk"])
        P.dve(lambda e: e.scalar_tensor_tensor(out=v(uuS), in0=v(aS), scalar=-1.0, in1=bc(3), op0=ALU.add, op1=ALU.mult),
              reads=["s_a", "s_rp"], writes=["s_uu"])
        P.dve(lambda e: e.scalar_tensor_tensor(out=v(keffS), in0=v(uuS), scalar=1.0, in1=mx[:, 2:4, :], op0=ALU.add, op1=ALU.mult),
              reads=["s_uu", "s_mix"], writes=["s_keff"])
        tmS = c7.f32([16, 20, 128])
        srcs = []
        for (t_, key, lo) in ((mixS, "s_mix", 0), (wS, "s_w", 0), (keffS, "s_keff", 0), (mixS, "s_mix", 4), (kkS, "s_kk", 0), (aS, "s_a", 0),
                              (qrotS, "s_qrot", 0), (gsS, "s_gs", 0), (gsS, "s_gs", 2)):
            for p in range(2):
                srcs.append((v(t_)[:, lo + p, :], key))
        srcs.append((v(qrotS)[:, 2, :], "s_qrot"))
        for n0 in range(0, len(srcs), 4):
            bank = 4 + (n0 // 4) % 4
            grp = srcs[n0:n0 + 4]
            for i_, (ap_, key) in enumerate(grp):
                P.pe(lambda e, ap_=ap_, i_=i_, bank=bank: e.transpose(out=psum[bank][0:NS, i_ * 128:(i_ + 1) * 128], in_=ap_, identity=cs["identf"][:]),
                     reads=[key, "k_identf"], writes=[f"ps{bank}"])
            n_ = len(grp)
            evac = P.act if (n0 // 4) % 2 == 0 else P.dve
            if (n0 // 4) % 2 == 0:
                P.act(lambda e, n0=n0, n_=n_, bank=bank: e.activation(out=v(tmS)[:, n0:n0 + n_, :].rearrange("p a b -> p (a b)"),
                                                                   in_=psum[bank][0:NS, 0:n_ * 128], func=AF.Copy),
                      reads=[f"ps{bank}"], writes=["s_tm", f"ps{bank}"])
            else:
                P.dve(lambda e, n0=n0, n_=n_, bank=bank: e.tensor_copy(out=v(tmS)[:, n0:n0 + n_, :].rearrange("p a b -> p (a b)"),
                                                                    in_=psum[bank][0:NS, 0:n_ * 128]),
                      reads=[f"ps{bank}"], writes=["s_tm", f"ps{bank}"])
        for kind in range(9):
            P.dma(lambda e, kind=kind: e.dma_start(out=smp_d[l].ap()[:, :, kind, :].rearrange("h s d -> s h d"),
                                                   in_=v(tmS)[:, 2 * kind:2 * kind + 2, :].rearrange("s t (h d) -> s (t h) d", h=2)),
                  reads=["s_tm"], writes=["smp_d"], eng=("sp" if kind % 2 == 0 else "act"), semkey=f"smp{kind % 2}")
        P.dma(lambda e: e.dma_start(out=kn_d[l].ap(), in_=v(tmS)[:, 18, 0:64]), reads=["s_tm"], writes=["kn_d"], semkey="kn")
        P.dma(lambda e: e.dma_start(out=vn_d[l].ap(), in_=vS[0]), reads=["s_vS"], writes=["vn_d"], semkey="vn")
        P.dma(lambda e: e.dma_start(out=cks[l][:, 0:127, :], in_=ck_in[l][:, 1:128, :]), writes=["cks"], semkey="cks0")
        P.dma(lambda e: e.dma_start(out=cvs[l][:, 0:127, :], in_=cv_in[l][:, 1:128, :]), writes=["cvs"], eng="act", semkey="cvs0")
        P.dma(lambda e: e.dma_start(out=cks[l][:, 127, :], in_=v(tmS)[:, 18, 0:64]), reads=["s_tm"], writes=["cks"], semkey="cks1")
        P.dma(lambda e: e.dma_start(out=cvs[l][:, 127, :], in_=vS[0]), reads=["s_vS"], writes=["cvs"], eng="act", semkey="cvs1")
        SH = c7.f32([64, 9, 64])
        SHv = v(SH)
        P.dma(lambda e: e.dma_start(out=SHv, in_=smp_d[l].ap().rearrange("h s k d -> (h s) k d")), reads=["smp_d"], writes=["s_SH"])
        KV = c7.f32([64, 129, 64])
        KVv = v(KV)
        tmpA = c7.f32([64, 33 * 64])
        scs = c7.f32([64, 129])
        pS = c7.f32([64, 129])
        sm2 = c7.f32([64, 8])
        oS = c7.f32([64, 64])
        prt = c7.f32([64, 64])
        zs = c7.f32([64, 2, 64])
        for h_ in range(4):
            q_ = "sp" if h_ % 2 == 0 else "act"
            P.dma(lambda e, h_=h_: e.dma_start(out=KVv[16 * h_:16 * h_ + 16, 0:128, :], in_=ck_in[l]), writes=["s_KV"], eng=q_, semkey=f"s_KV{h_}")
            P.dma(lambda e, h_=h_: e.dma_start(out=KVv[16 * h_:16 * h_ + 16, 128, :], in_=kn_d[l].ap()), reads=["kn_d"], writes=["s_KV"], eng=q_,
                  semkey=f"s_KV{h_}")
        PCH = [(0, 32), (32, 64), (64, 96), (96, 129)]
        for ci, (a_, b_) in enumerate(PCH):
            n_ = b_ - a_
            tv = tmpA[0][:, 0:n_ * 64].rearrange("p (n d) -> p n d", d=64)
            f_ = P.dve
            f_(lambda e, a_=a_, b_=b_, n_=n_, tv=tv: e.tensor_tensor(out=tv, in0=KVv[:, a_:b_, :], in1=SHv[:, 6, :].unsqueeze(1).to_broadcast([64, n_, 64]),
                                                              op=ALU.mult), reads=["s_KV", "s_SH"], writes=["s_tmpA"])
            P.dve(lambda e, a_=a_, b_=b_, tv=tv: e.tensor_reduce(out=scs[0][:, a_:b_], in_=tv, axis=AX.X, op=ALU.add), reads=["s_tmpA"], writes=["s_scs"])
        s2 = sm2[0]
        snkc = shpt[0][:, 192:193]
        P.dve(lambda e: e.tensor_reduce(out=s2[:, 0:1], in_=scs[0], axis=AX.X, op=ALU.max), reads=["s_scs"], writes=["s_sm2"])
        P.dve(lambda e: e.scalar_tensor_tensor(out=s2[:, 1:2], in0=s2[:, 0:1], scalar=0.125, in1=snkc, op0=ALU.mult, op1=ALU.max),
              reads=["s_sm2", "s_shpt"], writes=["s_sm2"])
        P.dve(lambda e: e.tensor_scalar_mul(out=s2[:, 2:3], in0=s2[:, 1:2], scalar1=-1.0), reads=["s_sm2"], writes=["s_sm2"])
        P.dve(lambda e: e.tensor_tensor(out=s2[:, 4:5], in0=snkc, in1=s2[:, 1:2], op=ALU.subtract), reads=["s_sm2", "s_shpt"], writes=["s_sm2"])
        P.act(lambda e: e.activation(out=pS[0], in_=scs[0], func=AF.Exp, scale=0.125, bias=s2[:, 2:3], accum_out=s2[:, 3:4]),
              reads=["s_scs", "s_sm2"], writes=["s_pS", "s_sm2"])
        P.act(lambda e: e.activation(out=s2[:, 4:5], in_=s2[:, 4:5], func=AF.Exp), reads=["s_sm2"], writes=["s_sm2"])
        P.dve(lambda e: e.tensor_tensor(out=s2[:, 5:6], in0=s2[:, 3:4], in1=s2[:, 4:5], op=ALU.add), reads=["s_sm2"], writes=["s_sm2"])
        P.dve(lambda e: e.reciprocal(out=s2[:, 6:7], in_=s2[:, 5:6]), reads=["s_sm2"], writes=["s_sm2"])
        for h_ in range(4):
            q_ = "sp" if h_ % 2 == 0 else "act"
            P.dma(lambda e, h_=h_: e.dma_start(out=KVv[16 * h_:16 * h_ + 16, 0:128, :], in_=cv_in[l]), reads=["s_scs"], writes=["s_KV"], eng=q_,
                  semkey=f"s_KV{h_}")
            P.dma(lambda e, h_=h_: e.dma_start(out=KVv[16 * h_:16 * h_ + 16, 128, :], in_=vn_d[l].ap()), reads=["vn_d", "s_scs"], writes=["s_KV"],
                  eng=q_, semkey=f"s_KV{h_}")
        for ci, (a_, b_) in enumerate(PCH):
            n_ = b_ - a_
            tv = tmpA[0][:, 0:n_ * 64].rearrange("p (d n) -> p d n", d=64)
            f_ = P.dve
            f_(lambda e, a_=a_, b_=b_, n_=n_, tv=tv: e.tensor_tensor(out=tv, in0=KVv[:, a_:b_, :].rearrange("p n d -> p d n"),
                                                              in1=pS[0][:, a_:b_].unsqueeze(1).to_broadcast([64, 64, n_]), op=ALU.mult),
               reads=["s_KV", "s_pS"], writes=["s_tmpA"])
            if ci == 0:
                P.dve(lambda e, tv=tv: e.tensor_reduce(out=oS[0], in_=tv, axis=AX.X, op=ALU.add), reads=["s_tmpA"], writes=["s_oS"])
            else:
                P.dve(lambda e, tv=tv: e.tensor_reduce(out=prt[0], in_=tv, axis=AX.X, op=ALU.add), reads=["s_tmpA"], writes=["s_prt"])
                P.dve(lambda e: e.tensor_tensor(out=oS[0], in0=oS[0], in1=prt[0], op=ALU.add), reads=["s_oS", "s_prt"], writes=["s_oS"])
        zsv = v(zs)
        P.dve(lambda e: e.tensor_scalar_mul(out=oS[0], in0=oS[0], scalar1=s2[:, 6:7]), reads=["s_oS", "s_sm2"], writes=["s_oS"])
        P.dve(lambda e: e.tensor_tensor(out=zsv[:, 0, :], in0=oS[0], in1=SHv[:, 7, :], op=ALU.mult), reads=["s_oS", "s_SH"], writes=["s_zs"])
        Ssm = c7.f32([64, 64, 64])
        tmpS = c7.f32([64, 64, 64])
        sv = c7.f32([64, 6, 64])
        g2 = c7.f32([64, 8])
        Sv_, Tv_, svv = v(Ssm), v(tmpS), v(sv)
        for h_ in range(4):
            P.dma(lambda e, h_=h_: e.dma_start(out=Sv_[16 * h_:16 * h_ + 16], in_=st_in[l][:, h_]), writes=["s_Ssm"],
                  eng=("sp" if h_ % 2 == 0 else "act"), semkey=f"s_Sl{h_}")
        bi = lambda ap_: ap_.unsqueeze(1).to_broadcast([64, 64, 64])
        bj = lambda ap_: ap_.unsqueeze(2).to_broadcast([64, 64, 64])
        P.dve(lambda e: e.tensor_scalar_mul(out=svv[:, 0, :], in0=SHv[:, 4, :], scalar1=-1.0), reads=["s_SH"], writes=["s_sv0"])
        P.dve(lambda e: e.tensor_tensor(out=svv[:, 1, :], in0=SHv[:, 4, :], in1=SHv[:, 5, :], op=ALU.mult), reads=["s_SH"], writes=["s_sv1"])
        P.dve(lambda e: e.tensor_tensor(out=Tv_, in0=Sv_, in1=bi(svv[:, 0, :]), op=ALU.mult), reads=["s_Ssm", "s_sv0"], writes=["s_tmpS"])
        P.dve(lambda e: e.tensor_reduce(out=svv[:, 2, :], in_=Tv_, axis=AX.X, op=ALU.add), reads=["s_tmpS"], writes=["s_sv2"])
        P.dve(lambda e: e.tensor_tensor(out=Sv_, in0=Sv_, in1=bi(SHv[:, 1, :]), op=ALU.mult), reads=["s_Ssm", "s_SH", "s_tmpS"], writes=["s_Ssm"])
        P.dve(lambda e: e.tensor_tensor(out=Tv_, in0=bj(svv[:, 2, :]), in1=bi(svv[:, 1, :]), op=ALU.mult), reads=["s_sv2", "s_sv1"], writes=["s_tmpS"])
        P.dve(lambda e: e.tensor_tensor(out=Sv_, in0=Sv_, in1=Tv_, op=ALU.add), reads=["s_Ssm", "s_tmpS"], writes=["s_Ssm"])
        P.dve(lambda e: e.tensor_tensor(out=Tv_, in0=bj(SHv[:, 3, :]), in1=bi(SHv[:, 2, :]), op=ALU.mult), reads=["s_SH"], writes=["s_tmpS"])
        P.dve(lambda e: e.tensor_tensor(out=Sv_, in0=Sv_, in1=Tv_, op=ALU.add), reads=["s_Ssm", "s_tmpS"], writes=["s_Ssm"])
        for h_ in range(4):
            P.dma(lambda e, h_=h_: e.dma_start(out=sws[l][:, h_], in_=Sv_[16 * h_:16 * h_ + 16]), reads=["s_Ssm"], writes=["sws"],
                  eng=("sp" if h_ % 2 == 0 else "act"), semkey=f"s_Ss{h_}")
        P.dve(lambda e: e.tensor_tensor(out=Tv_, in0=Sv_, in1=bi(SHv[:, 0, :]), op=ALU.mult), reads=["s_Ssm", "s_SH"], writes=["s_tmpS"])
        P.dve(lambda e: e.tensor_reduce(out=svv[:, 3, :], in_=Tv_, axis=AX.X, op=ALU.add), reads=["s_tmpS"], writes=["s_sv3"])
        g2v = g2[0]
        P.dve(lambda e: e.tensor_reduce(out=g2v[:, 0:1], in_=svv[:, 3, :], axis=AX.X, op=ALU.add), reads=["s_sv3"], writes=["s_g2"])
        P.dve(lambda e: e.tensor_scalar_mul(out=g2v[:, 0:1], in0=g2v[:, 0:1], scalar1=-1.0 / 64), reads=["s_g2"], writes=["s_g2"])
        P.dve(lambda e: e.tensor_scalar_add(out=svv[:, 3, :], in0=svv[:, 3, :], scalar1=g2v[:, 0:1]), reads=["s_sv3", "s_g2"], writes=["s_sv3"])
        P.dve(lambda e: e.tensor_tensor(out=svv[:, 4, :], in0=svv[:, 3, :], in1=svv[:, 3, :], op=ALU.mult), reads=["s_sv3"], writes=["s_sv4"])
        P.dve(lambda e: e.tensor_reduce(out=g2v[:, 1:2], in_=svv[:, 4, :], axis=AX.X, op=ALU.add), reads=["s_sv4"], writes=["s_g2"])
        P.act(lambda e: e.activation(out=g2v[:, 2:3], in_=g2v[:, 1:2], func=AF.Sqrt, scale=1.0 / 64, bias=epsc[0:64, 1:2]),
              reads=["s_g2", "epsc"], writes=["s_g2"])
        P.dve(lambda e: e.reciprocal(out=g2v[:, 3:4], in_=g2v[:, 2:3]), reads=["s_g2"], writes=["s_g2"])
        P.dve(lambda e: e.tensor_scalar_mul(out=svv[:, 3, :], in0=svv[:, 3, :], scalar1=g2v[:, 3:4]), reads=["s_sv3", "s_g2"], writes=["s_sv3"])
        P.dve(lambda e: e.tensor_tensor(out=svv[:, 3, :], in0=svv[:, 3, :], in1=shpt[0][:, 0:64], op=ALU.mult), reads=["s_sv3", "s_shpt"], writes=["s_sv3"])
        P.dve(lambda e: e.tensor_tensor(out=svv[:, 3, :], in0=svv[:, 3, :], in1=shpt[0][:, 64:128], op=ALU.add), reads=["s_sv3", "s_shpt"], writes=["s_sv3"])
        P.dve(lambda e: e.tensor_tensor(out=svv[:, 4, :], in0=SHv[:, 0, :], in1=SHv[:, 2, :], op=ALU.mult), reads=["s_SH", "s_g2"], writes=["s_sv4"])
        P.dve(lambda e: e.tensor_tensor(out=svv[:, 4, :], in0=svv[:, 4, :], in1=shpt[0][:, 128:192], op=ALU.mult), reads=["s_sv4", "s_shpt"], writes=["s_sv4"])
        P.dve(lambda e: e.tensor_reduce(out=g2v[:, 4:5], in_=svv[:, 4, :], axis=AX.X, op=ALU.add), reads=["s_sv4"], writes=["s_g2"])
        P.dve(lambda e: e.tensor_scalar_mul(out=svv[:, 5, :], in0=SHv[:, 3, :], scalar1=g2v[:, 4:5]), reads=["s_SH", "s_g2"], writes=["s_sv5"])
        P.dve(lambda e: e.tensor_tensor(out=svv[:, 3, :], in0=svv[:, 3, :], in1=svv[:, 5, :], op=ALU.add), reads=["s_sv3", "s_sv5"], writes=["s_sv3"])
        P.dve(lambda e: e.tensor_tensor(out=zsv[:, 1, :], in0=svv[:, 3, :], in1=SHv[:, 8, :], op=ALU.mult), reads=["s_sv3", "s_SH"], writes=["s_zs"])
        zst = c7.f32([16, 2, 4, 64])
        zTs = c7.bf([128, 4, NS])
        obS = c7.f32([16, D])
        P.dma(lambda e: e.dma_start(out=zs_d[l].ap().rearrange("h s k d -> (h s) k d"), in_=zsv), reads=["s_zs"], writes=["zs_d"], semkey="zs_d")
        for k_ in range(2):
            P.dma(lambda e, k_=k_: e.dma_start(out=v(zst)[:, k_, :, :], in_=zs_d[l].ap()[:, :, k_, :].rearrange("h s d -> s h d")), reads=["zs_d"], writes=["s_zst"], semkey="zst")
        zstv = v(zst)
        for k_ in range(2):
            for pr in range(2):
                i_ = k_ * 2 + pr
                P.pe(lambda e, k_=k_, pr=pr, i_=i_: e.transpose(out=psum[6][:, i_ * NS:(i_ + 1) * NS],
                                                                in_=zstv[:, k_, 2 * pr:2 * pr + 2, :].rearrange("s h d -> s (h d)"),
                                                                identity=cs["identf"][0:NS, 0:NS]), reads=["s_zst", "k_identf"], writes=["ps6"])
        P.act(lambda e: e.activation(out=zTs[0], in_=psum[6][:, 0:4 * NS], func=AF.Copy), reads=["ps6"], writes=["s_zTs", "ps6"])
        for cgp in range(4):
            for i_ in range(4):
                P.pe(lambda e, cgp=cgp, i_=i_: e.matmul(psum[cgp][0:NS, 0:512], lhsT=v(zTs)[:, i_, :], rhs=Wo[:, i_, cgp * 512:(cgp + 1) * 512],
                                                        start=(i_ == 0), stop=(i_ == 3)), reads=["s_zTs"], writes=[f"ps{cgp}"])
            P.dve(lambda e, cgp=cgp: e.tensor_copy(out=obS[0][:, cgp * 512:(cgp + 1) * 512], in_=psum[cgp][0:NS, 0:512]),
                  reads=[f"ps{cgp}"], writes=["s_obS", f"ps{cgp}"])
        P.dma(lambda e: e.dma_start(out=rss_in[l].ap(), in_=obS[0]), reads=["s_obS"], writes=[f"rss_in{l}"], semkey="obS")
        P.op("pool", lambda e: e.collective_compute("ReduceScatter", ALU.add, replica_groups=RG,
                                                     ins=[rss_in[l].ap().opt()], outs=[rss_out[l].ap().opt()]),
             reads=[f"rss_in{l}"], writes=[f"rss_out{l}"], cc=True, semkey="cc")

    def reduce_scatter(l):
        for q in range(2):
            P.op("pool", lambda e, q=q: e.collective_compute("ReduceScatter", ALU.add, replica_groups=RG,
                                                          ins=[rs_in[l][q].ap().opt()], outs=[rs_out[l][q].ap().opt()], dma_qos="P3"),
                 reads=[f"rs_in{l}"], writes=[f"rs_out{l}"], cc=True, semkey="cc", nb=True)

    def phase_O(l):
        nonlocal NOFF
        c4 = Carve()
        gateB = c4.f32([128, D])
        fgB = c4.f32([128, D]) if l == NL - 1 else None
        ot = [c4.f32([128, D]) for _ in range(2)]
        xo = [c4.f32([128, D]) for _ in range(2)]
        fs = c4.f32([128, 2])
        NOFF = c4.off
        P.dma(lambda e: e.dma_start(out=gateB[0][:, 0:512],
                                    in_=ago_mod.ap()[2 * 17:2 * 17 + 1, l * 1536 + 1024:(l + 1) * 1536].partition_broadcast(128)[:, 0, :]),
              reads=["ago_mod"], writes=["gateB"], semkey="gateB")
        P.dma(lambda e: e.dma_start(out=gateB[0][:, 512:2048],
                                    in_=ago_mod.ap()[3 * 17:3 * 17 + 1, l * 1536:(l + 1) * 1536].partition_broadcast(128)[:, 0, :]),
              reads=["ago_mod"], writes=["gateB"], semkey="gateB")
        if l == NL - 1:
            P.dma(lambda e: e.dma_start(out=fgB[0], in_=fg_row[0:1, :].partition_broadcast(128)[:, 0, :]), writes=["fgB"])
        xsrc = x_own if l == 0 else x1_d.ap()

        def o_loads(t):
            s = t % 2
            for q in range(2):
                P.dma(lambda e, t=t, s=s, q=q: e.dma_start(out=ot[s][0][:, q * 1024:(q + 1) * 1024], in_=rs_out[l][q].ap()[t * 128:(t + 1) * 128, :]),
                      reads=[f"rs_out{l}"], writes=[f"ot{s}"], semkey=f"ot{s}")
            P.dma(lambda e, t=t, s=s: e.dma_start(out=xo[s][0], in_=xsrc[t * 128:(t + 1) * 128, :]),
                  reads=(["x1_d"] if l > 0 else []), writes=[f"xo{s}"], semkey=f"xo{s}")

        o_loads(0)
        for t in range(8):
            s = t % 2
            if t + 1 < 8:
                o_loads(t + 1)
            P.dve(lambda e, s=s: e.tensor_tensor(out=ot[s][0], in0=ot[s][0], in1=gateB[0], op=ALU.mult), reads=[f"ot{s}", "gateB"], writes=[f"ot{s}"])
            P.dve(lambda e, s=s: e.tensor_tensor(out=xo[s][0], in0=xo[s][0], in1=ot[s][0], op=ALU.add), reads=[f"ot{s}", f"xo{s}"], writes=[f"xo{s}"])
            if l < NL - 1:
                P.dma(lambda e, t=t, s=s: e.dma_start(out=x1_d.ap()[t * 128:(t + 1) * 128, :], in_=xo[s][0]), reads=[f"xo{s}"], writes=["x1_d"],
                      semkey=f"x1st{s}")
                phase_N_tile(l + 1, t, xo[s][0], f"xo{s}")
            elif 'fin' not in os.environ.get('MK_SKIP', ''):
                P.act(lambda e, s=s: e.activation(out=ot[s][0], in_=xo[s][0], func=AF.Square, accum_out=fs[0][:, 0:1]),
                      reads=[f"xo{s}"], writes=[f"ot{s}", "fs"])
                P.act(lambda e: e.activation(out=fs[0][:, 1:2], in_=fs[0][:, 0:1], func=AF.Sqrt, scale=1.0 / D, bias=epsc[:, 0:1]),
                      reads=["fs", "epsc"], writes=["fs"])
                P.dve(lambda e: e.reciprocal(out=fs[0][:, 1:2], in_=fs[0][:, 1:2]), reads=["fs"], writes=["fs"])
                P.act(lambda e, s=s: e.activation(out=ot[s][0], in_=xo[s][0], func=AF.Copy, scale=fs[0][:, 1:2]),
                      reads=[f"xo{s}", "fs"], writes=[f"ot{s}"])
                P.dve(lambda e, s=s: e.tensor_tensor(out=ot[s][0], in0=ot[s][0], in1=fgB[0], op=ALU.mult), reads=[f"ot{s}", "fgB"], writes=[f"ot{s}"])
                P.dma(lambda e, t=t, s=s: e.dma_start(out=y_own[t * 128:(t + 1) * 128, :], in_=ot[s][0]), reads=[f"ot{s}"], writes=["y_own"],
                      semkey=f"yst{s}")


        c8 = Carve()
        c8.off = NOFF + (6200 if l < NL - 1 else 100)
        osT = c8.f32([4, D])
        xsT = c8.f32([4, D])
        gS = c8.f32([4, D])
        fq = c8.f32([4, 2])
        P.dma(lambda e: e.dma_start(out=osT[0], in_=rss_out[l].ap()), reads=[f"rss_out{l}"], writes=["osT"], semkey="osT")
        xss = xs_own if l == 0 else x1_d.ap()[1024:1028, :]
        P.dma(lambda e: e.dma_start(out=xsT[0], in_=xss), reads=(["x1_d"] if l > 0 else []), writes=["xsT"], eng="act")
        P.dma(lambda e: e.dma_start(out=gS[0], in_=smod_d.ap()[l, 2]), reads=["smod_d"], writes=["gS"], eng="act")
        P.dve(lambda e: e.tensor_tensor(out=osT[0], in0=osT[0], in1=gS[0], op=ALU.mult), reads=["osT", "gS"], writes=["osT"])
        P.dve(lambda e: e.tensor_tensor(out=xsT[0], in0=xsT[0], in1=osT[0], op=ALU.add), reads=["osT", "xsT"], writes=["xsT"])
        if l < NL - 1:
            P.dma(lambda e: e.dma_start(out=x1_d.ap()[1024:1028, :], in_=xsT[0]), reads=["xsT"], writes=["x1_d"], semkey="x1s")
            phase_N_samples(l + 1, xsT[0], "xsT", c8.off)
        else:
            P.act(lambda e: e.activation(out=osT[0], in_=xsT[0], func=AF.Square, accum_out=fq[0][:, 0:1]), reads=["xsT"], writes=["osT", "fq"])
            P.act(lambda e: e.activation(out=fq[0][:, 1:2], in_=fq[0][:, 0:1], func=AF.Sqrt, scale=1.0 / D, bias=epsc[0:4, 0:1]),
                  reads=["fq", "epsc"], writes=["fq"])
            P.dve(lambda e: e.reciprocal(out=fq[0][:, 1:2], in_=fq[0][:, 1:2]), reads=["fq"], writes=["fq"])
            P.act(lambda e: e.activation(out=osT[0], in_=xsT[0], func=AF.Copy, scale=fq[0][:, 1:2]), reads=["xsT", "fq"], writes=["osT"])
            P.dve(lambda e: e.tensor_tensor(out=osT[0], in0=osT[0], in1=fgB[0][0:4, :], op=ALU.mult), reads=["osT", "fgB"], writes=["osT"])
            P.dma(lambda e: e.dma_start(out=ys_own, in_=osT[0]), reads=["osT"], writes=["ys_own"], semkey="ysst")

    NLR = int(os.environ.get("MK_NL", NL))
    for l in range(NLR):
        P.epoch = l
        phase_M(l)
        reduce_scatter(l)
        if 'smp' not in os.environ.get('MK_SKIP', ''):
            phase_M_samples(l, WBIG[:, 0:16 * WCOLS].rearrange("p (k c) -> p k c", k=16), None)
        P.barrier()
        if STOP == f"R{l}":
            break
        phase_O(l)
        if STOP == f"O{l}a":
            break
        if l < NL - 1:
            allgather_h(l + 1)
        P.barrier()
        if STOP == f"O{l}":
            break
    print(f"[mk] sbuf bytes/partition = {sb_bytes[0]}", flush=True)
    P.emit()
    return nc


def prep_inputs(inp):
    f = lambda k: np.asarray(inp[k], dtype=np.float32)
    x_prompt, x_sample = f("x_prompt"), f("x_sample")
    consts = make_consts()
    w_in_full = f("w_in")
    maps = []
    for c in range(8):
        g, r = c // 4, c % 4
        ci = col_index(r)
        m = {}
        m["x_own"] = np.ascontiguousarray(x_prompt[g, r * 1024:(r + 1) * 1024])
        m["xs_own"] = np.ascontiguousarray(x_sample[16 * g + 4 * r:16 * g + 4 * r + 4, 0])
        cmat = np.concatenate([f("c_prompt")[g:g + 1], f("c_sample")[16 * g:16 * g + 16]], 0)
        m["cT"] = np.ascontiguousarray(cmat.T.reshape(16, 128, 17).transpose(1, 0, 2))
        m["wada"] = np.ascontiguousarray(f("w_ada")[:, :, r * 1536:(r + 1) * 1536])
        m["bada"] = np.ascontiguousarray(f("b_ada")[:, r * 1536:(r + 1) * 1536])
        m["ng_col"] = np.ascontiguousarray(f("norm_g").reshape(NL, 16, 128).transpose(2, 0, 1))
        m["ng_row"] = f("norm_g")
        m["fg_row"] = f("final_g").reshape(1, D)
        m["w_in"] = np.ascontiguousarray(w_in_full[:, :, ci])
        sh_cols = ci[3 * 128:10 * 128] - R_OFF
        m["mu_col"] = np.ascontiguousarray(f("mu_shift")[:, sh_cols].reshape(NL, 7, 128).transpose(0, 2, 1))
        ss = f("state_shift")[:, 16 * g:16 * g + 16][:, :, sh_cols]
        m["prev_s"] = np.ascontiguousarray(ss.reshape(NL, 16, 7, 128).transpose(0, 3, 2, 1))
        own = (np.arange(256) + 256 * r)
        pars = [f("w0"), f("a0"), f("k_k"), f("k_a"), f("r_k").reshape(NL, 1024), f("ln_w"), f("ln_b")]
        rp = np.stack([p_[:, own].reshape(NL, 2, 128) for p_ in pars], 2)
        m["rpar"] = np.ascontiguousarray(rp.transpose(0, 3, 2, 1))
        lw = np.concatenate([f("w_decay")[:, :, own], f("w_iclr")[:, :, own]], 1)
        m["loraw"] = np.ascontiguousarray(lw.reshape(NL, 128, 2, 128))
        sk = f("sinks")[:, 4 * r:4 * r + 4][:, [0, 2, 1, 3]]
        m["sinks_b"] = np.ascontiguousarray(np.broadcast_to(sk[:, None, :], (NL, 128, 4)))
        rows = np.concatenate([np.arange(256 * r, 256 * r + 256), np.arange(1024 + 256 * r, 1024 + 256 * r + 256)])
        m["w_out"] = np.ascontiguousarray(f("w_out")[:, rows, :])
        m["ck_in"] = np.ascontiguousarray(f("cache_k")[:, 16 * g:16 * g + 16, :, r, :])
        m["cv_in"] = np.ascontiguousarray(f("cache_v")[:, 16 * g:16 * g + 16, :, r, :])
        m["st_in"] = np.ascontiguousarray(f("state_wkv")[:, 16 * g:16 * g + 16, 4 * r:4 * r + 4])
        sel = np.zeros((16, 4), np.float32)
        for si in range(4):
            sel[4 * r + si, si] = 1.0
        m["selT"] = sel
        hsel = np.arange(4) + 4 * r
        lw_ = f("ln_w").reshape(NL, 16, 64)[:, hsel]
        lb_ = f("ln_b").reshape(NL, 16, 64)[:, hsel]
        rk_ = f("r_k")[:, hsel]
        sk_ = f("sinks")[:, hsel][:, :, None]
        shp_ = np.concatenate([lw_, lb_, rk_, sk_], 2)
        m["shpar"] = np.ascontiguousarray(np.broadcast_to(shp_[:, :, None], (NL, 4, 16, 193)).reshape(NL, 64, 193))
        for k, v_ in consts.items():
            m["c_" + k] = v_
        maps.append(m)
    return consts, maps


def assemble(res):
    y_prompt = np.zeros((2, SEQ, D), np.float32)
    y_sample = np.zeros((32, 1, D), np.float32)
    ckp = np.zeros((NL, 2, 128, 4, 64), np.float32)
    cvp = np.zeros_like(ckp)
    swp = np.zeros((NL, 2, 16, 64, 64), np.float32)
    shp = np.zeros((NL, 2, 3200), np.float32)
    cks = np.zeros((NL, 32, 128, 4, 64), np.float32)
    cvs = np.zeros_like(cks)
    sws = np.zeros((NL, 32, 16, 64, 64), np.float32)
    shs = np.zeros((NL, 32, 3200), np.float32)
    for c in range(8):
        g, r = c // 4, c % 4
        o = res[c]
        ci = col_index(r)
        sh_cols = ci[3 * 128:10 * 128] - R_OFF
        y_prompt[g, r * 1024:(r + 1) * 1024] = o["y_own"]
        y_sample[16 * g + 4 * r:16 * g + 4 * r + 4, 0] = o["ys_own"]
        ckp[:, g, :, r, :] = o["ckp"]
        cvp[:, g, :, r, :] = o["cvp"]
        t = o["swp"].reshape(NL, 2, 64, 2, 64)
        swp[:, g, 4 * r:4 * r + 4] = t.transpose(0, 3, 1, 4, 2).reshape(NL, 4, 64, 64)
        shp[:, g, sh_cols] = o["shp"].transpose(0, 2, 1).reshape(NL, 896)
        cks[:, 16 * g:16 * g + 16, :, r, :] = o["cks"]
        cvs[:, 16 * g:16 * g + 16, :, r, :] = o["cvs"]
        sws[:, 16 * g:16 * g + 16, 4 * r:4 * r + 4] = o["sws"]
        shs[:, 16 * g:16 * g + 16][:, :, sh_cols] = o["shs"].transpose(0, 3, 2, 1).reshape(NL, 16, 896)
    return (y_prompt, y_sample, ckp, cvp, swp, shp, cks, cvs, sws, shs)


def kernel(**inputs):
    consts, maps = prep_inputs(inputs)
    nc = build(consts)
    res = run_bass_kernel_spmd(nc, maps, core_ids=list(range(8)))
    global LAST_RES
    LAST_RES = res.results
    return assemble(res.results)
```

```python
import contextlib
import numpy as np
import concourse.bass as bass
import concourse.mybir as mybir
from concourse.bass_utils import run_bass_kernel_spmd

F32 = mybir.dt.float32
BF16 = mybir.dt.bfloat16
AF = mybir.ActivationFunctionType
ALU = mybir.AluOpType
AX = mybir.AxisListType

D = 2048
SEQ = 4096
NL = 2
HD = 64
WIN = 128
Q_OFF = 0
KA_OFF = 1024
VA_OFF = 1280
R_OFF = 1536
KR_OFF = 2560
VR_OFF = 3584
WD_OFF = 4608
AD_OFF = 4672
GA_OFF = 4736
GR_OFF = 5760
NCT = 14
WCOLS = NCT * 128 + 64
G = 256
NG = SEQ // G
CH = 64
CDEC = 0.6065306597126334
NEG = -30000.0
HC = 1088
ZC = 4 * SEQ + 64

ENGS = ("pe", "act", "dve", "pool", "sp")


class Op:
    __slots__ = ("eng", "fn", "deps", "signal", "sigval", "dma", "semkey", "cc", "epoch", "nb")

    def __init__(self, eng, fn, dma=False, semkey=None, cc=False):
        self.eng = eng
        self.fn = fn
        self.deps = []
        self.signal = False
        self.sigval = 0
        self.dma = dma
        self.semkey = semkey
        self.cc = cc
        self.epoch = 0
        self.nb = False


class Prog:
    def __init__(self, nc):
        self.nc = nc
        self.ops = {e: [] for e in ENGS}
        self.last_w = {}
        self.readers = {}
        self.all_ops = []
        self.last_dma = {}
        self.epoch = 0

    def op(self, eng, fn, reads=(), writes=(), dma=False, cc=False, semkey=None, nb=False):
        o = Op(eng, fn, dma=dma or cc, semkey=semkey, cc=cc)
        o.nb = nb
        o.epoch = self.epoch if eng == "pe" else 0
        deps = []
        for k in reads:
            w = self.last_w.get(k)
            if w is not None:
                deps.append(w)
        for k in writes:
            w = self.last_w.get(k)
            if w is not None:
                deps.append(w)
            deps.extend(self.readers.get(k, ()))
        seen = set()
        for d in deps:
            if id(d) in seen or d is o:
                continue
            seen.add(id(d))
            if (not d.dma) and d.eng == "pe" and eng == "pe" and not o.dma:
                continue
            o.deps.append(d)
            d.signal = True
        implied = set()
        for d2 in o.deps:
            for x_ in d2.deps:
                implied.add(id(x_))
        if implied:
            o.deps = [d for d in o.deps if id(d) not in implied]
        for k in reads:
            self.readers.setdefault(k, []).append(o)
        for k in writes:
            self.last_w[k] = o
            self.readers[k] = []
        if o.dma:
            if o.semkey is None:
                o.semkey = ("w",) + tuple(writes)
            HOT = ("hT", "Win", "ob", "ot", "xo", "x1st", "hTt_st", "yst", "xt", "wst", "Wo")
            if (not o.cc) and isinstance(o.semkey, str) and o.semkey.rstrip("0123456789_") in HOT:
                pass
            elif not o.cc:
                import zlib
                if eng == "pool":
                    o.semkey = ("dmapool_sw", zlib.crc32(repr(o.semkey).encode()) % 12)
                else:
                    o.semkey = ("dmapool_hw", zlib.crc32(repr(o.semkey).encode()) % 48)
            prev = self.last_dma.get(o.semkey)
            if prev is not None and all(prev is not d for d in o.deps):
                o.deps.append(prev)
                prev.signal = True
            self.last_dma[o.semkey] = o
        self.ops[eng].append(o)
        self.all_ops.append(o)
        return o

    def pe(self, fn, reads=(), writes=()):
        return self.op("pe", fn, reads, writes)

    def act(self, fn, reads=(), writes=()):
        return self.op("act", fn, reads, writes)

    def dve(self, fn, reads=(), writes=()):
        return self.op("dve", fn, reads, writes)

    def pool(self, fn, reads=(), writes=()):
        return self.op("pool", fn, reads, writes)

    def dma(self, fn, reads=(), writes=(), eng="sp", semkey=None):
        return self.op(eng, fn, reads, writes, dma=True, semkey=semkey)

    def barrier(self):
        lasts = []
        for e in ENGS:
            for o_ in reversed(self.ops[e]):
                if o_.fn is not None and not o_.nb:
                    lasts.append(o_)
                    break
        pend = [o for o in self.all_ops if o.dma and not o.signal and not o.nb]
        keep = {k: w for k, w in self.last_w.items() if w.nb}
        self.last_w = keep
        self.readers = {}
        for e in ENGS:
            o = Op(e, None)
            for d in lasts + pend:
                o.deps.append(d)
                d.signal = True
            self.ops[e].append(o)
            self.all_ops.append(o)

    def emit(self):
        nc = self.nc
        fin = Op("sp", None)
        for o in self.all_ops:
            if o.dma and not o.signal:
                fin.deps.append(o)
                o.signal = True
        cnt = {}
        keys = []
        for o in self.all_ops:
            if not o.signal:
                continue
            key = o.semkey if o.dma else ("eng", o.eng, o.epoch)
            if key not in cnt:
                cnt[key] = 0
                keys.append(key)
            cnt[key] += (16 if (o.dma and not o.cc) else 1)
            o.sigval = cnt[key]
        print("[mk] ops: " + ", ".join(f"{e}={len(self.ops[e])}" for e in ENGS) +
              f"; sems={len(cnt)}; maxval={max(cnt.values()) if cnt else 0}", flush=True)
        with contextlib.ExitStack() as st:
            st.enter_context(nc.allow_non_contiguous_dma(reason="small strided layout transfers"))
            sems = {}
            for i, key in enumerate(keys):
                sems[key] = st.enter_context(nc.semaphore(f"s{i}"))
            block = st.enter_context(nc.Block())

            def run(engname, eng):
                waited = {}
                lst = list(self.ops[engname])
                if engname == "sp":
                    lst = lst + [fin]
                for o in lst:
                    need = {}
                    for d in o.deps:
                        key = d.semkey if d.dma else ("eng", d.eng, d.epoch)
                        if d.sigval > need.get(key, 0):
                            need[key] = d.sigval
                    for key, v in need.items():
                        if waited.get(key, 0) >= v:
                            continue
                        eng.wait_ge(sems[key], v)
                        waited[key] = v
                    if o.fn is None:
                        continue
                    inst = o.fn(eng)
                    if o.signal:
                        key = o.semkey if o.dma else ("eng", o.eng, o.epoch)
                        inst.then_inc(sems[key], 16 if (o.dma and not o.cc) else 1)

            @block.sync
            def _(e):
                run("sp", e)

            @block.scalar
            def _(e):
                run("act", e)

            @block.vector
            def _(e):
                run("dve", e)

            @block.gpsimd
            def _(e):
                run("pool", e)

            @block.tensor
            def _(e):
                run("pe", e)


def col_index(r):
    cols = []
    for i in range(2):
        for hh in range(2):
            cols += list(range(Q_OFF + (4 * r + 2 * i + hh) * 64, Q_OFF + (4 * r + 2 * i + hh + 1) * 64))
    kc = list(range(KA_OFF + r * 64, KA_OFF + (r + 1) * 64))
    cols += kc + kc
    for off in (R_OFF, KR_OFF, VR_OFF):
        for i in range(2):
            cols += list(range(off + (4 * r + 2 * i) * 64, off + (4 * r + 2 * i + 2) * 64))
    cols += list(range(WD_OFF, WD_OFF + 64)) + list(range(AD_OFF, AD_OFF + 64))
    for off in (GA_OFF, GR_OFF):
        for i in range(2):
            cols += list(range(off + (4 * r + 2 * i) * 64, off + (4 * r + 2 * i + 2) * 64))
    cols += list(range(VA_OFF + r * 64, VA_OFF + (r + 1) * 64))
    assert len(cols) == WCOLS
    return np.array(cols)


def wout_row_perm():
    rows = []
    for r in range(4):
        rows += list(range(256 * r, 256 * r + 256))
        rows += list(range(1024 + 256 * r, 1024 + 256 * r + 256))
    return np.array(rows)


def make_consts():
    import ml_dtypes
    bf = ml_dtypes.bfloat16
    c = {}
    p = np.arange(128)
    d = p % 64
    inv_freq = (np.float32(500000.0) ** (-np.arange(8, dtype=np.float32) * np.float32(0.125))).astype(np.float32)
    pos = np.arange(SEQ, dtype=np.float32)
    ang = (pos[None, :] * inv_freq[(d % 8)][:, None]).astype(np.float32)
    rot = (d < 16)[:, None]
    c["cosT"] = np.where(rot, np.cos(ang), 1.0).astype(np.float32)
    c["sinT"] = np.where(rot, np.sin(ang), 0.0).astype(np.float32)
    angs = (np.float32(16384.0) * inv_freq[(d % 8)]).astype(np.float32)
    cs = np.stack([np.where(d < 16, np.cos(angs), 1.0), np.where(d < 16, np.sin(angs), 0.0)], 1)
    c["cs_s"] = cs.astype(np.float32)
    prot = np.zeros((128, 128), np.float32)
    for m in range(128):
        dm = m % 64
        if dm < 8:
            prot[m + 8, m] = -1.0
        elif dm < 16:
            prot[m - 8, m] = 1.0
    c["prot"] = prot.astype(bf)
    c["ident"] = np.eye(128, dtype=np.float32).astype(bf)
    c["identf"] = np.eye(128, dtype=np.float32)
    qi = np.arange(128)[:, None]
    kj = np.arange(256)[None, :]
    valid = (kj >= qi) & (kj <= qi + 128)
    c["maskb"] = np.where(valid, 0.0, NEG).astype(np.float32)
    c["maskb0"] = np.where(valid & (kj >= 128), 0.0, NEG).astype(np.float32)
    row = (np.arange(128) % 64)[:, None]
    col = np.arange(64)[None, :]
    strict = (row < col).astype(np.float32)
    incl = (row <= col).astype(np.float32)
    low = (row > col).astype(np.float32)
    c["maskG"] = np.concatenate([strict, incl, strict, incl, low], 1).astype(np.float32)
    c["id64"] = (row == col).astype(np.float32)
    c["bones"] = ((p[:, None] // 64) == (p[None, :] // 64)).astype(np.float32).astype(bf)
    c["ones"] = np.ones((128, 64), np.float32)
    return c


CONST_DT = {"cosT": F32, "sinT": F32, "cs_s": F32, "prot": BF16, "ident": BF16, "identf": F32,
            "maskb": F32, "maskb0": F32, "maskG": F32, "id64": F32, "bones": BF16, "ones": F32}
NPAR = 7


def build(consts, stop_after=None):
    nc = bass.Bass("TRN2", target_bir_lowering=False)
    P = Prog(nc)

    def din(name, shape, dt=F32):
        return nc.dram_tensor(name, list(shape), dt, kind="ExternalInput").ap()

    def dout(name, shape, dt=F32):
        return nc.dram_tensor(name, list(shape), dt, kind="ExternalOutput").ap()

    def dscr(name, shape, dt=F32):
        return nc.dram_tensor(name, list(shape), dt)

    sb_bytes = [0]

    def sb(name, shape, dt=F32):
        n = 1
        for s in shape[1:]:
            n *= s
        sb_bytes[0] += n * (4 if dt == F32 else 2)
        return nc.alloc_sbuf_tensor(name, list(shape), dt)

    x_own = din("x_own", [1024, D])
    xs_own = din("xs_own", [4, D])
    cT_in = din("cT", [128, 16, 17])
    wada = din("wada", [NL, D, 1536])
    bada = din("bada", [NL, 1536])
    ng_col = din("ng_col", [128, NL, 16])
    ng_row = din("ng_row", [NL, D])
    fg_row = din("fg_row", [1, D])
    w_in = din("w_in", [NL, D, WCOLS])
    mu_col = din("mu_col", [NL, 128, 7])
    prev_s = din("prev_s", [NL, 128, 7, 16])
    rpar = din("rpar", [NL, 128, NPAR, 2])
    loraw = din("loraw", [NL, 128, 2, 128])
    sinks_b = din("sinks_b", [NL, 128, 4])
    w_out = din("w_out", [NL, 512, D])
    ck_in = din("ck_in", [NL, 16, 128, 64])
    cv_in = din("cv_in", [NL, 16, 128, 64])
    st_in = din("st_in", [NL, 16, 4, 64, 64])
    selT_in = din("selT", [16, 4])
    shpar_in = din("shpar", [NL, 64, 193])
    cd = {k: din("c_" + k, v.shape, CONST_DT[k]) for k, v in consts.items()}
    y_own = dout("y_own", [1024, D])
    ys_own = dout("ys_own", [4, D])
    ckp = dout("ckp", [NL, 128, 64])
    cvp = dout("cvp", [NL, 128, 64])
    swp = dout("swp", [NL, 128, 2, 64])
    shp = dout("shp", [NL, 128, 7])
    cks = dout("cks", [NL, 16, 128, 64])
    cvs = dout("cvs", [NL, 16, 128, 64])
    sws = dout("sws", [NL, 16, 4, 64, 64])
    shs = dout("shs", [NL, 128, 7, 16])
    agi_mod = dscr("agi_mod", [17, NL * 1536])
    ago_mod = dscr("ago_mod", [4 * 17, NL * 1536])
    agi_h = [[dscr(f"agi_h{l}_{j}", [128, 2048], BF16) for j in range(8)] for l in range(NL)]
    agi_hs = [dscr(f"agi_hs{l}", [128, 64], BF16) for l in range(NL)]
    ago_hs = [dscr(f"ago_hs{l}", [512, 64], BF16) for l in range(NL)]
    ago_h = [[dscr(f"ago_h{l}_{j}", [4 * 128, 2048], BF16) for j in range(8)] for l in range(NL)]
    agi_z = [dscr(f"agi_z{l}", [128, ZC], BF16) for l in range(NL)]
    ago_z = [dscr(f"ago_z{l}", [4 * 128, ZC], BF16) for l in range(NL)]
    x1_d = dscr("x1_d", [1028, D])
    smod_d = dscr("smod_d", [NL, 3, 4, D])
    smp_d = [dscr(f"smp_d{l}", [4, 16, 9, 64]) for l in range(NL)]
    kn_d = [dscr(f"kn_d{l}", [16, 64]) for l in range(NL)]
    vn_d = [dscr(f"vn_d{l}", [16, 64]) for l in range(NL)]
    zs_d = [dscr(f"zs_d{l}", [4, 16, 2, 64]) for l in range(NL)]
    rs_in = [[dscr(f"rs_in{l}_{q}", [4 * 1024, 1024]) for q in range(2)] for l in range(NL)]
    rss_in = [dscr(f"rss_in{l}", [16, D]) for l in range(NL)]
    rss_out = [dscr(f"rss_out{l}", [4, D]) for l in range(NL)]
    rs_out = [[dscr(f"rs_out{l}_{q}", [1024, 1024]) for q in range(2)] for l in range(NL)]
    RG = [[0, 1, 2, 3], [4, 5, 6, 7]]

    cs = {}
    for k, v in consts.items():
        if k in ("cosT", "sinT"):
            continue
        cs[k] = sb("k_" + k, v.shape, CONST_DT[k])
        P.dma(lambda e, k=k: e.dma_start(out=cs[k][:], in_=cd[k]), writes=["k_" + k])
    WBIG = sb("WBIG", [128, 16 * 2048], BF16)
    SCR_N = 30000
    SCR = sb("SCR", [128, SCR_N], F32)
    modT = sb("modT", [128, NL, 48])
    ngc = sb("ngc", [128, NL, 16])
    Acol = sb("Acol", [128, NL, 16])
    P.dma(lambda e: e.dma_start(out=ngc[:], in_=ng_col), writes=["ngc"])
    epsc = sb("epsc", [128, 2])
    P.pool(lambda e: e.memset(epsc[:, 0:1], 1e-5), writes=["epsc"])
    P.pool(lambda e: e.memset(epsc[:, 1:2], 64e-5), writes=["epsc"])
    Wo = sb("Wo", [128, 4, D], BF16)
    psum = [nc.alloc_psum_tensor(f"ps{i}", [128, 512], F32) for i in range(8)]

    class Carve:
        def __init__(self):
            self.off = 0

        def f32(self, shape):
            n = int(np.prod(shape[1:]))
            ap = SCR[0:shape[0], self.off:self.off + n]
            self.off += n
            assert self.off <= SCR_N, self.off
            return ap, shape

        def bf(self, shape):
            n = int(np.prod(shape[1:]))
            w = (n + 1) // 2
            ap = SCR[0:shape[0], self.off:self.off + w].bitcast(BF16)[:, 0:n]
            self.off += w
            assert self.off <= SCR_N, self.off
            return ap, shape

    def v(t, pat=None, **kw):
        ap, shape = t
        if len(shape) == 2:
            return ap
        names = " ".join(f"d{i}" for i in range(1, len(shape)))
        kws = {f"d{i}": shape[i] for i in range(1, len(shape))}
        return ap.rearrange(f"p ({names}) -> p {names}", **kws)

    cv_ = Carve()
    cT = cv_.f32([128, 16, 17])
    wst = [cv_.f32([128, 1536]) for _ in range(4)]
    badab = cv_.f32([17, NL, 1536])
    modsb = cv_.f32([17, NL, 1536])
    P.dma(lambda e: e.dma_start(out=v(cT), in_=cT_in), writes=["cT"])
    P.act(lambda e: e.activation(out=cT[0], in_=cT[0], func=AF.Silu), reads=["cT"], writes=["cT"])
    for l in range(NL):
        P.dma(lambda e, l=l: e.dma_start(out=v(badab)[:, l, :], in_=bada[l:l + 1, :].partition_broadcast(17)[:, 0, :]),
              writes=["badab"], eng="act", semkey="badab")
    it = 0
    for l in range(NL):
        for k in range(16):
            s = it % 4
            it += 1
            P.dma(lambda e, l=l, k=k, s=s: e.dma_start(out=wst[s][0], in_=wada[l, k * 128:(k + 1) * 128, :]),
                  writes=[f"wst{s}"], eng=("sp" if s % 2 == 0 else "act"), semkey=f"wst{s}")
            for cg in range(3):
                P.pe(lambda e, k=k, s=s, cg=cg: e.matmul(psum[cg][0:17, :], lhsT=v(cT)[:, k, :],
                                                          rhs=wst[s][0][:, cg * 512:(cg + 1) * 512],
                                                          start=(k == 0), stop=(k == 15)),
                     reads=["cT", f"wst{s}"], writes=[f"ps{cg}"])
        for cg in range(3):
            P.dve(lambda e, l=l, cg=cg: e.tensor_tensor(out=v(modsb)[:, l, cg * 512:(cg + 1) * 512], in0=psum[cg][0:17, :],
                                                        in1=v(badab)[:, l, cg * 512:(cg + 1) * 512], op=ALU.add),
                  reads=[f"ps{cg}", "badab"], writes=["modsb"])
    P.dma(lambda e: e.dma_start(out=agi_mod.ap(), in_=modsb[0]), reads=["modsb"], writes=["agi_mod"])
    P.op("pool", lambda e: e.collective_compute("AllGather", ALU.bypass, replica_groups=RG,
                                                 ins=[agi_mod.ap().opt()], outs=[ago_mod.ap().opt()]),
         reads=["agi_mod"], writes=["ago_mod"], cc=True, semkey="cc")
    for l in range(NL):
        for r2 in range(4):
            P.dma(lambda e, l=l, r2=r2: e.dma_start(
                out=modT[:, l, r2 * 12:(r2 + 1) * 12],
                in_=ago_mod.ap()[r2 * 17, l * 1536:(l + 1) * 1536].rearrange("(c p) -> p c", p=128),
                allow_slow_non_contiguous=True), reads=["ago_mod"], writes=["modT"], semkey="modT")
    selT = cv_.f32([16, 4])
    smk = cv_.f32([16, D])
    smo = cv_.f32([4, D])
    P.dma(lambda e: e.dma_start(out=selT[0], in_=selT_in), writes=["selT"])
    SEG = {0: [(0, 0, 1536, 0), (1, 0, 512, 1536)], 1: [(1, 512, 1536, 0), (2, 0, 1024, 1024)], 2: [(2, 1024, 1536, 0), (3, 0, 1536, 512)]}
    for l in range(NL):
        for kind in range(3):
            for (r2, j0, j1, d0) in SEG[kind]:
                P.dma(lambda e, l=l, r2=r2, j0=j0, j1=j1, d0=d0: e.dma_start(
                    out=smk[0][:, d0:d0 + (j1 - j0)], in_=ago_mod.ap()[r2 * 17 + 1:r2 * 17 + 17, l * 1536 + j0:l * 1536 + j1]),
                    reads=["ago_mod"], writes=["smk"], semkey="smk")
            for cg in range(4):
                P.pe(lambda e, cg=cg: e.matmul(psum[cg][0:4, :], lhsT=selT[0], rhs=smk[0][:, cg * 512:(cg + 1) * 512], start=True, stop=True),
                     reads=["selT", "smk"], writes=[f"ps{cg}"])
                P.dve(lambda e, cg=cg: e.tensor_copy(out=smo[0][:, cg * 512:(cg + 1) * 512], in_=psum[cg][0:4, :]),
                      reads=[f"ps{cg}"], writes=["smo", f"ps{cg}"])
            P.dma(lambda e, l=l, kind=kind: e.dma_start(out=smod_d.ap()[l, kind], in_=smo[0]), reads=["smo"], writes=["smod_d"], semkey="smo")
    P.dve(lambda e: e.scalar_tensor_tensor(out=Acol[:], in0=modT[:, :, 16:32], scalar=1.0, in1=ngc[:],
                                           op0=ALU.add, op1=ALU.mult), reads=["modT", "ngc"], writes=["Acol"])
    P.barrier()
    import os
    STOP = os.environ.get("MK_STOP", "")
    if STOP == "A":
        P.emit()
        return nc

    def phase_N_tile(l, t, xt_ap, xkey):
        c2 = Carve()
        c2.off = NOFF
        junk = c2.f32([128, D])
        pp = t % 2
        bufs = [(c2.bf([128, D]), c2.f32([128, 2]), c2.bf([128, 16, 128])) for _ in range(2)]
        xn, ssq, hTt = bufs[pp]
        KJ, KX, KS, KH = f"junk{pp}", f"xn{pp}", f"ssq{pp}", f"hTt{pp}"
        P.act(lambda e: e.activation(out=junk[0], in_=xt_ap, func=AF.Square, accum_out=ssq[0][:, 0:1]),
              reads=[xkey], writes=[KJ, KS])
        P.act(lambda e: e.activation(out=ssq[0][:, 1:2], in_=ssq[0][:, 0:1], func=AF.Sqrt, scale=1.0 / D, bias=epsc[:, 0:1]),
              reads=[KS, "epsc"], writes=[KS])
        P.dve(lambda e: e.reciprocal(out=ssq[0][:, 1:2], in_=ssq[0][:, 1:2]), reads=[KS], writes=[KS])
        P.act(lambda e: e.activation(out=xn[0], in_=xt_ap, func=AF.Copy, scale=ssq[0][:, 1:2]),
              reads=[xkey, KS], writes=[KX])
        for half in range(2):
            pst = psum[4 + half]
            for kk in range(8):
                k = half * 8 + kk
                P.pe(lambda e, k=k, kk=kk, pst=pst: e.transpose(
                    out=pst[:].bitcast(BF16)[:, kk * 128:(kk + 1) * 128], in_=xn[0][:, k * 128:(k + 1) * 128],
                    identity=cs["ident"][:]), reads=[KX, "k_ident"], writes=[f"ps{4 + half}"])
            for kk in range(8):
                k = half * 8 + kk
                if half == 0:
                    P.act(lambda e, k=k, kk=kk, pst=pst, l=l: e.activation(
                        out=v(hTt)[:, k, :], in_=pst[:].bitcast(BF16)[:, kk * 128:(kk + 1) * 128], func=AF.Identity,
                        scale=Acol[:, l, k:k + 1], bias=modT[:, l, k:k + 1]),
                        reads=[f"ps{4 + half}", "Acol", "modT"], writes=[KH + "a"])
                else:
                    P.dve(lambda e, k=k, kk=kk, pst=pst, l=l: e.tensor_scalar(
                        out=v(hTt)[:, k, :], in0=pst[:].bitcast(BF16)[:, kk * 128:(kk + 1) * 128],
                        scalar1=Acol[:, l, k:k + 1], scalar2=modT[:, l, k:k + 1], op0=ALU.mult, op1=ALU.add),
                        reads=[f"ps{4 + half}", "Acol", "modT"], writes=[KH + "b"])
        P.dma(lambda e, l=l, t=t: e.dma_start(out=agi_h[l][t].ap().rearrange("p (k c) -> p k c", k=16), in_=v(hTt)),
              reads=[KH + "a", KH + "b"], writes=[f"agi_h{l}_{t}"], semkey=f"hTt_st{t % 2}")
        P.op("pool", lambda e, l=l, t=t: e.collective_compute("AllGather", ALU.bypass, replica_groups=RG,
                                                           ins=[agi_h[l][t].ap().opt()], outs=[ago_h[l][t].ap().opt()]),
             reads=[f"agi_h{l}_{t}"], writes=[f"ago_h{l}_{t}"], cc=True, semkey="cc", nb=True)

    def phase_N_samples(l, xs_ap, xkey, off):
        c6 = Carve()
        c6.off = off
        As = c6.f32([4, D])
        Bs = c6.f32([4, D])
        jk = c6.f32([4, D])
        hs_ = c6.bf([4, D])
        sq4 = c6.f32([4, 2])
        hTs = c6.bf([128, 16, 4])
        P.dma(lambda e: e.dma_start(out=As[0], in_=smod_d.ap()[l, 1]), reads=["smod_d"], writes=["As"], eng="act")
        P.dma(lambda e: e.dma_start(out=Bs[0], in_=smod_d.ap()[l, 0]), reads=["smod_d"], writes=["Bs"], eng="act")
        P.dma(lambda e: e.dma_start(out=jk[0], in_=ng_row[l:l + 1, :].partition_broadcast(4)[:, 0, :]), writes=["jk"], eng="act")
        P.dve(lambda e: e.scalar_tensor_tensor(out=As[0], in0=As[0], scalar=1.0, in1=jk[0], op0=ALU.add, op1=ALU.mult),
              reads=["As", "jk"], writes=["As"])
        P.act(lambda e: e.activation(out=jk[0], in_=xs_ap, func=AF.Square, accum_out=sq4[0][:, 0:1]), reads=[xkey, "As"], writes=["jk", "sq4"])
        P.act(lambda e: e.activation(out=sq4[0][:, 1:2], in_=sq4[0][:, 0:1], func=AF.Sqrt, scale=1.0 / D, bias=epsc[0:4, 0:1]),
              reads=["sq4", "epsc"], writes=["sq4"])
        P.dve(lambda e: e.reciprocal(out=sq4[0][:, 1:2], in_=sq4[0][:, 1:2]), reads=["sq4"], writes=["sq4"])
        P.act(lambda e: e.activation(out=jk[0], in_=xs_ap, func=AF.Copy, scale=sq4[0][:, 1:2]), reads=[xkey, "sq4"], writes=["jk"])
        P.dve(lambda e: e.tensor_tensor(out=jk[0], in0=jk[0], in1=As[0], op=ALU.mult), reads=["jk", "As"], writes=["jk"])
        P.dve(lambda e: e.tensor_tensor(out=hs_[0], in0=jk[0], in1=Bs[0], op=ALU.add), reads=["jk", "Bs"], writes=["hs_"])
        for k in range(16):
            P.pe(lambda e, k=k: e.transpose(out=psum[6][:].bitcast(BF16)[:, k * 4:(k + 1) * 4], in_=hs_[0][:, k * 128:(k + 1) * 128],
                                            identity=cs["ident"][0:4, 0:4]), reads=["hs_", "k_ident"], writes=["ps6"])
        P.act(lambda e: e.activation(out=hTs[0], in_=psum[6][:].bitcast(BF16)[:, 0:64], func=AF.Copy), reads=["ps6"], writes=["hTs", "ps6"])
        P.dma(lambda e: e.dma_start(out=agi_hs[l].ap().rearrange("p (k c) -> p k c", k=16), in_=v(hTs)),
              reads=["hTs"], writes=[f"agi_hs{l}"], semkey="hTs_st")
        P.op("pool", lambda e: e.collective_compute("AllGather", ALU.bypass, replica_groups=RG,
                                                     ins=[agi_hs[l].ap().opt()], outs=[ago_hs[l].ap().opt()]),
             reads=[f"agi_hs{l}"], writes=[f"ago_hs{l}"], cc=True, semkey="cc", nb=True)

    NOFF = 0
    c0 = Carve()
    xt = [c0.f32([128, D]) for _ in range(2)]
    NOFF = c0.off
    def n_load(t):
        s = t % 2
        P.dma(lambda e, t=t, s=s: e.dma_start(out=xt[s][0], in_=x_own[t * 128:(t + 1) * 128, :]),
              writes=[f"xt{s}"], semkey=f"xt{s}")

    n_load(0)
    for t in range(8):
        s = t % 2
        if t + 1 < 8:
            n_load(t + 1)
        phase_N_tile(0, t, xt[s][0], f"xt{s}")

    zpad = sb("zpad", [128, 2, 64], BF16)
    P.pool(lambda e: e.memset(zpad[:], 0.0), writes=["zpad"])

    def allgather_h(l):
        pass

    cx = Carve()
    cx.off = NOFF + 20000
    xs0 = cx.f32([4, D])
    P.dma(lambda e: e.dma_start(out=xs0[0], in_=xs_own), writes=["xs0"], eng="act")
    phase_N_samples(0, xs0[0], "xs0", NOFF + 8000)
    allgather_h(0)
    P.barrier()
    if STOP == "N":
        P.emit()
        return nc

    def phase_M(l):
        c3 = Carve()
        Win = WBIG[:, 0:16 * WCOLS].rearrange("p (k c) -> p k c", k=16)
        import os
        SKIP = os.environ.get("MK_SKIP", "")
        for k in range(16 if "win" not in SKIP else 0):
            for hf in range(4):
                P.dma(lambda e, k=k, hf=hf: e.dma_start(out=Win[:, k, hf * 464:(hf + 1) * 464],
                                                       in_=w_in[l, k * 128:(k + 1) * 128, hf * 464:(hf + 1) * 464]),
                      writes=[f"Win{k}_{hf}"], eng="pool", semkey=f"Win{(k * 4 + hf) % 4}")
        WK = [f"Win{k}_{hf}" for k in range(16) for hf in range(4)]
        hT = [c3.bf([128, 16, G]) for _ in range(2)]
        cst = [c3.f32([128, 2, G])] * 2
        cur = c3.f32([128, 7, G + 1])
        qb = c3.bf([128, 3, G])
        t1 = c3.f32([128, 3, G])
        t2 = c3.f32([128, 3, G])
        qrot = c3.bf([128, 2, G])
        krotf = c3.f32([128, G])
        kT = c3.bf([128, 128 + G])
        vb = c3.bf([128, 3, 64])
        vf = c3.f32([128, 64])
        kcf = c3.f32([128, 64])
        mu = c3.f32([128, 7])
        gsil2 = [c3.bf([128, 4, G]) for _ in range(2)]
        zT = c3.bf([128, 4, G])
        sc = c3.f32([128, 4, 256])
        yc = c3.f32([128, 8, 64])
        pb = (yc[0].bitcast(BF16)[:, 0:1024], [128, 4, 256])
        pTs = c3.bf([128, 1024])
        attb = c3.bf([128, 4, 64])
        sm = c3.f32([128, 8, 4])
        snk = c3.f32([128, 4])
        rp = c3.f32([128, NPAR, 2])
        loraW = c3.bf([128, 2, 128])
        P.dma(lambda e: e.dma_start(out=v(rp), in_=rpar[l]), writes=["rp"])
        P.dma(lambda e: e.dma_start(out=v(loraW), in_=loraw[l]), writes=["loraW"], eng="pool", semkey="loraW")
        mixed = c3.f32([128, 7, G])
        twad = c3.bf([128, G])
        sig = c3.f32([128, 2, G])
        aa = c3.f32([128, 2, G])
        Ls = c3.f32([128, 2, G])
        eL = c3.f32([128, 2, G])
        eLi = c3.f32([128, 2, G])
        eLp = c3.f32([128, 2, G])
        kkr = c3.f32([128, 2, G])
        sqb = c3.bf([128, 2, G])
        rn = c3.f32([128, 2, G])
        kk = c3.f32([128, 2, G])
        uu = c3.f32([128, 2, G])
        keff = c3.f32([128, 2, G])
        AR = c3.bf([128, 2, 4, 2, 64])
        Bt = c3.bf([128, 2, G])
        Kt = c3.bf([128, 2, G])
        vrb = c3.bf([128, 2, G])
        rkk = c3.bf([128, 2, G])
        bonus = c3.f32([128, 2, G])
        TK = c3.bf([128, 2, 4, 3, 64])
        MS = c3.bf([128, 8, 320])
        PQ = [c3.bf([128, 8, 128]) for _ in range(2)]
        TT = [c3.bf([128, 8, 64]) for _ in range(2)]
        Sst = c3.f32([128, 2, 64])
        Sb = c3.bf([128, 2, 64])
        Wb = c3.bf([128, 2, 64])
        Ub = c3.bf([128, 2, 64])
        Ybuf = c3.f32([128, 4, 2, 64])
        ysq = (sc[0][:, 0:512], [128, 8, 64])
        gs = c3.f32([128, 4, 8])
        yh = c3.bf([128, 8, 64])
        yz = rn
        ob = c3.f32([128, D])
        for i in range(4):
            P.dma(lambda e, i=i: e.dma_start(out=Wo[:, i, :], in_=w_out[l, i * 128:(i + 1) * 128, :]), writes=[f"Wo{i}"], eng="pool",
                  semkey=f"Wo{i % 2}")
        P.pool(lambda e: e.memset(Sst[0], 0.0), writes=["Sst"])
        P.pool(lambda e: e.memset(Sb[0], 0.0), writes=["Sb"])
        P.dma(lambda e: e.dma_start(out=snk[0], in_=sinks_b[l]), writes=["snk"])
        P.dma(lambda e: e.dma_start(out=mu[0], in_=mu_col[l]), writes=["mu"])
        if 'ms' not in SKIP:
            P.pool(lambda e: e.memset(cur[0], 0.0), writes=["cur"])
            P.pool(lambda e: e.memset(kT[0], 0.0), writes=["kT"])
            P.pool(lambda e: e.memset(vb[0], 0.0), writes=["vb"])
        import os
        NGR = int(os.environ.get('MK_NG', NG))
        sched = [("inproj", 0)]
        for g_ in range(NGR):
            sched.append(("att", g_))
            if g_ + 1 < NGR:
                sched.append(("inproj", g_ + 1))
            sched.append(("rw", g_))
        for (sec, g) in sched:
            s = g % 2
            r2, cg = g // 4, (g % 4) * G
            if sec == "inproj":
                for hf in range(2):
                    t_ = 2 * (g % 4) + hf
                    P.dma(lambda e, s=s, r2=r2, hf=hf, t_=t_: e.dma_start(
                        out=v(hT[s])[:, :, hf * 128:(hf + 1) * 128],
                        in_=ago_h[l][t_].ap()[r2 * 128:(r2 + 1) * 128, :].rearrange("p (k c) -> p k c", k=16)),
                        reads=[f"ago_h{l}_{t_}"], writes=[f"hT{s}"], eng=("sp" if hf == 0 else "act"), semkey=f"hT{s}_{hf}")
                P.dma(lambda e, s=s, g=g: e.dma_start(out=v(cst[s])[:, 0, :], in_=cd["cosT"][:, g * G:(g + 1) * G]),
                      writes=["cst0"], eng="act", semkey="cst0")
                P.dma(lambda e, s=s, g=g: e.dma_start(out=v(cst[s])[:, 1, :], in_=cd["sinT"][:, g * G:(g + 1) * G]),
                      writes=["cst0"], eng="act", semkey="cst0")
                for ct in range(NCT if 'mm' not in SKIP else 0):
                    pi = ct % 4
                    for k in range(16):
                        P.pe(lambda e, ct=ct, k=k, s=s, pi=pi: e.matmul(
                            psum[pi][:, 0:G], lhsT=Win[:, k, ct * 128:(ct + 1) * 128], rhs=v(hT[s])[:, k, :],
                            start=(k == 0), stop=(k == 15)), reads=WK + [f"hT{s}"], writes=[f"ps{pi}"])
                    if 'ev' in SKIP:
                        continue
                    if ct < 3:
                        P.act(lambda e, ct=ct, pi=pi: e.activation(out=v(qb)[:, ct, :], in_=psum[pi][:, 0:G], func=AF.Copy),
                              reads=[f"ps{pi}"], writes=["qb", f"ps{pi}"])
                        if 'evd' not in SKIP:
                            P.dve(lambda e, ct=ct, pi=pi, s=s: e.tensor_tensor(out=v(t2)[:, ct, :], in0=psum[pi][:, 0:G],
                                                                             in1=v(cst[s])[:, 0, :], op=ALU.mult),
                                  reads=[f"ps{pi}", "cst0"], writes=["t2", f"ps{pi}"])
                    elif ct < 10:
                        P.act(lambda e, ct=ct, pi=pi: e.activation(out=v(cur)[:, ct - 3, 1:G + 1], in_=psum[pi][:, 0:G],
                                                                    func=AF.Copy), reads=[f"ps{pi}"], writes=["cur"])
                    else:
                        P.act(lambda e, ct=ct, pi=pi, gq=gsil2[g % 2]: e.activation(out=v(gq)[:, ct - 10, :], in_=psum[pi][:, 0:G], func=AF.Silu),
                              reads=[f"ps{pi}"], writes=[f"gsil{g % 2}", f"ps{pi}"])
                for ct in range(3 if 'rope' not in SKIP else 0):
                    pi = 4 + ct % 2
                    P.pe(lambda e, ct=ct, pi=pi: e.matmul(psum[pi][:, 0:G], lhsT=cs["prot"][:], rhs=v(qb)[:, ct, :],
                                                         start=True, stop=True), reads=["qb", "k_prot"], writes=[f"ps{pi}"])
                    P.dve(lambda e, ct=ct, pi=pi, s=s: e.tensor_tensor(out=v(t1)[:, ct, :], in0=psum[pi][:, 0:G],
                                                                     in1=v(cst[s])[:, 1, :], op=ALU.mult),
                          reads=[f"ps{pi}", "cst0"], writes=["t1"])
                P.dve(lambda e: e.tensor_tensor(out=v(qrot), in0=v(t1)[:, 0:2, :], in1=v(t2)[:, 0:2, :], op=ALU.add),
                      reads=["t1", "t2"], writes=["qrot"])
                P.dve(lambda e: e.tensor_tensor(out=krotf[0], in0=v(t1)[:, 2, :], in1=v(t2)[:, 2, :], op=ALU.add),
                      reads=["t1", "t2"], writes=["krotf"])
                P.act(lambda e: e.activation(out=kT[0][:, 128:128 + G], in_=krotf[0], func=AF.Copy),
                      reads=["krotf"], writes=["kT"])
                for bl in range(G // 128 if 'vv' not in SKIP else 0):
                    for k in range(16):
                        P.pe(lambda e, k=k, s=s, bl=bl: e.matmul(
                            psum[6][:, 0:64], lhsT=v(hT[s])[:, k, bl * 128:(bl + 1) * 128], rhs=Win[:, k, NCT * 128:NCT * 128 + 64],
                            start=(k == 0), stop=(k == 15)), reads=WK + [f"hT{s}"], writes=["ps6"])
                    P.act(lambda e, bl=bl: e.activation(out=v(vb)[:, 1 + bl, :], in_=psum[6][:, 0:64], func=AF.Copy),
                          reads=["ps6"], writes=["vb", "ps6"])
                    if g == NGR - 1 and bl == G // 128 - 1:
                        P.dve(lambda e: e.tensor_copy(out=vf[0], in_=psum[6][:, 0:64]), reads=["ps6"], writes=["vf", "ps6"])
                        P.dma(lambda e: e.dma_start(out=cvp[l], in_=vf[0]), reads=["vf"], writes=["cvp"], semkey="cvp")

            if sec == "att":
                SLOT_H = [0, 2, 1, 3]
                for bl in range(G // 128 if 'att' not in SKIP else 0):
                    mk = "maskb0" if (g == 0 and bl == 0) else "maskb"
                    for slot in range(4):
                        h = SLOT_H[slot]
                        i, base = h // 2, (h % 2) * 64
                        bank = 4 + slot // 2
                        P.pe(lambda e, i=i, base=base, bank=bank, slot=slot, bl=bl: e.matmul(
                            psum[bank][:, (slot % 2) * 256:(slot % 2) * 256 + 256],
                            lhsT=v(qrot)[base:base + 64, i, bl * 128:(bl + 1) * 128],
                            rhs=kT[0][base:base + 64, bl * 128:bl * 128 + 256], start=True, stop=True),
                            reads=["qrot", "kT"], writes=[f"ps{bank}"])
                    for bk in range(2):
                        P.dve(lambda e, bk=bk, mk=mk: e.tensor_tensor(
                            out=v(sc)[:, 2 * bk:2 * bk + 2, :], in0=psum[4 + bk][:, :].rearrange("p (a b) -> p a b", a=2),
                            in1=cs[mk][:].unsqueeze(1).to_broadcast([128, 2, 256]), op=ALU.add),
                            reads=[f"ps{4 + bk}", "k_" + mk], writes=["sc", f"ps{4 + bk}"])
                    smv = v(sm)
                    P.dve(lambda e: e.tensor_reduce(out=smv[:, 0, :], in_=v(sc), axis=AX.X, op=ALU.max), reads=["sc"], writes=["sm"])
                    P.dve(lambda e: e.scalar_tensor_tensor(out=smv[:, 1, :], in0=smv[:, 0, :], scalar=0.125, in1=snk[0],
                                                           op0=ALU.mult, op1=ALU.max), reads=["sm", "snk"], writes=["sm"])
                    P.dve(lambda e: e.tensor_scalar_mul(out=smv[:, 2, :], in0=smv[:, 1, :], scalar1=-1.0), reads=["sm"], writes=["sm"])
                    P.dve(lambda e: e.tensor_tensor(out=smv[:, 4, :], in0=snk[0], in1=smv[:, 1, :], op=ALU.subtract),
                          reads=["sm", "snk"], writes=["sm"])
                    for slot in range(4):
                        P.act(lambda e, slot=slot: e.activation(out=v(pb)[:, slot, :], in_=v(sc)[:, slot, :], func=AF.Exp, scale=0.125,
                                                                bias=smv[:, 2, slot:slot + 1], accum_out=smv[:, 3, slot:slot + 1]),
                              reads=["sc", "sm"], writes=["pb", "sm"])
                    P.act(lambda e: e.activation(out=smv[:, 4, :], in_=smv[:, 4, :], func=AF.Exp), reads=["sm"], writes=["sm"])
                    P.dve(lambda e: e.tensor_tensor(out=smv[:, 5, :], in0=smv[:, 3, :], in1=smv[:, 4, :], op=ALU.add), reads=["sm"], writes=["sm"])
                    P.dve(lambda e: e.reciprocal(out=smv[:, 6, :], in_=smv[:, 5, :]), reads=["sm"], writes=["sm"])
                    for slot in range(4):
                        for hf in range(2):
                            P.pe(lambda e, slot=slot, hf=hf: e.transpose(
                                out=psum[6][:].bitcast(BF16)[:, (slot * 2 + hf) * 128:(slot * 2 + hf + 1) * 128],
                                in_=v(pb)[:, slot, hf * 128:(hf + 1) * 128], identity=cs["ident"][:]),
                                reads=["pb", "k_ident"], writes=["ps6"])
                    P.act(lambda e: e.activation(out=pTs[0], in_=psum[6][:].bitcast(BF16), func=AF.Copy),
                          reads=["ps6"], writes=["pTs", "ps6"])
                    for slot in range(4):
                        for hf in range(2):
                            P.pe(lambda e, slot=slot, hf=hf, bl=bl: e.matmul(
                                psum[7][:, slot * 64:(slot + 1) * 64], lhsT=pTs[0][:, (slot * 2 + hf) * 128:(slot * 2 + hf + 1) * 128],
                                rhs=v(vb)[:, bl + hf, :], start=(hf == 0), stop=(hf == 1)),
                                reads=["pTs", "vb"], writes=["ps7"])
                    P.dve(lambda e: e.tensor_tensor(out=v(attb), in0=psum[7][:, 0:256].rearrange("p (a b) -> p a b", a=4),
                                                    in1=smv[:, 6, :].unsqueeze(2).to_broadcast([128, 4, 64]), op=ALU.mult),
                          reads=["ps7", "sm"], writes=["attb", "ps7"])
                    for slot in range(4):
                        h = SLOT_H[slot]
                        i, base = h // 2, (h % 2) * 64
                        P.pe(lambda e, slot=slot, i=i, base=base: e.transpose(
                            out=psum[4][:].bitcast(BF16)[base:base + 64, i * 128:(i + 1) * 128],
                            in_=v(attb)[:, slot, :], identity=cs["ident"][:]),
                            reads=["attb", "k_ident"], writes=["ps4"])
                    P.dve(lambda e, bl=bl, gq=gsil2[g % 2]: e.tensor_tensor(
                        out=v(zT)[:, 0:2, bl * 128:(bl + 1) * 128],
                        in0=psum[4][:].bitcast(BF16)[:, 0:256].rearrange("p (a b) -> p a b", a=2),
                        in1=v(gq)[:, 0:2, bl * 128:(bl + 1) * 128], op=ALU.mult),
                        reads=["ps4", f"gsil{g % 2}"], writes=["zT", "ps4"])

                if 'rw' not in SKIP:
                    curn = v(cur)[:, :, 1:G + 1]
                    P.dve(lambda e: e.tensor_tensor(out=v(mixed), in0=v(cur)[:, :, 0:G], in1=curn, op=ALU.subtract),
                           reads=["cur"], writes=["mixed"])
                    P.dve(lambda e: e.tensor_tensor(out=v(mixed), in0=v(mixed), in1=mu[0].unsqueeze(2).to_broadcast([128, 7, G]), op=ALU.mult),
                          reads=["mixed", "mu"], writes=["mixed"])
                    P.dve(lambda e: e.tensor_tensor(out=v(mixed), in0=v(mixed), in1=curn, op=ALU.add), reads=["mixed", "cur"], writes=["mixed"])
                if g == NGR - 1 and 'tr' not in SKIP:
                    P.pe(lambda e: e.transpose(out=psum[7][:, 0:128], in_=krotf[0][:, G - 128:G], identity=cs["identf"][:]),
                         reads=["krotf", "k_identf"], writes=["ps7"])
                    P.dve(lambda e: e.tensor_copy(out=kcf[0], in_=psum[7][:, 0:64]), reads=["ps7"], writes=["kcf"])
                    P.dma(lambda e: e.dma_start(out=ckp[l], in_=kcf[0]), reads=["kcf"], writes=["ckp"], semkey="ckp")
                    P.dma(lambda e: e.dma_start(out=shp[l], in_=v(cur)[:, :, G]), reads=["cur"], writes=["shp"], semkey="shp")
                if 'carry' in SKIP:
                    continue
                P.dve(lambda e: e.tensor_copy(out=v(cur)[:, :, 0:1], in_=v(cur)[:, :, G:G + 1]), reads=["cur"], writes=["cur"])
                P.dve(lambda e: e.tensor_copy(out=kT[0][:, 0:128], in_=kT[0][:, G:G + 128]), reads=["kT"], writes=["kT"])
                P.dve(lambda e: e.tensor_copy(out=v(vb)[:, 0, :], in_=v(vb)[:, G // 128, :]), reads=["vb"], writes=["vb"])
            if sec == "rw":
                if 'rw' not in SKIP:
                    curn = v(cur)[:, :, 1:G + 1]
                    mx = v(mixed)
                    P.act(lambda e: e.activation(out=twad[0][0:64, :], in_=mx[0:64, 6, :], func=AF.Tanh), reads=["mixed"], writes=["twad"])
                    P.act(lambda e: e.activation(out=twad[0][64:128, :], in_=mx[64:128, 6, :], func=AF.Copy), reads=["mixed"], writes=["twad"])
                    P.act(lambda e: e.activation(out=v(vrb), in_=mx[:, 4:6, :], func=AF.Copy), reads=["mixed"], writes=["vrb"])
                    rpv = v(rp)
                    for p in range(2):
                        P.pe(lambda e, p=p: e.matmul(psum[p][:, 0:G], lhsT=v(loraW)[0:64, p, :], rhs=twad[0][0:64, :], start=True, stop=True),
                             reads=["loraW", "twad"], writes=[f"ps{p}"])
                        P.act(lambda e, p=p: e.activation(out=v(sig)[:, p, :], in_=psum[p][:, 0:G], func=AF.Sigmoid, bias=rpv[:, 0, p:p + 1]),
                              reads=[f"ps{p}", "rp"], writes=["sig", f"ps{p}"])
                        P.pe(lambda e, p=p: e.matmul(psum[2 + p][:, 0:G], lhsT=v(loraW)[64:128, p, :], rhs=twad[0][64:128, :], start=True, stop=True),
                             reads=["loraW", "twad"], writes=[f"ps{2 + p}"])
                        P.act(lambda e, p=p: e.activation(out=v(aa)[:, p, :], in_=psum[2 + p][:, 0:G], func=AF.Sigmoid, bias=rpv[:, 1, p:p + 1]),
                              reads=[f"ps{2 + p}", "rp"], writes=["aa", f"ps{2 + p}"])
                    for p in range(2):
                        for c in range(4):
                            P.dve(lambda e, p=p, c=c: e.tensor_tensor_scan(
                                out=v(Ls)[:, p, c * 64:(c + 1) * 64], data0=cs["ones"][:, 0:64], data1=v(sig)[:, p, c * 64:(c + 1) * 64],
                                initial=0.0, op0=ALU.mult, op1=ALU.add), reads=["sig", "k_ones"], writes=["Ls"])
                    P.act(lambda e: e.activation(out=v(eL), in_=v(Ls), func=AF.Exp, scale=-CDEC), reads=["Ls"], writes=["eL"])
                    P.act(lambda e: e.activation(out=v(eLi), in_=v(Ls), func=AF.Exp, scale=CDEC), reads=["Ls"], writes=["eLi"])
                    P.dve(lambda e: e.tensor_tensor(out=v(eLp), in0=v(Ls), in1=v(sig), op=ALU.subtract), reads=["Ls", "sig"], writes=["eLp"])
                    P.act(lambda e: e.activation(out=v(eLp), in_=v(eLp), func=AF.Exp, scale=-CDEC), reads=["eLp"], writes=["eLp"])
                    bc = lambda j: rpv[:, j, :].unsqueeze(2).to_broadcast([128, 2, G])
                    P.dve(lambda e: e.tensor_tensor(out=v(kkr), in0=mx[:, 2:4, :], in1=bc(2), op=ALU.mult), reads=["mixed", "rp"], writes=["kkr"])
                    P.dve(lambda e: e.tensor_tensor(out=v(sqb), in0=v(kkr), in1=v(kkr), op=ALU.mult), reads=["kkr"], writes=["sqb"])
                    for p in range(2):
                        P.pe(lambda e, p=p: e.matmul(psum[p][:, 0:G], lhsT=cs["bones"][:], rhs=v(sqb)[:, p, :], start=True, stop=True),
                             reads=["sqb", "k_bones"], writes=[f"ps{p}"])
                        P.act(lambda e, p=p: e.activation(out=v(rn)[:, p, :], in_=psum[p][:, 0:G], func=AF.Sqrt),
                              reads=[f"ps{p}"], writes=["rn", f"ps{p}"])
                    P.dve(lambda e: e.tensor_scalar_max(out=v(rn), in0=v(rn), scalar1=1e-12), reads=["rn"], writes=["rn"])
                    P.dve(lambda e: e.reciprocal(out=v(rn), in_=v(rn)), reads=["rn"], writes=["rn"])
                    P.dve(lambda e: e.tensor_tensor(out=v(kk), in0=v(kkr), in1=v(rn), op=ALU.mult), reads=["kkr", "rn"], writes=["kk"])
                    P.dve(lambda e: e.scalar_tensor_tensor(out=v(uu), in0=v(aa), scalar=-1.0, in1=bc(3), op0=ALU.add, op1=ALU.mult),
                          reads=["aa", "rp"], writes=["uu"])
                    P.dve(lambda e: e.scalar_tensor_tensor(out=v(keff), in0=v(uu), scalar=1.0, in1=mx[:, 2:4, :], op0=ALU.add, op1=ALU.mult),
                          reads=["uu", "mixed"], writes=["keff"])
                    v4 = lambda t_: v(t_).rearrange("p a (c t) -> p a c t", c=4)
                    ARv = v(AR)
                    P.dve(lambda e: e.scalar_tensor_tensor(out=ARv[:, :, :, 0, :], in0=v4(kk), scalar=-1.0, in1=v4(eLp), op0=ALU.mult, op1=ALU.mult),
                          reads=["kk", "eLp"], writes=["AR"])
                    P.dve(lambda e: e.tensor_tensor(out=ARv[:, :, :, 1, :], in0=mx[:, 0:2, :].rearrange("p a (c t) -> p a c t", c=4), in1=v4(eL), op=ALU.mult),
                           reads=["mixed", "eL"], writes=["AR"])
                    P.dve(lambda e: e.tensor_tensor(out=v(uu), in0=v(kk), in1=v(aa), op=ALU.mult), reads=["kk", "aa", "keff"], writes=["uu"])
                    P.dve(lambda e: e.tensor_tensor(out=v(Bt), in0=v(uu), in1=v(eLi), op=ALU.mult), reads=["uu", "eLi"], writes=["Bt"])
                    P.dve(lambda e: e.tensor_tensor(out=v(Kt), in0=v(keff), in1=v(eLi), op=ALU.mult), reads=["keff", "eLi"], writes=["Kt"])
                    P.dve(lambda e: e.tensor_tensor(out=v(kkr), in0=mx[:, 0:2, :], in1=v(keff), op=ALU.mult), reads=["mixed", "keff", "kk"], writes=["kkr"])
                    P.dve(lambda e: e.tensor_tensor(out=v(rkk), in0=v(kkr), in1=bc(4), op=ALU.mult), reads=["kkr", "rp"], writes=["rkk"])
                    for p in range(2):
                        P.pe(lambda e, p=p: e.matmul(psum[2 + p][:, 0:G], lhsT=cs["bones"][:], rhs=v(rkk)[:, p, :], start=True, stop=True),
                             reads=["rkk", "k_bones"], writes=[f"ps{2 + p}"])
                        P.dve(lambda e, p=p: e.tensor_tensor(out=v(bonus)[:, p, :], in0=psum[2 + p][:, 0:G], in1=mx[:, 4 + p, :], op=ALU.mult),
                              reads=[f"ps{2 + p}", "mixed"], writes=["bonus", f"ps{2 + p}"])
                    TKv = v(TK)
                    for p in range(2):
                        for c in range(4):
                            for wi, src in enumerate((Bt, Kt, vrb)):
                                for hh in range(2):
                                    hs = slice(hh * 64, hh * 64 + 64)
                                    P.pe(lambda e, p=p, c=c, wi=wi, src=src, hs=hs: e.transpose(
                                        out=psum[4 + p][:].bitcast(BF16)[hs, (c * 3 + wi) * 64:(c * 3 + wi + 1) * 64],
                                        in_=v(src)[hs, p, c * 64:(c + 1) * 64], identity=cs["ident"][hs, hs]),
                                        reads=["Bt", "Kt", "vrb", "k_ident"], writes=[f"ps{4 + p}"])
                        P.act(lambda e, p=p: e.activation(out=TKv[:, p, :, :, :].rearrange("p c w t -> p (c w t)"),
                                                          in_=psum[4 + p][:].bitcast(BF16)[:, 0:768], func=AF.Copy),
                              reads=[f"ps{4 + p}"], writes=["TK", f"ps{4 + p}"])
                    MSv = v(MS)
                    for p in range(2):
                        for c in range(4):
                            it = p * 4 + c
                            bank = 6 + it % 2
                            for hh in range(2):
                                hs = slice(hh * 64, hh * 64 + 64)
                                P.pe(lambda e, p=p, c=c, hs=hs, bank=bank: e.matmul(
                                    psum[bank][hs, 0:128], lhsT=v(Bt)[hs, p, c * 64:(c + 1) * 64],
                                    rhs=ARv[hs, p, c, :, :].rearrange("p a t -> p (a t)"), start=True, stop=True),
                                    reads=["Bt", "AR"], writes=[f"ps{bank}"])
                                P.pe(lambda e, p=p, c=c, hs=hs, bank=bank: e.matmul(
                                    psum[bank][hs, 128:256], lhsT=v(Kt)[hs, p, c * 64:(c + 1) * 64],
                                    rhs=ARv[hs, p, c, :, :].rearrange("p a t -> p (a t)"), start=True, stop=True),
                                    reads=["Kt", "AR"], writes=[f"ps{bank}"])
                                P.pe(lambda e, p=p, c=c, hs=hs, bank=bank: e.matmul(
                                    psum[bank][hs, 256:320], lhsT=ARv[hs, p, c, 0, :], rhs=v(Bt)[hs, p, c * 64:(c + 1) * 64],
                                    start=True, stop=True), reads=["Bt", "AR"], writes=[f"ps{bank}"])
                            P.dve(lambda e, it=it, bank=bank: e.tensor_tensor(out=MSv[:, it, :], in0=psum[bank][:, 0:320], in1=cs["maskG"][:], op=ALU.mult),
                                  reads=[f"ps{bank}", "k_maskG"], writes=["MS", f"ps{bank}"])
                    P.dve(lambda e: e.tensor_tensor(out=v(TT[0]), in0=MSv[:, :, 0:64], in1=cs["id64"][:].unsqueeze(1).to_broadcast([128, 8, 64]), op=ALU.add),
                          reads=["MS", "k_id64"], writes=["TT0"])
                    for k in range(1, 6):
                        src_i, dst_i = (k - 1) % 2, k % 2
                        PQs, PQd = v(PQ[src_i]), v(PQ[dst_i])
                        Pprev = (lambda it_: MSv[:, it_, 0:64]) if k == 1 else (lambda it_, PQs=PQs: PQs[:, it_, 0:64])
                        Qprev = (lambda it_: MSv[:, it_, 256:320]) if k == 1 else (lambda it_, PQs=PQs: PQs[:, it_, 64:128])
                        rk_ = ["MS"] if k == 1 else [f"PQ{src_i}a", f"PQ{src_i}b"]
                        for it in range(8):
                            bank = it // 4
                            for hh in range(2):
                                hs = slice(hh * 64, hh * 64 + 64)
                                P.pe(lambda e, it=it, hs=hs, bank=bank, Pprev=Pprev, Qprev=Qprev: e.matmul(
                                    psum[bank][hs, (it % 4) * 128:(it % 4) * 128 + 64], lhsT=Qprev(it)[hs, :], rhs=Pprev(it)[hs, :], start=True, stop=True),
                                    reads=rk_, writes=[f"ps{bank}"])
                                P.pe(lambda e, it=it, hs=hs, bank=bank, Pprev=Pprev, Qprev=Qprev: e.matmul(
                                    psum[bank][hs, (it % 4) * 128 + 64:(it % 4) * 128 + 128], lhsT=Pprev(it)[hs, :], rhs=Qprev(it)[hs, :], start=True, stop=True),
                                    reads=rk_, writes=[f"ps{bank}"])
                        P.act(lambda e, PQd=PQd: e.activation(out=PQd[:, 0:4, :].rearrange("p a b -> p (a b)"), in_=psum[0][:, 0:512], func=AF.Copy),
                              reads=["ps0"], writes=[f"PQ{dst_i}a", "ps0"])
                        P.dve(lambda e, PQd=PQd: e.tensor_copy(out=PQd[:, 4:8, :].rearrange("p a b -> p (a b)"), in_=psum[1][:, 0:512]),
                              reads=["ps1"], writes=[f"PQ{dst_i}b", "ps1"])
                        Ts, Td = v(TT[src_i]), v(TT[dst_i])
                        for it in range(8):
                            for hh in range(2):
                                hs = slice(hh * 64, hh * 64 + 64)
                                P.pe(lambda e, it=it, hs=hs, PQd=PQd, Ts=Ts: e.matmul(
                                    psum[2][hs, it * 64:(it + 1) * 64], lhsT=PQd[hs, it, 64:128], rhs=Ts[hs, it, :], start=True, stop=True),
                                    reads=[f"PQ{dst_i}a", f"PQ{dst_i}b", f"TT{src_i}"], writes=["ps2"])
                        P.dve(lambda e, Ts=Ts, Td=Td: e.tensor_tensor(out=Td, in0=psum[2][:, 0:512].rearrange("p (a b) -> p a b", a=8), in1=Ts, op=ALU.add),
                              reads=["ps2", f"TT{src_i}"], writes=[f"TT{dst_i}", "ps2"])
                    Tfin = v(TT[1])
                    Sv, Sbv, Wbv, Ubv, Yv = v(Sst), v(Sb), v(Wb), v(Ub), v(Ybuf)
                    for c in range(4):
                        for p in range(2):
                            it = p * 4 + c
                            for hh in range(2):
                                hs = slice(hh * 64, hh * 64 + 64)
                                P.pe(lambda e, p=p, c=c, hs=hs: e.matmul(psum[3][hs, p * 64:(p + 1) * 64], lhsT=ARv[hs, p, c, 0, :], rhs=Sbv[hs, p, :],
                                                                         start=True, stop=False), reads=["AR", "Sb"], writes=["ps3"])
                                P.pe(lambda e, p=p, c=c, hs=hs, it=it: e.matmul(psum[3][hs, p * 64:(p + 1) * 64], lhsT=MSv[hs, it, 128:192], rhs=TKv[hs, p, c, 2, :],
                                                                                start=False, stop=True), reads=["MS", "TK"], writes=["ps3"])
                        P.act(lambda e: e.activation(out=Wbv.rearrange("p a b -> p (a b)"), in_=psum[3][:, 0:128], func=AF.Copy),
                              reads=["ps3"], writes=["Wb", "ps3"])
                        for p in range(2):
                            it = p * 4 + c
                            for hh in range(2):
                                hs = slice(hh * 64, hh * 64 + 64)
                                P.pe(lambda e, p=p, hs=hs, it=it: e.matmul(psum[4][hs, p * 64:(p + 1) * 64], lhsT=Tfin[hs, it, :], rhs=Wbv[hs, p, :],
                                                                           start=True, stop=True), reads=["TT1", "Wb"], writes=["ps4"])
                        P.act(lambda e: e.activation(out=Ubv.rearrange("p a b -> p (a b)"), in_=psum[4][:, 0:128], func=AF.Copy),
                              reads=["ps4"], writes=["Ub", "ps4"])
                        for p in range(2):
                            for hh in range(2):
                                hs = slice(hh * 64, hh * 64 + 64)
                                P.pe(lambda e, p=p, c=c, hs=hs: e.matmul(psum[6][hs, p * 64:(p + 1) * 64], lhsT=TKv[hs, p, c, 0, :], rhs=Ubv[hs, p, :],
                                                                         start=True, stop=False), reads=["TK", "Ub"], writes=["ps6"])
                                P.pe(lambda e, p=p, c=c, hs=hs: e.matmul(psum[6][hs, p * 64:(p + 1) * 64], lhsT=TKv[hs, p, c, 1, :], rhs=TKv[hs, p, c, 2, :],
                                                                         start=False, stop=True), reads=["TK"], writes=["ps6"])
                        for p in range(2):
                            it = p * 4 + c
                            for hh in range(2):
                                hs = slice(hh * 64, hh * 64 + 64)
                                P.pe(lambda e, p=p, c=c, hs=hs: e.matmul(psum[5][hs, p * 64:(p + 1) * 64], lhsT=ARv[hs, p, c, 1, :], rhs=Sbv[hs, p, :],
                                                                         start=True, stop=False), reads=["AR", "Sb"], writes=["ps5"])
                                P.pe(lambda e, p=p, hs=hs, it=it: e.matmul(psum[5][hs, p * 64:(p + 1) * 64], lhsT=MSv[hs, it, 64:128], rhs=Ubv[hs, p, :],
                                                                           start=False, stop=False), reads=["MS", "Ub"], writes=["ps5"])
                                P.pe(lambda e, p=p, c=c, hs=hs, it=it: e.matmul(psum[5][hs, p * 64:(p + 1) * 64], lhsT=MSv[hs, it, 192:256], rhs=TKv[hs, p, c, 2, :],
                                                                                start=False, stop=True), reads=["MS", "TK"], writes=["ps5"])
                        P.dve(lambda e: e.tensor_tensor(out=Sv, in0=psum[6][:, 0:128].rearrange("p (a b) -> p a b", a=2), in1=Sv, op=ALU.add),
                              reads=["ps6", "Sst"], writes=["Sst", "ps6"])
                        P.dve(lambda e, c=c: e.tensor_tensor(out=Sv, in0=Sv, in1=v(eL)[:, :, c * 64 + 63:c * 64 + 64].to_broadcast([128, 2, 64]), op=ALU.mult),
                              reads=["Sst", "eL"], writes=["Sst"])
                        P.act(lambda e: e.activation(out=Sbv, in_=Sv, func=AF.Copy), reads=["Sst"], writes=["Sb"])
                        P.dve(lambda e, c=c: e.tensor_copy(out=Yv[:, c, :, :], in_=psum[5][:, 0:128].rearrange("p (a b) -> p a b", a=2)),
                              reads=["ps5"], writes=["Ybuf", "ps5"])
                    if g == NGR - 1:
                        P.dma(lambda e: e.dma_start(out=swp[l], in_=Sv), reads=["Sst"], writes=["swp"], semkey="swp")
                    gsv = v(gs)
                    Y8 = Yv.rearrange("p c a b -> p (c a) b")
                    P.dve(lambda e: e.tensor_reduce(out=gsv[:, 0, :], in_=Y8, axis=AX.X, op=ALU.add), reads=["Ybuf"], writes=["gs"])
                    P.dve(lambda e: e.tensor_scalar_mul(out=gsv[:, 0, :], in0=gsv[:, 0, :], scalar1=-1.0 / 64), reads=["gs"], writes=["gs"])
                    P.dve(lambda e: e.tensor_tensor(out=v(yc), in0=Y8, in1=gsv[:, 0, :].unsqueeze(2).to_broadcast([128, 8, 64]), op=ALU.add),
                           reads=["Ybuf", "gs"], writes=["pb"])
                    P.dve(lambda e: e.tensor_tensor(out=v(ysq), in0=v(yc), in1=v(yc), op=ALU.mult), reads=["pb"], writes=["sc"])
                    P.dve(lambda e: e.tensor_reduce(out=gsv[:, 1, :], in_=v(ysq), axis=AX.X, op=ALU.add), reads=["sc"], writes=["gs"])
                    P.act(lambda e: e.activation(out=gsv[:, 2, :], in_=gsv[:, 1, :], func=AF.Sqrt, scale=1.0 / 64, bias=epsc[:, 1:2]),
                          reads=["gs", "epsc"], writes=["gs"])
                    P.dve(lambda e: e.reciprocal(out=gsv[:, 3, :], in_=gsv[:, 2, :]), reads=["gs"], writes=["gs"])
                    P.dve(lambda e: e.tensor_tensor(out=v(yh), in0=v(yc), in1=gsv[:, 3, :].unsqueeze(2).to_broadcast([128, 8, 64]), op=ALU.mult),
                          reads=["pb", "gs"], writes=["yh"])
                    for c in range(4):
                        for p in range(2):
                            for hh in range(2):
                                hs = slice(hh * 64, hh * 64 + 64)
                                P.pe(lambda e, c=c, p=p, hs=hs: e.transpose(
                                    out=psum[7][:].bitcast(BF16)[hs, p * G + c * 64:p * G + (c + 1) * 64],
                                    in_=v(yh)[hs, c * 2 + p, :], identity=cs["ident"][hs, hs]), reads=["yh", "k_ident"], writes=["ps7"])
                    for p in range(2):
                        P.act(lambda e, p=p: e.activation(out=v(yz)[:, p, :], in_=psum[7][:].bitcast(BF16)[:, p * G:(p + 1) * G], func=AF.Identity,
                                                          scale=rpv[:, 5, p:p + 1], bias=rpv[:, 6, p:p + 1]),
                              reads=["ps7", "rp"], writes=["rn", "ps7"])
                    P.dve(lambda e: e.tensor_tensor(out=v(yz), in0=v(yz), in1=v(bonus), op=ALU.add), reads=["rn", "bonus"], writes=["rn"])
                    P.dve(lambda e, gq=gsil2[g % 2]: e.tensor_tensor(out=v(zT)[:, 2:4, :], in0=v(yz), in1=v(gq)[:, 2:4, :], op=ALU.mult),
                          reads=["rn", f"gsil{g % 2}"], writes=["zT"])

                for bl in range(G // 128 if 'op' not in SKIP else 0):
                    for cgp in range(4):
                        for i in range(4):
                            P.pe(lambda e, bl=bl, cgp=cgp, i=i: e.matmul(psum[cgp][:, 0:512], lhsT=v(zT)[:, i, bl * 128:(bl + 1) * 128],
                                                                         rhs=Wo[:, i, cgp * 512:(cgp + 1) * 512], start=(i == 0), stop=(i == 3)),
                                 reads=["zT"] + [f"Wo{j}" for j in range(4)], writes=[f"ps{cgp}"])
                        if cgp % 2 == 0:
                            P.act(lambda e, cgp=cgp: e.activation(out=ob[0][:, cgp * 512:(cgp + 1) * 512], in_=psum[cgp][:, 0:512], func=AF.Copy),
                                  reads=[f"ps{cgp}"], writes=["ob", f"ps{cgp}"])
                        else:
                            P.dve(lambda e, cgp=cgp: e.tensor_copy(out=ob[0][:, cgp * 512:(cgp + 1) * 512], in_=psum[cgp][:, 0:512]),
                                  reads=[f"ps{cgp}"], writes=["ob", f"ps{cgp}"])
                    tok0 = g * G + bl * 128
                    row0 = tok0
                    for q in range(2):
                        P.dma(lambda e, q=q, row0=row0: e.dma_start(out=rs_in[l][q].ap()[row0:row0 + 128, :], in_=ob[0][:, q * 1024:(q + 1) * 1024]),
                              reads=["ob"], writes=[f"rs_in{l}"], semkey=f"ob{q}")

    def phase_M_samples(l, Win, WK):
        P.barrier()
        NS = 16
        c7 = Carve()
        hTs16 = c7.bf([128, 16, NS])
        rp = c7.f32([128, NPAR, 2])
        loraW = c7.bf([128, 2, 128])
        mu = c7.f32([128, 7])
        prv = c7.f32([128, 7, NS])
        shpt = c7.f32([64, 193])
        P.dma(lambda e: e.dma_start(out=v(rp), in_=rpar[l]), writes=["s_rp"])
        loraF = c7.f32([128, 2, 128])
        P.dma(lambda e: e.dma_start(out=v(loraF), in_=loraw[l]), writes=["s_loraF"])
        P.act(lambda e: e.activation(out=v(loraW), in_=v(loraF), func=AF.Copy), reads=["s_loraF"], writes=["s_loraW"])
        P.dma(lambda e: e.dma_start(out=mu[0], in_=mu_col[l]), writes=["s_mu"])
        P.dma(lambda e: e.dma_start(out=v(prv), in_=prev_s[l]), writes=["s_prv"])
        P.dma(lambda e: e.dma_start(out=shpt[0], in_=shpar_in[l]), writes=["s_shpt"])
        curS = c7.f32([128, 7, NS])
        mixS = c7.f32([128, 7, NS])
        qf = c7.f32([128, 3, NS])
        qbS = c7.bf([128, 3, NS])
        t1S = c7.f32([128, 3, NS])
        qrotS = c7.f32([128, 3, NS])
        gsS = c7.f32([128, 4, NS])
        vS = c7.f32([16, 64])
        for r2 in range(4):
            P.dma(lambda e, r2=r2: e.dma_start(out=v(hTs16)[:, :, 4 * r2:4 * r2 + 4],
                                               in_=ago_hs[l].ap()[r2 * 128:(r2 + 1) * 128, :].rearrange("p (k c) -> p k c", k=16)),
                  reads=[f"ago_hs{l}"], writes=["s_hT"], eng=("sp" if r2 % 2 == 0 else "act"), semkey=f"s_hT{r2 % 2}")
        for ct in range(NCT):
            pi = ct % 4
            for k in range(16):
                P.pe(lambda e, ct=ct, k=k, pi=pi: e.matmul(psum[pi][:, 0:NS], lhsT=Win[:, k, ct * 128:(ct + 1) * 128], rhs=v(hTs16)[:, k, :],
                                                           start=(k == 0), stop=(k == 15)), reads=["s_hT"], writes=[f"ps{pi}"])
            if ct < 3:
                P.act(lambda e, ct=ct, pi=pi: e.activation(out=v(qf)[:, ct, :], in_=psum[pi][:, 0:NS], func=AF.Copy),
                      reads=[f"ps{pi}"], writes=["s_qf", f"ps{pi}"])
            elif ct < 10:
                P.act(lambda e, ct=ct, pi=pi: e.activation(out=v(curS)[:, ct - 3, :], in_=psum[pi][:, 0:NS], func=AF.Copy),
                      reads=[f"ps{pi}"], writes=["s_cur", f"ps{pi}"])
            else:
                P.act(lambda e, ct=ct, pi=pi: e.activation(out=v(gsS)[:, ct - 10, :], in_=psum[pi][:, 0:NS], func=AF.Silu),
                      reads=[f"ps{pi}"], writes=["s_gs", f"ps{pi}"])
        for k in range(16):
            P.pe(lambda e, k=k: e.matmul(psum[4][0:NS, 0:64], lhsT=v(hTs16)[:, k, :], rhs=Win[:, k, NCT * 128:NCT * 128 + 64],
                                         start=(k == 0), stop=(k == 15)), reads=["s_hT"], writes=["ps4"])
        P.act(lambda e: e.activation(out=vS[0], in_=psum[4][0:NS, 0:64], func=AF.Copy), reads=["ps4"], writes=["s_vS", "ps4"])
        P.dma(lambda e: e.dma_start(out=shs[l], in_=v(curS)), reads=["s_cur"], writes=["shs"], semkey="shs")
        P.act(lambda e: e.activation(out=v(qbS), in_=v(qf), func=AF.Copy), reads=["s_qf"], writes=["s_qb"])
        for ct in range(3):
            P.pe(lambda e, ct=ct: e.matmul(psum[5][:, ct * NS:(ct + 1) * NS], lhsT=cs["prot"][:], rhs=v(qbS)[:, ct, :], start=True, stop=True),
                 reads=["s_qb", "k_prot"], writes=["ps5"])
        P.dve(lambda e: e.tensor_scalar_mul(out=t1S[0], in0=psum[5][:, 0:3 * NS], scalar1=cs["cs_s"][:, 1:2]), reads=["ps5", "k_cs_s"],
              writes=["s_t1", "ps5"])
        P.dve(lambda e: e.scalar_tensor_tensor(out=qrotS[0], in0=qf[0], scalar=cs["cs_s"][:, 0:1], in1=t1S[0], op0=ALU.mult, op1=ALU.add),
              reads=["s_qf", "s_t1", "k_cs_s"], writes=["s_qrot"])
        P.dve(lambda e: e.tensor_tensor(out=v(mixS), in0=v(prv), in1=v(curS), op=ALU.subtract), reads=["s_prv", "s_cur"], writes=["s_mix"])
        P.dve(lambda e: e.tensor_tensor(out=v(mixS), in0=v(mixS), in1=mu[0].unsqueeze(2).to_broadcast([128, 7, NS]), op=ALU.mult),
              reads=["s_mix", "s_mu"], writes=["s_mix"])
        P.dve(lambda e: e.tensor_tensor(out=v(mixS), in0=v(mixS), in1=v(curS), op=ALU.add), reads=["s_mix", "s_cur"], writes=["s_mix"])
        mx = v(mixS)
        rpv = v(rp)
        twadS = c7.bf([128, NS])
        sigS = c7.f32([128, 2, NS])
        aS = c7.f32([128, 2, NS])
        wS = c7.f32([128, 2, NS])
        kkrS = c7.f32([128, 2, NS])
        sqS = c7.bf([128, 2, NS])
        rnS = c7.f32([128, 2, NS])
        kkS = c7.f32([128, 2, NS])
        uuS = c7.f32([128, 2, NS])
        keffS = c7.f32([128, 2, NS])
        P.act(lambda e: e.activation(out=twadS[0][0:64, :], in_=mx[0:64, 6, :], func=AF.Tanh), reads=["s_mix"], writes=["s_twad"])
        P.act(lambda e: e.activation(out=twadS[0][64:128, :], in_=mx[64:128, 6, :], func=AF.Copy), reads=["s_mix"], writes=["s_twad"])
        for p in range(2):
            P.pe(lambda e, p=p: e.matmul(psum[p][:, 0:NS], lhsT=v(loraW)[0:64, p, :], rhs=twadS[0][0:64, :], start=True, stop=True),
                 reads=["s_loraW", "s_twad"], writes=[f"ps{p}"])
            P.act(lambda e, p=p: e.activation(out=v(sigS)[:, p, :], in_=psum[p][:, 0:NS], func=AF.Sigmoid, bias=rpv[:, 0, p:p + 1]),
                  reads=[f"ps{p}", "s_rp"], writes=["s_sig", f"ps{p}"])
            P.pe(lambda e, p=p: e.matmul(psum[2 + p][:, 0:NS], lhsT=v(loraW)[64:128, p, :], rhs=twadS[0][64:128, :], start=True, stop=True),
                 reads=["s_loraW", "s_twad"], writes=[f"ps{2 + p}"])
            P.act(lambda e, p=p: e.activation(out=v(aS)[:, p, :], in_=psum[2 + p][:, 0:NS], func=AF.Sigmoid, bias=rpv[:, 1, p:p + 1]),
                  reads=[f"ps{2 + p}", "s_rp"], writes=["s_a", f"ps{2 + p}"])
        P.act(lambda e: e.activation(out=v(wS), in_=v(sigS), func=AF.Exp, scale=-CDEC), reads=["s_sig"], writes=["s_w"])
        bc = lambda j: rpv[:, j, :].unsqueeze(2).to_broadcast([128, 2, NS])
        P.dve(lambda e: e.tensor_tensor(out=v(kkrS), in0=mx[:, 2:4, :], in1=bc(2), op=ALU.mult), reads=["s_mix", "s_rp"], writes=["s_kkr"])
        P.dve(lambda e: e.tensor_tensor(out=v(sqS), in0=v(kkrS), in1=v(kkrS), op=ALU.mult), reads=["s_kkr"], writes=["s_sq"])
        for p in range(2):
            P.pe(lambda e, p=p: e.matmul(psum[p][:, 0:NS], lhsT=cs["bones"][:], rhs=v(sqS)[:, p, :], start=True, stop=True),
                 reads=["s_sq", "k_bones"], writes=[f"ps{p}"])
            P.act(lambda e, p=p: e.activation(out=v(rnS)[:, p, :], in_=psum[p][:, 0:NS], func=AF.Sqrt), reads=[f"ps{p}"], writes=["s_rn", f"ps{p}"])
        P.dve(lambda e: e.tensor_scalar_max(out=v(rnS), in0=v(rnS), scalar1=1e-12), reads=["s_rn"], writes=["s_rn"])
        P.dve(lambda e: e.reciprocal(out=v(rnS), in_=v(rnS)), reads=["s_rn"], writes=["s_rn"])
        P.dve(lambda e: e.tensor_tensor(out=v(kkS), in0=v(kkrS), in1=v(rnS), op=ALU.mult), reads=["s_kkr", "s_rn"], writes=["s_kk"])
        P.dve(lambda e: e.scalar_tensor_tensor(out=v(uuS), in0=v(aS), scalar=-1.0, in1=bc(3), op0=ALU.add, op1=ALU.mult),
              reads=["s_a", "s_rp"], writes=["s_uu"])
        P.dve(lambda e: e.scalar_tensor_tensor(out=v(keffS), in0=v(uuS), scalar=1.0, in1=mx[:, 2:4, :], op0=ALU.add, op1=ALU.mult),
              reads=["s_uu", "s_mix"], writes=["s_keff"])
        tmS = c7.f32([16, 20, 128])
        srcs = []
        for (t_, key, lo) in ((mixS, "s_mix", 0), (wS, "s_w", 0), (keffS, "s_keff", 0), (mixS, "s_mix", 4), (kkS, "s_kk", 0), (aS, "s_a", 0),
                              (qrotS, "s_qrot", 0), (gsS, "s_gs", 0), (gsS, "s_gs", 2)):
            for p in range(2):
                srcs.append((v(t_)[:, lo + p, :], key))
        srcs.append((v(qrotS)[:, 2, :], "s_qrot"))
        for n0 in range(0, len(srcs), 4):
            bank = 4 + (n0 // 4) % 4
            grp = srcs[n0:n0 + 4]
            for i_, (ap_, key) in enumerate(grp):
                P.pe(lambda e, ap_=ap_, i_=i_, bank=bank: e.transpose(out=psum[bank][0:NS, i_ * 128:(i_ + 1) * 128], in_=ap_, identity=cs["identf"][:]),
                     reads=[key, "k_identf"], writes=[f"ps{bank}"])
            n_ = len(grp)
            evac = P.act if (n0 // 4) % 2 == 0 else P.dve
            if (n0 // 4) % 2 == 0:
                P.act(lambda e, n0=n0, n_=n_, bank=bank: e.activation(out=v(tmS)[:, n0:n0 + n_, :].rearrange("p a b -> p (a b)"),
                                                                   in_=psum[bank][0:NS, 0:n_ * 128], func=AF.Copy),
                      reads=[f"ps{bank}"], writes=["s_tm", f"ps{bank}"])
            else:
                P.dve(lambda e, n0=n0, n_=n_, bank=bank: e.tensor_copy(out=v(tmS)[:, n0:n0 + n_, :].rearrange("p a b -> p (a b)"),
                                                                    in_=psum[bank][0:NS, 0:n_ * 128]),
                      reads=[f"ps{bank}"], writes=["s_tm", f"ps{bank}"])
        for kind in range(9):
            P.dma(lambda e, kind=kind: e.dma_start(out=smp_d[l].ap()[:, :, kind, :].rearrange("h s d -> s h d"),
                                                   in_=v(tmS)[:, 2 * kind:2 * kind + 2, :].rearrange("s t (h d) -> s (t h) d", h=2)),
                  reads=["s_tm"], writes=["smp_d"], eng=("sp" if kind % 2 == 0 else "act"), semkey=f"smp{kind % 2}")
        P.dma(lambda e: e.dma_start(out=kn_d[l].ap(), in_=v(tmS)[:, 18, 0:64]), reads=["s_tm"], writes=["kn_d"], semkey="kn")
        P.dma(lambda e: e.dma_start(out=vn_d[l].ap(), in_=vS[0]), reads=["s_vS"], writes=["vn_d"], semkey="vn")
        P.dma(lambda e: e.dma_start(out=cks[l][:, 0:127, :], in_=ck_in[l][:, 1:128, :]), writes=["cks"], semkey="cks0")
        P.dma(lambda e: e.dma_start(out=cvs[l][:, 0:127, :], in_=cv_in[l][:, 1:128, :]), writes=["cvs"], eng="act", semkey="cvs0")
        P.dma(lambda e: e.dma_start(out=cks[l][:, 127, :], in_=v(tmS)[:, 18, 0:64]), reads=["s_tm"], writes=["cks"], semkey="cks1")
        P.dma(lambda e: e.dma_start(out=cvs[l][:, 127, :], in_=vS[0]), reads=["s_vS"], writes=["cvs"], eng="act", semkey="cvs1")
        SH = c7.f32([64, 9, 64])
        SHv = v(SH)
        P.dma(lambda e: e.dma_start(out=SHv, in_=smp_d[l].ap().rearrange("h s k d -> (h s) k d")), reads=["smp_d"], writes=["s_SH"])
        KV = c7.f32([64, 129, 64])
        KVv = v(KV)
        tmpA = c7.f32([64, 33 * 64])
        scs = c7.f32([64, 129])
        pS = c7.f32([64, 129])
        sm2 = c7.f32([64, 8])
        oS = c7.f32([64, 64])
        prt = c7.f32([64, 64])
        zs = c7.f32([64, 2, 64])
        for h_ in range(4):
            q_ = "sp" if h_ % 2 == 0 else "act"
            P.dma(lambda e, h_=h_: e.dma_start(out=KVv[16 * h_:16 * h_ + 16, 0:128, :], in_=ck_in[l]), writes=["s_KV"], eng=q_, semkey=f"s_KV{h_}")
            P.dma(lambda e, h_=h_: e.dma_start(out=KVv[16 * h_:16 * h_ + 16, 128, :], in_=kn_d[l].ap()), reads=["kn_d"], writes=["s_KV"], eng=q_,
                  semkey=f"s_KV{h_}")
        PCH = [(0, 32), (32, 64), (64, 96), (96, 129)]
        for ci, (a_, b_) in enumerate(PCH):
            n_ = b_ - a_
            tv = tmpA[0][:, 0:n_ * 64].rearrange("p (n d) -> p n d", d=64)
            f_ = P.dve
            f_(lambda e, a_=a_, b_=b_, n_=n_, tv=tv: e.tensor_tensor(out=tv, in0=KVv[:, a_:b_, :], in1=SHv[:, 6, :].unsqueeze(1).to_broadcast([64, n_, 64]),
                                                              op=ALU.mult), reads=["s_KV", "s_SH"], writes=["s_tmpA"])
            P.dve(lambda e, a_=a_, b_=b_, tv=tv: e.tensor_reduce(out=scs[0][:, a_:b_], in_=tv, axis=AX.X, op=ALU.add), reads=["s_tmpA"], writes=["s_scs"])
        s2 = sm2[0]
        snkc = shpt[0][:, 192:193]
        P.dve(lambda e: e.tensor_reduce(out=s2[:, 0:1], in_=scs[0], axis=AX.X, op=ALU.max), reads=["s_scs"], writes=["s_sm2"])
        P.dve(lambda e: e.scalar_tensor_tensor(out=s2[:, 1:2], in0=s2[:, 0:1], scalar=0.125, in1=snkc, op0=ALU.mult, op1=ALU.max),
              reads=["s_sm2", "s_shpt"], writes=["s_sm2"])
        P.dve(lambda e: e.tensor_scalar_mul(out=s2[:, 2:3], in0=s2[:, 1:2], scalar1=-1.0), reads=["s_sm2"], writes=["s_sm2"])
        P.dve(lambda e: e.tensor_tensor(out=s2[:, 4:5], in0=snkc, in1=s2[:, 1:2], op=ALU.subtract), reads=["s_sm2", "s_shpt"], writes=["s_sm2"])
        P.act(lambda e: e.activation(out=pS[0], in_=scs[0], func=AF.Exp, scale=0.125, bias=s2[:, 2:3], accum_out=s2[:, 3:4]),
              reads=["s_scs", "s_sm2"], writes=["s_pS", "s_sm2"])
        P.act(lambda e: e.activation(out=s2[:, 4:5], in_=s2[:, 4:5], func=AF.Exp), reads=["s_sm2"], writes=["s_sm2"])
        P.dve(lambda e: e.tensor_tensor(out=s2[:, 5:6], in0=s2[:, 3:4], in1=s2[:, 4:5], op=ALU.add), reads=["s_sm2"], writes=["s_sm2"])
        P.dve(lambda e: e.reciprocal(out=s2[:, 6:7], in_=s2[:, 5:6]), reads=["s_sm2"], writes=["s_sm2"])
        for h_ in range(4):
            q_ = "sp" if h_ % 2 == 0 else "act"
            P.dma(lambda e, h_=h_: e.dma_start(out=KVv[16 * h_:16 * h_ + 16, 0:128, :], in_=cv_in[l]), reads=["s_scs"], writes=["s_KV"], eng=q_,
                  semkey=f"s_KV{h_}")
            P.dma(lambda e, h_=h_: e.dma_start(out=KVv[16 * h_:16 * h_ + 16, 128, :], in_=vn_d[l].ap()), reads=["vn_d", "s_scs"], writes=["s_KV"],
                  eng=q_, semkey=f"s_KV{h_}")
        for ci, (a_, b_) in enumerate(PCH):
            n_ = b_ - a_
            tv = tmpA[0][:, 0:n_ * 64].rearrange("p (d n) -> p d n", d=64)
            f_ = P.dve
            f_(lambda e, a_=a_, b_=b_, n_=n_, tv=tv: e.tensor_tensor(out=tv, in0=KVv[:, a_:b_, :].rearrange("p n d -> p d n"),
                                                              in1=pS[0][:, a_:b_].unsqueeze(1).to_broadcast([64, 64, n_]), op=ALU.mult),
               reads=["s_KV", "s_pS"], writes=["s_tmpA"])
            if ci == 0:
                P.dve(lambda e, tv=tv: e.tensor_reduce(out=oS[0], in_=tv, axis=AX.X, op=ALU.add), reads=["s_tmpA"], writes=["s_oS"])
            else:
                P.dve(lambda e, tv=tv: e.tensor_reduce(out=prt[0], in_=tv, axis=AX.X, op=ALU.add), reads=["s_tmpA"], writes=["s_prt"])
                P.dve(lambda e: e.tensor_tensor(out=oS[0], in0=oS[0], in1=prt[0], op=ALU.add), reads=["s_oS", "s_prt"], writes=["s_oS"])
        zsv = v(zs)
        P.dve(lambda e: e.tensor_scalar_mul(out=oS[0], in0=oS[0], scalar1=s2[:, 6:7]), reads=["s_oS", "s_sm2"], writes=["s_oS"])
        P.dve(lambda e: e.tensor_tensor(out=zsv[:, 0, :], in0=oS[0], in1=SHv[:, 7, :], op=ALU.mult), reads=["s_oS", "s_SH"], writes=["s_zs"])
        Ssm = c7.f32([64, 64, 64])
        tmpS = c7.f32([64, 64, 64])
        sv = c7.f32([64, 6, 64])
        g2 = c7.f32([64, 8])
        Sv_, Tv_, svv = v(Ssm), v(tmpS), v(sv)
        for h_ in range(4):
            P.dma(lambda e, h_=h_: e.dma_start(out=Sv_[16 * h_:16 * h_ + 16], in_=st_in[l][:, h_]), writes=["s_Ssm"],
                  eng=("sp" if h_ % 2 == 0 else "act"), semkey=f"s_Sl{h_}")
        bi = lambda ap_: ap_.unsqueeze(1).to_broadcast([64, 64, 64])
        bj = lambda ap_: ap_.unsqueeze(2).to_broadcast([64, 64, 64])
        P.dve(lambda e: e.tensor_scalar_mul(out=svv[:, 0, :], in0=SHv[:, 4, :], scalar1=-1.0), reads=["s_SH"], writes=["s_sv0"])
        P.dve(lambda e: e.tensor_tensor(out=svv[:, 1, :], in0=SHv[:, 4, :], in1=SHv[:, 5, :], op=ALU.mult), reads=["s_SH"], writes=["s_sv1"])
        P.dve(lambda e: e.tensor_tensor(out=Tv_, in0=Sv_, in1=bi(svv[:, 0, :]), op=ALU.mult), reads=["s_Ssm", "s_sv0"], writes=["s_tmpS"])
        P.dve(lambda e: e.tensor_reduce(out=svv[:, 2, :], in_=Tv_, axis=AX.X, op=ALU.add), reads=["s_tmpS"], writes=["s_sv2"])
        P.dve(lambda e: e.tensor_tensor(out=Sv_, in0=Sv_, in1=bi(SHv[:, 1, :]), op=ALU.mult), reads=["s_Ssm", "s_SH", "s_tmpS"], writes=["s_Ssm"])
        P.dve(lambda e: e.tensor_tensor(out=Tv_, in0=bj(svv[:, 2, :]), in1=bi(svv[:, 1, :]), op=ALU.mult), reads=["s_sv2", "s_sv1"], writes=["s_tmpS"])
        P.dve(lambda e: e.tensor_tensor(out=Sv_, in0=Sv_, in1=Tv_, op=ALU.add), reads=["s_Ssm", "s_tmpS"], writes=["s_Ssm"])
        P.dve(lambda e: e.tensor_tensor(out=Tv_, in0=bj(SHv[:, 3, :]), in1=bi(SHv[:, 2, :]), op=ALU.mult), reads=["s_SH"], writes=["s_tmpS"])
        P.dve(lambda e: e.tensor_tensor(out=Sv_, in0=Sv_, in1=Tv_, op=ALU.add), reads=["s_Ssm", "s_tmpS"], writes=["s_Ssm"])
        for h_ in range(4):
            P.dma(lambda e, h_=h_: e.dma_start(out=sws[l][:, h_], in_=Sv_[16 * h_:16 * h_ + 16]), reads=["s_Ssm"], writes=["sws"],
                  eng=("sp" if h_ % 2 == 0 else "act"), semkey=f"s_Ss{h_}")
        P.dve(lambda e: e.tensor_tensor(out=Tv_, in0=Sv_, in1=bi(SHv[:, 0, :]), op=ALU.mult), reads=["s_Ssm", "s_SH"], writes=["s_tmpS"])
        P.dve(lambda e: e.tensor_reduce(out=svv[:, 3, :], in_=Tv_, axis=AX.X, op=ALU.add), reads=["s_tmpS"], writes=["s_sv3"])
        g2v = g2[0]
        P.dve(lambda e: e.tensor_reduce(out=g2v[:, 0:1], in_=svv[:, 3, :], axis=AX.X, op=ALU.add), reads=["s_sv3"], writes=["s_g2"])
        P.dve(lambda e: e.tensor_scalar_mul(out=g2v[:, 0:1], in0=g2v[:, 0:1], scalar1=-1.0 / 64), reads=["s_g2"], writes=["s_g2"])
        P.dve(lambda e: e.tensor_scalar_add(out=svv[:, 3, :], in0=svv[:, 3, :], scalar1=g2v[:, 0:1]), reads=["s_sv3", "s_g2"], writes=["s_sv3"])
        P.dve(lambda e: e.tensor_tensor(out=svv[:, 4, :], in0=svv[:, 3, :], in1=svv[:, 3, :], op=ALU.mult), reads=["s_sv3"], writes=["s_sv4"])
        P.dve(lambda e: e.tensor_reduce(out=g2v[:, 1:2], in_=svv[:, 4, :], axis=AX.X, op=ALU.add), reads=["s_sv4"], writes=["s_g2"])
        P.act(lambda e: e.activation(out=g2v[:, 2:3], in_=g2v[:, 1:2], func=AF.Sqrt, scale=1.0 / 64, bias=epsc[0:64, 1:2]),
              reads=["s_g2", "epsc"], writes=["s_g2"])
        P.dve(lambda e: e.reciprocal(out=g2v[:, 3:4], in_=g2v[:, 2:3]), reads=["s_g2"], writes=["s_g2"])
        P.dve(lambda e: e.tensor_scalar_mul(out=svv[:, 3, :], in0=svv[:, 3, :], scalar1=g2v[:, 3:4]), reads=["s_sv3", "s_g2"], writes=["s_sv3"])
        P.dve(lambda e: e.tensor_tensor(out=svv[:, 3, :], in0=svv[:, 3, :], in1=shpt[0][:, 0:64], op=ALU.mult), reads=["s_sv3", "s_shpt"], writes=["s_sv3"])
        P.dve(lambda e: e.tensor_tensor(out=svv[:, 3, :], in0=svv[:, 3, :], in1=shpt[0][:, 64:128], op=ALU.add), reads=["s_sv3", "s_shpt"], writes=["s_sv3"])
        P.dve(lambda e: e.tensor_tensor(out=svv[:, 4, :], in0=SHv[:, 0, :], in1=SHv[:, 2, :], op=ALU.mult), reads=["s_SH", "s_g2"], writes=["s_sv4"])
        P.dve(lambda e: e.tensor_tensor(out=svv[:, 4, :], in0=svv[:, 4, :], in1=shpt[0][:, 128:192], op=ALU.mult), reads=["s_sv4", "s_shpt"], writes=["s_sv4"])
        P.dve(lambda e: e.tensor_reduce(out=g2v[:, 4:5], in_=svv[:, 4, :], axis=AX.X, op=ALU.add), reads=["s_sv4"], writes=["s_g2"])
        P.dve(lambda e: e.tensor_scalar_mul(out=svv[:, 5, :], in0=SHv[:, 3, :], scalar1=g2v[:, 4:5]), reads=["s_SH", "s_g2"], writes=["s_sv5"])
        P.dve(lambda e: e.tensor_tensor(out=svv[:, 3, :], in0=svv[:, 3, :], in1=svv[:, 5, :], op=ALU.add), reads=["s_sv3", "s_sv5"], writes=["s_sv3"])
        P.dve(lambda e: e.tensor_tensor(out=zsv[:, 1, :], in0=svv[:, 3, :], in1=SHv[:, 8, :], op=ALU.mult), reads=["s_sv3", "s_SH"], writes=["s_zs"])
        zst = c7.f32([16, 2, 4, 64])
        zTs = c7.bf([128, 4, NS])
        obS = c7.f32([16, D])
        P.dma(lambda e: e.dma_start(out=zs_d[l].ap().rearrange("h s k d -> (h s) k d"), in_=zsv), reads=["s_zs"], writes=["zs_d"], semkey="zs_d")
        for k_ in range(2):
            P.dma(lambda e, k_=k_: e.dma_start(out=v(zst)[:, k_, :, :], in_=zs_d[l].ap()[:, :, k_, :].rearrange("h s d -> s h d")), reads=["zs_d"], writes=["s_zst"], semkey="zst")
        zstv = v(zst)
        for k_ in range(2):
            for pr in range(2):
                i_ = k_ * 2 + pr
                P.pe(lambda e, k_=k_, pr=pr, i_=i_: e.transpose(out=psum[6][:, i_ * NS:(i_ + 1) * NS],
                                                                in_=zstv[:, k_, 2 * pr:2 * pr + 2, :].rearrange("s h d -> s (h d)"),
                                                                identity=cs["identf"][0:NS, 0:NS]), reads=["s_zst", "k_identf"], writes=["ps6"])
        P.act(lambda e: e.activation(out=zTs[0], in_=psum[6][:, 0:4 * NS], func=AF.Copy), reads=["ps6"], writes=["s_zTs", "ps6"])
        for cgp in range(4):
            for i_ in range(4):
                P.pe(lambda e, cgp=cgp, i_=i_: e.matmul(psum[cgp][0:NS, 0:512], lhsT=v(zTs)[:, i_, :], rhs=Wo[:, i_, cgp * 512:(cgp + 1) * 512],
                                                        start=(i_ == 0), stop=(i_ == 3)), reads=["s_zTs"], writes=[f"ps{cgp}"])
            P.dve(lambda e, cgp=cgp: e.tensor_copy(out=obS[0][:, cgp * 512:(cgp + 1) * 512], in_=psum[cgp][0:NS, 0:512]),
                  reads=[f"ps{cgp}"], writes=["s_obS", f"ps{cgp}"])
        P.dma(lambda e: e.dma_start(out=rss_in[l].ap(), in_=obS[0]), reads=["s_obS"], writes=[f"rss_in{l}"], semkey="obS")
        P.op("pool", lambda e: e.collective_compute("ReduceScatter", ALU.add, replica_groups=RG,
                                                     ins=[rss_in[l].ap().opt()], outs=[rss_out[l].ap().opt()]),
             reads=[f"rss_in{l}"], writes=[f"rss_out{l}"], cc=True, semkey="cc")

    def reduce_scatter(l):
        for q in range(2):
            P.op("pool", lambda e, q=q: e.collective_compute("ReduceScatter", ALU.add, replica_groups=RG,
                                                          ins=[rs_in[l][q].ap().opt()], outs=[rs_out[l][q].ap().opt()], dma_qos="P3"),
                 reads=[f"rs_in{l}"], writes=[f"rs_out{l}"], cc=True, semkey="cc", nb=True)

    def phase_O(l):
        nonlocal NOFF
        c4 = Carve()
        gateB = c4.f32([128, D])
        fgB = c4.f32([128, D]) if l == NL - 1 else None
        ot = [c4.f32([128, D]) for _ in range(2)]
        xo = [c4.f32([128, D]) for _ in range(2)]
        fs = c4.f32([128, 2])
        NOFF = c4.off
        P.dma(lambda e: e.dma_start(out=gateB[0][:, 0:512],
                                    in_=ago_mod.ap()[2 * 17:2 * 17 + 1, l * 1536 + 1024:(l + 1) * 1536].partition_broadcast(128)[:, 0, :]),
              reads=["ago_mod"], writes=["gateB"], semkey="gateB")
        P.dma(lambda e: e.dma_start(out=gateB[0][:, 512:2048],
                                    in_=ago_mod.ap()[3 * 17:3 * 17 + 1, l * 1536:(l + 1) * 1536].partition_broadcast(128)[:, 0, :]),
              reads=["ago_mod"], writes=["gateB"], semkey="gateB")
        if l == NL - 1:
            P.dma(lambda e: e.dma_start(out=fgB[0], in_=fg_row[0:1, :].partition_broadcast(128)[:, 0, :]), writes=["fgB"])
        xsrc = x_own if l == 0 else x1_d.ap()

        def o_loads(t):
            s = t % 2
            for q in range(2):
                P.dma(lambda e, t=t, s=s, q=q: e.dma_start(out=ot[s][0][:, q * 1024:(q + 1) * 1024], in_=rs_out[l][q].ap()[t * 128:(t + 1) * 128, :]),
                      reads=[f"rs_out{l}"], writes=[f"ot{s}"], semkey=f"ot{s}")
            P.dma(lambda e, t=t, s=s: e.dma_start(out=xo[s][0], in_=xsrc[t * 128:(t + 1) * 128, :]),
                  reads=(["x1_d"] if l > 0 else []), writes=[f"xo{s}"], semkey=f"xo{s}")

        o_loads(0)
        for t in range(8):
            s = t % 2
            if t + 1 < 8:
                o_loads(t + 1)
            P.dve(lambda e, s=s: e.tensor_tensor(out=ot[s][0], in0=ot[s][0], in1=gateB[0], op=ALU.mult), reads=[f"ot{s}", "gateB"], writes=[f"ot{s}"])
            P.dve(lambda e, s=s: e.tensor_tensor(out=xo[s][0], in0=xo[s][0], in1=ot[s][0], op=ALU.add), reads=[f"ot{s}", f"xo{s}"], writes=[f"xo{s}"])
            if l < NL - 1:
                P.dma(lambda e, t=t, s=s: e.dma_start(out=x1_d.ap()[t * 128:(t + 1) * 128, :], in_=xo[s][0]), reads=[f"xo{s}"], writes=["x1_d"],
                      semkey=f"x1st{s}")
                phase_N_tile(l + 1, t, xo[s][0], f"xo{s}")
            elif 'fin' not in os.environ.get('MK_SKIP', ''):
                P.act(lambda e, s=s: e.activation(out=ot[s][0], in_=xo[s][0], func=AF.Square, accum_out=fs[0][:, 0:1]),
                      reads=[f"xo{s}"], writes=[f"ot{s}", "fs"])
                P.act(lambda e: e.activation(out=fs[0][:, 1:2], in_=fs[0][:, 0:1], func=AF.Sqrt, scale=1.0 / D, bias=epsc[:, 0:1]),
                      reads=["fs", "epsc"], writes=["fs"])
                P.dve(lambda e: e.reciprocal(out=fs[0][:, 1:2], in_=fs[0][:, 1:2]), reads=["fs"], writes=["fs"])
                P.act(lambda e, s=s: e.activation(out=ot[s][0], in_=xo[s][0], func=AF.Copy, scale=fs[0][:, 1:2]),
                      reads=[f"xo{s}", "fs"], writes=[f"ot{s}"])
                P.dve(lambda e, s=s: e.tensor_tensor(out=ot[s][0], in0=ot[s][0], in1=fgB[0], op=ALU.mult), reads=[f"ot{s}", "fgB"], writes=[f"ot{s}"])
                P.dma(lambda e, t=t, s=s: e.dma_start(out=y_own[t * 128:(t + 1) * 128, :], in_=ot[s][0]), reads=[f"ot{s}"], writes=["y_own"],
                      semkey=f"yst{s}")


        c8 = Carve()
        c8.off = NOFF + (6200 if l < NL - 1 else 100)
        osT = c8.f32([4, D])
        xsT = c8.f32([4, D])
        gS = c8.f32([4, D])
        fq = c8.f32([4, 2])
        P.dma(lambda e: e.dma_start(out=osT[0], in_=rss_out[l].ap()), reads=[f"rss_out{l}"], writes=["osT"], semkey="osT")
        xss = xs_own if l == 0 else x1_d.ap()[1024:1028, :]
        P.dma(lambda e: e.dma_start(out=xsT[0], in_=xss), reads=(["x1_d"] if l > 0 else []), writes=["xsT"], eng="act")
        P.dma(lambda e: e.dma_start(out=gS[0], in_=smod_d.ap()[l, 2]), reads=["smod_d"], writes=["gS"], eng="act")
        P.dve(lambda e: e.tensor_tensor(out=osT[0], in0=osT[0], in1=gS[0], op=ALU.mult), reads=["osT", "gS"], writes=["osT"])
        P.dve(lambda e: e.tensor_tensor(out=xsT[0], in0=xsT[0], in1=osT[0], op=ALU.add), reads=["osT", "xsT"], writes=["xsT"])
        if l < NL - 1:
            P.dma(lambda e: e.dma_start(out=x1_d.ap()[1024:1028, :], in_=xsT[0]), reads=["xsT"], writes=["x1_d"], semkey="x1s")
            phase_N_samples(l + 1, xsT[0], "xsT", c8.off)
        else:
            P.act(lambda e: e.activation(out=osT[0], in_=xsT[0], func=AF.Square, accum_out=fq[0][:, 0:1]), reads=["xsT"], writes=["osT", "fq"])
            P.act(lambda e: e.activation(out=fq[0][:, 1:2], in_=fq[0][:, 0:1], func=AF.Sqrt, scale=1.0 / D, bias=epsc[0:4, 0:1]),
                  reads=["fq", "epsc"], writes=["fq"])
            P.dve(lambda e: e.reciprocal(out=fq[0][:, 1:2], in_=fq[0][:, 1:2]), reads=["fq"], writes=["fq"])
            P.act(lambda e: e.activation(out=osT[0], in_=xsT[0], func=AF.Copy, scale=fq[0][:, 1:2]), reads=["xsT", "fq"], writes=["osT"])
            P.dve(lambda e: e.tensor_tensor(out=osT[0], in0=osT[0], in1=fgB[0][0:4, :], op=ALU.mult), reads=["osT", "fgB"], writes=["osT"])
            P.dma(lambda e: e.dma_start(out=ys_own, in_=osT[0]), reads=["osT"], writes=["ys_own"], semkey="ysst")

    NLR = int(os.environ.get("MK_NL", NL))
    for l in range(NLR):
        P.epoch = l
        phase_M(l)
        reduce_scatter(l)
        if 'smp' not in os.environ.get('MK_SKIP', ''):
            phase_M_samples(l, WBIG[:, 0:16 * WCOLS].rearrange("p (k c) -> p k c", k=16), None)
        P.barrier()
        if STOP == f"R{l}":
            break
        phase_O(l)
        if STOP == f"O{l}a":
            break
        if l < NL - 1:
            allgather_h(l + 1)
        P.barrier()
        if STOP == f"O{l}":
            break
    print(f"[mk] sbuf bytes/partition = {sb_bytes[0]}", flush=True)
    P.emit()
    return nc


def prep_inputs(inp):
    f = lambda k: np.asarray(inp[k], dtype=np.float32)
    x_prompt, x_sample = f("x_prompt"), f("x_sample")
    consts = make_consts()
    w_in_full = f("w_in")
    maps = []
    for c in range(8):
        g, r = c // 4, c % 4
        ci = col_index(r)
        m = {}
        m["x_own"] = np.ascontiguousarray(x_prompt[g, r * 1024:(r + 1) * 1024])
        m["xs_own"] = np.ascontiguousarray(x_sample[16 * g + 4 * r:16 * g + 4 * r + 4, 0])
        cmat = np.concatenate([f("c_prompt")[g:g + 1], f("c_sample")[16 * g:16 * g + 16]], 0)
        m["cT"] = np.ascontiguousarray(cmat.T.reshape(16, 128, 17).transpose(1, 0, 2))
        m["wada"] = np.ascontiguousarray(f("w_ada")[:, :, r * 1536:(r + 1) * 1536])
        m["bada"] = np.ascontiguousarray(f("b_ada")[:, r * 1536:(r + 1) * 1536])
        m["ng_col"] = np.ascontiguousarray(f("norm_g").reshape(NL, 16, 128).transpose(2, 0, 1))
        m["ng_row"] = f("norm_g")
        m["fg_row"] = f("final_g").reshape(1, D)
        m["w_in"] = np.ascontiguousarray(w_in_full[:, :, ci])
        sh_cols = ci[3 * 128:10 * 128] - R_OFF
        m["mu_col"] = np.ascontiguousarray(f("mu_shift")[:, sh_cols].reshape(NL, 7, 128).transpose(0, 2, 1))
        ss = f("state_shift")[:, 16 * g:16 * g + 16][:, :, sh_cols]
        m["prev_s"] = np.ascontiguousarray(ss.reshape(NL, 16, 7, 128).transpose(0, 3, 2, 1))
        own = (np.arange(256) + 256 * r)
        pars = [f("w0"), f("a0"), f("k_k"), f("k_a"), f("r_k").reshape(NL, 1024), f("ln_w"), f("ln_b")]
        rp = np.stack([p_[:, own].reshape(NL, 2, 128) for p_ in pars], 2)
        m["rpar"] = np.ascontiguousarray(rp.transpose(0, 3, 2, 1))
        lw = np.concatenate([f("w_decay")[:, :, own], f("w_iclr")[:, :, own]], 1)
        m["loraw"] = np.ascontiguousarray(lw.reshape(NL, 128, 2, 128))
        sk = f("sinks")[:, 4 * r:4 * r + 4][:, [0, 2, 1, 3]]
        m["sinks_b"] = np.ascontiguousarray(np.broadcast_to(sk[:, None, :], (NL, 128, 4)))
        rows = np.concatenate([np.arange(256 * r, 256 * r + 256), np.arange(1024 + 256 * r, 1024 + 256 * r + 256)])
        m["w_out"] = np.ascontiguousarray(f("w_out")[:, rows, :])
        m["ck_in"] = np.ascontiguousarray(f("cache_k")[:, 16 * g:16 * g + 16, :, r, :])
        m["cv_in"] = np.ascontiguousarray(f("cache_v")[:, 16 * g:16 * g + 16, :, r, :])
        m["st_in"] = np.ascontiguousarray(f("state_wkv")[:, 16 * g:16 * g + 16, 4 * r:4 * r + 4])
        sel = np.zeros((16, 4), np.float32)
        for si in range(4):
            sel[4 * r + si, si] = 1.0
        m["selT"] = sel
        hsel = np.arange(4) + 4 * r
        lw_ = f("ln_w").reshape(NL, 16, 64)[:, hsel]
        lb_ = f("ln_b").reshape(NL, 16, 64)[:, hsel]
        rk_ = f("r_k")[:, hsel]
        sk_ = f("sinks")[:, hsel][:, :, None]
        shp_ = np.concatenate([lw_, lb_, rk_, sk_], 2)
        m["shpar"] = np.ascontiguousarray(np.broadcast_to(shp_[:, :, None], (NL, 4, 16, 193)).reshape(NL, 64, 193))
        for k, v_ in consts.items():
            m["c_" + k] = v_
        maps.append(m)
    return consts, maps


def assemble(res):
    y_prompt = np.zeros((2, SEQ, D), np.float32)
    y_sample = np.zeros((32, 1, D), np.float32)
    ckp = np.zeros((NL, 2, 128, 4, 64), np.float32)
    cvp = np.zeros_like(ckp)
    swp = np.zeros((NL, 2, 16, 64, 64), np.float32)
    shp = np.zeros((NL, 2, 3200), np.float32)
    cks = np.zeros((NL, 32, 128, 4, 64), np.float32)
    cvs = np.zeros_like(cks)
    sws = np.zeros((NL, 32, 16, 64, 64), np.float32)
    shs = np.zeros((NL, 32, 3200), np.float32)
    for c in range(8):
        g, r = c // 4, c % 4
        o = res[c]
        ci = col_index(r)
        sh_cols = ci[3 * 128:10 * 128] - R_OFF
        y_prompt[g, r * 1024:(r + 1) * 1024] = o["y_own"]
        y_sample[16 * g + 4 * r:16 * g + 4 * r + 4, 0] = o["ys_own"]
        ckp[:, g, :, r, :] = o["ckp"]
        cvp[:, g, :, r, :] = o["cvp"]
        t = o["swp"].reshape(NL, 2, 64, 2, 64)
        swp[:, g, 4 * r:4 * r + 4] = t.transpose(0, 3, 1, 4, 2).reshape(NL, 4, 64, 64)
        shp[:, g, sh_cols] = o["shp"].transpose(0, 2, 1).reshape(NL, 896)
        cks[:, 16 * g:16 * g + 16, :, r, :] = o["cks"]
        cvs[:, 16 * g:16 * g + 16, :, r, :] = o["cvs"]
        sws[:, 16 * g:16 * g + 16, 4 * r:4 * r + 4] = o["sws"]
        shs[:, 16 * g:16 * g + 16][:, :, sh_cols] = o["shs"].transpose(0, 3, 2, 1).reshape(NL, 16, 896)
    return (y_prompt, y_sample, ckp, cvp, swp, shp, cks, cvs, sws, shs)


def kernel(**inputs):
    consts, maps = prep_inputs(inputs)
    nc = build(consts)
    res = run_bass_kernel_spmd(nc, maps, core_ids=list(range(8)))
    global LAST_RES
    LAST_RES = res.results
    return assemble(res.results)
```

```python
import contextlib
import numpy as np
import concourse.bass as bass
import concourse.mybir as mybir
from concourse.bass_utils import run_bass_kernel_spmd

F32 = mybir.dt.float32
BF16 = mybir.dt.bfloat16
AF = mybir.ActivationFunctionType
ALU = mybir.AluOpType
AX = mybir.AxisListType

D = 2048
SEQ = 4096
NL = 2
HD = 64
WIN = 128
Q_OFF = 0
KA_OFF = 1024
VA_OFF = 1280
R_OFF = 1536
KR_OFF = 2560
VR_OFF = 3584
WD_OFF = 4608
AD_OFF = 4672
GA_OFF = 4736
GR_OFF = 5760
NCT = 14
WCOLS = NCT * 128 + 64
G = 256
NG = SEQ // G
CH = 64
CDEC = 0.6065306597126334
NEG = -30000.0
HC = 1088
ZC = 4 * SEQ + 64

ENGS = ("pe", "act", "dve", "pool", "sp")


class Op:
    __slots__ = ("eng", "fn", "deps", "signal", "sigval", "dma", "semkey", "cc", "epoch", "nb")

    def __init__(self, eng, fn, dma=False, semkey=None, cc=False):
        self.eng = eng
        self.fn = fn
        self.deps = []
        self.signal = False
        self.sigval = 0
        self.dma = dma
        self.semkey = semkey
        self.cc = cc
        self.epoch = 0
        self.nb = False


class Prog:
    def __init__(self, nc):
        self.nc = nc
        self.ops = {e: [] for e in ENGS}
        self.last_w = {}
        self.readers = {}
        self.all_ops = []
        self.last_dma = {}
        self.epoch = 0

    def op(self, eng, fn, reads=(), writes=(), dma=False, cc=False, semkey=None, nb=False):
        o = Op(eng, fn, dma=dma or cc, semkey=semkey, cc=cc)
        o.nb = nb
        o.epoch = self.epoch if eng == "pe" else 0
        deps = []
        for k in reads:
            w = self.last_w.get(k)
            if w is not None:
                deps.append(w)
        for k in writes:
            w = self.last_w.get(k)
            if w is not None:
                deps.append(w)
            deps.extend(self.readers.get(k, ()))
        seen = set()
        for d in deps:
            if id(d) in seen or d is o:
                continue
            seen.add(id(d))
            if (not d.dma) and d.eng == "pe" and eng == "pe" and not o.dma:
                continue
            o.deps.append(d)
            d.signal = True
        implied = set()
        for d2 in o.deps:
            for x_ in d2.deps:
                implied.add(id(x_))
        if implied:
            o.deps = [d for d in o.deps if id(d) not in implied]
        for k in reads:
            self.readers.setdefault(k, []).append(o)
        for k in writes:
            self.last_w[k] = o
            self.readers[k] = []
        if o.dma:
            if o.semkey is None:
                o.semkey = ("w",) + tuple(writes)
            HOT = ("hT", "Win", "ob", "ot", "xo", "x1st", "hTt_st", "yst", "xt", "wst", "Wo")
            if (not o.cc) and isinstance(o.semkey, str) and o.semkey.rstrip("0123456789_") in HOT:
                pass
            elif not o.cc:
                import zlib
                if eng == "pool":
                    o.semkey = ("dmapool_sw", zlib.crc32(repr(o.semkey).encode()) % 12)
                else:
                    o.semkey = ("dmapool_hw", zlib.crc32(repr(o.semkey).encode()) % 48)
            prev = self.last_dma.get(o.semkey)
            if prev is not None and all(prev is not d for d in o.deps):
                o.deps.append(prev)
                prev.signal = True
            self.last_dma[o.semkey] = o
        self.ops[eng].append(o)
        self.all_ops.append(o)
        return o

    def pe(self, fn, reads=(), writes=()):
        return self.op("pe", fn, reads, writes)

    def act(self, fn, reads=(), writes=()):
        return self.op("act", fn, reads, writes)

    def dve(self, fn, reads=(), writes=()):
        return self.op("dve", fn, reads, writes)

    def pool(self, fn, reads=(), writes=()):
        return self.op("pool", fn, reads, writes)

    def dma(self, fn, reads=(), writes=(), eng="sp", semkey=None):
        return self.op(eng, fn, reads, writes, dma=True, semkey=semkey)

    def barrier(self):
        lasts = []
        for e in ENGS:
            for o_ in reversed(self.ops[e]):
                if o_.fn is not None and not o_.nb:
                    lasts.append(o_)
                    break
        pend = [o for o in self.all_ops if o.dma and not o.signal and not o.nb]
        keep = {k: w for k, w in self.last_w.items() if w.nb}
        self.last_w = keep
        self.readers = {}
        for e in ENGS:
            o = Op(e, None)
            for d in lasts + pend:
                o.deps.append(d)
                d.signal = True
            self.ops[e].append(o)
            self.all_ops.append(o)

    def emit(self):
        nc = self.nc
        fin = Op("sp", None)
        for o in self.all_ops:
            if o.dma and not o.signal:
                fin.deps.append(o)
                o.signal = True
        cnt = {}
        keys = []
        for o in self.all_ops:
            if not o.signal:
                continue
            key = o.semkey if o.dma else ("eng", o.eng, o.epoch)
            if key not in cnt:
                cnt[key] = 0
                keys.append(key)
            cnt[key] += (16 if (o.dma and not o.cc) else 1)
            o.sigval = cnt[key]
        print("[mk] ops: " + ", ".join(f"{e}={len(self.ops[e])}" for e in ENGS) +
              f"; sems={len(cnt)}; maxval={max(cnt.values()) if cnt else 0}", flush=True)
        with contextlib.ExitStack() as st:
            st.enter_context(nc.allow_non_contiguous_dma(reason="small strided layout transfers"))
            sems = {}
            for i, key in enumerate(keys):
                sems[key] = st.enter_context(nc.semaphore(f"s{i}"))
            block = st.enter_context(nc.Block())

            def run(engname, eng):
                waited = {}
                lst = list(self.ops[engname])
                if engname == "sp":
                    lst = lst + [fin]
                for o in lst:
                    need = {}
                    for d in o.deps:
                        key = d.semkey if d.dma else ("eng", d.eng, d.epoch)
                        if d.sigval > need.get(key, 0):
                            need[key] = d.sigval
                    for key, v in need.items():
                        if waited.get(key, 0) >= v:
                            continue
                        eng.wait_ge(sems[key], v)
                        waited[key] = v
                    if o.fn is None:
                        continue
                    inst = o.fn(eng)
                    if o.signal:
                        key = o.semkey if o.dma else ("eng", o.eng, o.epoch)
                        inst.then_inc(sems[key], 16 if (o.dma and not o.cc) else 1)

            @block.sync
            def _(e):
                run("sp", e)

            @block.scalar
            def _(e):
                run("act", e)

            @block.vector
            def _(e):
                run("dve", e)

            @block.gpsimd
            def _(e):
                run("pool", e)

            @block.tensor
            def _(e):
                run("pe", e)


def col_index(r):
    cols = []
    for i in range(2):
        for hh in range(2):
            cols += list(range(Q_OFF + (4 * r + 2 * i + hh) * 64, Q_OFF + (4 * r + 2 * i + hh + 1) * 64))
    kc = list(range(KA_OFF + r * 64, KA_OFF + (r + 1) * 64))
    cols += kc + kc
    for off in (R_OFF, KR_OFF, VR_OFF):
        for i in range(2):
            cols += list(range(off + (4 * r + 2 * i) * 64, off + (4 * r + 2 * i + 2) * 64))
    cols += list(range(WD_OFF, WD_OFF + 64)) + list(range(AD_OFF, AD_OFF + 64))
    for off in (GA_OFF, GR_OFF):
        for i in range(2):
            cols += list(range(off + (4 * r + 2 * i) * 64, off + (4 * r + 2 * i + 2) * 64))
    cols += list(range(VA_OFF + r * 64, VA_OFF + (r + 1) * 64))
    assert len(cols) == WCOLS
    return np.array(cols)


def wout_row_perm():
    rows = []
    for r in range(4):
        rows += list(range(256 * r, 256 * r + 256))
        rows += list(range(1024 + 256 * r, 1024 + 256 * r + 256))
    return np.array(rows)


def make_consts():
    import ml_dtypes
    bf = ml_dtypes.bfloat16
    c = {}
    p = np.arange(128)
    d = p % 64
    inv_freq = (np.float32(500000.0) ** (-np.arange(8, dtype=np.float32) * np.float32(0.125))).astype(np.float32)
    pos = np.arange(SEQ, dtype=np.float32)
    ang = (pos[None, :] * inv_freq[(d % 8)][:, None]).astype(np.float32)
    rot = (d < 16)[:, None]
    c["cosT"] = np.where(rot, np.cos(ang), 1.0).astype(np.float32)
    c["sinT"] = np.where(rot, np.sin(ang), 0.0).astype(np.float32)
    angs = (np.float32(16384.0) * inv_freq[(d % 8)]).astype(np.float32)
    cs = np.stack([np.where(d < 16, np.cos(angs), 1.0), np.where(d < 16, np.sin(angs), 0.0)], 1)
    c["cs_s"] = cs.astype(np.float32)
    prot = np.zeros((128, 128), np.float32)
    for m in range(128):
        dm = m % 64
        if dm < 8:
            prot[m + 8, m] = -1.0
        elif dm < 16:
            prot[m - 8, m] = 1.0
    c["prot"] = prot.astype(bf)
    c["ident"] = np.eye(128, dtype=np.float32).astype(bf)
    c["identf"] = np.eye(128, dtype=np.float32)
    qi = np.arange(128)[:, None]
    kj = np.arange(256)[None, :]
    valid = (kj >= qi) & (kj <= qi + 128)
    c["maskb"] = np.where(valid, 0.0, NEG).astype(np.float32)
    c["maskb0"] = np.where(valid & (kj >= 128), 0.0, NEG).astype(np.float32)
    row = (np.arange(128) % 64)[:, None]
    col = np.arange(64)[None, :]
    strict = (row < col).astype(np.float32)
    incl = (row <= col).astype(np.float32)
    low = (row > col).astype(np.float32)
    c["maskG"] = np.concatenate([strict, incl, strict, incl, low], 1).astype(np.float32)
    c["id64"] = (row == col).astype(np.float32)
    c["bones"] = ((p[:, None] // 64) == (p[None, :] // 64)).astype(np.float32).astype(bf)
    c["ones"] = np.ones((128, 64), np.float32)
    return c


CONST_DT = {"cosT": F32, "sinT": F32, "cs_s": F32, "prot": BF16, "ident": BF16, "identf": F32,
            "maskb": F32, "maskb0": F32, "maskG": F32, "id64": F32, "bones": BF16, "ones": F32}
NPAR = 7


def build(consts, stop_after=None):
    nc = bass.Bass("TRN2", target_bir_lowering=False)
    P = Prog(nc)

    def din(name, shape, dt=F32):
        return nc.dram_tensor(name, list(shape), dt, kind="ExternalInput").ap()

    def dout(name, shape, dt=F32):
        return nc.dram_tensor(name, list(shape), dt, kind="ExternalOutput").ap()

    def dscr(name, shape, dt=F32):
        return nc.dram_tensor(name, list(shape), dt)

    sb_bytes = [0]

    def sb(name, shape, dt=F32):
        n = 1
        for s in shape[1:]:
            n *= s
        sb_bytes[0] += n * (4 if dt == F32 else 2)
        return nc.alloc_sbuf_tensor(name, list(shape), dt)

    x_own = din("x_own", [1024, D])
    xs_own = din("xs_own", [4, D])
    cT_in = din("cT", [128, 16, 17])
    wada = din("wada", [NL, D, 1536])
    bada = din("bada", [NL, 1536])
    ng_col = din("ng_col", [128, NL, 16])
    ng_row = din("ng_row", [NL, D])
    fg_row = din("fg_row", [1, D])
    w_in = din("w_in", [NL, D, WCOLS])
    mu_col = din("mu_col", [NL, 128, 7])
    prev_s = din("prev_s", [NL, 128, 7, 16])
    rpar = din("rpar", [NL, 128, NPAR, 2])
    loraw = din("loraw", [NL, 128, 2, 128])
    sinks_b = din("sinks_b", [NL, 128, 4])
    w_out = din("w_out", [NL, 512, D])
    ck_in = din("ck_in", [NL, 16, 128, 64])
    cv_in = din("cv_in", [NL, 16, 128, 64])
    st_in = din("st_in", [NL, 16, 4, 64, 64])
    selT_in = din("selT", [16, 4])
    shpar_in = din("shpar", [NL, 64, 193])
    cd = {k: din("c_" + k, v.shape, CONST_DT[k]) for k, v in consts.items()}
    y_own = dout("y_own", [1024, D])
    ys_own = dout("ys_own", [4, D])
    ckp = dout("ckp", [NL, 128, 64])
    cvp = dout("cvp", [NL, 128, 64])
    swp = dout("swp", [NL, 128, 2, 64])
    shp = dout("shp", [NL, 128, 7])
    cks = dout("cks", [NL, 16, 128, 64])
    cvs = dout("cvs", [NL, 16, 128, 64])
    sws = dout("sws", [NL, 16, 4, 64, 64])
    shs = dout("shs", [NL, 128, 7, 16])
    agi_mod = dscr("agi_mod", [17, NL * 1536])
    ago_mod = dscr("ago_mod", [4 * 17, NL * 1536])
    agi_h = [[dscr(f"agi_h{l}_{j}", [128, 2048], BF16) for j in range(8)] for l in range(NL)]
    agi_hs = [dscr(f"agi_hs{l}", [128, 64], BF16) for l in range(NL)]
    ago_hs = [dscr(f"ago_hs{l}", [512, 64], BF16) for l in range(NL)]
    ago_h = [[dscr(f"ago_h{l}_{j}", [4 * 128, 2048], BF16) for j in range(8)] for l in range(NL)]
    agi_z = [dscr(f"agi_z{l}", [128, ZC], BF16) for l in range(NL)]
    ago_z = [dscr(f"ago_z{l}", [4 * 128, ZC], BF16) for l in range(NL)]
    x1_d = dscr("x1_d", [1028, D])
    smod_d = dscr("smod_d", [NL, 3, 4, D])
    smp_d = [dscr(f"smp_d{l}", [4, 16, 9, 64]) for l in range(NL)]
    kn_d = [dscr(f"kn_d{l}", [16, 64]) for l in range(NL)]
    vn_d = [dscr(f"vn_d{l}", [16, 64]) for l in range(NL)]
    zs_d = [dscr(f"zs_d{l}", [4, 16, 2, 64]) for l in range(NL)]
    rs_in = [[dscr(f"rs_in{l}_{q}", [4 * 1024, 1024]) for q in range(2)] for l in range(NL)]
    rss_in = [dscr(f"rss_in{l}", [16, D]) for l in range(NL)]
    rss_out = [dscr(f"rss_out{l}", [4, D]) for l in range(NL)]
    rs_out = [[dscr(f"rs_out{l}_{q}", [1024, 1024]) for q in range(2)] for l in range(NL)]
    RG = [[0, 1, 2, 3], [4, 5, 6, 7]]

    cs = {}
    for k, v in consts.items():
        if k in ("cosT", "sinT"):
            continue
        cs[k] = sb("k_" + k, v.shape, CONST_DT[k])
        P.dma(lambda e, k=k: e.dma_start(out=cs[k][:], in_=cd[k]), writes=["k_" + k])
    WBIG = sb("WBIG", [128, 16 * 2048], BF16)
    SCR_N = 30000
    SCR = sb("SCR", [128, SCR_N], F32)
    modT = sb("modT", [128, NL, 48])
    ngc = sb("ngc", [128, NL, 16])
    Acol = sb("Acol", [128, NL, 16])
    P.dma(lambda e: e.dma_start(out=ngc[:], in_=ng_col), writes=["ngc"])
    epsc = sb("epsc", [128, 2])
    P.pool(lambda e: e.memset(epsc[:, 0:1], 1e-5), writes=["epsc"])
    P.pool(lambda e: e.memset(epsc[:, 1:2], 64e-5), writes=["epsc"])
    Wo = sb("Wo", [128, 4, D], BF16)
    psum = [nc.alloc_psum_tensor(f"ps{i}", [128, 512], F32) for i in range(8)]

    class Carve:
        def __init__(self):
            self.off = 0

        def f32(self, shape):
            n = int(np.prod(shape[1:]))
            ap = SCR[0:shape[0], self.off:self.off + n]
            self.off += n
            assert self.off <= SCR_N, self.off
            return ap, shape

        def bf(self, shape):
            n = int(np.prod(shape[1:]))
            w = (n + 1) // 2
            ap = SCR[0:shape[0], self.off:self.off + w].bitcast(BF16)[:, 0:n]
            self.off += w
            assert self.off <= SCR_N, self.off
            return ap, shape

    def v(t, pat=None, **kw):
        ap, shape = t
        if len(shape) == 2:
            return ap
        names = " ".join(f"d{i}" for i in range(1, len(shape)))
        kws = {f"d{i}": shape[i] for i in range(1, len(shape))}
        return ap.rearrange(f"p ({names}) -> p {names}", **kws)

    cv_ = Carve()
    cT = cv_.f32([128, 16, 17])
    wst = [cv_.f32([128, 1536]) for _ in range(4)]
    badab = cv_.f32([17, NL, 1536])
    modsb = cv_.f32([17, NL, 1536])
    P.dma(lambda e: e.dma_start(out=v(cT), in_=cT_in), writes=["cT"])
    P.act(lambda e: e.activation(out=cT[0], in_=cT[0], func=AF.Silu), reads=["cT"], writes=["cT"])
    for l in range(NL):
        P.dma(lambda e, l=l: e.dma_start(out=v(badab)[:, l, :], in_=bada[l:l + 1, :].partition_broadcast(17)[:, 0, :]),
              writes=["badab"], eng="act", semkey="badab")
    it = 0
    for l in range(NL):
        for k in range(16):
            s = it % 4
            it += 1
            P.dma(lambda e, l=l, k=k, s=s: e.dma_start(out=wst[s][0], in_=wada[l, k * 128:(k + 1) * 128, :]),
                  writes=[f"wst{s}"], eng=("sp" if s % 2 == 0 else "act"), semkey=f"wst{s}")
            for cg in range(3):
                P.pe(lambda e, k=k, s=s, cg=cg: e.matmul(psum[cg][0:17, :], lhsT=v(cT)[:, k, :],
                                                          rhs=wst[s][0][:, cg * 512:(cg + 1) * 512],
                                                          start=(k == 0), stop=(k == 15)),
                     reads=["cT", f"wst{s}"], writes=[f"ps{cg}"])
        for cg in range(3):
            P.dve(lambda e, l=l, cg=cg: e.tensor_tensor(out=v(modsb)[:, l, cg * 512:(cg + 1) * 512], in0=psum[cg][0:17, :],
                                                        in1=v(badab)[:, l, cg * 512:(cg + 1) * 512], op=ALU.add),
                  reads=[f"ps{cg}", "badab"], writes=["modsb"])
    P.dma(lambda e: e.dma_start(out=agi_mod.ap(), in_=modsb[0]), reads=["modsb"], writes=["agi_mod"])
    P.op("pool", lambda e: e.collective_compute("AllGather", ALU.bypass, replica_groups=RG,
                                                 ins=[agi_mod.ap().opt()], outs=[ago_mod.ap().opt()]),
         reads=["agi_mod"], writes=["ago_mod"], cc=True, semkey="cc")
    for l in range(NL):
        for r2 in range(4):
            P.dma(lambda e, l=l, r2=r2: e.dma_start(
                out=modT[:, l, r2 * 12:(r2 + 1) * 12],
                in_=ago_mod.ap()[r2 * 17, l * 1536:(l + 1) * 1536].rearrange("(c p) -> p c", p=128),
                allow_slow_non_contiguous=True), reads=["ago_mod"], writes=["modT"], semkey="modT")
    selT = cv_.f32([16, 4])
    smk = cv_.f32([16, D])
    smo = cv_.f32([4, D])
    P.dma(lambda e: e.dma_start(out=selT[0], in_=selT_in), writes=["selT"])
    SEG = {0: [(0, 0, 1536, 0), (1, 0, 512, 1536)], 1: [(1, 512, 1536, 0), (2, 0, 1024, 1024)], 2: [(2, 1024, 1536, 0), (3, 0, 1536, 512)]}
    for l in range(NL):
        for kind in range(3):
            for (r2, j0, j1, d0) in SEG[kind]:
                P.dma(lambda e, l=l, r2=r2, j0=j0, j1=j1, d0=d0: e.dma_start(
                    out=smk[0][:, d0:d0 + (j1 - j0)], in_=ago_mod.ap()[r2 * 17 + 1:r2 * 17 + 17, l * 1536 + j0:l * 1536 + j1]),
                    reads=["ago_mod"], writes=["smk"], semkey="smk")
            for cg in range(4):
                P.pe(lambda e, cg=cg: e.matmul(psum[cg][0:4, :], lhsT=selT[0], rhs=smk[0][:, cg * 512:(cg + 1) * 512], start=True, stop=True),
                     reads=["selT", "smk"], writes=[f"ps{cg}"])
                P.dve(lambda e, cg=cg: e.tensor_copy(out=smo[0][:, cg * 512:(cg + 1) * 512], in_=psum[cg][0:4, :]),
                      reads=[f"ps{cg}"], writes=["smo", f"ps{cg}"])
            P.dma(lambda e, l=l, kind=kind: e.dma_start(out=smod_d.ap()[l, kind], in_=smo[0]), reads=["smo"], writes=["smod_d"], semkey="smo")
    P.dve(lambda e: e.scalar_tensor_tensor(out=Acol[:], in0=modT[:, :, 16:32], scalar=1.0, in1=ngc[:],
                                           op0=ALU.add, op1=ALU.mult), reads=["modT", "ngc"], writes=["Acol"])
    P.barrier()
    import os
    STOP = os.environ.get("MK_STOP", "")
    if STOP == "A":
        P.emit()
        return nc

    def phase_N_tile(l, t, xt_ap, xkey):
        c2 = Carve()
        c2.off = NOFF
        junk = c2.f32([128, D])
        pp = t % 2
        bufs = [(c2.bf([128, D]), c2.f32([128, 2]), c2.bf([128, 16, 128])) for _ in range(2)]
        xn, ssq, hTt = bufs[pp]
        KJ, KX, KS, KH = f"junk{pp}", f"xn{pp}", f"ssq{pp}", f"hTt{pp}"
        P.act(lambda e: e.activation(out=junk[0], in_=xt_ap, func=AF.Square, accum_out=ssq[0][:, 0:1]),
              reads=[xkey], writes=[KJ, KS])
        P.act(lambda e: e.activation(out=ssq[0][:, 1:2], in_=ssq[0][:, 0:1], func=AF.Sqrt, scale=1.0 / D, bias=epsc[:, 0:1]),
              reads=[KS, "epsc"], writes=[KS])
        P.dve(lambda e: e.reciprocal(out=ssq[0][:, 1:2], in_=ssq[0][:, 1:2]), reads=[KS], writes=[KS])
        P.act(lambda e: e.activation(out=xn[0], in_=xt_ap, func=AF.Copy, scale=ssq[0][:, 1:2]),
              reads=[xkey, KS], writes=[KX])
        for half in range(2):
            pst = psum[4 + half]
            for kk in range(8):
                k = half * 8 + kk
                P.pe(lambda e, k=k, kk=kk, pst=pst: e.transpose(
                    out=pst[:].bitcast(BF16)[:, kk * 128:(kk + 1) * 128], in_=xn[0][:, k * 128:(k + 1) * 128],
                    identity=cs["ident"][:]), reads=[KX, "k_ident"], writes=[f"ps{4 + half}"])
            for kk in range(8):
                k = half * 8 + kk
                if half == 0:
                    P.act(lambda e, k=k, kk=kk, pst=pst, l=l: e.activation(
                        out=v(hTt)[:, k, :], in_=pst[:].bitcast(BF16)[:, kk * 128:(kk + 1) * 128], func=AF.Identity,
                        scale=Acol[:, l, k:k + 1], bias=modT[:, l, k:k + 1]),
                        reads=[f"ps{4 + half}", "Acol", "modT"], writes=[KH + "a"])
                else:
                    P.dve(lambda e, k=k, kk=kk, pst=pst, l=l: e.tensor_scalar(
                        out=v(hTt)[:, k, :], in0=pst[:].bitcast(BF16)[:, kk * 128:(kk + 1) * 128],
                        scalar1=Acol[:, l, k:k + 1], scalar2=modT[:, l, k:k + 1], op0=ALU.mult, op1=ALU.add),
                        reads=[f"ps{4 + half}", "Acol", "modT"], writes=[KH + "b"])
        P.dma(lambda e, l=l, t=t: e.dma_start(out=agi_h[l][t].ap().rearrange("p (k c) -> p k c", k=16), in_=v(hTt)),
              reads=[KH + "a", KH + "b"], writes=[f"agi_h{l}_{t}"], semkey=f"hTt_st{t % 2}")
        P.op("pool", lambda e, l=l, t=t: e.collective_compute("AllGather", ALU.bypass, replica_groups=RG,
                                                           ins=[agi_h[l][t].ap().opt()], outs=[ago_h[l][t].ap().opt()]),
             reads=[f"agi_h{l}_{t}"], writes=[f"ago_h{l}_{t}"], cc=True, semkey="cc", nb=True)

    def phase_N_samples(l, xs_ap, xkey, off):
        c6 = Carve()
        c6.off = off
        As = c6.f32([4, D])
        Bs = c6.f32([4, D])
        jk = c6.f32([4, D])
        hs_ = c6.bf([4, D])
        sq4 = c6.f32([4, 2])
        hTs = c6.bf([128, 16, 4])
        P.dma(lambda e: e.dma_start(out=As[0], in_=smod_d.ap()[l, 1]), reads=["smod_d"], writes=["As"], eng="act")
        P.dma(lambda e: e.dma_start(out=Bs[0], in_=smod_d.ap()[l, 0]), reads=["smod_d"], writes=["Bs"], eng="act")
        P.dma(lambda e: e.dma_start(out=jk[0], in_=ng_row[l:l + 1, :].partition_broadcast(4)[:, 0, :]), writes=["jk"], eng="act")
        P.dve(lambda e: e.scalar_tensor_tensor(out=As[0], in0=As[0], scalar=1.0, in1=jk[0], op0=ALU.add, op1=ALU.mult),
              reads=["As", "jk"], writes=["As"])
        P.act(lambda e: e.activation(out=jk[0], in_=xs_ap, func=AF.Square, accum_out=sq4[0][:, 0:1]), reads=[xkey, "As"], writes=["jk", "sq4"])
        P.act(lambda e: e.activation(out=sq4[0][:, 1:2], in_=sq4[0][:, 0:1], func=AF.Sqrt, scale=1.0 / D, bias=epsc[0:4, 0:1]),
              reads=["sq4", "epsc"], writes=["sq4"])
        P.dve(lambda e: e.reciprocal(out=sq4[0][:, 1:2], in_=sq4[0][:, 1:2]), reads=["sq4"], writes=["sq4"])
        P.act(lambda e: e.activation(out=jk[0], in_=xs_ap, func=AF.Copy, scale=sq4[0][:, 1:2]), reads=[xkey, "sq4"], writes=["jk"])
        P.dve(lambda e: e.tensor_tensor(out=jk[0], in0=jk[0], in1=As[0], op=ALU.mult), reads=["jk", "As"], writes=["jk"])
        P.dve(lambda e: e.tensor_tensor(out=hs_[0], in0=jk[0], in1=Bs[0], op=ALU.add), reads=["jk", "Bs"], writes=["hs_"])
        for k in range(16):
            P.pe(lambda e, k=k: e.transpose(out=psum[6][:].bitcast(BF16)[:, k * 4:(k + 1) * 4], in_=hs_[0][:, k * 128:(k + 1) * 128],
                                            identity=cs["ident"][0:4, 0:4]), reads=["hs_", "k_ident"], writes=["ps6"])
        P.act(lambda e: e.activation(out=hTs[0], in_=psum[6][:].bitcast(BF16)[:, 0:64], func=AF.Copy), reads=["ps6"], writes=["hTs", "ps6"])
        P.dma(lambda e: e.dma_start(out=agi_hs[l].ap().rearrange("p (k c) -> p k c", k=16), in_=v(hTs)),
              reads=["hTs"], writes=[f"agi_hs{l}"], semkey="hTs_st")
        P.op("pool", lambda e: e.collective_compute("AllGather", ALU.bypass, replica_groups=RG,
                                                     ins=[agi_hs[l].ap().opt()], outs=[ago_hs[l].ap().opt()]),
             reads=[f"agi_hs{l}"], writes=[f"ago_hs{l}"], cc=True, semkey="cc", nb=True)

    NOFF = 0
    c0 = Carve()
    xt = [c0.f32([128, D]) for _ in range(2)]
    NOFF = c0.off
    def n_load(t):
        s = t % 2
        P.dma(lambda e, t=t, s=s: e.dma_start(out=xt[s][0], in_=x_own[t * 128:(t + 1) * 128, :]),
              writes=[f"xt{s}"], semkey=f"xt{s}")

    n_load(0)
    for t in range(8):
        s = t % 2
        if t + 1 < 8:
            n_load(t + 1)
        phase_N_tile(0, t, xt[s][0], f"xt{s}")

    zpad = sb("zpad", [128, 2, 64], BF16)
    P.pool(lambda e: e.memset(zpad[:], 0.0), writes=["zpad"])

    def allgather_h(l):
        pass

    cx = Carve()
    cx.off = NOFF + 20000
    xs0 = cx.f32([4, D])
    P.dma(lambda e: e.dma_start(out=xs0[0], in_=xs_own), writes=["xs0"], eng="act")
    phase_N_samples(0, xs0[0], "xs0", NOFF + 8000)
    allgather_h(0)
    P.barrier()
    if STOP == "N":
        P.emit()
        return nc

    def phase_M(l):
        c3 = Carve()
        Win = WBIG[:, 0:16 * WCOLS].rearrange("p (k c) -> p k c", k=16)
        import os
        SKIP = os.environ.get("MK_SKIP", "")
        for k in range(16 if "win" not in SKIP else 0):
            for hf in range(4):
                P.dma(lambda e, k=k, hf=hf: e.dma_start(out=Win[:, k, hf * 464:(hf + 1) * 464],
                                                       in_=w_in[l, k * 128:(k + 1) * 128, hf * 464:(hf + 1) * 464]),
                      writes=[f"Win{k}_{hf}"], eng="pool", semkey=f"Win{(k * 4 + hf) % 4}")
        WK = [f"Win{k}_{hf}" for k in range(16) for hf in range(4)]
        hT = [c3.bf([128, 16, G]) for _ in range(2)]
        cst = [c3.f32([128, 2, G])] * 2
        cur = c3.f32([128, 7, G + 1])
        qb = c3.bf([128, 3, G])
        t1 = c3.f32([128, 3, G])
        t2 = c3.f32([128, 3, G])
        qrot = c3.bf([128, 2, G])
        krotf = c3.f32([128, G])
        kT = c3.bf([128, 128 + G])
        vb = c3.bf([128, 3, 64])
        vf = c3.f32([128, 64])
        kcf = c3.f32([128, 64])
        mu = c3.f32([128, 7])
        gsil2 = [c3.bf([128, 4, G]) for _ in range(2)]
        zT = c3.bf([128, 4, G])
        sc = c3.f32([128, 4, 256])
        yc = c3.f32([128, 8, 64])
        pb = (yc[0].bitcast(BF16)[:, 0:1024], [128, 4, 256])
        pTs = c3.bf([128, 1024])
        attb = c3.bf([128, 4, 64])
        sm = c3.f32([128, 8, 4])
        snk = c3.f32([128, 4])
        rp = c3.f32([128, NPAR, 2])
        loraW = c3.bf([128, 2, 128])
        P.dma(lambda e: e.dma_start(out=v(rp), in_=rpar[l]), writes=["rp"])
        P.dma(lambda e: e.dma_start(out=v(loraW), in_=loraw[l]), writes=["loraW"], eng="pool", semkey="loraW")
        mixed = c3.f32([128, 7, G])
        twad = c3.bf([128, G])
        sig = c3.f32([128, 2, G])
        aa = c3.f32([128, 2, G])
        Ls = c3.f32([128, 2, G])
        eL = c3.f32([128, 2, G])
        eLi = c3.f32([128, 2, G])
        eLp = c3.f32([128, 2, G])
        kkr = c3.f32([128, 2, G])
        sqb = c3.bf([128, 2, G])
        rn = c3.f32([128, 2, G])
        kk = c3.f32([128, 2, G])
        uu = c3.f32([128, 2, G])
        keff = c3.f32([128, 2, G])
        AR = c3.bf([128, 2, 4, 2, 64])
        Bt = c3.bf([128, 2, G])
        Kt = c3.bf([128, 2, G])
        vrb = c3.bf([128, 2, G])
        rkk = c3.bf([128, 2, G])
        bonus = c3.f32([128, 2, G])
        TK = c3.bf([128, 2, 4, 3, 64])
        MS = c3.bf([128, 8, 320])
        PQ = [c3.bf([128, 8, 128]) for _ in range(2)]
        TT = [c3.bf([128, 8, 64]) for _ in range(2)]
        Sst = c3.f32([128, 2, 64])
        Sb = c3.bf([128, 2, 64])
        Wb = c3.bf([128, 2, 64])
        Ub = c3.bf([128, 2, 64])
        Ybuf = c3.f32([128, 4, 2, 64])
        ysq = (sc[0][:, 0:512], [128, 8, 64])
        gs = c3.f32([128, 4, 8])
        yh = c3.bf([128, 8, 64])
        yz = rn
        ob = c3.f32([128, D])
        for i in range(4):
            P.dma(lambda e, i=i: e.dma_start(out=Wo[:, i, :], in_=w_out[l, i * 128:(i + 1) * 128, :]), writes=[f"Wo{i}"], eng="pool",
                  semkey=f"Wo{i % 2}")
        P.pool(lambda e: e.memset(Sst[0], 0.0), writes=["Sst"])
        P.pool(lambda e: e.memset(Sb[0], 0.0), writes=["Sb"])
        P.dma(lambda e: e.dma_start(out=snk[0], in_=sinks_b[l]), writes=["snk"])
        P.dma(lambda e: e.dma_start(out=mu[0], in_=mu_col[l]), writes=["mu"])
        if 'ms' not in SKIP:
            P.pool(lambda e: e.memset(cur[0], 0.0), writes=["cur"])
            P.pool(lambda e: e.memset(kT[0], 0.0), writes=["kT"])
            P.pool(lambda e: e.memset(vb[0], 0.0), writes=["vb"])
        import os
        NGR = int(os.environ.get('MK_NG', NG))
        sched = [("inproj", 0)]
        for g_ in range(NGR):
            sched.append(("att", g_))
            if g_ + 1 < NGR:
                sched.append(("inproj", g_ + 1))
            sched.append(("rw", g_))
        for (sec, g) in sched:
            s = g % 2
            r2, cg = g // 4, (g % 4) * G
            if sec == "inproj":
                for hf in range(2):
                    t_ = 2 * (g % 4) + hf
                    P.dma(lambda e, s=s, r2=r2, hf=hf, t_=t_: e.dma_start(
                        out=v(hT[s])[:, :, hf * 128:(hf + 1) * 128],
                        in_=ago_h[l][t_].ap()[r2 * 128:(r2 + 1) * 128, :].rearrange("p (k c) -> p k c", k=16)),
                        reads=[f"ago_h{l}_{t_}"], writes=[f"hT{s}"], eng=("sp" if hf == 0 else "act"), semkey=f"hT{s}_{hf}")
                P.dma(lambda e, s=s, g=g: e.dma_start(out=v(cst[s])[:, 0, :], in_=cd["cosT"][:, g * G:(g + 1) * G]),
                      writes=["cst0"], eng="act", semkey="cst0")
                P.dma(lambda e, s=s, g=g: e.dma_start(out=v(cst[s])[:, 1, :], in_=cd["sinT"][:, g * G:(g + 1) * G]),
                      writes=["cst0"], eng="act", semkey="cst0")
                for ct in range(NCT if 'mm' not in SKIP else 0):
                    pi = ct % 4
                    for k in range(16):
                        P.pe(lambda e, ct=ct, k=k, s=s, pi=pi: e.matmul(
                            psum[pi][:, 0:G], lhsT=Win[:, k, ct * 128:(ct + 1) * 128], rhs=v(hT[s])[:, k, :],
                            start=(k == 0), stop=(k == 15)), reads=WK + [f"hT{s}"], writes=[f"ps{pi}"])
                    if 'ev' in SKIP:
                        continue
                    if ct < 3:
                        P.act(lambda e, ct=ct, pi=pi: e.activation(out=v(qb)[:, ct, :], in_=psum[pi][:, 0:G], func=AF.Copy),
                              reads=[f"ps{pi}"], writes=["qb", f"ps{pi}"])
                        if 'evd' not in SKIP:
                            P.dve(lambda e, ct=ct, pi=pi, s=s: e.tensor_tensor(out=v(t2)[:, ct, :], in0=psum[pi][:, 0:G],
                                                                             in1=v(cst[s])[:, 0, :], op=ALU.mult),
                                  reads=[f"ps{pi}", "cst0"], writes=["t2", f"ps{pi}"])
                    elif ct < 10:
                        P.act(lambda e, ct=ct, pi=pi: e.activation(out=v(cur)[:, ct - 3, 1:G + 1], in_=psum[pi][:, 0:G],
                                                                    func=AF.Copy), reads=[f"ps{pi}"], writes=["cur"])
                    else:
                        P.act(lambda e, ct=ct, pi=pi, gq=gsil2[g % 2]: e.activation(out=v(gq)[:, ct - 10, :], in_=psum[pi][:, 0:G], func=AF.Silu),
                              reads=[f"ps{pi}"], writes=[f"gsil{g % 2}", f"ps{pi}"])
                for ct in range(3 if 'rope' not in SKIP else 0):
                    pi = 4 + ct % 2
                    P.pe(lambda e, ct=ct, pi=pi: e.matmul(psum[pi][:, 0:G], lhsT=cs["prot"][:], rhs=v(qb)[:, ct, :],
                                                         start=True, stop=True), reads=["qb", "k_prot"], writes=[f"ps{pi}"])
                    P.dve(lambda e, ct=ct, pi=pi, s=s: e.tensor_tensor(out=v(t1)[:, ct, :], in0=psum[pi][:, 0:G],
                                                                     in1=v(cst[s])[:, 1, :], op=ALU.mult),
                          reads=[f"ps{pi}", "cst0"], writes=["t1"])
                P.dve(lambda e: e.tensor_tensor(out=v(qrot), in0=v(t1)[:, 0:2, :], in1=v(t2)[:, 0:2, :], op=ALU.add),
                      reads=["t1", "t2"], writes=["qrot"])
                P.dve(lambda e: e.tensor_tensor(out=krotf[0], in0=v(t1)[:, 2, :], in1=v(t2)[:, 2, :], op=ALU.add),
                      reads=["t1", "t2"], writes=["krotf"])
                P.act(lambda e: e.activation(out=kT[0][:, 128:128 + G], in_=krotf[0], func=AF.Copy),
                      reads=["krotf"], writes=["kT"])
                for bl in range(G // 128 if 'vv' not in SKIP else 0):
                    for k in range(16):
                        P.pe(lambda e, k=k, s=s, bl=bl: e.matmul(
                            psum[6][:, 0:64], lhsT=v(hT[s])[:, k, bl * 128:(bl + 1) * 128], rhs=Win[:, k, NCT * 128:NCT * 128 + 64],
                            start=(k == 0), stop=(k == 15)), reads=WK + [f"hT{s}"], writes=["ps6"])
                    P.act(lambda e, bl=bl: e.activation(out=v(vb)[:, 1 + bl, :], in_=psum[6][:, 0:64], func=AF.Copy),
                          reads=["ps6"], writes=["vb", "ps6"])
                    if g == NGR - 1 and bl == G // 128 - 1:
                        P.dve(lambda e: e.tensor_copy(out=vf[0], in_=psum[6][:, 0:64]), reads=["ps6"], writes=["vf", "ps6"])
                        P.dma(lambda e: e.dma_start(out=cvp[l], in_=vf[0]), reads=["vf"], writes=["cvp"], semkey="cvp")

            if sec == "att":
                SLOT_H = [0, 2, 1, 3]
                for bl in range(G // 128 if 'att' not in SKIP else 0):
                    mk = "maskb0" if (g == 0 and bl == 0) else "maskb"
                    for slot in range(4):
                        h = SLOT_H[slot]
                        i, base = h // 2, (h % 2) * 64
                        bank = 4 + slot // 2
                        P.pe(lambda e, i=i, base=base, bank=bank, slot=slot, bl=bl: e.matmul(
                            psum[bank][:, (slot % 2) * 256:(slot % 2) * 256 + 256],
                            lhsT=v(qrot)[base:base + 64, i, bl * 128:(bl + 1) * 128],
                            rhs=kT[0][base:base + 64, bl * 128:bl * 128 + 256], start=True, stop=True),
                            reads=["qrot", "kT"], writes=[f"ps{bank}"])
                    for bk in range(2):
                        P.dve(lambda e, bk=bk, mk=mk: e.tensor_tensor(
                            out=v(sc)[:, 2 * bk:2 * bk + 2, :], in0=psum[4 + bk][:, :].rearrange("p (a b) -> p a b", a=2),
                            in1=cs[mk][:].unsqueeze(1).to_broadcast([128, 2, 256]), op=ALU.add),
                            reads=[f"ps{4 + bk}", "k_" + mk], writes=["sc", f"ps{4 + bk}"])
                    smv = v(sm)
                    P.dve(lambda e: e.tensor_reduce(out=smv[:, 0, :], in_=v(sc), axis=AX.X, op=ALU.max), reads=["sc"], writes=["sm"])
                    P.dve(lambda e: e.scalar_tensor_tensor(out=smv[:, 1, :], in0=smv[:, 0, :], scalar=0.125, in1=snk[0],
                                                           op0=ALU.mult, op1=ALU.max), reads=["sm", "snk"], writes=["sm"])
                    P.dve(lambda e: e.tensor_scalar_mul(out=smv[:, 2, :], in0=smv[:, 1, :], scalar1=-1.0), reads=["sm"], writes=["sm"])
                    P.dve(lambda e: e.tensor_tensor(out=smv[:, 4, :], in0=snk[0], in1=smv[:, 1, :], op=ALU.subtract),
                          reads=["sm", "snk"], writes=["sm"])
                    for slot in range(4):
                        P.act(lambda e, slot=slot: e.activation(out=v(pb)[:, slot, :], in_=v(sc)[:, slot, :], func=AF.Exp, scale=0.125,
                                                                bias=smv[:, 2, slot:slot + 1], accum_out=smv[:, 3, slot:slot + 1]),
                              reads=["sc", "sm"], writes=["pb", "sm"])
                    P.act(lambda e: e.activation(out=smv[:, 4, :], in_=smv[:, 4, :], func=AF.Exp), reads=["sm"], writes=["sm"])
                    P.dve(lambda e: e.tensor_tensor(out=smv[:, 5, :], in0=smv[:, 3, :], in1=smv[:, 4, :], op=ALU.add), reads=["sm"], writes=["sm"])
                    P.dve(lambda e: e.reciprocal(out=smv[:, 6, :], in_=smv[:, 5, :]), reads=["sm"], writes=["sm"])
                    for slot in range(4):
                        for hf in range(2):
                            P.pe(lambda e, slot=slot, hf=hf: e.transpose(
                                out=psum[6][:].bitcast(BF16)[:, (slot * 2 + hf) * 128:(slot * 2 + hf + 1) * 128],
                                in_=v(pb)[:, slot, hf * 128:(hf + 1) * 128], identity=cs["ident"][:]),
                                reads=["pb", "k_ident"], writes=["ps6"])
                    P.act(lambda e: e.activation(out=pTs[0], in_=psum[6][:].bitcast(BF16), func=AF.Copy),
                          reads=["ps6"], writes=["pTs", "ps6"])
                    for slot in range(4):
                        for hf in range(2):
                            P.pe(lambda e, slot=slot, hf=hf, bl=bl: e.matmul(
                                psum[7][:, slot * 64:(slot + 1) * 64], lhsT=pTs[0][:, (slot * 2 + hf) * 128:(slot * 2 + hf + 1) * 128],
                                rhs=v(vb)[:, bl + hf, :], start=(hf == 0), stop=(hf == 1)),
                                reads=["pTs", "vb"], writes=["ps7"])
                    P.dve(lambda e: e.tensor_tensor(out=v(attb), in0=psum[7][:, 0:256].rearrange("p (a b) -> p a b", a=4),
                                                    in1=smv[:, 6, :].unsqueeze(2).to_broadcast([128, 4, 64]), op=ALU.mult),
                          reads=["ps7", "sm"], writes=["attb", "ps7"])
                    for slot in range(4):
                        h = SLOT_H[slot]
                        i, base = h // 2, (h % 2) * 64
                        P.pe(lambda e, slot=slot, i=i, base=base: e.transpose(
                            out=psum[4][:].bitcast(BF16)[base:base + 64, i * 128:(i + 1) * 128],
                            in_=v(attb)[:, slot, :], identity=cs["ident"][:]),
                            reads=["attb", "k_ident"], writes=["ps4"])
                    P.dve(lambda e, bl=bl, gq=gsil2[g % 2]: e.tensor_tensor(
                        out=v(zT)[:, 0:2, bl * 128:(bl + 1) * 128],
                        in0=psum[4][:].bitcast(BF16)[:, 0:256].rearrange("p (a b) -> p a b", a=2),
                        in1=v(gq)[:, 0:2, bl * 128:(bl + 1) * 128], op=ALU.mult),
                        reads=["ps4", f"gsil{g % 2}"], writes=["zT", "ps4"])

                if 'rw' not in SKIP:
                    curn = v(cur)[:, :, 1:G + 1]
                    P.dve(lambda e: e.tensor_tensor(out=v(mixed), in0=v(cur)[:, :, 0:G], in1=curn, op=ALU.subtract),
                           reads=["cur"], writes=["mixed"])
                    P.dve(lambda e: e.tensor_tensor(out=v(mixed), in0=v(mixed), in1=mu[0].unsqueeze(2).to_broadcast([128, 7, G]), op=ALU.mult),
                          reads=["mixed", "mu"], writes=["mixed"])
                    P.dve(lambda e: e.tensor_tensor(out=v(mixed), in0=v(mixed), in1=curn, op=ALU.add), reads=["mixed", "cur"], writes=["mixed"])
                if g == NGR - 1 and 'tr' not in SKIP:
                    P.pe(lambda e: e.transpose(out=psum[7][:, 0:128], in_=krotf[0][:, G - 128:G], identity=cs["identf"][:]),
                         reads=["krotf", "k_identf"], writes=["ps7"])
                    P.dve(lambda e: e.tensor_copy(out=kcf[0], in_=psum[7][:, 0:64]), reads=["ps7"], writes=["kcf"])
                    P.dma(lambda e: e.dma_start(out=ckp[l], in_=kcf[0]), reads=["kcf"], writes=["ckp"], semkey="ckp")
                    P.dma(lambda e: e.dma_start(out=shp[l], in_=v(cur)[:, :, G]), reads=["cur"], writes=["shp"], semkey="shp")
                if 'carry' in SKIP:
                    continue
                P.dve(lambda e: e.tensor_copy(out=v(cur)[:, :, 0:1], in_=v(cur)[:, :, G:G + 1]), reads=["cur"], writes=["cur"])
                P.dve(lambda e: e.tensor_copy(out=kT[0][:, 0:128], in_=kT[0][:, G:G + 128]), reads=["kT"], writes=["kT"])
                P.dve(lambda e: e.tensor_copy(out=v(vb)[:, 0, :], in_=v(vb)[:, G // 128, :]), reads=["vb"], writes=["vb"])
            if sec == "rw":
                if 'rw' not in SKIP:
                    curn = v(cur)[:, :, 1:G + 1]
                    mx = v(mixed)
                    P.act(lambda e: e.activation(out=twad[0][0:64, :], in_=mx[0:64, 6, :], func=AF.Tanh), reads=["mixed"], writes=["twad"])
                    P.act(lambda e: e.activation(out=twad[0][64:128, :], in_=mx[64:128, 6, :], func=AF.Copy), reads=["mixed"], writes=["twad"])
                    P.act(lambda e: e.activation(out=v(vrb), in_=mx[:, 4:6, :], func=AF.Copy), reads=["mixed"], writes=["vrb"])
                    rpv = v(rp)
                    for p in range(2):
                        P.pe(lambda e, p=p: e.matmul(psum[p][:, 0:G], lhsT=v(loraW)[0:64, p, :], rhs=twad[0][0:64, :], start=True, stop=True),
                             reads=["loraW", "twad"], writes=[f"ps{p}"])
                        P.act(lambda e, p=p: e.activation(out=v(sig)[:, p, :], in_=psum[p][:, 0:G], func=AF.Sigmoid, bias=rpv[:, 0, p:p + 1]),
                              reads=[f"ps{p}", "rp"], writes=["sig", f"ps{p}"])
                        P.pe(lambda e, p=p: e.matmul(psum[2 + p][:, 0:G], lhsT=v(loraW)[64:128, p, :], rhs=twad[0][64:128, :], start=True, stop=True),
                             reads=["loraW", "twad"], writes=[f"ps{2 + p}"])
                        P.act(lambda e, p=p: e.activation(out=v(aa)[:, p, :], in_=psum[2 + p][:, 0:G], func=AF.Sigmoid, bias=rpv[:, 1, p:p + 1]),
                              reads=[f"ps{2 + p}", "rp"], writes=["aa", f"ps{2 + p}"])
                    for p in range(2):
                        for c in range(4):
                            P.dve(lambda e, p=p, c=c: e.tensor_tensor_scan(
                                out=v(Ls)[:, p, c * 64:(c + 1) * 64], data0=cs["ones"][:, 0:64], data1=v(sig)[:, p, c * 64:(c + 1) * 64],
                                initial=0.0, op0=ALU.mult, op1=ALU.add), reads=["sig", "k_ones"], writes=["Ls"])
                    P.act(lambda e: e.activation(out=v(eL), in_=v(Ls), func=AF.Exp, scale=-CDEC), reads=["Ls"], writes=["eL"])
                    P.act(lambda e: e.activation(out=v(eLi), in_=v(Ls), func=AF.Exp, scale=CDEC), reads=["Ls"], writes=["eLi"])
                    P.dve(lambda e: e.tensor_tensor(out=v(eLp), in0=v(Ls), in1=v(sig), op=ALU.subtract), reads=["Ls", "sig"], writes=["eLp"])
                    P.act(lambda e: e.activation(out=v(eLp), in_=v(eLp), func=AF.Exp, scale=-CDEC), reads=["eLp"], writes=["eLp"])
                    bc = lambda j: rpv[:, j, :].unsqueeze(2).to_broadcast([128, 2, G])
                    P.dve(lambda e: e.tensor_tensor(out=v(kkr), in0=mx[:, 2:4, :], in1=bc(2), op=ALU.mult), reads=["mixed", "rp"], writes=["kkr"])
                    P.dve(lambda e: e.tensor_tensor(out=v(sqb), in0=v(kkr), in1=v(kkr), op=ALU.mult), reads=["kkr"], writes=["sqb"])
                    for p in range(2):
                        P.pe(lambda e, p=p: e.matmul(psum[p][:, 0:G], lhsT=cs["bones"][:], rhs=v(sqb)[:, p, :], start=True, stop=True),
                             reads=["sqb", "k_bones"], writes=[f"ps{p}"])
                        P.act(lambda e, p=p: e.activation(out=v(rn)[:, p, :], in_=psum[p][:, 0:G], func=AF.Sqrt),
                              reads=[f"ps{p}"], writes=["rn", f"ps{p}"])
                    P.dve(lambda e: e.tensor_scalar_max(out=v(rn), in0=v(rn), scalar1=1e-12), reads=["rn"], writes=["rn"])
                    P.dve(lambda e: e.reciprocal(out=v(rn), in_=v(rn)), reads=["rn"], writes=["rn"])
                    P.dve(lambda e: e.tensor_tensor(out=v(kk), in0=v(kkr), in1=v(rn), op=ALU.mult), reads=["kkr", "rn"], writes=["kk"])
                    P.dve(lambda e: e.scalar_tensor_tensor(out=v(uu), in0=v(aa), scalar=-1.0, in1=bc(3), op0=ALU.add, op1=ALU.mult),
                          reads=["aa", "rp"], writes=["uu"])
                    P.dve(lambda e: e.scalar_tensor_tensor(out=v(keff), in0=v(uu), scalar=1.0, in1=mx[:, 2:4, :], op0=ALU.add, op1=ALU.mult),
                          reads=["uu", "mixed"], writes=["keff"])
                    v4 = lambda t_: v(t_).rearrange("p a (c t) -> p a c t", c=4)
                    ARv = v(AR)
                    P.dve(lambda e: e.scalar_tensor_tensor(out=ARv[:, :, :, 0, :], in0=v4(kk), scalar=-1.0, in1=v4(eLp), op0=ALU.mult, op1=ALU.mult),
                          reads=["kk", "eLp"], writes=["AR"])
                    P.dve(lambda e: e.tensor_tensor(out=ARv[:, :, :, 1, :], in0=mx[:, 0:2, :].rearrange("p a (c t) -> p a c t", c=4), in1=v4(eL), op=ALU.mult),
                           reads=["mixed", "eL"], writes=["AR"])
                    P.dve(lambda e: e.tensor_tensor(out=v(uu), in0=v(kk), in1=v(aa), op=ALU.mult), reads=["kk", "aa", "keff"], writes=["uu"])
                    P.dve(lambda e: e.tensor_tensor(out=v(Bt), in0=v(uu), in1=v(eLi), op=ALU.mult), reads=["uu", "eLi"], writes=["Bt"])
                    P.dve(lambda e: e.tensor_tensor(out=v(Kt), in0=v(keff), in1=v(eLi), op=ALU.mult), reads=["keff", "eLi"], writes=["Kt"])
                    P.dve(lambda e: e.tensor_tensor(out=v(kkr), in0=mx[:, 0:2, :], in1=v(keff), op=ALU.mult), reads=["mixed", "keff", "kk"], writes=["kkr"])
                    P.dve(lambda e: e.tensor_tensor(out=v(rkk), in0=v(kkr), in1=bc(4), op=ALU.mult), reads=["kkr", "rp"], writes=["rkk"])
                    for p in range(2):
                        P.pe(lambda e, p=p: e.matmul(psum[2 + p][:, 0:G], lhsT=cs["bones"][:], rhs=v(rkk)[:, p, :], start=True, stop=True),
                             reads=["rkk", "k_bones"], writes=[f"ps{2 + p}"])
                        P.dve(lambda e, p=p: e.tensor_tensor(out=v(bonus)[:, p, :], in0=psum[2 + p][:, 0:G], in1=mx[:, 4 + p, :], op=ALU.mult),
                              reads=[f"ps{2 + p}", "mixed"], writes=["bonus", f"ps{2 + p}"])
                    TKv = v(TK)
                    for p in range(2):
                        for c in range(4):
                            for wi, src in enumerate((Bt, Kt, vrb)):
                                for hh in range(2):
                                    hs = slice(hh * 64, hh * 64 + 64)
                                    P.pe(lambda e, p=p, c=c, wi=wi, src=src, hs=hs: e.transpose(
                                        out=psum[4 + p][:].bitcast(BF16)[hs, (c * 3 + wi) * 64:(c * 3 + wi + 1) * 64],
                                        in_=v(src)[hs, p, c * 64:(c + 1) * 64], identity=cs["ident"][hs, hs]),
                                        reads=["Bt", "Kt", "vrb", "k_ident"], writes=[f"ps{4 + p}"])
                        if p == 0:
                            P.act(lambda e, p=p: e.activation(out=TKv[:, p, :, :, :].rearrange("p c w t -> p (c w t)"),
                                                              in_=psum[4 + p][:].bitcast(BF16)[:, 0:768], func=AF.Copy),
                                  reads=[f"ps{4 + p}"], writes=["TK0", f"ps{4 + p}"])
                        else:
                            P.dve(lambda e, p=p: e.tensor_copy(out=TKv[:, p, :, :, :].rearrange("p c w t -> p (c w t)"),
                                                               in_=psum[4 + p][:].bitcast(BF16)[:, 0:768]),
                                  reads=[f"ps{4 + p}"], writes=["TK1", f"ps{4 + p}"])
                    MSv = v(MS)
                    for p in range(2):
                        for c in range(4):
                            it = p * 4 + c
                            bank = 6 + it % 2
                            for hh in range(2):
                                hs = slice(hh * 64, hh * 64 + 64)
                                P.pe(lambda e, p=p, c=c, hs=hs, bank=bank: e.matmul(
                                    psum[bank][hs, 0:128], lhsT=v(Bt)[hs, p, c * 64:(c + 1) * 64],
                                    rhs=ARv[hs, p, c, :, :].rearrange("p a t -> p (a t)"), start=True, stop=True),
                                    reads=["Bt", "AR"], writes=[f"ps{bank}"])
                                P.pe(lambda e, p=p, c=c, hs=hs, bank=bank: e.matmul(
                                    psum[bank][hs, 128:256], lhsT=v(Kt)[hs, p, c * 64:(c + 1) * 64],
                                    rhs=ARv[hs, p, c, :, :].rearrange("p a t -> p (a t)"), start=True, stop=True),
                                    reads=["Kt", "AR"], writes=[f"ps{bank}"])
                                P.pe(lambda e, p=p, c=c, hs=hs, bank=bank: e.matmul(
                                    psum[bank][hs, 256:320], lhsT=ARv[hs, p, c, 0, :], rhs=v(Bt)[hs, p, c * 64:(c + 1) * 64],
                                    start=True, stop=True), reads=["Bt", "AR"], writes=[f"ps{bank}"])
                            P.dve(lambda e, it=it, bank=bank: e.tensor_tensor(out=MSv[:, it, :], in0=psum[bank][:, 0:320], in1=cs["maskG"][:], op=ALU.mult),
                                  reads=[f"ps{bank}", "k_maskG"], writes=["MS", f"ps{bank}"])
                    P.dve(lambda e: e.tensor_tensor(out=v(TT[0]), in0=MSv[:, :, 0:64], in1=cs["id64"][:].unsqueeze(1).to_broadcast([128, 8, 64]), op=ALU.add),
                          reads=["MS", "k_id64"], writes=["TT0"])
                    for k in range(1, 6):
                        src_i, dst_i = (k - 1) % 2, k % 2
                        PQs, PQd = v(PQ[src_i]), v(PQ[dst_i])
                        Pprev = (lambda it_: MSv[:, it_, 0:64]) if k == 1 else (lambda it_, PQs=PQs: PQs[:, it_, 0:64])
                        Qprev = (lambda it_: MSv[:, it_, 256:320]) if k == 1 else (lambda it_, PQs=PQs: PQs[:, it_, 64:128])
                        rk_ = ["MS"] if k == 1 else [f"PQ{src_i}a", f"PQ{src_i}b"]
                        for it in range(8):
                            bank = it // 4
                            for hh in range(2):
                                hs = slice(hh * 64, hh * 64 + 64)
                                P.pe(lambda e, it=it, hs=hs, bank=bank, Pprev=Pprev, Qprev=Qprev: e.matmul(
                                    psum[bank][hs, (it % 4) * 128:(it % 4) * 128 + 64], lhsT=Qprev(it)[hs, :], rhs=Pprev(it)[hs, :], start=True, stop=True),
                                    reads=rk_, writes=[f"ps{bank}"])
                                P.pe(lambda e, it=it, hs=hs, bank=bank, Pprev=Pprev, Qprev=Qprev: e.matmul(
                                    psum[bank][hs, (it % 4) * 128 + 64:(it % 4) * 128 + 128], lhsT=Pprev(it)[hs, :], rhs=Qprev(it)[hs, :], start=True, stop=True),
                                    reads=rk_, writes=[f"ps{bank}"])
                        P.act(lambda e, PQd=PQd: e.activation(out=PQd[:, 0:4, :].rearrange("p a b -> p (a b)"), in_=psum[0][:, 0:512], func=AF.Copy),
                              reads=["ps0"], writes=[f"PQ{dst_i}a", "ps0"])
                        P.dve(lambda e, PQd=PQd: e.tensor_copy(out=PQd[:, 4:8, :].rearrange("p a b -> p (a b)"), in_=psum[1][:, 0:512]),
                              reads=["ps1"], writes=[f"PQ{dst_i}b", "ps1"])
                        Ts, Td = v(TT[src_i]), v(TT[dst_i])
                        for it in range(8):
                            for hh in range(2):
                                hs = slice(hh * 64, hh * 64 + 64)
                                P.pe(lambda e, it=it, hs=hs, PQd=PQd, Ts=Ts: e.matmul(
                                    psum[2][hs, it * 64:(it + 1) * 64], lhsT=PQd[hs, it, 64:128], rhs=Ts[hs, it, :], start=True, stop=True),
                                    reads=[f"PQ{dst_i}a", f"PQ{dst_i}b", f"TT{src_i}"], writes=["ps2"])
                        P.dve(lambda e, Ts=Ts, Td=Td: e.tensor_tensor(out=Td, in0=psum[2][:, 0:512].rearrange("p (a b) -> p a b", a=8), in1=Ts, op=ALU.add),
                              reads=["ps2", f"TT{src_i}"], writes=[f"TT{dst_i}", "ps2"])
                    Tfin = v(TT[1])
                    Sv, Sbv, Wbv, Ubv, Yv = v(Sst), v(Sb), v(Wb), v(Ub), v(Ybuf)
                    for c in range(4):
                        for p in range(2):
                            it = p * 4 + c
                            for hh in range(2):
                                hs = slice(hh * 64, hh * 64 + 64)
                                P.pe(lambda e, p=p, c=c, hs=hs: e.matmul(psum[3][hs, p * 64:(p + 1) * 64], lhsT=ARv[hs, p, c, 0, :], rhs=Sbv[hs, p, :],
                                                                         start=True, stop=False), reads=["AR", "Sb"], writes=["ps3"])
                                P.pe(lambda e, p=p, c=c, hs=hs, it=it: e.matmul(psum[3][hs, p * 64:(p + 1) * 64], lhsT=MSv[hs, it, 128:192], rhs=TKv[hs, p, c, 2, :],
                                                                                start=False, stop=True), reads=["MS", "TK0", "TK1"], writes=["ps3"])
                        P.act(lambda e: e.activation(out=Wbv.rearrange("p a b -> p (a b)"), in_=psum[3][:, 0:128], func=AF.Copy),
                              reads=["ps3"], writes=["Wb", "ps3"])
                        for p in range(2):
                            it = p * 4 + c
                            for hh in range(2):
                                hs = slice(hh * 64, hh * 64 + 64)
                                P.pe(lambda e, p=p, hs=hs, it=it: e.matmul(psum[4][hs, p * 64:(p + 1) * 64], lhsT=Tfin[hs, it, :], rhs=Wbv[hs, p, :],
                                                                           start=True, stop=True), reads=["TT1", "Wb"], writes=["ps4"])
                        P.act(lambda e: e.activation(out=Ubv.rearrange("p a b -> p (a b)"), in_=psum[4][:, 0:128], func=AF.Copy),
                              reads=["ps4"], writes=["Ub", "ps4"])
                        for p in range(2):
                            for hh in range(2):
                                hs = slice(hh * 64, hh * 64 + 64)
                                P.pe(lambda e, p=p, c=c, hs=hs: e.matmul(psum[6][hs, p * 64:(p + 1) * 64], lhsT=TKv[hs, p, c, 0, :], rhs=Ubv[hs, p, :],
                                                                         start=True, stop=False), reads=["TK0", "TK1", "Ub"], writes=["ps6"])
                                P.pe(lambda e, p=p, c=c, hs=hs: e.matmul(psum[6][hs, p * 64:(p + 1) * 64], lhsT=TKv[hs, p, c, 1, :], rhs=TKv[hs, p, c, 2, :],
                                                                         start=False, stop=True), reads=["TK0", "TK1"], writes=["ps6"])
                        for p in range(2):
                            it = p * 4 + c
                            for hh in range(2):
                                hs = slice(hh * 64, hh * 64 + 64)
                                P.pe(lambda e, p=p, c=c, hs=hs: e.matmul(psum[5][hs, p * 64:(p + 1) * 64], lhsT=ARv[hs, p, c, 1, :], rhs=Sbv[hs, p, :],
                                                                         start=True, stop=False), reads=["AR", "Sb"], writes=["ps5"])
                                P.pe(lambda e, p=p, hs=hs, it=it: e.matmul(psum[5][hs, p * 64:(p + 1) * 64], lhsT=MSv[hs, it, 64:128], rhs=Ubv[hs, p, :],
                                                                           start=False, stop=False), reads=["MS", "Ub"], writes=["ps5"])
                                P.pe(lambda e, p=p, c=c, hs=hs, it=it: e.matmul(psum[5][hs, p * 64:(p + 1) * 64], lhsT=MSv[hs, it, 192:256], rhs=TKv[hs, p, c, 2, :],
                                                                                start=False, stop=True), reads=["MS", "TK0", "TK1"], writes=["ps5"])
                        P.dve(lambda e: e.tensor_tensor(out=Sv, in0=psum[6][:, 0:128].rearrange("p (a b) -> p a b", a=2), in1=Sv, op=ALU.add),
                              reads=["ps6", "Sst"], writes=["Sst", "ps6"])
                        P.dve(lambda e, c=c: e.tensor_tensor(out=Sv, in0=Sv, in1=v(eL)[:, :, c * 64 + 63:c * 64 + 64].to_broadcast([128, 2, 64]), op=ALU.mult),
                              reads=["Sst", "eL"], writes=["Sst"])
                        P.act(lambda e: e.activation(out=Sbv, in_=Sv, func=AF.Copy), reads=["Sst"], writes=["Sb"])
                        P.dve(lambda e, c=c: e.tensor_copy(out=Yv[:, c, :, :], in_=psum[5][:, 0:128].rearrange("p (a b) -> p a b", a=2)),
                              reads=["ps5"], writes=["Ybuf", "ps5"])
                    if g == NGR - 1:
                        P.dma(lambda e: e.dma_start(out=swp[l], in_=Sv), reads=["Sst"], writes=["swp"], semkey="swp")
                    gsv = v(gs)
                    Y8 = Yv.rearrange("p c a b -> p (c a) b")
                    P.dve(lambda e: e.tensor_reduce(out=gsv[:, 0, :], in_=Y8, axis=AX.X, op=ALU.add), reads=["Ybuf"], writes=["gs"])
                    P.dve(lambda e: e.tensor_scalar_mul(out=gsv[:, 0, :], in0=gsv[:, 0, :], scalar1=-1.0 / 64), reads=["gs"], writes=["gs"])
                    P.dve(lambda e: e.tensor_tensor(out=v(yc), in0=Y8, in1=gsv[:, 0, :].unsqueeze(2).to_broadcast([128, 8, 64]), op=ALU.add),
                           reads=["Ybuf", "gs"], writes=["pb"])
                    P.dve(lambda e: e.tensor_tensor(out=v(ysq), in0=v(yc), in1=v(yc), op=ALU.mult), reads=["pb"], writes=["sc"])
                    P.dve(lambda e: e.tensor_reduce(out=gsv[:, 1, :], in_=v(ysq), axis=AX.X, op=ALU.add), reads=["sc"], writes=["gs"])
                    P.act(lambda e: e.activation(out=gsv[:, 2, :], in_=gsv[:, 1, :], func=AF.Sqrt, scale=1.0 / 64, bias=epsc[:, 1:2]),
                          reads=["gs", "epsc"], writes=["gs"])
                    P.dve(lambda e: e.reciprocal(out=gsv[:, 3, :], in_=gsv[:, 2, :]), reads=["gs"], writes=["gs"])
                    P.dve(lambda e: e.tensor_tensor(out=v(yh), in0=v(yc), in1=gsv[:, 3, :].unsqueeze(2).to_broadcast([128, 8, 64]), op=ALU.mult),
                          reads=["pb", "gs"], writes=["yh"])
                    for c in range(4):
                        for p in range(2):
                            for hh in range(2):
                                hs = slice(hh * 64, hh * 64 + 64)
                                P.pe(lambda e, c=c, p=p, hs=hs: e.transpose(
                                    out=psum[7][:].bitcast(BF16)[hs, p * G + c * 64:p * G + (c + 1) * 64],
                                    in_=v(yh)[hs, c * 2 + p, :], identity=cs["ident"][hs, hs]), reads=["yh", "k_ident"], writes=["ps7"])
                    for p in range(2):
                        P.act(lambda e, p=p: e.activation(out=v(yz)[:, p, :], in_=psum[7][:].bitcast(BF16)[:, p * G:(p + 1) * G], func=AF.Identity,
                                                          scale=rpv[:, 5, p:p + 1], bias=rpv[:, 6, p:p + 1]),
                              reads=["ps7", "rp"], writes=["rn", "ps7"])
                    P.dve(lambda e: e.tensor_tensor(out=v(yz), in0=v(yz), in1=v(bonus), op=ALU.add), reads=["rn", "bonus"], writes=["rn"])
                    P.dve(lambda e, gq=gsil2[g % 2]: e.tensor_tensor(out=v(zT)[:, 2:4, :], in0=v(yz), in1=v(gq)[:, 2:4, :], op=ALU.mult),
                          reads=["rn", f"gsil{g % 2}"], writes=["zT"])

                for bl in range(G // 128 if 'op' not in SKIP else 0):
                    for cgp in range(4):
                        for i in range(4):
                            P.pe(lambda e, bl=bl, cgp=cgp, i=i: e.matmul(psum[cgp][:, 0:512], lhsT=v(zT)[:, i, bl * 128:(bl + 1) * 128],
                                                                         rhs=Wo[:, i, cgp * 512:(cgp + 1) * 512], start=(i == 0), stop=(i == 3)),
                                 reads=["zT"] + [f"Wo{j}" for j in range(4)], writes=[f"ps{cgp}"])
                        if cgp % 2 == 0:
                            P.act(lambda e, cgp=cgp: e.activation(out=ob[0][:, cgp * 512:(cgp + 1) * 512], in_=psum[cgp][:, 0:512], func=AF.Copy),
                                  reads=[f"ps{cgp}"], writes=["ob", f"ps{cgp}"])
                        else:
                            P.dve(lambda e, cgp=cgp: e.tensor_copy(out=ob[0][:, cgp * 512:(cgp + 1) * 512], in_=psum[cgp][:, 0:512]),
                                  reads=[f"ps{cgp}"], writes=["ob", f"ps{cgp}"])
                    tok0 = g * G + bl * 128
                    row0 = tok0
                    for q in range(2):
                        P.dma(lambda e, q=q, row0=row0: e.dma_start(out=rs_in[l][q].ap()[row0:row0 + 128, :], in_=ob[0][:, q * 1024:(q + 1) * 1024]),
                              reads=["ob"], writes=[f"rs_in{l}"], semkey=f"ob{q}")

    def phase_M_samples(l, Win, WK):
        P.barrier()
        NS = 16
        c7 = Carve()
        hTs16 = c7.bf([128, 16, NS])
        rp = c7.f32([128, NPAR, 2])
        loraW = c7.bf([128, 2, 128])
        mu = c7.f32([128, 7])
        prv = c7.f32([128, 7, NS])
        shpt = c7.f32([64, 193])
        P.dma(lambda e: e.dma_start(out=v(rp), in_=rpar[l]), writes=["s_rp"])
        loraF = c7.f32([128, 2, 128])
        P.dma(lambda e: e.dma_start(out=v(loraF), in_=loraw[l]), writes=["s_loraF"])
        P.act(lambda e: e.activation(out=v(loraW), in_=v(loraF), func=AF.Copy), reads=["s_loraF"], writes=["s_loraW"])
        P.dma(lambda e: e.dma_start(out=mu[0], in_=mu_col[l]), writes=["s_mu"])
        P.dma(lambda e: e.dma_start(out=v(prv), in_=prev_s[l]), writes=["s_prv"])
        P.dma(lambda e: e.dma_start(out=shpt[0], in_=shpar_in[l]), writes=["s_shpt"])
        curS = c7.f32([128, 7, NS])
        mixS = c7.f32([128, 7, NS])
        qf = c7.f32([128, 3, NS])
        qbS = c7.bf([128, 3, NS])
        t1S = c7.f32([128, 3, NS])
        qrotS = c7.f32([128, 3, NS])
        gsS = c7.f32([128, 4, NS])
        vS = c7.f32([16, 64])
        for r2 in range(4):
            P.dma(lambda e, r2=r2: e.dma_start(out=v(hTs16)[:, :, 4 * r2:4 * r2 + 4],
                                               in_=ago_hs[l].ap()[r2 * 128:(r2 + 1) * 128, :].rearrange("p (k c) -> p k c", k=16)),
                  reads=[f"ago_hs{l}"], writes=["s_hT"], eng=("sp" if r2 % 2 == 0 else "act"), semkey=f"s_hT{r2 % 2}")
        for ct in range(NCT):
            pi = ct % 4
            for k in range(16):
                P.pe(lambda e, ct=ct, k=k, pi=pi: e.matmul(psum[pi][:, 0:NS], lhsT=Win[:, k, ct * 128:(ct + 1) * 128], rhs=v(hTs16)[:, k, :],
                                                           start=(k == 0), stop=(k == 15)), reads=["s_hT"], writes=[f"ps{pi}"])
            if ct < 3:
                P.act(lambda e, ct=ct, pi=pi: e.activation(out=v(qf)[:, ct, :], in_=psum[pi][:, 0:NS], func=AF.Copy),
                      reads=[f"ps{pi}"], writes=["s_qf", f"ps{pi}"])
            elif ct < 10:
                P.act(lambda e, ct=ct, pi=pi: e.activation(out=v(curS)[:, ct - 3, :], in_=psum[pi][:, 0:NS], func=AF.Copy),
                      reads=[f"ps{pi}"], writes=["s_cur", f"ps{pi}"])
            else:
                P.act(lambda e, ct=ct, pi=pi: e.activation(out=v(gsS)[:, ct - 10, :], in_=psum[pi][:, 0:NS], func=AF.Silu),
                      reads=[f"ps{pi}"], writes=["s_gs", f"ps{pi}"])
        for k in range(16):
            P.pe(lambda e, k=k: e.matmul(psum[4][0:NS, 0:64], lhsT=v(hTs16)[:, k, :], rhs=Win[:, k, NCT * 128:NCT * 128 + 64],
                                         start=(k == 0), stop=(k == 15)), reads=["s_hT"], writes=["ps4"])
        P.act(lambda e: e.activation(out=vS[0], in_=psum[4][0:NS, 0:64], func=AF.Copy), reads=["ps4"], writes=["s_vS", "ps4"])
        P.dma(lambda e: e.dma_start(out=shs[l], in_=v(curS)), reads=["s_cur"], writes=["shs"], semkey="shs")
        P.act(lambda e: e.activation(out=v(qbS), in_=v(qf), func=AF.Copy), reads=["s_qf"], writes=["s_qb"])
        for ct in range(3):
            P.pe(lambda e, ct=ct: e.matmul(psum[5][:, ct * NS:(ct + 1) * NS], lhsT=cs["prot"][:], rhs=v(qbS)[:, ct, :], start=True, stop=True),
                 reads=["s_qb", "k_prot"], writes=["ps5"])
        P.dve(lambda e: e.tensor_scalar_mul(out=t1S[0], in0=psum[5][:, 0:3 * NS], scalar1=cs["cs_s"][:, 1:2]), reads=["ps5", "k_cs_s"],
              writes=["s_t1", "ps5"])
        P.dve(lambda e: e.scalar_tensor_tensor(out=qrotS[0], in0=qf[0], scalar=cs["cs_s"][:, 0:1], in1=t1S[0], op0=ALU.mult, op1=ALU.add),
              reads=["s_qf", "s_t1", "k_cs_s"], writes=["s_qrot"])
        P.dve(lambda e: e.tensor_tensor(out=v(mixS), in0=v(prv), in1=v(curS), op=ALU.subtract), reads=["s_prv", "s_cur"], writes=["s_mix"])
        P.dve(lambda e: e.tensor_tensor(out=v(mixS), in0=v(mixS), in1=mu[0].unsqueeze(2).to_broadcast([128, 7, NS]), op=ALU.mult),
              reads=["s_mix", "s_mu"], writes=["s_mix"])
        P.dve(lambda e: e.tensor_tensor(out=v(mixS), in0=v(mixS), in1=v(curS), op=ALU.add), reads=["s_mix", "s_cur"], writes=["s_mix"])
        mx = v(mixS)
        rpv = v(rp)
        twadS = c7.bf([128, NS])
        sigS = c7.f32([128, 2, NS])
        aS = c7.f32([128, 2, NS])
        wS = c7.f32([128, 2, NS])
        kkrS = c7.f32([128, 2, NS])
        sqS = c7.bf([128, 2, NS])
        rnS = c7.f32([128, 2, NS])
        kkS = c7.f32([128, 2, NS])
        uuS = c7.f32([128, 2, NS])
        keffS = c7.f32([128, 2, NS])
        P.act(lambda e: e.activation(out=twadS[0][0:64, :], in_=mx[0:64, 6, :], func=AF.Tanh), reads=["s_mix"], writes=["s_twad"])
        P.act(lambda e: e.activation(out=twadS[0][64:128, :], in_=mx[64:128, 6, :], func=AF.Copy), reads=["s_mix"], writes=["s_twad"])
        for p in range(2):
            P.pe(lambda e, p=p: e.matmul(psum[p][:, 0:NS], lhsT=v(loraW)[0:64, p, :], rhs=twadS[0][0:64, :], start=True, stop=True),
                 reads=["s_loraW", "s_twad"], writes=[f"ps{p}"])
            P.act(lambda e, p=p: e.activation(out=v(sigS)[:, p, :], in_=psum[p][:, 0:NS], func=AF.Sigmoid, bias=rpv[:, 0, p:p + 1]),
                  reads=[f"ps{p}", "s_rp"], writes=["s_sig", f"ps{p}"])
            P.pe(lambda e, p=p: e.matmul(psum[2 + p][:, 0:NS], lhsT=v(loraW)[64:128, p, :], rhs=twadS[0][64:128, :], start=True, stop=True),
                 reads=["s_loraW", "s_twad"], writes=[f"ps{2 + p}"])
            P.act(lambda e, p=p: e.activation(out=v(aS)[:, p, :], in_=psum[2 + p][:, 0:NS], func=AF.Sigmoid, bias=rpv[:, 1, p:p + 1]),
                  reads=[f"ps{2 + p}", "s_rp"], writes=["s_a", f"ps{2 + p}"])
        P.act(lambda e: e.activation(out=v(wS), in_=v(sigS), func=AF.Exp, scale=-CDEC), reads=["s_sig"], writes=["s_w"])
        bc = lambda j: rpv[:, j, :].unsqueeze(2).to_broadcast([128, 2, NS])
        P.dve(lambda e: e.tensor_tensor(out=v(kkrS), in0=mx[:, 2:4, :], in1=bc(2), op=ALU.mult), reads=["s_mix", "s_rp"], writes=["s_kkr"])
        P.dve(lambda e: e.tensor_tensor(out=v(sqS), in0=v(kkrS), in1=v(kkrS), op=ALU.mult), reads=["s_kkr"], writes=["s_sq"])
        for p in range(2):
            P.pe(lambda e, p=p: e.matmul(psum[p][:, 0:NS], lhsT=cs["bones"][:], rhs=v(sqS)[:, p, :], start=True, stop=True),
                 reads=["s_sq", "k_bones"], writes=[f"ps{p}"])
            P.act(lambda e, p=p: e.activation(out=v(rnS)[:, p, :], in_=psum[p][:, 0:NS], func=AF.Sqrt), reads=[f"ps{p}"], writes=["s_rn", f"ps{p}"])
        P.dve(lambda e: e.tensor_scalar_max(out=v(rnS), in0=v(rnS), scalar1=1e-12), reads=["s_rn"], writes=["s_rn"])
        P.dve(lambda e: e.reciprocal(out=v(rnS), in_=v(rnS)), reads=["s_rn"], writes=["s_rn"])
        P.dve(lambda e: e.tensor_tensor(out=v(kkS), in0=v(kkrS), in1=v(rnS), op=ALU.mult), reads=["s_kkr", "s_rn"], writes=["s_kk"])
        P.dve(lambda e: e.scalar_tensor_tensor(out=v(uuS), in0=v(aS), scalar=-1.0, in1=bc(3), op0=ALU.add, op1=ALU.mult),
              reads=["s_a", "s_rp"], writes=["s_uu"])
        P.dve(lambda e: e.scalar_tensor_tensor(out=v(keffS), in0=v(uuS), scalar=1.0, in1=mx[:, 2:4, :], op0=ALU.add, op1=ALU.mult),
              reads=["s_uu", "s_mix"], writes=["s_keff"])
        tmS = c7.f32([16, 20, 128])
        srcs = []
        for (t_, key, lo) in ((mixS, "s_mix", 0), (wS, "s_w", 0), (keffS, "s_keff", 0), (mixS, "s_mix", 4), (kkS, "s_kk", 0), (aS, "s_a", 0),
                              (qrotS, "s_qrot", 0), (gsS, "s_gs", 0), (gsS, "s_gs", 2)):
            for p in range(2):
                srcs.append((v(t_)[:, lo + p, :], key))
        srcs.append((v(qrotS)[:, 2, :], "s_qrot"))
        for n0 in range(0, len(srcs), 4):
            bank = 4 + (n0 // 4) % 4
            grp = srcs[n0:n0 + 4]
            for i_, (ap_, key) in enumerate(grp):
                P.pe(lambda e, ap_=ap_, i_=i_, bank=bank: e.transpose(out=psum[bank][0:NS, i_ * 128:(i_ + 1) * 128], in_=ap_, identity=cs["identf"][:]),
                     reads=[key, "k_identf"], writes=[f"ps{bank}"])
            n_ = len(grp)
            evac = P.act if (n0 // 4) % 2 == 0 else P.dve
            if (n0 // 4) % 2 == 0:
                P.act(lambda e, n0=n0, n_=n_, bank=bank: e.activation(out=v(tmS)[:, n0:n0 + n_, :].rearrange("p a b -> p (a b)"),
                                                                   in_=psum[bank][0:NS, 0:n_ * 128], func=AF.Copy),
                      reads=[f"ps{bank}"], writes=["s_tm", f"ps{bank}"])
            else:
                P.dve(lambda e, n0=n0, n_=n_, bank=bank: e.tensor_copy(out=v(tmS)[:, n0:n0 + n_, :].rearrange("p a b -> p (a b)"),
                                                                    in_=psum[bank][0:NS, 0:n_ * 128]),
                      reads=[f"ps{bank}"], writes=["s_tm", f"ps{bank}"])
        for kind in range(9):
            P.dma(lambda e, kind=kind: e.dma_start(out=smp_d[l].ap()[:, :, kind, :].rearrange("h s d -> s h d"),
                                                   in_=v(tmS)[:, 2 * kind:2 * kind + 2, :].rearrange("s t (h d) -> s (t h) d", h=2)),
                  reads=["s_tm"], writes=["smp_d"], eng=("sp" if kind % 2 == 0 else "act"), semkey=f"smp{kind % 2}")
        P.dma(lambda e: e.dma_start(out=kn_d[l].ap(), in_=v(tmS)[:, 18, 0:64]), reads=["s_tm"], writes=["kn_d"], semkey="kn")
        P.dma(lambda e: e.dma_start(out=vn_d[l].ap(), in_=vS[0]), reads=["s_vS"], writes=["vn_d"], semkey="vn")
        P.dma(lambda e: e.dma_start(out=cks[l][:, 0:127, :], in_=ck_in[l][:, 1:128, :]), writes=["cks"], semkey="cks0")
        P.dma(lambda e: e.dma_start(out=cvs[l][:, 0:127, :], in_=cv_in[l][:, 1:128, :]), writes=["cvs"], eng="act", semkey="cvs0")
        P.dma(lambda e: e.dma_start(out=cks[l][:, 127, :], in_=v(tmS)[:, 18, 0:64]), reads=["s_tm"], writes=["cks"], semkey="cks1")
        P.dma(lambda e: e.dma_start(out=cvs[l][:, 127, :], in_=vS[0]), reads=["s_vS"], writes=["cvs"], eng="act", semkey="cvs1")
        SH = c7.f32([64, 9, 64])
        SHv = v(SH)
        P.dma(lambda e: e.dma_start(out=SHv, in_=smp_d[l].ap().rearrange("h s k d -> (h s) k d")), reads=["smp_d"], writes=["s_SH"])
        KV = c7.f32([64, 129, 64])
        KVv = v(KV)
        tmpA = c7.f32([64, 33 * 64])
        scs = c7.f32([64, 129])
        pS = c7.f32([64, 129])
        sm2 = c7.f32([64, 8])
        oS = c7.f32([64, 64])
        prt = c7.f32([64, 64])
        zs = c7.f32([64, 2, 64])
        for h_ in range(4):
            q_ = "sp" if h_ % 2 == 0 else "act"
            P.dma(lambda e, h_=h_: e.dma_start(out=KVv[16 * h_:16 * h_ + 16, 0:128, :], in_=ck_in[l]), writes=["s_KV"], eng=q_, semkey=f"s_KV{h_}")
            P.dma(lambda e, h_=h_: e.dma_start(out=KVv[16 * h_:16 * h_ + 16, 128, :], in_=kn_d[l].ap()), reads=["kn_d"], writes=["s_KV"], eng=q_,
                  semkey=f"s_KV{h_}")
        PCH = [(0, 32), (32, 64), (64, 96), (96, 129)]
        for ci, (a_, b_) in enumerate(PCH):
            n_ = b_ - a_
            tv = tmpA[0][:, 0:n_ * 64].rearrange("p (n d) -> p n d", d=64)
            f_ = P.dve
            f_(lambda e, a_=a_, b_=b_, n_=n_, tv=tv: e.tensor_tensor(out=tv, in0=KVv[:, a_:b_, :], in1=SHv[:, 6, :].unsqueeze(1).to_broadcast([64, n_, 64]),
                                                              op=ALU.mult), reads=["s_KV", "s_SH"], writes=["s_tmpA"])
            P.dve(lambda e, a_=a_, b_=b_, tv=tv: e.tensor_reduce(out=scs[0][:, a_:b_], in_=tv, axis=AX.X, op=ALU.add), reads=["s_tmpA"], writes=["s_scs"])
        s2 = sm2[0]
        snkc = shpt[0][:, 192:193]
        P.dve(lambda e: e.tensor_reduce(out=s2[:, 0:1], in_=scs[0], axis=AX.X, op=ALU.max), reads=["s_scs"], writes=["s_sm2"])
        P.dve(lambda e: e.scalar_tensor_tensor(out=s2[:, 1:2], in0=s2[:, 0:1], scalar=0.125, in1=snkc, op0=ALU.mult, op1=ALU.max),
              reads=["s_sm2", "s_shpt"], writes=["s_sm2"])
        P.dve(lambda e: e.tensor_scalar_mul(out=s2[:, 2:3], in0=s2[:, 1:2], scalar1=-1.0), reads=["s_sm2"], writes=["s_sm2"])
        P.dve(lambda e: e.tensor_tensor(out=s2[:, 4:5], in0=snkc, in1=s2[:, 1:2], op=ALU.subtract), reads=["s_sm2", "s_shpt"], writes=["s_sm2"])
        P.act(lambda e: e.activation(out=pS[0], in_=scs[0], func=AF.Exp, scale=0.125, bias=s2[:, 2:3], accum_out=s2[:, 3:4]),
              reads=["s_scs", "s_sm2"], writes=["s_pS", "s_sm2"])
        P.act(lambda e: e.activation(out=s2[:, 4:5], in_=s2[:, 4:5], func=AF.Exp), reads=["s_sm2"], writes=["s_sm2"])
        P.dve(lambda e: e.tensor_tensor(out=s2[:, 5:6], in0=s2[:, 3:4], in1=s2[:, 4:5], op=ALU.add), reads=["s_sm2"], writes=["s_sm2"])
        P.dve(lambda e: e.reciprocal(out=s2[:, 6:7], in_=s2[:, 5:6]), reads=["s_sm2"], writes=["s_sm2"])
        for h_ in range(4):
            q_ = "sp" if h_ % 2 == 0 else "act"
            P.dma(lambda e, h_=h_: e.dma_start(out=KVv[16 * h_:16 * h_ + 16, 0:128, :], in_=cv_in[l]), reads=["s_scs"], writes=["s_KV"], eng=q_,
                  semkey=f"s_KV{h_}")
            P.dma(lambda e, h_=h_: e.dma_start(out=KVv[16 * h_:16 * h_ + 16, 128, :], in_=vn_d[l].ap()), reads=["vn_d", "s_scs"], writes=["s_KV"],
                  eng=q_, semkey=f"s_KV{h_}")
        for ci, (a_, b_) in enumerate(PCH):
            n_ = b_ - a_
            tv = tmpA[0][:, 0:n_ * 64].rearrange("p (d n) -> p d n", d=64)
            f_ = P.dve
            f_(lambda e, a_=a_, b_=b_, n_=n_, tv=tv: e.tensor_tensor(out=tv, in0=KVv[:, a_:b_, :].rearrange("p n d -> p d n"),
                                                              in1=pS[0][:, a_:b_].unsqueeze(1).to_broadcast([64, 64, n_]), op=ALU.mult),
               reads=["s_KV", "s_pS"], writes=["s_tmpA"])
            if ci == 0:
                P.dve(lambda e, tv=tv: e.tensor_reduce(out=oS[0], in_=tv, axis=AX.X, op=ALU.add), reads=["s_tmpA"], writes=["s_oS"])
            else:
                P.dve(lambda e, tv=tv: e.tensor_reduce(out=prt[0], in_=tv, axis=AX.X, op=ALU.add), reads=["s_tmpA"], writes=["s_prt"])
                P.dve(lambda e: e.tensor_tensor(out=oS[0], in0=oS[0], in1=prt[0], op=ALU.add), reads=["s_oS", "s_prt"], writes=["s_oS"])
        zsv = v(zs)
        P.dve(lambda e: e.tensor_scalar_mul(out=oS[0], in0=oS[0], scalar1=s2[:, 6:7]), reads=["s_oS", "s_sm2"], writes=["s_oS"])
        P.dve(lambda e: e.tensor_tensor(out=zsv[:, 0, :], in0=oS[0], in1=SHv[:, 7, :], op=ALU.mult), reads=["s_oS", "s_SH"], writes=["s_zs"])
        Ssm = c7.f32([64, 64, 64])
        tmpS = c7.f32([64, 64, 64])
        sv = c7.f32([64, 6, 64])
        g2 = c7.f32([64, 8])
        Sv_, Tv_, svv = v(Ssm), v(tmpS), v(sv)
        for h_ in range(4):
            P.dma(lambda e, h_=h_: e.dma_start(out=Sv_[16 * h_:16 * h_ + 16], in_=st_in[l][:, h_]), writes=["s_Ssm"],
                  eng=("sp" if h_ % 2 == 0 else "act"), semkey=f"s_Sl{h_}")
        bi = lambda ap_: ap_.unsqueeze(1).to_broadcast([64, 64, 64])
        bj = lambda ap_: ap_.unsqueeze(2).to_broadcast([64, 64, 64])
        P.dve(lambda e: e.tensor_scalar_mul(out=svv[:, 0, :], in0=SHv[:, 4, :], scalar1=-1.0), reads=["s_SH"], writes=["s_sv0"])
        P.dve(lambda e: e.tensor_tensor(out=svv[:, 1, :], in0=SHv[:, 4, :], in1=SHv[:, 5, :], op=ALU.mult), reads=["s_SH"], writes=["s_sv1"])
        P.dve(lambda e: e.tensor_tensor(out=Tv_, in0=Sv_, in1=bi(svv[:, 0, :]), op=ALU.mult), reads=["s_Ssm", "s_sv0"], writes=["s_tmpS"])
        P.dve(lambda e: e.tensor_reduce(out=svv[:, 2, :], in_=Tv_, axis=AX.X, op=ALU.add), reads=["s_tmpS"], writes=["s_sv2"])
        P.dve(lambda e: e.tensor_tensor(out=Sv_, in0=Sv_, in1=bi(SHv[:, 1, :]), op=ALU.mult), reads=["s_Ssm", "s_SH", "s_tmpS"], writes=["s_Ssm"])
        P.dve(lambda e: e.tensor_tensor(out=Tv_, in0=bj(svv[:, 2, :]), in1=bi(svv[:, 1, :]), op=ALU.mult), reads=["s_sv2", "s_sv1"], writes=["s_tmpS"])
        P.dve(lambda e: e.tensor_tensor(out=Sv_, in0=Sv_, in1=Tv_, op=ALU.add), reads=["s_Ssm", "s_tmpS"], writes=["s_Ssm"])
        P.dve(lambda e: e.tensor_tensor(out=Tv_, in0=bj(SHv[:, 3, :]), in1=bi(SHv[:, 2, :]), op=ALU.mult), reads=["s_SH"], writes=["s_tmpS"])
        P.dve(lambda e: e.tensor_tensor(out=Sv_, in0=Sv_, in1=Tv_, op=ALU.add), reads=["s_Ssm", "s_tmpS"], writes=["s_Ssm"])
        for h_ in range(4):
            P.dma(lambda e, h_=h_: e.dma_start(out=sws[l][:, h_], in_=Sv_[16 * h_:16 * h_ + 16]), reads=["s_Ssm"], writes=["sws"],
                  eng=("sp" if h_ % 2 == 0 else "act"), semkey=f"s_Ss{h_}")
        P.dve(lambda e: e.tensor_tensor(out=Tv_, in0=Sv_, in1=bi(SHv[:, 0, :]), op=ALU.mult), reads=["s_Ssm", "s_SH"], writes=["s_tmpS"])
        P.dve(lambda e: e.tensor_reduce(out=svv[:, 3, :], in_=Tv_, axis=AX.X, op=ALU.add), reads=["s_tmpS"], writes=["s_sv3"])
        g2v = g2[0]
        P.dve(lambda e: e.tensor_reduce(out=g2v[:, 0:1], in_=svv[:, 3, :], axis=AX.X, op=ALU.add), reads=["s_sv3"], writes=["s_g2"])
        P.dve(lambda e: e.tensor_scalar_mul(out=g2v[:, 0:1], in0=g2v[:, 0:1], scalar1=-1.0 / 64), reads=["s_g2"], writes=["s_g2"])
        P.dve(lambda e: e.tensor_scalar_add(out=svv[:, 3, :], in0=svv[:, 3, :], scalar1=g2v[:, 0:1]), reads=["s_sv3", "s_g2"], writes=["s_sv3"])
        P.dve(lambda e: e.tensor_tensor(out=svv[:, 4, :], in0=svv[:, 3, :], in1=svv[:, 3, :], op=ALU.mult), reads=["s_sv3"], writes=["s_sv4"])
        P.dve(lambda e: e.tensor_reduce(out=g2v[:, 1:2], in_=svv[:, 4, :], axis=AX.X, op=ALU.add), reads=["s_sv4"], writes=["s_g2"])
        P.act(lambda e: e.activation(out=g2v[:, 2:3], in_=g2v[:, 1:2], func=AF.Sqrt, scale=1.0 / 64, bias=epsc[0:64, 1:2]),
              reads=["s_g2", "epsc"], writes=["s_g2"])
        P.dve(lambda e: e.reciprocal(out=g2v[:, 3:4], in_=g2v[:, 2:3]), reads=["s_g2"], writes=["s_g2"])
        P.dve(lambda e: e.tensor_scalar_mul(out=svv[:, 3, :], in0=svv[:, 3, :], scalar1=g2v[:, 3:4]), reads=["s_sv3", "s_g2"], writes=["s_sv3"])
        P.dve(lambda e: e.tensor_tensor(out=svv[:, 3, :], in0=svv[:, 3, :], in1=shpt[0][:, 0:64], op=ALU.mult), reads=["s_sv3", "s_shpt"], writes=["s_sv3"])
        P.dve(lambda e: e.tensor_tensor(out=svv[:, 3, :], in0=svv[:, 3, :], in1=shpt[0][:, 64:128], op=ALU.add), reads=["s_sv3", "s_shpt"], writes=["s_sv3"])
        P.dve(lambda e: e.tensor_tensor(out=svv[:, 4, :], in0=SHv[:, 0, :], in1=SHv[:, 2, :], op=ALU.mult), reads=["s_SH", "s_g2"], writes=["s_sv4"])
        P.dve(lambda e: e.tensor_tensor(out=svv[:, 4, :], in0=svv[:, 4, :], in1=shpt[0][:, 128:192], op=ALU.mult), reads=["s_sv4", "s_shpt"], writes=["s_sv4"])
        P.dve(lambda e: e.tensor_reduce(out=g2v[:, 4:5], in_=svv[:, 4, :], axis=AX.X, op=ALU.add), reads=["s_sv4"], writes=["s_g2"])
        P.dve(lambda e: e.tensor_scalar_mul(out=svv[:, 5, :], in0=SHv[:, 3, :], scalar1=g2v[:, 4:5]), reads=["s_SH", "s_g2"], writes=["s_sv5"])
        P.dve(lambda e: e.tensor_tensor(out=svv[:, 3, :], in0=svv[:, 3, :], in1=svv[:, 5, :], op=ALU.add), reads=["s_sv3", "s_sv5"], writes=["s_sv3"])
        P.dve(lambda e: e.tensor_tensor(out=zsv[:, 1, :], in0=svv[:, 3, :], in1=SHv[:, 8, :], op=ALU.mult), reads=["s_sv3", "s_SH"], writes=["s_zs"])
        zst = c7.f32([16, 2, 4, 64])
        zTs = c7.bf([128, 4, NS])
        obS = c7.f32([16, D])
        P.dma(lambda e: e.dma_start(out=zs_d[l].ap().rearrange("h s k d -> (h s) k d"), in_=zsv), reads=["s_zs"], writes=["zs_d"], semkey="zs_d")
        for k_ in range(2):
            P.dma(lambda e, k_=k_: e.dma_start(out=v(zst)[:, k_, :, :], in_=zs_d[l].ap()[:, :, k_, :].rearrange("h s d -> s h d")), reads=["zs_d"], writes=["s_zst"], semkey="zst")
        zstv = v(zst)
        for k_ in range(2):
            for pr in range(2):
                i_ = k_ * 2 + pr
                P.pe(lambda e, k_=k_, pr=pr, i_=i_: e.transpose(out=psum[6][:, i_ * NS:(i_ + 1) * NS],
                                                                in_=zstv[:, k_, 2 * pr:2 * pr + 2, :].rearrange("s h d -> s (h d)"),
                                                                identity=cs["identf"][0:NS, 0:NS]), reads=["s_zst", "k_identf"], writes=["ps6"])
        P.act(lambda e: e.activation(out=zTs[0], in_=psum[6][:, 0:4 * NS], func=AF.Copy), reads=["ps6"], writes=["s_zTs", "ps6"])
        for cgp in range(4):
            for i_ in range(4):
                P.pe(lambda e, cgp=cgp, i_=i_: e.matmul(psum[cgp][0:NS, 0:512], lhsT=v(zTs)[:, i_, :], rhs=Wo[:, i_, cgp * 512:(cgp + 1) * 512],
                                                        start=(i_ == 0), stop=(i_ == 3)), reads=["s_zTs"], writes=[f"ps{cgp}"])
            P.dve(lambda e, cgp=cgp: e.tensor_copy(out=obS[0][:, cgp * 512:(cgp + 1) * 512], in_=psum[cgp][0:NS, 0:512]),
                  reads=[f"ps{cgp}"], writes=["s_obS", f"ps{cgp}"])
        P.dma(lambda e: e.dma_start(out=rss_in[l].ap(), in_=obS[0]), reads=["s_obS"], writes=[f"rss_in{l}"], semkey="obS")
        P.op("pool", lambda e: e.collective_compute("ReduceScatter", ALU.add, replica_groups=RG,
                                                     ins=[rss_in[l].ap().opt()], outs=[rss_out[l].ap().opt()]),
             reads=[f"rss_in{l}"], writes=[f"rss_out{l}"], cc=True, semkey="cc")

    def reduce_scatter(l):
        for q in range(2):
            P.op("pool", lambda e, q=q: e.collective_compute("ReduceScatter", ALU.add, replica_groups=RG,
                                                          ins=[rs_in[l][q].ap().opt()], outs=[rs_out[l][q].ap().opt()], dma_qos="P3"),
                 reads=[f"rs_in{l}"], writes=[f"rs_out{l}"], cc=True, semkey="cc", nb=True)

    def phase_O(l):
        nonlocal NOFF
        c4 = Carve()
        gateB = c4.f32([128, D])
        fgB = c4.f32([128, D]) if l == NL - 1 else None
        ot = [c4.f32([128, D]) for _ in range(2)]
        xo = [c4.f32([128, D]) for _ in range(2)]
        fs = c4.f32([128, 2])
        NOFF = c4.off
        P.dma(lambda e: e.dma_start(out=gateB[0][:, 0:512],
                                    in_=ago_mod.ap()[2 * 17:2 * 17 + 1, l * 1536 + 1024:(l + 1) * 1536].partition_broadcast(128)[:, 0, :]),
              reads=["ago_mod"], writes=["gateB"], semkey="gateB")
        P.dma(lambda e: e.dma_start(out=gateB[0][:, 512:2048],
                                    in_=ago_mod.ap()[3 * 17:3 * 17 + 1, l * 1536:(l + 1) * 1536].partition_broadcast(128)[:, 0, :]),
              reads=["ago_mod"], writes=["gateB"], semkey="gateB")
        if l == NL - 1:
            P.dma(lambda e: e.dma_start(out=fgB[0], in_=fg_row[0:1, :].partition_broadcast(128)[:, 0, :]), writes=["fgB"])
        xsrc = x_own if l == 0 else x1_d.ap()

        def o_loads(t):
            s = t % 2
            for q in range(2):
                P.dma(lambda e, t=t, s=s, q=q: e.dma_start(out=ot[s][0][:, q * 1024:(q + 1) * 1024], in_=rs_out[l][q].ap()[t * 128:(t + 1) * 128, :]),
                      reads=[f"rs_out{l}"], writes=[f"ot{s}"], semkey=f"ot{s}")
            P.dma(lambda e, t=t, s=s: e.dma_start(out=xo[s][0], in_=xsrc[t * 128:(t + 1) * 128, :]),
                  reads=(["x1_d"] if l > 0 else []), writes=[f"xo{s}"], semkey=f"xo{s}")

        o_loads(0)
        for t in range(8):
            s = t % 2
            if t + 1 < 8:
                o_loads(t + 1)
            P.dve(lambda e, s=s: e.tensor_tensor(out=ot[s][0], in0=ot[s][0], in1=gateB[0], op=ALU.mult), reads=[f"ot{s}", "gateB"], writes=[f"ot{s}"])
            P.dve(lambda e, s=s: e.tensor_tensor(out=xo[s][0], in0=xo[s][0], in1=ot[s][0], op=ALU.add), reads=[f"ot{s}", f"xo{s}"], writes=[f"xo{s}"])
            if l < NL - 1:
                P.dma(lambda e, t=t, s=s: e.dma_start(out=x1_d.ap()[t * 128:(t + 1) * 128, :], in_=xo[s][0]), reads=[f"xo{s}"], writes=["x1_d"],
                      semkey=f"x1st{s}")
                phase_N_tile(l + 1, t, xo[s][0], f"xo{s}")
            elif 'fin' not in os.environ.get('MK_SKIP', ''):
                P.act(lambda e, s=s: e.activation(out=ot[s][0], in_=xo[s][0], func=AF.Square, accum_out=fs[0][:, 0:1]),
                      reads=[f"xo{s}"], writes=[f"ot{s}", "fs"])
                P.act(lambda e: e.activation(out=fs[0][:, 1:2], in_=fs[0][:, 0:1], func=AF.Sqrt, scale=1.0 / D, bias=epsc[:, 0:1]),
                      reads=["fs", "epsc"], writes=["fs"])
                P.dve(lambda e: e.reciprocal(out=fs[0][:, 1:2], in_=fs[0][:, 1:2]), reads=["fs"], writes=["fs"])
                P.act(lambda e, s=s: e.activation(out=ot[s][0], in_=xo[s][0], func=AF.Copy, scale=fs[0][:, 1:2]),
                      reads=[f"xo{s}", "fs"], writes=[f"ot{s}"])
                P.dve(lambda e, s=s: e.tensor_tensor(out=ot[s][0], in0=ot[s][0], in1=fgB[0], op=ALU.mult), reads=[f"ot{s}", "fgB"], writes=[f"ot{s}"])
                P.dma(lambda e, t=t, s=s: e.dma_start(out=y_own[t * 128:(t + 1) * 128, :], in_=ot[s][0]), reads=[f"ot{s}"], writes=["y_own"],
                      semkey=f"yst{s}")


        c8 = Carve()
        c8.off = NOFF + (6200 if l < NL - 1 else 100)
        osT = c8.f32([4, D])
        xsT = c8.f32([4, D])
        gS = c8.f32([4, D])
        fq = c8.f32([4, 2])
        P.dma(lambda e: e.dma_start(out=osT[0], in_=rss_out[l].ap()), reads=[f"rss_out{l}"], writes=["osT"], semkey="osT")
        xss = xs_own if l == 0 else x1_d.ap()[1024:1028, :]
        P.dma(lambda e: e.dma_start(out=xsT[0], in_=xss), reads=(["x1_d"] if l > 0 else []), writes=["xsT"], eng="act")
        P.dma(lambda e: e.dma_start(out=gS[0], in_=smod_d.ap()[l, 2]), reads=["smod_d"], writes=["gS"], eng="act")
        P.dve(lambda e: e.tensor_tensor(out=osT[0], in0=osT[0], in1=gS[0], op=ALU.mult), reads=["osT", "gS"], writes=["osT"])
        P.dve(lambda e: e.tensor_tensor(out=xsT[0], in0=xsT[0], in1=osT[0], op=ALU.add), reads=["osT", "xsT"], writes=["xsT"])
        if l < NL - 1:
            P.dma(lambda e: e.dma_start(out=x1_d.ap()[1024:1028, :], in_=xsT[0]), reads=["xsT"], writes=["x1_d"], semkey="x1s")
            phase_N_samples(l + 1, xsT[0], "xsT", c8.off)
        else:
            P.act(lambda e: e.activation(out=osT[0], in_=xsT[0], func=AF.Square, accum_out=fq[0][:, 0:1]), reads=["xsT"], writes=["osT", "fq"])
            P.act(lambda e: e.activation(out=fq[0][:, 1:2], in_=fq[0][:, 0:1], func=AF.Sqrt, scale=1.0 / D, bias=epsc[0:4, 0:1]),
                  reads=["fq", "epsc"], writes=["fq"])
            P.dve(lambda e: e.reciprocal(out=fq[0][:, 1:2], in_=fq[0][:, 1:2]), reads=["fq"], writes=["fq"])
            P.act(lambda e: e.activation(out=osT[0], in_=xsT[0], func=AF.Copy, scale=fq[0][:, 1:2]), reads=["xsT", "fq"], writes=["osT"])
            P.dve(lambda e: e.tensor_tensor(out=osT[0], in0=osT[0], in1=fgB[0][0:4, :], op=ALU.mult), reads=["osT", "fgB"], writes=["osT"])
            P.dma(lambda e: e.dma_start(out=ys_own, in_=osT[0]), reads=["osT"], writes=["ys_own"], semkey="ysst")

    NLR = int(os.environ.get("MK_NL", NL))
    for l in range(NLR):
        P.epoch = l
        phase_M(l)
        reduce_scatter(l)
        if 'smp' not in os.environ.get('MK_SKIP', ''):
            phase_M_samples(l, WBIG[:, 0:16 * WCOLS].rearrange("p (k c) -> p k c", k=16), None)
        P.barrier()
        if STOP == f"R{l}":
            break
        phase_O(l)
        if STOP == f"O{l}a":
            break
        if l < NL - 1:
            allgather_h(l + 1)
        P.barrier()
        if STOP == f"O{l}":
            break
    print(f"[mk] sbuf bytes/partition = {sb_bytes[0]}", flush=True)
    P.emit()
    return nc


def prep_inputs(inp):
    f = lambda k: np.asarray(inp[k], dtype=np.float32)
    x_prompt, x_sample = f("x_prompt"), f("x_sample")
    consts = make_consts()
    w_in_full = f("w_in")
    maps = []
    for c in range(8):
        g, r = c // 4, c % 4
        ci = col_index(r)
        m = {}
        m["x_own"] = np.ascontiguousarray(x_prompt[g, r * 1024:(r + 1) * 1024])
        m["xs_own"] = np.ascontiguousarray(x_sample[16 * g + 4 * r:16 * g + 4 * r + 4, 0])
        cmat = np.concatenate([f("c_prompt")[g:g + 1], f("c_sample")[16 * g:16 * g + 16]], 0)
        m["cT"] = np.ascontiguousarray(cmat.T.reshape(16, 128, 17).transpose(1, 0, 2))
        m["wada"] = np.ascontiguousarray(f("w_ada")[:, :, r * 1536:(r + 1) * 1536])
        m["bada"] = np.ascontiguousarray(f("b_ada")[:, r * 1536:(r + 1) * 1536])
        m["ng_col"] = np.ascontiguousarray(f("norm_g").reshape(NL, 16, 128).transpose(2, 0, 1))
        m["ng_row"] = f("norm_g")
        m["fg_row"] = f("final_g").reshape(1, D)
        m["w_in"] = np.ascontiguousarray(w_in_full[:, :, ci])
        sh_cols = ci[3 * 128:10 * 128] - R_OFF
        m["mu_col"] = np.ascontiguousarray(f("mu_shift")[:, sh_cols].reshape(NL, 7, 128).transpose(0, 2, 1))
        ss = f("state_shift")[:, 16 * g:16 * g + 16][:, :, sh_cols]
        m["prev_s"] = np.ascontiguousarray(ss.reshape(NL, 16, 7, 128).transpose(0, 3, 2, 1))
        own = (np.arange(256) + 256 * r)
        pars = [f("w0"), f("a0"), f("k_k"), f("k_a"), f("r_k").reshape(NL, 1024), f("ln_w"), f("ln_b")]
        rp = np.stack([p_[:, own].reshape(NL, 2, 128) for p_ in pars], 2)
        m["rpar"] = np.ascontiguousarray(rp.transpose(0, 3, 2, 1))
        lw = np.concatenate([f("w_decay")[:, :, own], f("w_iclr")[:, :, own]], 1)
        m["loraw"] = np.ascontiguousarray(lw.reshape(NL, 128, 2, 128))
        sk = f("sinks")[:, 4 * r:4 * r + 4][:, [0, 2, 1, 3]]
        m["sinks_b"] = np.ascontiguousarray(np.broadcast_to(sk[:, None, :], (NL, 128, 4)))
        rows = np.concatenate([np.arange(256 * r, 256 * r + 256), np.arange(1024 + 256 * r, 1024 + 256 * r + 256)])
        m["w_out"] = np.ascontiguousarray(f("w_out")[:, rows, :])
        m["ck_in"] = np.ascontiguousarray(f("cache_k")[:, 16 * g:16 * g + 16, :, r, :])
        m["cv_in"] = np.ascontiguousarray(f("cache_v")[:, 16 * g:16 * g + 16, :, r, :])
        m["st_in"] = np.ascontiguousarray(f("state_wkv")[:, 16 * g:16 * g + 16, 4 * r:4 * r + 4])
        sel = np.zeros((16, 4), np.float32)
        for si in range(4):
            sel[4 * r + si, si] = 1.0
        m["selT"] = sel
        hsel = np.arange(4) + 4 * r
        lw_ = f("ln_w").reshape(NL, 16, 64)[:, hsel]
        lb_ = f("ln_b").reshape(NL, 16, 64)[:, hsel]
        rk_ = f("r_k")[:, hsel]
        sk_ = f("sinks")[:, hsel][:, :, None]
        shp_ = np.concatenate([lw_, lb_, rk_, sk_], 2)
        m["shpar"] = np.ascontiguousarray(np.broadcast_to(shp_[:, :, None], (NL, 4, 16, 193)).reshape(NL, 64, 193))
        for k, v_ in consts.items():
            m["c_" + k] = v_
        maps.append(m)
    return consts, maps


def assemble(res):
    y_prompt = np.zeros((2, SEQ, D), np.float32)
    y_sample = np.zeros((32, 1, D), np.float32)
    ckp = np.zeros((NL, 2, 128, 4, 64), np.float32)
    cvp = np.zeros_like(ckp)
    swp = np.zeros((NL, 2, 16, 64, 64), np.float32)
    shp = np.zeros((NL, 2, 3200), np.float32)
    cks = np.zeros((NL, 32, 128, 4, 64), np.float32)
    cvs = np.zeros_like(cks)
    sws = np.zeros((NL, 32, 16, 64, 64), np.float32)
    shs = np.zeros((NL, 32, 3200), np.float32)
    for c in range(8):
        g, r = c // 4, c % 4
        o = res[c]
        ci = col_index(r)
        sh_cols = ci[3 * 128:10 * 128] - R_OFF
        y_prompt[g, r * 1024:(r + 1) * 1024] = o["y_own"]
        y_sample[16 * g + 4 * r:16 * g + 4 * r + 4, 0] = o["ys_own"]
        ckp[:, g, :, r, :] = o["ckp"]
        cvp[:, g, :, r, :] = o["cvp"]
        t = o["swp"].reshape(NL, 2, 64, 2, 64)
        swp[:, g, 4 * r:4 * r + 4] = t.transpose(0, 3, 1, 4, 2).reshape(NL, 4, 64, 64)
        shp[:, g, sh_cols] = o["shp"].transpose(0, 2, 1).reshape(NL, 896)
        cks[:, 16 * g:16 * g + 16, :, r, :] = o["cks"]
        cvs[:, 16 * g:16 * g + 16, :, r, :] = o["cvs"]
        sws[:, 16 * g:16 * g + 16, 4 * r:4 * r + 4] = o["sws"]
        shs[:, 16 * g:16 * g + 16][:, :, sh_cols] = o["shs"].transpose(0, 3, 2, 1).reshape(NL, 16, 896)
    return (y_prompt, y_sample, ckp, cvp, swp, shp, cks, cvs, sws, shs)


def kernel(**inputs):
    consts, maps = prep_inputs(inputs)
    nc = build(consts)
    res = run_bass_kernel_spmd(nc, maps, core_ids=list(range(8)))
    global LAST_RES
    LAST_RES = res.results
    return assemble(res.results)
```
